# Optimizing a Trainium2 kernel written in Bass

```python
import math
import jax, jax.numpy as jnp
from jax import lax
import numpy as np

D_MODEL = 1024
BATCH = 8
SEQ = 2048
DEPTH = 2
DEC_BATCH = 16
DEC_SEQ = 16
PAST_LEN = 1024

CHUNK = 64
N_HEADS = 8
HEAD_DIM = 64
V_DIM = 2 * HEAD_DIM
W_ATTN = N_HEADS * V_DIM
QK_COLS = 2 * N_HEADS * HEAD_DIM
GROUP_CH = 16
W_SSM = 1024
N_GROUPS = W_SSM // GROUP_CH
STATE_DIM = 64
Q_BLOCK = 128
EPS = 1e-6
IN_COLS = 2 * QK_COLS + 2 * W_ATTN + 2 * W_SSM + 2 * D_MODEL
IN_SPLITS = (QK_COLS, 2 * QK_COLS, 2 * QK_COLS + W_ATTN, 2 * QK_COLS + 2 * W_ATTN,
             2 * QK_COLS + 2 * W_ATTN + W_SSM, 2 * QK_COLS + 2 * W_ATTN + 2 * W_SSM)

kernel_name = "hybrid_diffattn_s5_streaming_step"


def rmsnorm(x, g):
    x32 = x.astype(jnp.float32)
    y = x32 * lax.rsqrt(jnp.mean(x32 * x32, axis=-1, keepdims=True) + EPS)
    return (y * g.astype(jnp.float32)).astype(x.dtype)


def alibi_slopes():
    return jnp.exp2(-8.0 * (jnp.arange(N_HEADS, dtype=jnp.float32) + 1.0) / N_HEADS)


def diff_attn_core(q, k, v, q_pos, k_pos, lam, slopes):
    s = jnp.einsum('bqmhd,bkmhd->bmhqk', q, k.astype(jnp.float32))
    dist = jnp.abs(q_pos[:, None] - k_pos[None, :]).astype(jnp.float32)
    allowed = (k_pos[None, :] // CHUNK) <= (q_pos[:, None] // CHUNK)
    s = s - slopes[:, None, None] * dist
    s = jnp.where(allowed, s, -jnp.inf)
    p = jax.nn.softmax(s, axis=-1)
    w = p[:, 0] - lam * p[:, 1]
    return jnp.einsum('bhqk,bkhe->bqhe', w, v.astype(jnp.float32))


def attn_prompt(q, k, v, lam, slopes):
    b, t = q.shape[0], q.shape[1]
    nb = t // Q_BLOCK
    qb = q.reshape(b, nb, Q_BLOCK, 2, N_HEADS, HEAD_DIM).transpose(1, 0, 2, 3, 4, 5)
    k_pos = jnp.arange(t)

    def one_block(args):
        q_blk, i = args
        q_pos = i * Q_BLOCK + jnp.arange(Q_BLOCK)
        return diff_attn_core(q_blk, k, v, q_pos, k_pos, lam, slopes)

    o = lax.map(one_block, (qb, jnp.arange(nb)))
    return o.transpose(1, 0, 2, 3, 4).reshape(b, t, N_HEADS, V_DIM)


def attn_sample(q, k, v, k_past, v_past, lam, slopes):
    t = q.shape[1]
    p_len = k_past.shape[1]
    k_all = jnp.concatenate([k_past.astype(k.dtype), k], axis=1)
    v_all = jnp.concatenate([v_past.astype(v.dtype), v], axis=1)
    q_pos = p_len + jnp.arange(t)
    k_pos = jnp.arange(p_len + t)
    return diff_attn_core(q, k_all, v_all, q_pos, k_pos, lam, slopes)


def s5_scan(u, h0_re, h0_im, lam_re, lam_im, log_dt, b_re, b_im, c_re, c_im, d_skip):
    b, t, _ = u.shape
    f32 = jnp.float32
    u32 = u.astype(f32).reshape(b, t, N_GROUPS, GROUP_CH)
    lr, li = lam_re.astype(f32), lam_im.astype(f32)
    dt = jnp.exp(log_dt.astype(f32))[:, None]
    mag = jnp.exp(lr * dt)
    ang = li * dt
    a_re, a_im = mag * jnp.cos(ang), mag * jnp.sin(ang)
    z_re, z_im = a_re - 1.0, a_im
    den = lr * lr + li * li
    w_re = (z_re * lr + z_im * li) / den
    w_im = (z_im * lr - z_re * li) / den
    br, bi = b_re.astype(f32), b_im.astype(f32)
    bb_re = w_re[..., None] * br - w_im[..., None] * bi
    bb_im = w_re[..., None] * bi + w_im[..., None] * br
    bu_re = jnp.einsum('btgn,gpn->btgp', u32, bb_re)
    bu_im = jnp.einsum('btgn,gpn->btgp', u32, bb_im)
    h0r, h0i = h0_re.astype(f32), h0_im.astype(f32)
    bu_re = bu_re.at[:, 0].add(a_re * h0r - a_im * h0i)
    bu_im = bu_im.at[:, 0].add(a_re * h0i + a_im * h0r)
    a_re_t = jnp.broadcast_to(a_re, (1, t, N_GROUPS, STATE_DIM))
    a_im_t = jnp.broadcast_to(a_im, (1, t, N_GROUPS, STATE_DIM))

    def combine(e1, e2):
        a1r, a1i, b1r, b1i = e1
        a2r, a2i, b2r, b2i = e2
        return (a1r * a2r - a1i * a2i, a1r * a2i + a1i * a2r,
                a2r * b1r - a2i * b1i + b2r, a2r * b1i + a2i * b1r + b2i)

    _, _, hr, hi = lax.associative_scan(combine, (a_re_t, a_im_t, bu_re, bu_im), axis=1)
    y = (jnp.einsum('btgp,gnp->btgn', hr, c_re.astype(f32))
         - jnp.einsum('btgp,gnp->btgn', hi, c_im.astype(f32)))
    y = y.reshape(b, t, W_SSM) + d_skip.astype(f32) * u32.reshape(b, t, W_SSM)
    return y.astype(u.dtype), hr[:, -1], hi[:, -1]


def mixer_layer(x, k_past, v_past, h0_re, h0_im, layer_idx,
                norm_g, w_in, q_norm_g, k_norm_g, lam_q1, lam_k1, lam_q2, lam_k2, subln_g,
                w_attn_proj, lam_re, lam_im, log_dt, b_re, b_im, c_re, c_im, d_skip,
                w_glu, b_glu, w_ssm_proj, w_out):
    b, t, _ = x.shape
    f32 = jnp.float32
    xn = rmsnorm(x, norm_g)
    h = xn @ w_in
    q, k, v, ga, u, gs, gl = jnp.split(h, IN_SPLITS, axis=-1)
    q = q.reshape(b, t, 2, N_HEADS, HEAD_DIM)
    k = k.reshape(b, t, 2, N_HEADS, HEAD_DIM)
    v = v.reshape(b, t, N_HEADS, V_DIM)
    q = rmsnorm(q, q_norm_g).astype(f32) * (HEAD_DIM ** -0.5)
    k = rmsnorm(k, k_norm_g)
    lam_init = 0.8 - 0.6 * math.exp(-0.3 * layer_idx)
    lam = (jnp.exp(jnp.sum(lam_q1.astype(f32) * lam_k1.astype(f32)))
           - jnp.exp(jnp.sum(lam_q2.astype(f32) * lam_k2.astype(f32))) + lam_init)
    slopes = alibi_slopes()
    if k_past is None:
        o = attn_prompt(q, k, v, lam, slopes)
    else:
        o = attn_sample(q, k, v, k_past, v_past, lam, slopes)
    o = rmsnorm(o, subln_g) * (1.0 - lam_init)
    o = o.reshape(b, t, W_ATTN).astype(x.dtype) * jax.nn.silu(ga)
    ya = o @ w_attn_proj
    ys, hr, hi = s5_scan(u, h0_re, h0_im, lam_re, lam_im, log_dt, b_re, b_im, c_re, c_im, d_skip)
    ys = jax.nn.gelu(ys)
    ys = ys * jax.nn.sigmoid(ys @ w_glu + b_glu)
    ys = (ys * jax.nn.silu(gs)) @ w_ssm_proj
    g_a, g_s = jnp.split(gl, 2, axis=-1)
    merged = jax.nn.sigmoid(g_a) * ya + jax.nn.sigmoid(g_s) * ys
    x = x + merged @ w_out
    return x, k, v, hr, hi


def setup_inputs(seed: int = 0) -> dict:
    key = jax.random.key(seed)
    ks = jax.random.split(key, 32)
    f32 = jnp.float32
    nrm = lambda k, s, sc: sc * jax.random.normal(k, s, f32)
    lam_im = (jnp.pi * jnp.arange(STATE_DIM, dtype=f32))[None, None, :] + nrm(ks[11], (DEPTH, N_GROUPS, STATE_DIM), 0.01)
    return {
        "x_prompt": nrm(ks[0], (BATCH, SEQ, D_MODEL), 1.0),
        "x_sample": nrm(ks[1], (DEC_BATCH, DEC_SEQ, D_MODEL), 1.0),
        "cache_k": nrm(ks[2], (DEPTH, DEC_BATCH, PAST_LEN, 2, N_HEADS, HEAD_DIM), 1.0),
        "cache_v": nrm(ks[3], (DEPTH, DEC_BATCH, PAST_LEN, N_HEADS, V_DIM), 1.0),
        "state_ssm_re": nrm(ks[4], (DEPTH, DEC_BATCH, N_GROUPS, STATE_DIM), 0.5),
        "state_ssm_im": nrm(ks[5], (DEPTH, DEC_BATCH, N_GROUPS, STATE_DIM), 0.5),
        "norm_g": 1.0 + nrm(ks[6], (DEPTH, D_MODEL), 0.02),
        "w_in": nrm(ks[7], (DEPTH, D_MODEL, IN_COLS), D_MODEL ** -0.5),
        "q_norm_g": 1.0 + nrm(ks[8], (DEPTH, HEAD_DIM), 0.02),
        "k_norm_g": 1.0 + nrm(ks[9], (DEPTH, HEAD_DIM), 0.02),
        "lam_q1": nrm(ks[12], (DEPTH, HEAD_DIM), 0.1),
        "lam_k1": nrm(ks[13], (DEPTH, HEAD_DIM), 0.1),
        "lam_q2": nrm(ks[14], (DEPTH, HEAD_DIM), 0.1),
        "lam_k2": nrm(ks[15], (DEPTH, HEAD_DIM), 0.1),
        "subln_g": 1.0 + nrm(ks[16], (DEPTH, V_DIM), 0.02),
        "w_attn_proj": nrm(ks[17], (DEPTH, W_ATTN, D_MODEL), W_ATTN ** -0.5),
        "ssm_lambda_re": -0.5 + nrm(ks[10], (DEPTH, N_GROUPS, STATE_DIM), 0.01),
        "ssm_lambda_im": lam_im,
        "ssm_log_dt": jax.random.uniform(ks[18], (DEPTH, N_GROUPS), f32, math.log(1e-3), math.log(1e-1)),
        "ssm_b_re": nrm(ks[19], (DEPTH, N_GROUPS, STATE_DIM, GROUP_CH), (2.0 * GROUP_CH) ** -0.5),
        "ssm_b_im": nrm(ks[20], (DEPTH, N_GROUPS, STATE_DIM, GROUP_CH), (2.0 * GROUP_CH) ** -0.5),
        "ssm_c_re": nrm(ks[21], (DEPTH, N_GROUPS, GROUP_CH, STATE_DIM), (2.0 * STATE_DIM) ** -0.5),
        "ssm_c_im": nrm(ks[22], (DEPTH, N_GROUPS, GROUP_CH, STATE_DIM), (2.0 * STATE_DIM) ** -0.5),
        "ssm_d": nrm(ks[23], (DEPTH, W_SSM), 1.0),
        "w_glu": nrm(ks[24], (DEPTH, W_SSM, W_SSM), W_SSM ** -0.5),
        "b_glu": nrm(ks[25], (DEPTH, W_SSM), 0.02),
        "w_ssm_proj": nrm(ks[26], (DEPTH, W_SSM, D_MODEL), W_SSM ** -0.5),
        "w_out": nrm(ks[27], (DEPTH, D_MODEL, D_MODEL), D_MODEL ** -0.5),
    }


def reference(x_prompt, x_sample, cache_k, cache_v, state_ssm_re, state_ssm_im,
              norm_g, w_in, q_norm_g, k_norm_g, lam_q1, lam_k1, lam_q2, lam_k2, subln_g,
              w_attn_proj, ssm_lambda_re, ssm_lambda_im, ssm_log_dt, ssm_b_re, ssm_b_im,
              ssm_c_re, ssm_c_im, ssm_d, w_glu, b_glu, w_ssm_proj, w_out):
    yp, ys = x_prompt, x_sample
    kp_l, vp_l, hrp_l, hip_l = [], [], [], []
    ks_l, vs_l, hrs_l, his_l = [], [], [], []
    zeros_h = jnp.zeros((x_prompt.shape[0], N_GROUPS, STATE_DIM), jnp.float32)
    for l in range(DEPTH):
        lw = (norm_g[l], w_in[l], q_norm_g[l], k_norm_g[l], lam_q1[l], lam_k1[l], lam_q2[l], lam_k2[l],
              subln_g[l], w_attn_proj[l], ssm_lambda_re[l], ssm_lambda_im[l], ssm_log_dt[l],
              ssm_b_re[l], ssm_b_im[l], ssm_c_re[l], ssm_c_im[l], ssm_d[l], w_glu[l], b_glu[l],
              w_ssm_proj[l], w_out[l])
        yp, kp, vp, hrp, hip = mixer_layer(yp, None, None, zeros_h, zeros_h, l, *lw)
        ys, kn, vn, hrs, his = mixer_layer(ys, cache_k[l], cache_v[l], state_ssm_re[l], state_ssm_im[l], l, *lw)
        kp_l.append(kp); vp_l.append(vp); hrp_l.append(hrp); hip_l.append(hip)
        ks_l.append(kn); vs_l.append(vn); hrs_l.append(hrs); his_l.append(his)
    return (yp, ys,
            jnp.stack(kp_l), jnp.stack(vp_l), jnp.stack(hrp_l), jnp.stack(hip_l),
            jnp.stack(ks_l), jnp.stack(vs_l), jnp.stack(hrs_l), jnp.stack(his_l))
```

```python
import math
import numpy as np
import concourse.bass as bass
import concourse.mybir as mybir
from concourse.bass_utils import run_bass_kernel_spmd

F32 = mybir.dt.float32
BF16 = mybir.dt.bfloat16
I32 = mybir.dt.int32
ALU = mybir.AluOpType
AF = mybir.ActivationFunctionType
AX = mybir.AxisListType

D = 1024
T = 2048
NS = 16
PAST = 1024
NT = T + 2 * NS
H = 8
DEPTH = 2
EPS = 1e-6
INC = 8192
NCORES = 8
SLOPES = [2.0 ** (-(h + 1)) for h in range(H)]


class Tr:
    def __init__(self, nc):
        self.nc = nc
        self.eng = {}
        for name, h in (("pe", nc.tensor), ("act", nc.scalar), ("dve", nc.vector), ("pool", nc.gpsimd), ("sp", nc.sync)):
            self.eng[name] = dict(h=h, sem=nc.alloc_semaphore(name="s_" + name), cnt=0, seen={})
        self.lastw = {}
        self.readers = {}
        self.dsem = {}
        self.out_tokens = []

    def _deps(self, reads, writes):
        deps = []
        for k in reads:
            if k in self.lastw:
                deps.append(self.lastw[k])
        for k in writes:
            if k in self.lastw:
                deps.append(self.lastw[k])
            deps.extend(self.readers.get(k, ()))
        return deps

    def _wait(self, ename, deps):
        e = self.eng[ename]
        best = {}
        for (sname, sem, val) in deps:
            if ename == "pe" and sname == "s_pe":
                continue
            if val > best.get(sname, (None, 0))[1]:
                best[sname] = (sem, val)
        for sname, (sem, val) in best.items():
            if e["seen"].get(sname, 0) < val:
                e["h"].wait_ge(sem, val)
                e["seen"][sname] = val

    def _commit(self, tok, reads, writes):
        for k in writes:
            self.lastw[k] = tok
            self.readers[k] = []
        for k in reads:
            self.readers.setdefault(k, []).append(tok)

    def op(self, ename, fn, reads=(), writes=()):
        e = self.eng[ename]
        self._wait(ename, self._deps(reads, writes))
        ins = fn(e["h"])
        e["cnt"] += 1
        ins.then_inc(e["sem"], 1)
        self._commit(("s_" + ename, e["sem"], e["cnt"]), reads, writes)

    def mms(self, lst, reads=(), writes=()):
        e = self.eng["pe"]
        self._wait("pe", self._deps(reads, writes))
        ins = None
        for (out, lhsT, rhs, kw) in lst:
            ins = self.nc.tensor.matmul(out, lhsT=lhsT, rhs=rhs, **kw)
        e["cnt"] += 1
        ins.then_inc(e["sem"], 1)
        self._commit(("s_pe", e["sem"], e["cnt"]), reads, writes)

    def dma(self, q, slot, out, in_, reads=(), writes=(), is_out=False):
        if slot.startswith("o_") or slot == "dbg":
            slot = "st_" + "_".join(str(x) for x in (reads[0] if isinstance(reads[0], tuple) else (reads[0],))) if reads else slot
        else:
            slot = "ld_" + "_".join(str(x) for x in (writes[0] if isinstance(writes[0], tuple) else (writes[0],)))
        if slot not in self.dsem:
            self.dsem[slot] = [self.nc.alloc_semaphore(name="d_" + slot), 0]
        ds = self.dsem[slot]
        self._wait(q, self._deps(reads, writes))
        self.eng[q]["h"].dma_start(out=out, in_=in_, allow_slow_non_contiguous=True).then_inc(ds[0], 16)
        ds[1] += 16
        tok = ("d_" + slot, ds[0], ds[1])
        self._commit(tok, reads, writes)
        if is_out:
            self.out_tokens.append(tok)

    def barrier(self):
        toks = [("s_" + n, e["sem"], e["cnt"]) for n, e in self.eng.items() if e["cnt"] > 0]
        toks += [("d_" + k, v[0], v[1]) for k, v in self.dsem.items() if v[1] > 0]
        for n in self.eng:
            e = self.eng[n]
            for (sname, sem, val) in toks:
                if sname == "s_" + n:
                    continue
                if e["seen"].get(sname, 0) < val:
                    e["h"].wait_ge(sem, val)
                    e["seen"][sname] = val
        self.lastw = {}
        self.readers = {}

    def finish(self):
        best = {}
        for (sname, sem, val) in self.out_tokens:
            if val > best.get(sname, (None, 0))[1]:
                best[sname] = (sem, val)
        for sname, (sem, val) in best.items():
            self.nc.sync.wait_ge(sem, val)


def dap(t, off, dims):
    return bass.AP(t, off, [list(d) for d in dims])


class StopBuild(Exception):
    pass


def build(stop=None):
    nc = bass.Bass("TRN2", target_bir_lowering=False)
    dt_in = lambda n, s: nc.dram_tensor(n, s, F32, kind="ExternalInput")
    dt_out = lambda n, s: nc.dram_tensor(n, s, F32, kind="ExternalOutput")
    x_p = dt_in("x_p", [T, D]); x_s = dt_in("x_s", [2, NS, D])
    ck = dt_in("ck", [DEPTH, 2, PAST, 2, H, 64]); cv = dt_in("cv", [DEPTH, 2, PAST, H, 128])
    sre = dt_in("sre", [DEPTH, 2, 64, 64]); sim = dt_in("sim", [DEPTH, 2, 64, 64])
    norm_g = dt_in("norm_g", [DEPTH, D]); w_in = dt_in("w_in", [DEPTH, D, INC])
    qng = dt_in("qng", [DEPTH, 64]); kng = dt_in("kng", [DEPTH, 64])
    lq1 = dt_in("lq1", [DEPTH, 64]); lk1 = dt_in("lk1", [DEPTH, 64]); lq2 = dt_in("lq2", [DEPTH, 64]); lk2 = dt_in("lk2", [DEPTH, 64])
    subg = dt_in("subg", [DEPTH, 128]); w_ap = dt_in("w_ap", [DEPTH, D, D])
    lamr = dt_in("lamr", [DEPTH, 64, 64]); lami = dt_in("lami", [DEPTH, 64, 64]); ldt = dt_in("ldt", [DEPTH, 64])
    bre = dt_in("bre", [DEPTH, 64, 64, 16]); bim = dt_in("bim", [DEPTH, 64, 64, 16])
    cre = dt_in("cre", [DEPTH, 64, 16, 64]); cim = dt_in("cim", [DEPTH, 64, 16, 64])
    dsk = dt_in("dsk", [DEPTH, D]); w_glu = dt_in("w_glu", [DEPTH, D, D]); b_glu = dt_in("b_glu", [DEPTH, D])
    w_sp = dt_in("w_sp", [DEPTH, D, D]); w_o = dt_in("w_o", [DEPTH, D, D])
    c_id = dt_in("c_id", [128, 128]); c_dm = dt_in("c_dm", [128, H, 128])
    c_qa = dt_in("c_qa", [4, H, T]); c_ka = dt_in("c_ka", [4, H, T]); c_par = dt_in("c_par", [128, 2]); c_io = dt_in("c_io", [128, 128])
    y_p = dt_out("y_p", [T, D]); y_s = dt_out("y_s", [2, NS, D])
    nk_p = dt_out("nk_p", [DEPTH, T, 2, H, 64]); nv_p = dt_out("nv_p", [DEPTH, T, H, 128])
    hr_p = dt_out("hr_p", [DEPTH, 64, 64]); hi_p = dt_out("hi_p", [DEPTH, 64, 64])
    nk_s = dt_out("nk_s", [DEPTH, 2, NS, 2, H, 64]); nv_s = dt_out("nv_s", [DEPTH, 2, NS, H, 128])
    hr_s = dt_out("hr_s", [DEPTH, 2, 64, 64]); hi_s = dt_out("hi_s", [DEPTH, 2, 64, 64])
    xmid = nc.dram_tensor("xmid", [NT, D], F32, kind="Internal")

    tr = Tr(nc)
    import contextlib

    ucnt = [0]

    def SBT(es, n, s, d=F32):
        ucnt[0] += 1
        return es.enter_context(nc.sbuf_tensor("%s_%d" % (n, ucnt[0]), s, d))

    def stop_here(name, dumps):
        if stop != name:
            return
        tr.barrier()
        for dn, ap in dumps.items():
            shp = list(ap.shape)
            dtn = nc.dram_tensor("dbg_" + dn, shp, ap.dtype, kind="ExternalOutput")
            tr.dma("sp", "dbg", dtn.ap(), ap, (), [("dbg", dn)], is_out=True)
        raise StopBuild()

    ges = contextlib.ExitStack()
    ps = [ges.enter_context(nc.psum_tensor("ps%d" % i, [128, 512], F32)) for i in range(7)]
    psT = ges.enter_context(nc.psum_tensor("psT", [128, 1024], BF16))
    xnT = SBT(ges, "xnT", [128, 8, NT], BF16)
    ident = SBT(ges, "ident", [128, 128], BF16)
    dmat = SBT(ges, "dmat", [128, H, 128], BF16)
    ones64 = SBT(ges, "ones64", [64, 64], BF16)
    par = SBT(ges, "par", [128, 2])
    epsc = SBT(ges, "epsc", [128, 1])
    sm = SBT(ges, "sm", [128, 16])
    pp = SBT(ges, "pp", [128, 32])
    lv = SBT(ges, "lv", [128, 4, 64])
    subB = SBT(ges, "subB", [128, 128])
    identF = SBT(ges, "identF", [128, 128])
    stg = SBT(ges, "stg", [64, 64]); stg2 = SBT(ges, "stg2", [64, 64]); stg3 = SBT(ges, "stg3", [64, 64])
    prow = SBT(ges, "prow", [32, 128]); dtb = SBT(ges, "dtb", [128, 64])

    def ACT(out, in_, func, reads, writes, **kw):
        tr.op("act", lambda e: e.activation(out=out, in_=in_, func=func, **kw), reads, writes)

    def VTT(out, a, b, op, reads, writes, eng="dve"):
        tr.op(eng, lambda e: e.tensor_tensor(out=out, in0=a, in1=b, op=op), reads, writes)

    def VTS(out, a, s1, s2, op0, op1, reads, writes, eng="dve"):
        if op1 is None:
            tr.op(eng, lambda e: e.tensor_scalar(out=out, in0=a, scalar1=s1, scalar2=None, op0=op0), reads, writes)
        else:
            tr.op(eng, lambda e: e.tensor_scalar(out=out, in0=a, scalar1=s1, scalar2=s2, op0=op0, op1=op1), reads, writes)

    def STT(out, a, s, b, op0, op1, reads, writes, eng="dve"):
        tr.op(eng, lambda e: e.scalar_tensor_tensor(out=out, in0=a, scalar=s, in1=b, op0=op0, op1=op1), reads, writes)

    def VCP(out, in_, reads, writes, eng="dve"):
        tr.op(eng, lambda e: e.tensor_copy(out=out, in_=in_), reads, writes)

    def ACP(out, in_, reads, writes):
        tr.op("act", lambda e: e.copy(out=out, in_=in_), reads, writes)

    def MEMSET(ap, val, writes, eng="dve"):
        tr.op(eng, lambda e: e.memset(ap, val), (), writes)

    def RSUM(out, in_, reads, writes):
        tr.op("dve", lambda e: e.reduce_sum(out=out, in_=in_, axis=AX.X), reads, writes)

    def RSQ(out, in_, scale, n, reads, writes):
        ACT(out, in_, AF.Ln, list(reads) + ["epsc"], writes, scale=scale, bias=epsc[0:n, :])
        ACT(out, out, AF.Exp, writes, writes, scale=-0.5)

    def TRANSP(out, in_, n, reads, writes):
        tr.op("pe", lambda e: e.transpose(out=out, in_=in_, identity=ident[0:n, 0:n]), list(reads) + ["ident"], writes)

    tr.dma("pool", "c0", ident[:], c_id.ap(), (), ["ident"])
    tr.dma("sp", "c1", identF[:], c_id.ap(), (), ["identF"])

    def TRANSPF(out, in_, n, reads, writes):
        tr.op("pe", lambda e: e.transpose(out=out, in_=in_, identity=identF[0:n, 0:n]), list(reads) + ["identF"], writes)

    def load_GP(dst, dkey, tensor, off):
        tr.dma("sp", "stg", stg[:], dap(tensor, off, [[64, 64], [1, 64]]), (), ["stg"])
        TRANSPF(ps[4][0:64, 0:64], stg[:], 64, ["stg"], ["ps4"])
        VCP(dst[0:64, :], ps[4][0:64, 0:64:2], ["ps4"], [dkey])
        VCP(dst[64:128, :], ps[4][0:64, 1:64:2], ["ps4"], [dkey])

    def store_GP(tensor, off, src, skey, okey):
        VCP(stg2[:, 0:64:2], src[0:64, :], [skey], ["stg2"])
        VCP(stg2[:, 1:64:2], src[64:128, :], [skey], ["stg2"])
        TRANSPF(ps[4][0:64, 0:64], stg2[:], 64, ["stg2"], ["ps4"])
        VCP(stg3[:], ps[4][0:64, 0:64], ["ps4"], ["stg3"])
        tr.dma("sp", "o_h", dap(tensor, off, [[64, 64], [1, 64]]), stg3[:], ["stg3"], [okey], is_out=True)
    tr.dma("pool", "c0", dmat[:], c_dm.ap(), (), ["dmat"])
    tr.dma("sp", "c1", par[:], c_par.ap(), (), ["par"])
    MEMSET(ones64[:], 1.0, ["ones64"])
    MEMSET(epsc[:], EPS, ["epsc"])

    tiles = [("p", i, 128, 128 * i) for i in range(16)] + [("s", b, NS, T + NS * b) for b in range(2)]
    blocks = [(512 * i, 512) for i in range(4)] + [(T, 2 * NS)]
    TWO_PI = 2.0 * math.pi

    try:
      for l in range(DEPTH):
        lam_init = 0.8 - 0.6 * math.exp(-0.3 * l)
        with contextlib.ExitStack() as es:
            ngb = SBT(es, "ngb", [128, D]); xt = [SBT(es, "xt%d" % i, [128, D]) for i in range(2)]
            scr = SBT(es, "scr", [128, D]); xnb = SBT(es, "xnb", [128, D], BF16)
            tr.dma("sp", "c1", ngb[:], dap(norm_g, l * D, [[0, 128], [1, D]]), (), ["ngb"])
            MEMSET(prow[:], 0.0, ["prow"])
            tr.dma("sp", "c1", prow[0:1, :].rearrange("p (a b) -> p a b", b=64), dap(qng, l * 64, [[64, 1], [0, 2], [1, 64]]), (), ["prow"])
            tr.dma("sp", "c1", prow[1:2, :].rearrange("p (a b) -> p a b", b=64), dap(kng, l * 64, [[64, 1], [0, 2], [1, 64]]), (), ["prow"])
            tr.dma("sp", "c1", prow[3:11, :], dap(b_glu, l * D, [[128, 8], [1, 128]]), (), ["prow"])
            tr.dma("sp", "c1", prow[11:19, :], dap(dsk, l * D, [[128, 8], [1, 128]]), (), ["prow"])
            TRANSPF(ps[4][:, 0:32], prow[:], 32, ["prow"], ["ps4"])
            VCP(pp[:], ps[4][:, 0:32], ["ps4"], ["pp0", "pp1", "pp2", "pp3", "pp11"])
            VTS(pp[:, 0:1], pp[:, 0:1], 0.125, None, ALU.mult, None, ["pp0"], ["pp0"])
            for j, tnsr in enumerate((lq1, lk1, lq2, lk2)):
                tr.dma("sp", "c1", lv[:, j, :], dap(tnsr, l * 64, [[0, 128], [1, 64]]), (), ["lv%d" % j])
            VTT(lv[:, 0, :], lv[:, 0, :], lv[:, 1, :], ALU.mult, ["lv0", "lv1"], ["lv0"])
            VTT(lv[:, 2, :], lv[:, 2, :], lv[:, 3, :], ALU.mult, ["lv2", "lv3"], ["lv2"])
            RSUM(sm[:, 0:1], lv[:, 0, :], ["lv0"], ["sm0"])
            RSUM(sm[:, 1:2], lv[:, 2, :], ["lv2"], ["sm1"])
            ACT(sm[:, 0:1], sm[:, 0:1], AF.Exp, ["sm0"], ["sm0"])
            ACT(sm[:, 1:2], sm[:, 1:2], AF.Exp, ["sm1"], ["sm1"])
            VTT(sm[:, 0:1], sm[:, 1:2], sm[:, 0:1], ALU.subtract, ["sm0", "sm1"], ["sm0"])
            VTS(pp[:, 2:3], sm[:, 0:1], -lam_init, None, ALU.add, None, ["sm0"], ["pp2"])
            tr.dma("sp", "c1", subB[:], dap(subg, l * 128, [[0, 128], [1, 128]]), (), ["subB"])
            VTS(subB[:], subB[:], 1.0 - lam_init, None, ALU.mult, None, ["subB"], ["subB"])
            for ti, (kind, i, n, c0) in enumerate(tiles):
                xb = xt[ti % 2]; xk = "xt%d" % (ti % 2)
                if l == 0:
                    src = x_p.ap()[128 * i:128 * i + 128, :] if kind == "p" else x_s.ap()[i]
                else:
                    src = xmid.ap()[c0:c0 + n, :]
                tr.dma("sp", xk, xb[0:n, :], src, [("xmid", ti)], [xk])
                VTT(scr[0:n, :], xb[0:n, :], xb[0:n, :], ALU.mult, [xk], ["scr"])
                RSUM(sm[0:n, 2:3], scr[0:n, :], ["scr"], ["sm2"])
                RSQ(sm[0:n, 2:3], sm[0:n, 2:3], 1.0 / D, n, ["sm2"], ["sm2"])
                STT(xnb[0:n, :], xb[0:n, :], sm[0:n, 2:3], ngb[0:n, :], ALU.mult, ALU.mult, [xk, "sm2", "ngb"], ["xnb"])
                for kc in range(8):
                    TRANSP(psT[:, 128 * kc:128 * kc + n], xnb[0:n, 128 * kc:128 * kc + 128], n, ["xnb"], ["psT"])
                VCP(xnT[:, :, c0:c0 + n], psT[:].rearrange("p (k t) -> p k t", t=128)[:, :, 0:n], ["psT"], ["xnT"])
            tr.barrier()
            stop_here("p0_%d" % l, dict(xnT=xnT[:], pp=pp[:], subB=subB[:]))

        yes = contextlib.ExitStack()
        ygT = SBT(yes, "ygT", [128, 8, NT], BF16)

        with contextlib.ExitStack() as es:
            tabCb = SBT(es, "tabCb", [128, 32, 128], BF16); tabSb = SBT(es, "tabSb", [128, 32, 128], BF16)
            tabCl = SBT(es, "tabCl", [128, 2, 32]); tabSl = SBT(es, "tabSl", [128, 2, 32])
            SinT = SBT(es, "SinT", [128, 64, 128], BF16); Cst = SBT(es, "Cst", [128, 64, 128], BF16)
            P = SBT(es, "P", [128, 20, 32])
            PI = SBT(es, "PI", [128, 32], I32)
            Hc = SBT(es, "Hc", [128, 3, 2, 32])
            RH = SBT(es, "RH", [128, 2, 4])
            es2 = contextlib.ExitStack()
            tabC = SBT(es2, "tabC", [128, 32, 128]); tabS = SBT(es2, "tabS", [128, 32, 128])
            es3 = contextlib.ExitStack()
            tmpA = SBT(es3, "tmpA", [128, 32, 128]); tmpB = SBT(es3, "tmpB", [128, 32, 128], I32); io = SBT(es3, "io", [128, 128])
            LR, LI, DT, ER, ANG, RHO, AR, AI, S1, C1, WR, WI, DEN, TA, TB, TC = [P[:, i, :] for i in range(16)]
            pk = lambda i: "P%d" % i
            load_GP(LR, pk(0), lamr, l * 4096)
            load_GP(LI, pk(1), lami, l * 4096)
            tr.dma("sp", "c1", dtb[:], dap(ldt, l * 64, [[0, 128], [1, 64]]), (), ["dtb"])
            VCP(P[0:64, 2, :], dtb[0:64, 0:64:2], ["dtb"], [pk(2)])
            VCP(P[64:128, 2, :], dtb[64:128, 1:64:2], ["dtb"], [pk(2)])
            VTS(TC, DT, 1.0 / 8, None, ALU.mult, None, [pk(2)], [pk(15)])
            VTS(DT, TC, 1.0 / 14, 1.0, ALU.mult, ALU.add, [pk(15)], [pk(2)])
            for kk in range(13, 0, -1):
                VTT(DT, DT, TC, ALU.mult, [pk(2), pk(15)], [pk(2)])
                VTS(DT, DT, 1.0 / kk, 1.0, ALU.mult, ALU.add, [pk(2)], [pk(2)])
            for _ in range(3):
                VTT(DT, DT, DT, ALU.mult, [pk(2)], [pk(2)])
            VTT(ER, LR, DT, ALU.mult, [pk(0), pk(2)], [pk(3)])
            VTT(ANG, LI, DT, ALU.mult, [pk(1), pk(2)], [pk(4)])
            VTS(TA, ANG, 1.0 / TWO_PI, None, ALU.mult, None, [pk(4)], [pk(13)])
            VCP(PI[:], TA, [pk(13)], ["PI"])
            VCP(TA, PI[:], ["PI"], [pk(13)])
            STT(ANG, TA, -TWO_PI, ANG, ALU.mult, ALU.add, [pk(13), pk(4)], [pk(4)])
            VTS(ANG, ANG, -3.14159, 3.14159, ALU.max, ALU.min, [pk(4)], [pk(4)])
            VTS(RHO, ER, 1.0 / 8, 1.0, ALU.mult, ALU.add, [pk(3)], [pk(5)])
            for kk in (7, 6, 5, 4, 3, 2, 1):
                VTT(RHO, RHO, ER, ALU.mult, [pk(5), pk(3)], [pk(5)])
                VTS(RHO, RHO, 1.0 / kk, 1.0, ALU.mult, ALU.add, [pk(5)], [pk(5)])
            ACT(S1, ANG, AF.Sin, [pk(4)], [pk(8)])
            ACT(TB, ANG, AF.Sin, [pk(4)], [pk(14)], scale=0.5)
            VTT(TB, TB, TB, ALU.mult, [pk(14)], [pk(14)])
            VTS(C1, TB, -2.0, 1.0, ALU.mult, ALU.add, [pk(14)], [pk(9)])
            VTT(AR, RHO, C1, ALU.mult, [pk(5), pk(9)], [pk(6)])
            VTT(AI, RHO, S1, ALU.mult, [pk(5), pk(8)], [pk(7)])
            VTS(TA, AR, -1.0, None, ALU.add, None, [pk(6)], [pk(13)])
            VTT(DEN, LR, LR, ALU.mult, [pk(0)], [pk(12)])
            VTT(TB, LI, LI, ALU.mult, [pk(1)], [pk(14)])
            VTT(DEN, DEN, TB, ALU.add, [pk(12), pk(14)], [pk(12)])
            tr.op("dve", lambda e: e.reciprocal(out=DEN, in_=DEN), [pk(12)], [pk(12)])
            VTT(WR, TA, LR, ALU.mult, [pk(13), pk(0)], [pk(10)])
            VTT(TB, AI, LI, ALU.mult, [pk(7), pk(1)], [pk(14)])
            VTT(WR, WR, TB, ALU.add, [pk(10), pk(14)], [pk(10)])
            VTT(WR, WR, DEN, ALU.mult, [pk(10), pk(12)], [pk(10)])
            VTT(WI, AI, LR, ALU.mult, [pk(7), pk(0)], [pk(11)])
            VTT(TB, TA, LI, ALU.mult, [pk(13), pk(1)], [pk(14)])
            VTT(WI, WI, TB, ALU.subtract, [pk(11), pk(14)], [pk(11)])
            VTT(WI, WI, DEN, ALU.mult, [pk(11), pk(12)], [pk(11)])
            tr.dma("sp", "c1", io[:], c_io.ap(), (), ["io"])
            angb = ANG.unsqueeze(2).broadcast_to([128, 32, 128]); iob = io[:].unsqueeze(1).broadcast_to([128, 32, 128])
            VTT(tabS[:], angb, iob, ALU.mult, [pk(4), "io"], ["tabS"])

            def rred(tb, key):
                VTS(tmpA[:], tb[:], 1.0 / TWO_PI, None, ALU.mult, None, [key], ["tmpA"])
                VCP(tmpB[:], tmpA[:], ["tmpA"], ["tmpB"])
                VCP(tmpA[:], tmpB[:], ["tmpB"], ["tmpA"])
                STT(tb[:], tmpA[:], -TWO_PI, tb[:], ALU.mult, ALU.add, ["tmpA", key], [key])
                VTS(tb[:], tb[:], -3.14159, 3.14159, ALU.max, ALU.min, [key], [key])

            rred(tabS, "tabS")
            VTS(tabC[:], tabS[:], math.pi / 2, None, ALU.add, None, ["tabS"], ["tabC"])
            rred(tabC, "tabC")
            ACT(tabS[:], tabS[:], AF.Sin, ["tabS"], ["tabS"])
            ACT(tabC[:], tabC[:], AF.Sin, ["tabC"], ["tabC"])
            ACP(tabCb[:], tabC[:], ["tabC"], ["tabCb"])
            ACP(tabSb[:], tabS[:], ["tabS"], ["tabSb"])
            for j_, col_ in enumerate((127, NS - 1)):
                VCP(tabCl[:, j_, :], tabC[:, :, col_], ["tabC"], ["tabCl"])
                VCP(tabSl[:, j_, :], tabS[:, :, col_], ["tabS"], ["tabSl"])
            tr.barrier()
            es3.close()
            BBw = SBT(es2, "BBw", [128, 64, 128], BF16)
            Bn = SBT(es2, "Bn", [128, 2, 32, 16]); bb = SBT(es2, "bb", [128, 2, 32, 16]); btmp = SBT(es2, "btmp", [128, 32, 16])
            Cn = SBT(es2, "Cn", [128, 2, 8, 64]); Cw = SBT(es2, "Cw", [128, 2, 8, 128], BF16)
            tr.dma("sp", "c1", Bn[:, 0], dap(bre, l * 65536, [[16, 128], [2048, 32], [1, 16]]), (), ["Bn"])
            tr.dma("sp", "c1", Bn[:, 1], dap(bim, l * 65536, [[16, 128], [2048, 32], [1, 16]]), (), ["Bn"])
            wrb = WR.unsqueeze(2).broadcast_to([128, 32, 16]); wib = WI.unsqueeze(2).broadcast_to([128, 32, 16])
            VTT(bb[:, 0], Bn[:, 0], wrb, ALU.mult, ["Bn", pk(10)], ["bb"])
            VTT(btmp[:], Bn[:, 1], wib, ALU.mult, ["Bn", pk(11)], ["btmp"])
            VTT(bb[:, 0], bb[:, 0], btmp[:], ALU.subtract, ["bb", "btmp"], ["bb"])
            VTT(bb[:, 1], Bn[:, 1], wrb, ALU.mult, ["Bn", pk(10)], ["bb"])
            VTT(btmp[:], Bn[:, 0], wib, ALU.mult, ["Bn", pk(11)], ["btmp"])
            VTT(bb[:, 1], bb[:, 1], btmp[:], ALU.add, ["bb", "btmp"], ["bb"])
            MEMSET(BBw[:], 0.0, ["BBw"])
            BBw4 = BBw[:].rearrange("p (q r) c -> p q r c", r=2)
            for gs in range(2):
                for r in range(4):
                    for ri in range(2):
                        VCP(BBw4[64 * gs:64 * gs + 64, r::4, ri, 32 * r + 16 * gs:32 * r + 16 * gs + 16],
                            bb[64 * gs:64 * gs + 64, ri, r::4, :], ["bb", "BBw"], ["BBw"])
            for g4 in range(16):
                for j in range(4):
                    TRANSP(psT[:, 128 * j:128 * j + 128], BBw[:, 4 * g4 + j, :], 128, ["BBw"], ["psT"])
                VCP(SinT[:, 4 * g4:4 * g4 + 4, :], psT[:, 0:512].rearrange("p (a b) -> p a b", b=128), ["psT"], ["SinT"])
            tr.dma("sp", "c1", Cn[:, 0], dap(cre, l * 65536, [[64, 128], [8192, 8], [1, 64]]), (), ["Cn"])
            tr.dma("sp", "c1", Cn[:, 1], dap(cim, l * 65536, [[64, 128], [8192, 8], [1, 64]]), (), ["Cn"])
            VTS(Cn[:, 1], Cn[:, 1], -1.0, None, ALU.mult, None, ["Cn"], ["Cn"])
            for ri in range(2):
                for gs in range(2):
                    VTS(Cw[:, ri, :, 64 * gs:64 * gs + 64], Cn[:, ri], par[:, gs:gs + 1], None, ALU.mult, None, ["Cn", "par"], ["Cw"])
            MEMSET(Cst[:], 0.0, ["Cst"])
            Cst4 = Cst[:].rearrange("p (c r i) m -> p c r i m", r=4, i=2)
            for ri in range(2):
                for c4 in range(2):
                    for j in range(4):
                        TRANSP(psT[:, 128 * j:128 * j + 128], Cw[:, ri, 4 * c4 + j, :], 128, ["Cw"], ["psT"])
                    pv = psT[:, 0:512].rearrange("p (a b) -> p a b", b=128)
                    for r in range(4):
                        VCP(Cst4[:, 4 * c4:4 * c4 + 4, r, ri, 32 * r:32 * r + 32], pv[:, :, 32 * r:32 * r + 32], ["psT", "Cst"], ["Cst"])
            MEMSET(Hc[:, 0], 0.0, ["Hc0"])
            for b in range(2):
                load_GP(Hc[:, 1 + b, 0, :], "Hc%d" % (1 + b), sre, (l * 2 + b) * 4096)
                load_GP(Hc[:, 1 + b, 1, :], "Hc%d" % (1 + b), sim, (l * 2 + b) * 4096)
            tr.barrier()
            stop_here("ssmsetup_%d" % l, dict(P=P[:], tabC=tabC[:], tabS=tabS[:], SinT=SinT[:], Cst=Cst[:], Hc=Hc[:], bb=bb[:]))
            es2.close()
            uTcs = [SBT(es, "uTc%d" % i, [128, NT], BF16) for i in range(2)]
            wus = [SBT(es, "wu%d" % i, [128, 8, 128], BF16) for i in range(2)]
            d0s = [SBT(es, "d0%d" % i, [128, 4, 128]) for i in range(2)]
            W = [SBT(es, "W%d" % i, [128, 4, 128]) for i in range(2)]
            Z = [[SBT(es, "Z%d%d" % (i, j), [128, 4, 128]) for j in range(2)] for i in range(2)]
            T1 = SBT(es, "T1", [128, 4, 128], BF16); T2 = SBT(es, "T2", [128, 4, 128], BF16)
            Q1 = SBT(es, "Q1", [128, 4, 128], BF16); Q2 = SBT(es, "Q2", [128, 4, 128], BF16)
            BUb = [SBT(es, "BUb%d" % i, [128, 4, 128], BF16) for i in range(2)]
            Zb = [SBT(es, "Zb%d" % i, [128, 4, 128], BF16) for i in range(2)]
            Hf = [[SBT(es, "Hf%d%d" % (i, j), [128, 4, 128]) for j in range(2)] for i in range(2)]
            Hb = [[SBT(es, "Hb%d%d" % (i, j), [128, 4, 128], BF16) for j in range(2)] for i in range(2)]
            yts = [SBT(es, "yt%d" % i, [128, 128]) for i in range(2)]
            cs = SBT(es, "cs", [128, 4, 4])

            def load_wu(cc):
                tr.dma("pool", "wu", wus[cc % 2][:], dap(w_in, l * D * INC + 4096 + 128 * cc, [[INC, 128], [INC * 128, 8], [1, 128]]), (), ["wu%d" % (cc % 2)])

            load_wu(0)
            ucnt2 = [0]
            for c in range(8):
                wu = wus[c % 2]; wuk = "wu%d" % (c % 2)
                uTc = uTcs[c % 2]; uk = "uTc%d" % (c % 2)
                d0 = d0s[c % 2]; dk = "d0%d" % (c % 2)
                if c + 1 < 8:
                    load_wu(c + 1)
                for bi, (b0, bn) in enumerate(blocks):
                    u_ = ucnt2[0] % 2; ucnt2[0] += 1
                    pb = ps[5 + u_]; pkk = "ps%d" % (5 + u_)
                    tr.mms([(pb[:, 0:bn], wu[:, kc, :], xnT[:, kc, b0:b0 + bn], dict(start=(kc == 0), stop=(kc == 7))) for kc in range(8)],
                           [wuk, "xnT"], [pkk])
                    ACP(uTc[:, b0:b0 + bn], pb[:, 0:bn], [pkk], [uk])
                VCP(d0[:], RHO[:, 4 * c:4 * c + 4].unsqueeze(2).broadcast_to([128, 4, 128]), [pk(5)], [dk], eng="pool")
                MEMSET(d0[:, :, 0:1], 0.0, [dk], eng="pool")

                def emit_bu(ti):
                    kind, i, nt, c0 = tiles[ti]
                    for ri in range(2):
                        tr.mms([(ps[ri][:, 128 * r:128 * r + nt], SinT[:, (4 * c + r) * 2 + ri, :], uTc[:, c0:c0 + nt], dict(start=True, stop=True)) for r in range(4)],
                               ["SinT", uk], ["ps%d" % ri])

                def emit_in(ti):
                    kind, i, nt, c0 = tiles[ti]
                    par_ = ti % 2
                    slot = 0 if kind == "p" else 1 + i
                    hk = "Hc%d" % slot
                    tcs = tabCb[:, 4 * c:4 * c + 4, 0:nt]; tss = tabSb[:, 4 * c:4 * c + 4, 0:nt]
                    for ri in range(2):
                        ACP(BUb[ri][:, :, 0:nt], ps[ri][:].rearrange("p (r i) -> p r i", i=128)[:, :, 0:nt], ["ps%d" % ri], ["BUb%d" % ri])
                    bur = BUb[0][:, :, 0:nt]; bui = BUb[1][:, :, 0:nt]
                    VTT(T1[:, :, 0:nt], bur, tcs, ALU.mult, ["BUb0", "tabCb"], ["T1"])
                    VTT(T2[:, :, 0:nt], bui, tss, ALU.mult, ["BUb1", "tabSb"], ["T2"])
                    VTT(W[0][:, :, 0:nt], T1[:, :, 0:nt], T2[:, :, 0:nt], ALU.add, ["T1", "T2"], ["W0"])
                    VTT(T1[:, :, 0:nt], bui, tcs, ALU.mult, ["BUb1", "tabCb"], ["T1"])
                    VTT(T2[:, :, 0:nt], bur, tss, ALU.mult, ["BUb0", "tabSb"], ["T2"])
                    VTT(W[1][:, :, 0:nt], T1[:, :, 0:nt], T2[:, :, 0:nt], ALU.subtract, ["T1", "T2"], ["W1"])
                    for ri in range(2):
                        VTT(RH[:, ri, :], Hc[:, slot, ri, 4 * c:4 * c + 4], RHO[:, 4 * c:4 * c + 4], ALU.mult, [hk, pk(5)], ["RH"])
                        VTT(W[ri][:, :, 0:1], W[ri][:, :, 0:1], RH[:, ri, :].unsqueeze(2), ALU.add, ["W%d" % ri, "RH"], ["W%d" % ri])
                    for ri in range(2):
                        Zt = Z[ri][par_]; zk = "Z%d%d" % (ri, par_)
                        if nt == 128:
                            zin = W[ri][:].rearrange("p r i -> p (r i)"); zout = Zt[:].rearrange("p r i -> p (r i)")
                            dd = d0[:].rearrange("p r i -> p (r i)")
                            tr.op("dve", lambda e: e.tensor_tensor_scan(out=zout, data0=dd, data1=zin, initial=0.0, op0=ALU.mult, op1=ALU.add),
                                  ["W%d" % ri, dk], [zk])
                        else:
                            for r in range(4):
                                tr.op("dve", lambda e: e.tensor_tensor_scan(out=Zt[:, r, 0:nt], data0=d0[:, r, 0:nt], data1=W[ri][:, r, 0:nt],
                                                                            initial=0.0, op0=ALU.mult, op1=ALU.add),
                                      ["W%d" % ri, dk], [zk])
                    zrl = Z[0][par_][:, :, nt - 1]; zil = Z[1][par_][:, :, nt - 1]
                    jl_ = 0 if nt == 128 else 1
                    cl = tabCl[:, jl_, 4 * c:4 * c + 4]; sl = tabSl[:, jl_, 4 * c:4 * c + 4]
                    zk0 = "Z0%d" % par_; zk1 = "Z1%d" % par_
                    VTT(cs[:, 0, :], zrl, cl, ALU.mult, [zk0, "tabCl"], ["cs0"])
                    VTT(cs[:, 1, :], zil, sl, ALU.mult, [zk1, "tabSl"], ["cs1"])
                    VTT(Hc[:, slot, 0, 4 * c:4 * c + 4], cs[:, 0, :], cs[:, 1, :], ALU.subtract, ["cs0", "cs1", hk], [hk])
                    VTT(cs[:, 2, :], zrl, sl, ALU.mult, [zk0, "tabSl"], ["cs2"])
                    VTT(cs[:, 3, :], zil, cl, ALU.mult, [zk1, "tabCl"], ["cs3"])
                    VTT(Hc[:, slot, 1, 4 * c:4 * c + 4], cs[:, 2, :], cs[:, 3, :], ALU.add, ["cs2", "cs3", hk], [hk])

                def emit_out(ti):
                    kind, i, nt, c0 = tiles[ti]
                    par_ = ti % 2
                    tcs = tabCb[:, 4 * c:4 * c + 4, 0:nt]; tss = tabSb[:, 4 * c:4 * c + 4, 0:nt]
                    for ri in range(2):
                        ACP(Zb[ri][:, :, 0:nt], Z[ri][par_][:, :, 0:nt], ["Z%d%d" % (ri, par_)], ["Zb%d" % ri])
                    zr = Zb[0][:, :, 0:nt]; zi = Zb[1][:, :, 0:nt]
                    VTT(Q1[:, :, 0:nt], zr, tcs, ALU.mult, ["Zb0", "tabCb"], ["Q1"])
                    VTT(Q2[:, :, 0:nt], zi, tss, ALU.mult, ["Zb1", "tabSb"], ["Q2"])
                    VTT(Hb[0][par_][:, :, 0:nt], Q1[:, :, 0:nt], Q2[:, :, 0:nt], ALU.subtract, ["Q1", "Q2"], ["Hb0%d" % par_])
                    VTT(Q1[:, :, 0:nt], zr, tss, ALU.mult, ["Zb0", "tabSb"], ["Q1"])
                    VTT(Q2[:, :, 0:nt], zi, tcs, ALU.mult, ["Zb1", "tabCb"], ["Q2"])
                    VTT(Hb[1][par_][:, :, 0:nt], Q1[:, :, 0:nt], Q2[:, :, 0:nt], ALU.add, ["Q1", "Q2"], ["Hb1%d" % par_])

                def emit_y(ti):
                    kind, i, nt, c0 = tiles[ti]
                    pb = ps[2 + ti % 2]; pkk = "ps%d" % (2 + ti % 2)
                    yt = yts[ti % 2]; yk = "yt%d" % (ti % 2)
                    par_ = ti % 2
                    lst = []
                    for r in range(4):
                        for ri in range(2):
                            lst.append((pb[:, 0:nt], Cst[:, (4 * c + r) * 2 + ri, :], Hb[ri][par_][:, r, 0:nt], dict(start=(len(lst) == 0), stop=(len(lst) == 7))))
                    tr.mms(lst, ["Cst", "Hb0%d" % par_, "Hb1%d" % par_], [pkk])
                    STT(yt[:, 0:nt], uTc[:, c0:c0 + nt], pp[:, 11 + c:12 + c], pb[:, 0:nt], ALU.mult, ALU.add, [uk, "pp11", pkk], [yk])
                    ACT(ygT[:, c, c0:c0 + nt], yt[:, 0:nt], AF.Gelu, [yk], ["ygT"])

                ntl = len(tiles)
                emit_bu(0)
                for ti in range(ntl):
                    emit_in(ti)
                    if ti + 1 < ntl:
                        emit_bu(ti + 1)
                    emit_out(ti)
                    if ti >= 1:
                        emit_y(ti - 1)
                emit_y(ntl - 1)
            for slot, (tr_, ti_) in enumerate(((hr_p, hi_p), (hr_s, hi_s), (hr_s, hi_s))):
                for ri, tt in enumerate((tr_, ti_)):
                    off = l * 4096 if slot == 0 else (l * 2 + slot - 1) * 4096
                    store_GP(tt, off, Hc[:, slot, ri, :], "Hc%d" % slot, ("oh", l, slot, ri))
            tr.barrier()
            stop_here("ssm_%d" % l, dict(ygT=ygT[:], Hc=Hc[:]))

        oes = contextlib.ExitStack()
        oT = SBT(oes, "oT", [128, 8, NT], BF16)
        with contextlib.ExitStack() as es:
            wbs = [SBT(es, "wb%d" % i, [128, 8, 512], BF16) for i in range(2)]
            Qt = [SBT(es, "Qt%d" % m, [128, NT], BF16) for m in range(2)]
            Kt = [SBT(es, "Kt%d" % m, [128, NT], BF16) for m in range(2)]
            Kp = [[SBT(es, "Kp%d%d" % (b, m), [128, PAST + NS], BF16) for m in range(2)] for b in range(2)]
            Vx = SBT(es, "Vx", [128, 18, 160], BF16)
            Vp = [SBT(es, "Vp%d" % b, [128, 8, 160], BF16) for b in range(2)]
            Gt = SBT(es, "Gt", [128, 18, 128])
            Pb = [SBT(es, "Pb%d" % i, [128, 512], BF16) for i in range(4)]
            sqs = [SBT(es, "sq%d" % i, [128, 512], BF16) for i in range(2)]
            rss = [SBT(es, "rs%d" % i, [128, 512]) for i in range(2)]
            ksts = [SBT(es, "kst%d" % i, [128, 2, 64]) for i in range(4)]
            vsts = [SBT(es, "vst%d" % i, [128, 128]) for i in range(4)]
            sgs = [SBT(es, "sg%d" % i, [128, 128]) for i in range(2)]
            ckbs = [SBT(es, "ckb%d" % i, [128, 8, 128], BF16) for i in range(2)]
            osbs = [SBT(es, "osb%d" % i, [128, 128]) for i in range(4)]
            ots = [SBT(es, "ot%d" % i, [128, 128]) for i in range(4)]
            onbs = [SBT(es, "onb%d" % i, [128, 128], BF16) for i in range(4)]
            fss = [SBT(es, "fs%d" % i, [128, 8]) for i in range(4)]
            onesbd = SBT(es, "onesbd", [128, 128], BF16)
            MEMSET(onesbd[:], 0.0, ["onesbd"])
            MEMSET(onesbd[0:64, 0:64], 1.0, ["onesbd"])
            MEMSET(onesbd[64:128, 64:128], 1.0, ["onesbd"])
            MEMSET(Vx[:, :, 128:129], 1.0, ["Vx"])
            for b in range(2):
                MEMSET(Vp[b][:, :, 128:129], 1.0, ["Vp%d" % b])
            pcount = [0]
            fcount = [0]
            ocnt = [0]
            Ocp = [[SBT(es, "Ocp%d%d" % (i, j), [128, 387]) for j in range(3)] for i in range(2)]
            wbase = l * D * INC

            def load_wb(hh):
                wbt = wbs[hh % 2]; wk = "wb%d" % (hh % 2)
                for j, colo in enumerate((hh * 64, 512 + hh * 64, 1024 + hh * 64, 1536 + hh * 64)):
                    tr.dma("pool", "wb", wbt[:, :, 64 * j:64 * j + 64], dap(w_in, wbase + colo, [[INC, 128], [INC * 128, 8], [1, 64]]), (), [wk])
                tr.dma("pool", "wb", wbt[:, :, 256:384], dap(w_in, wbase + 2048 + hh * 128, [[INC, 128], [INC * 128, 8], [1, 128]]), (), [wk])
                tr.dma("pool", "wb", wbt[:, :, 384:512], dap(w_in, wbase + 3072 + hh * 128, [[INC, 128], [INC * 128, 8], [1, 128]]), (), [wk])

            load_wb(0)
            for h in range(H):
                wb = wbs[h % 2]; wk = "wb%d" % (h % 2)
                for m in range(2):
                    tr.dma("pool", "aug", Qt[m][64:68, 0:T], dap(c_qa, h * T, [[H * T, 4], [1, T]]), (), ["QtA%d" % m])
                    tr.dma("pool", "aug", Kt[m][64:68, 0:T], dap(c_ka, h * T, [[H * T, 4], [1, T]]), (), ["KtA%d" % m])
                    for b in range(2):
                        tr.dma("pool", "aug", Qt[m][64:68, T + NS * b:T + NS * b + NS], dap(c_qa, h * T + PAST, [[H * T, 4], [1, NS]]), (), ["QtA%d" % m])
                        tr.dma("pool", "aug", Kp[b][m][64:68, :], dap(c_ka, h * T, [[H * T, 4], [1, PAST + NS]]), (), ["KpA%d%d" % (b, m)])
                for b in range(2):
                    for m in range(2):
                        tr.dma("pool", "ck", ckbs[b][:, :, 64 * m:64 * m + 64],
                               dap(ck, ((l * 2 + b) * PAST) * 1024 + m * 512 + h * 64, [[1024, 128], [1024 * 128, 8], [1, 64]]), (), ["ckb%d" % b])
                    tr.dma("pool", "cv", Vp[b][:, :, 0:128], dap(cv, ((l * 2 + b) * PAST) * 1024 + h * 128, [[1024, 128], [1024 * 128, 8], [1, 128]]), (), ["Vp%d" % b])
                if h + 1 < H:
                    load_wb(h + 1)
                def emit_vga(ti):
                    kind, i, n, c0 = tiles[ti]
                    u_ = ti % 2
                    pbv = ps[2 + u_]; pkv = "ps%d" % (2 + u_)
                    pbg = ps[0 + u_]; pkg = "ps%d" % (0 + u_)
                    r4 = ti % 4
                    vst = vsts[r4]; vk_ = "vst%d" % r4; sg = sgs[ti % 2]; gk_ = "sg%d" % (ti % 2)
                    tr.mms([(pbv[0:n, 0:128], xnT[:, kc, c0:c0 + n], wb[:, kc, 256:384], dict(start=(kc == 0), stop=(kc == 7))) for kc in range(8)],
                           [wk, "xnT"], [pkv])
                    tr.mms([(pbg[0:n, 0:128], xnT[:, kc, c0:c0 + n], wb[:, kc, 384:512], dict(start=(kc == 0), stop=(kc == 7))) for kc in range(8)],
                           [wk, "xnT"], [pkg])
                    VCP(vst[0:n, :], pbv[0:n, 0:128], [pkv], [vk_])
                    VCP(Vx[0:n, ti, 0:128], pbv[0:n, 0:128], [pkv], ["Vx"])
                    if kind == "p":
                        dst = dap(nv_p, (l * T + c0) * 1024 + h * 128, [[1024, n], [1, 128]])
                    else:
                        dst = dap(nv_s, ((l * 2 + i) * NS) * 1024 + h * 128, [[1024, n], [1, 128]])
                    tr.dma("sp", "o_v", dst, vst[0:n, :], [vk_], [("ov", l, h, ti)], is_out=True)
                    ACT(sg[0:n, :], pbg[0:n, 0:128], AF.Silu, [pkg], [gk_])
                    VTT(Gt[0:n, ti, :], sg[0:n, :], subB[0:n, :], ALU.mult, [gk_, "subB"], ["Gt"])

                vga_next = [0]
                for pr in range(2):
                    isq = pr == 0
                    dsts = Qt if isq else Kt
                    dks = ["Qt0", "Qt1"] if isq else ["Kt0", "Kt1"]
                    gci = 0 if isq else 1
                    for (b0, bn) in blocks:
                        u_ = pcount[0] % 2; pcount[0] += 1
                        pb = ps[5 + u_]; pkk = "ps%d" % (5 + u_)
                        sq = sqs[u_]; rs = rss[u_]; sk_ = "sq%d" % u_; rk_ = "rs%d" % u_
                        tr.mms([(pb[:, 0:bn], wb[:, kc, 128 * pr:128 * pr + 128], xnT[:, kc, b0:b0 + bn], dict(start=(kc == 0), stop=(kc == 7))) for kc in range(8)],
                               [wk, "xnT"], [pkk])
                        ACT(sq[:, 0:bn], pb[:, 0:bn], AF.Square, [pkk], [sk_])
                        tr.mms([(ps[4][:, 0:bn], onesbd[:], sq[:, 0:bn], dict(start=True, stop=True))], [sk_, "onesbd"], ["ps4"])
                        ACT(rs[:, 0:bn], ps[4][:, 0:bn], AF.Ln, ["ps4", "epsc"], [rk_], scale=1.0 / 64, bias=epsc[:, :])
                        ACT(rs[:, 0:bn], rs[:, 0:bn], AF.Exp, [rk_], [rk_], scale=-0.5)
                        for m in range(2):
                            STT(dsts[m][0:64, b0:b0 + bn], pb[64 * m:64 * m + 64, 0:bn], pp[64 * m:64 * m + 64, gci:gci + 1], rs[64 * m:64 * m + 64, 0:bn],
                                ALU.mult, ALU.mult, [pkk, rk_, "pp0", "pp1"], [dks[m]])
                        for _ in range(2):
                            if vga_next[0] < len(tiles):
                                emit_vga(vga_next[0]); vga_next[0] += 1
                while vga_next[0] < len(tiles):
                    emit_vga(vga_next[0]); vga_next[0] += 1
                stop_here("attnB", dict(Qt0=Qt[0][:], Kt0=Kt[0][:], Qt1=Qt[1][:], Kt1=Kt[1][:]))
                for ti, (kind, i, n, c0) in enumerate(tiles):
                    r4 = ti % 4; r2 = ti % 2
                    kst = ksts[r4]; kk_ = "kst%d" % r4; tk_ = "psT"
                    for m in range(2):
                        TRANSP(psT[0:n, 256 + 128 * r2 + 64 * m:256 + 128 * r2 + 64 * m + 64], Kt[m][0:64, c0:c0 + n], 64, ["Kt%d" % m], [tk_])
                    VCP(kst[0:n].rearrange("p m d -> p (m d)"), psT[0:n, 256 + 128 * r2:256 + 128 * r2 + 128], [tk_], [kk_])
                    if kind == "p":
                        dst = dap(nk_p, (l * T + c0) * 1024 + h * 64, [[1024, n], [512, 2], [1, 64]])
                    else:
                        dst = dap(nk_s, ((l * 2 + i) * NS) * 1024 + h * 64, [[1024, n], [512, 2], [1, 64]])
                    tr.dma("sp", "o_k", dst, kst[0:n], [kk_], [("ok", l, h, ti)], is_out=True)
                stop_here("attnD", dict(Gt=Gt[:]))
                for b in range(2):
                    ckb = ckbs[b]
                    for m in range(2):
                        for tt in range(8):
                            TRANSP(psT[0:64, 128 * tt:128 * tt + 128], ckb[:, tt, 64 * m:64 * m + 64], 128, ["ckb%d" % b], ["psT"])
                        VCP(Kp[b][m][0:64, 0:PAST], psT[0:64, :], ["psT"], ["Kp%d%d" % (b, m)])
                        VCP(Kp[b][m][0:64, PAST:PAST + NS], Kt[m][0:64, T + NS * b:T + NS * b + NS], ["Kt%d" % m], ["Kp%d%d" % (b, m)])

                stop_here("attnE", dict(Kp00=Kp[0][0][:]))
                def attend(qcols, qpos0, nq, ktiles, gt_tile):
                    steps = []
                    blocks_ = []
                    for qb0 in range(0, nq, 512):
                        nqb = min(512, nq - qb0)
                        qts = [(qb0 + 128 * i, min(128, nq - qb0 - 128 * i)) for i in range((nqb + 127) // 128)]
                        lastk = {}
                        vis = []
                        for kj, (Kap, Vap, nk, kp0) in enumerate(ktiles):
                            first = None
                            diag = False
                            for qi, (q0, nqi) in enumerate(qts):
                                tpos = qpos0 + q0
                                if kp0 // 64 > (tpos + nqi - 1) // 64:
                                    continue
                                if first is None:
                                    first = qi
                                    diag = (kp0 + nk - 1) > tpos
                                lastk[qi] = kj
                            if first is not None:
                                vis.append((kj, first, diag))
                        blk = dict(qb0=qb0, nqb=nqb, qts=qts, lastk=lastk, started=set(), nsteps=2 * len(vis))
                        blocks_.append(blk)
                        for (kj, first, diag) in vis:
                            for m in range(2):
                                steps.append(dict(blk=blk, kj=kj, first=first, diag=diag, m=m))
                        steps[-1]["last"] = True

                    def emit_S(st):
                        blk = st["blk"]; qts = blk["qts"]; m = st["m"]
                        Kap, Vap, nk, kp0 = ktiles[st["kj"]]
                        c0 = qts[st["first"]][0]; ncol = blk["qb0"] + blk["nqb"] - c0
                        si = pcount[0] % 2; pbi = pcount[0] % 4; pcount[0] += 1
                        S = ps[si]; sk = "ps%d" % si
                        lst = [(S[0:nk, 0:ncol], Kap[m], Qt[m][0:68, qcols + c0:qcols + c0 + ncol], dict(start=True, stop=not st["diag"]))]
                        if st["diag"]:
                            nqd = qts[st["first"]][1]
                            lst.append((S[0:nk, 0:nqd], ident[0:nk, 0:nk], dmat[0:nk, h, 0:nqd], dict(start=False, stop=True)))
                        tr.mms(lst, ["Qt%d" % m, "Kt%d" % m, "Kp0%d" % m, "Kp1%d" % m, "QtA%d" % m, "KtA%d" % m, "KpA0%d" % m, "KpA1%d" % m, "ident", "dmat"], [sk])
                        Pt = Pb[pbi]; pk_ = "Pb%d" % pbi
                        ACT(Pt[0:nk, 0:ncol], S[0:nk, 0:ncol], AF.Exp, [sk], [pk_])
                        st["P"] = (Pt, pk_, c0)

                    def emit_PV(st):
                        blk = st["blk"]; qts = blk["qts"]; m = st["m"]
                        Kap, Vap, nk, kp0 = ktiles[st["kj"]]
                        Pt, pk_, c0 = st["P"]
                        for qi in range(st["first"], len(qts)):
                            q0, nqi = qts[qi]
                            a_ = m * 4 + qi
                            bank = 2 + a_ // 3
                            O = ps[bank][0:nqi, 129 * (a_ % 3):129 * (a_ % 3) + 129]
                            stf = bank not in blk["started"]
                            blk["started"].add(bank)
                            tr.mms([(O, Pt[0:nk, q0 - c0:q0 - c0 + nqi], Vap, dict(start=stf, stop=(blk["lastk"][qi] == st["kj"]), skip_group_check=True))],
                                   [pk_, "Vx", "Vp0", "Vp1"], ["ps%d" % bank])

                    def finalize(blk):
                        qts = blk["qts"]
                        nqt = len(qts)
                        oset = ocnt[0] % 2; ocnt[0] += 1
                        nmax = max(n for (_, n) in qts)
                        used = sorted(set(2 + (m * 4 + qi) // 3 for qi in range(nqt) for m in range(2)))
                        for bank in used:
                            ACP(Ocp[oset][bank - 2][0:nmax, :], ps[bank][0:nmax, 0:387], ["ps%d" % bank], ["ocp%d_%d" % (oset, bank)])
                        Aps = []
                        for qi, (q0, nqi) in enumerate(qts):
                            A = []
                            for m in range(2):
                                a_ = m * 4 + qi
                                bank = 2 + a_ // 3
                                A.append((Ocp[oset][bank - 2][0:nqi, 129 * (a_ % 3):129 * (a_ % 3) + 129], "ocp%d_%d" % (oset, bank)))
                            Aps.append(A)
                        for qi, (q0, n) in enumerate(qts):
                            A = Aps[qi]; fs = fss[qi]; ot = ots[qi]; osb = osbs[qi]
                            fk = "fs%d_" % qi; otk = "ot%d" % qi; osk = "osb%d" % qi
                            tr.op("dve", lambda e: e.reciprocal(out=fs[0:n, 0:1], in_=A[0][0][:, 128:129]), [A[0][1]], [fk + "0"])
                            tr.op("dve", lambda e: e.reciprocal(out=fs[0:n, 1:2], in_=A[1][0][:, 128:129]), [A[1][1]], [fk + "1"])
                            VTT(fs[0:n, 1:2], fs[0:n, 1:2], pp[0:n, 2:3], ALU.mult, [fk + "1", "pp2"], [fk + "1"])
                            VTS(ot[0:n, :], A[1][0][:, 0:128], fs[0:n, 1:2], None, ALU.mult, None, [A[1][1], fk + "1"], [otk])
                            STT(osb[0:n, :], A[0][0][:, 0:128], fs[0:n, 0:1], ot[0:n, :], ALU.mult, ALU.add, [A[0][1], fk + "0", otk], [osk])
                            VTT(ot[0:n, :], osb[0:n, :], osb[0:n, :], ALU.mult, [osk], [otk])
                            RSUM(fs[0:n, 2:3], ot[0:n, :], [otk], [fk + "2"])
                        for qi, (q0, n) in enumerate(qts):
                            fs = fss[qi]; fk = "fs%d_" % qi
                            RSQ(fs[0:n, 2:3], fs[0:n, 2:3], 1.0 / 128, n, [fk + "2"], [fk + "2"])
                        for qi, (q0, n) in enumerate(qts):
                            fs = fss[qi]; osb = osbs[qi]; onb = onbs[qi]
                            STT(onb[0:n, :], osb[0:n, :], fs[0:n, 2:3], Gt[0:n, gt_tile(q0), :], ALU.mult, ALU.mult,
                                ["osb%d" % qi, "fs%d_2" % qi, "Gt"], ["onb%d" % qi])

                        def deferred():
                            for qi, (q0, n) in enumerate(qts):
                                TRANSP(psT[:, 128 * qi:128 * qi + n], onbs[qi][0:n, :], n, ["onb%d" % qi], ["psT"])
                            qa = qts[0][0]; ntot = qts[-1][0] + qts[-1][1] - qa
                            if all(n == 128 for (_, n) in qts):
                                VCP(oT[:, h, qcols + qa:qcols + qa + ntot], psT[:, 0:ntot], ["psT"], ["oT"])
                            else:
                                for qi, (q0, n) in enumerate(qts):
                                    VCP(oT[:, h, qcols + q0:qcols + q0 + n], psT[:, 128 * qi:128 * qi + n], ["psT"], ["oT"])
                        return deferred

                    pending = []
                    since = 0
                    emit_S(steps[0])
                    for i, st in enumerate(steps):
                        if i + 1 < len(steps):
                            emit_S(steps[i + 1])
                        emit_PV(st)
                        since += 1
                        if pending and since >= 3:
                            pending.pop(0)()
                        if st.get("last"):
                            if pending:
                                pending.pop(0)()
                            pending.append(finalize(st["blk"]))
                            since = 0
                    while pending:
                        pending.pop(0)()

                kts = [([Kt[0][0:68, 128 * j:128 * j + 128], Kt[1][0:68, 128 * j:128 * j + 128]], Vx[:, j, 0:129], 128, 128 * j) for j in range(16)]
                attend(0, 0, T, kts, lambda q0: q0 // 128)
                stop_here("attnF", dict(oT=oT[:]))
                for b in range(2):
                    kts = [([Kp[b][0][0:68, 128 * j:128 * j + 128], Kp[b][1][0:68, 128 * j:128 * j + 128]], Vp[b][:, j, 0:129], 128, 128 * j) for j in range(8)]
                    kts.append(([Kp[b][0][0:68, PAST:PAST + NS], Kp[b][1][0:68, PAST:PAST + NS]], Vx[0:NS, 16 + b, 0:129], NS, PAST))
                    attend(T + NS * b, PAST, NS, kts, lambda q0, b=b: 16 + b)
                stop_here("attnh%d_%d" % (h, l), dict(oT=oT[:], Qt0=Qt[0][:], Kt0=Kt[0][:], Qt1=Qt[1][:], Kt1=Kt[1][:], Vx=Vx[:], Gt=Gt[:], Kp00=Kp[0][0][:], Vp0=Vp[0][:]))
            tr.barrier()
            stop_here("attn_%d" % l, dict(oT=oT[:]))

        with contextlib.ExitStack() as es:
            ysgT = SBT(es, "ysgT", [128, 8, NT], BF16)
            wA = SBT(es, "wA", [128, 8, D], BF16); wB = SBT(es, "wB", [128, 8, D], BF16)
            tmp = [SBT(es, "tmp%d" % i, [128, 512]) for i in range(2)]
            tb16 = SBT(es, "tb16", [128, 512], BF16)
            xr = [SBT(es, "xr%d" % i, [128, D]) for i in range(2)]

            def loadw(dst, key, tnsr, base, rowlen):
                tr.dma("pool", key, dst[:], dap(tnsr, base, [[rowlen, 128], [rowlen * 128, 8], [1, D]]), (), [key])

            def proj(wt, wkey, src, skey, oc, b0, bn, pi):
                pb = ps[pi]
                tr.mms([(pb[:, 0:bn], wt[:, kc, 128 * oc:128 * oc + 128], src[:, kc, b0:b0 + bn], dict(start=(kc == 0), stop=(kc == 7))) for kc in range(8)],
                       [wkey, skey], ["ps%d" % pi])
                return pb[:, 0:bn], "ps%d" % pi

            cnt = [0]
            loadw(wA, "wA", w_glu, l * D * D, D)
            loadw(wB, "wB", w_in, l * D * INC + 5120, INC)
            for oc in range(8):
                for (b0, bn) in blocks:
                    pi = cnt[0] % 2; cnt[0] += 1
                    pa, ka = proj(wA, "wA", ygT, "ygT", oc, b0, bn, pi)
                    ACT(tmp[0][:, 0:bn], pa, AF.Sigmoid, [ka, "pp3"], ["tmp0"], bias=pp[:, 3 + oc:4 + oc])
                    VTT(ysgT[:, oc, b0:b0 + bn], ygT[:, oc, b0:b0 + bn], tmp[0][:, 0:bn], ALU.mult, ["ygT", "tmp0"], [("ysg", oc)])
            for oc in range(8):
                for (b0, bn) in blocks:
                    pi = cnt[0] % 2; cnt[0] += 1
                    pa, ka = proj(wB, "wB", xnT, "xnT", oc, b0, bn, pi)
                    ACT(tb16[:, 0:bn], pa, AF.Silu, [ka], ["tb16"])
                    VTT(ysgT[:, oc, b0:b0 + bn], ysgT[:, oc, b0:b0 + bn], tb16[:, 0:bn], ALU.mult, [("ysg", oc), "tb16"], [("ysg", oc)])
            allysg = [("ysg", oc) for oc in range(8)]
            loadw(wA, "wA", w_sp, l * D * D, D)
            loadw(wB, "wB", w_in, l * D * INC + 7168, INC)
            for oc in range(8):
                for (b0, bn) in blocks:
                    pi = cnt[0] % 2; cnt[0] += 1
                    pg, kg = proj(wB, "wB", xnT, "xnT", oc, b0, bn, 2 + pi)
                    ACT(tmp[0][:, 0:bn], pg, AF.Sigmoid, [kg], ["tmp0"])
                    pb = ps[pi]
                    tr.mms([(pb[:, 0:bn], wA[:, kc, 128 * oc:128 * oc + 128], ysgT[:, kc, b0:b0 + bn], dict(start=(kc == 0), stop=(kc == 7))) for kc in range(8)],
                           ["wA"] + allysg, ["ps%d" % pi])
                    VTT(ygT[:, oc, b0:b0 + bn], pb[:, 0:bn], tmp[0][:, 0:bn], ALU.mult, ["ps%d" % pi, "tmp0"], ["ygT"])
            loadw(wA, "wA", w_ap, l * D * D, D)
            loadw(wB, "wB", w_in, l * D * INC + 6144, INC)
            for oc in range(8):
                for (b0, bn) in blocks:
                    pi = cnt[0] % 2; cnt[0] += 1
                    pg, kg = proj(wB, "wB", xnT, "xnT", oc, b0, bn, 2 + pi)
                    ACT(tmp[0][:, 0:bn], pg, AF.Sigmoid, [kg], ["tmp0"])
                    pa, ka = proj(wA, "wA", oT, "oT", oc, b0, bn, pi)
                    VTT(tmp[1][:, 0:bn], pa, tmp[0][:, 0:bn], ALU.mult, [ka, "tmp0"], ["tmp1"])
                    VTT(ygT[:, oc, b0:b0 + bn], ygT[:, oc, b0:b0 + bn], tmp[1][:, 0:bn], ALU.add, ["ygT", "tmp1"], ["ygT"])
            loadw(wA, "wA", w_o, l * D * D, D)
            for ti, (kind, i, n, c0) in enumerate(tiles):
                xb = xr[ti % 2]; xk = "xr%d" % (ti % 2)
                if l == 0:
                    src = x_p.ap()[128 * i:128 * i + 128, :] if kind == "p" else x_s.ap()[i]
                else:
                    src = xmid.ap()[c0:c0 + n, :]
                tr.dma("sp", xk, xb[0:n, :], src, [("xmid", ti)], [xk])
                for hb in range(2):
                    pi = cnt[0] % 2; cnt[0] += 1
                    pb = ps[pi]
                    tr.mms([(pb[0:n, :], ygT[:, kc, c0:c0 + n], wA[:, kc, 512 * hb:512 * hb + 512], dict(start=(kc == 0), stop=(kc == 7))) for kc in range(8)],
                           ["wA", "ygT"], ["ps%d" % pi])
                    VTT(xb[0:n, 512 * hb:512 * hb + 512], xb[0:n, 512 * hb:512 * hb + 512], pb[0:n, :], ALU.add, [xk, "ps%d" % pi], [xk])
                if l == DEPTH - 1:
                    dst = y_p.ap()[128 * i:128 * i + 128, :] if kind == "p" else y_s.ap()[i]
                    tr.dma("sp", "o_y", dst, xb[0:n, :], [xk], [("oy", ti)], is_out=True)
                else:
                    tr.dma("sp", "o_x", xmid.ap()[c0:c0 + n, :], xb[0:n, :], [xk], [("xmid2", ti)])
            tr.barrier()
            stop_here("merge_%d" % l, dict(mT=ygT[:], ysgT=ysgT[:]))
        oes.close()
        yes.close()

    except StopBuild:
        pass
    tr.finish()
    return nc


_NC = None


def _consts():
    ident = np.eye(128, dtype=np.float32)
    dm = np.zeros((128, H, 128), np.float32)
    s = np.arange(128)[:, None]; t = np.arange(128)[None, :]
    for h in range(H):
        d = np.where(s > t, -2.0 * SLOPES[h] * (s - t), 0.0)
        d = np.where((s // 64) > (t // 64), -30000.0, d)
        dm[:, h, :] = d
    pos = np.arange(T)
    hi = (pos // 128) * 128.0; lo = (pos % 128) * 1.0
    qa = np.zeros((4, H, T), np.float32); ka = np.zeros((4, H, T), np.float32)
    for h in range(H):
        qa[0, h] = -SLOPES[h] * hi; qa[1, h] = -SLOPES[h] * lo; qa[2, h] = 1.0; qa[3, h] = 1.0
        ka[0, h] = 1.0; ka[1, h] = 1.0; ka[2, h] = SLOPES[h] * hi; ka[3, h] = SLOPES[h] * lo
    par = np.zeros((128, 2), np.float32)
    g8 = np.arange(128) // 16
    par[:, 0] = (g8 % 2 == 0); par[:, 1] = (g8 % 2 == 1)
    io = np.tile(np.arange(1, 129, dtype=np.float32)[None, :], (128, 1))
    return dict(c_id=ident, c_dm=dm, c_qa=qa, c_ka=ka, c_par=par, c_io=io)


def kernel(x_prompt, x_sample, cache_k, cache_v, state_ssm_re, state_ssm_im,
           norm_g, w_in, q_norm_g, k_norm_g, lam_q1, lam_k1, lam_q2, lam_k2, subln_g,
           w_attn_proj, ssm_lambda_re, ssm_lambda_im, ssm_log_dt, ssm_b_re, ssm_b_im,
           ssm_c_re, ssm_c_im, ssm_d, w_glu, b_glu, w_ssm_proj, w_out):
    global _NC
    if _NC is None:
        _NC = build()
    nc = _NC
    f = lambda a: np.ascontiguousarray(np.asarray(a, dtype=np.float32))
    shared = dict(norm_g=f(norm_g), w_in=f(w_in), qng=f(q_norm_g), kng=f(k_norm_g), lq1=f(lam_q1), lk1=f(lam_k1),
                  lq2=f(lam_q2), lk2=f(lam_k2), subg=f(subln_g), w_ap=f(w_attn_proj), lamr=f(ssm_lambda_re),
                  lami=f(ssm_lambda_im), ldt=f(ssm_log_dt), bre=f(ssm_b_re), bim=f(ssm_b_im), cre=f(ssm_c_re),
                  cim=f(ssm_c_im), dsk=f(ssm_d), w_glu=f(w_glu), b_glu=f(b_glu), w_sp=f(w_ssm_proj), w_o=f(w_out))
    shared.update(_consts())
    x_prompt = np.asarray(x_prompt); x_sample = np.asarray(x_sample)
    cache_k = np.asarray(cache_k); cache_v = np.asarray(cache_v)
    state_ssm_re = np.asarray(state_ssm_re); state_ssm_im = np.asarray(state_ssm_im)
    in_maps = []
    for c in range(NCORES):
        m = dict(shared)
        m["x_p"] = f(x_prompt[c]); m["x_s"] = f(x_sample[2 * c:2 * c + 2])
        m["ck"] = f(cache_k[:, 2 * c:2 * c + 2]); m["cv"] = f(cache_v[:, 2 * c:2 * c + 2])
        m["sre"] = f(state_ssm_re[:, 2 * c:2 * c + 2]); m["sim"] = f(state_ssm_im[:, 2 * c:2 * c + 2])
        in_maps.append(m)
    res = run_bass_kernel_spmd(nc, in_maps, core_ids=list(range(NCORES)))
    R = res.results
    cat0 = lambda k: np.stack([R[c][k] for c in range(NCORES)], axis=0)
    y_p = cat0("y_p")
    y_s = np.concatenate([R[c]["y_s"] for c in range(NCORES)], axis=0)
    nk_p = np.stack([R[c]["nk_p"] for c in range(NCORES)], axis=1)
    nv_p = np.stack([R[c]["nv_p"] for c in range(NCORES)], axis=1)
    hr_p = np.stack([R[c]["hr_p"] for c in range(NCORES)], axis=1)
    hi_p = np.stack([R[c]["hi_p"] for c in range(NCORES)], axis=1)
    nk_s = np.concatenate([R[c]["nk_s"] for c in range(NCORES)], axis=1)
    nv_s = np.concatenate([R[c]["nv_s"] for c in range(NCORES)], axis=1)
    hr_s = np.concatenate([R[c]["hr_s"] for c in range(NCORES)], axis=1)
    hi_s = np.concatenate([R[c]["hi_s"] for c in range(NCORES)], axis=1)
    return tuple(np.ascontiguousarray(a.astype(np.float32)) for a in (y_p, y_s, nk_p, nv_p, hr_p, hi_p, nk_s, nv_s, hr_s, hi_s))
```

```python
import math
import numpy as np
import concourse.bass as bass
import concourse.mybir as mybir
from concourse.bass_utils import run_bass_kernel_spmd

F32 = mybir.dt.float32
BF16 = mybir.dt.bfloat16
I32 = mybir.dt.int32
ALU = mybir.AluOpType
AF = mybir.ActivationFunctionType
AX = mybir.AxisListType

D = 1024
T = 2048
NS = 16
PAST = 1024
NT = T + 2 * NS
H = 8
DEPTH = 2
EPS = 1e-6
INC = 8192
NCORES = 8
SLOPES = [2.0 ** (-(h + 1)) for h in range(H)]


class Tr:
    def __init__(self, nc):
        self.nc = nc
        self.eng = {}
        for name, h in (("pe", nc.tensor), ("act", nc.scalar), ("dve", nc.vector), ("pool", nc.gpsimd), ("sp", nc.sync)):
            self.eng[name] = dict(h=h, sem=nc.alloc_semaphore(name="s_" + name), cnt=0, seen={})
        self.lastw = {}
        self.readers = {}
        self.dsem = {}
        self.out_tokens = []

    def _deps(self, reads, writes):
        deps = []
        for k in reads:
            if k in self.lastw:
                deps.append(self.lastw[k])
        for k in writes:
            if k in self.lastw:
                deps.append(self.lastw[k])
            deps.extend(self.readers.get(k, ()))
        return deps

    def _wait(self, ename, deps):
        e = self.eng[ename]
        best = {}
        for (sname, sem, val) in deps:
            if ename == "pe" and sname == "s_pe":
                continue
            if val > best.get(sname, (None, 0))[1]:
                best[sname] = (sem, val)
        for sname, (sem, val) in best.items():
            if e["seen"].get(sname, 0) < val:
                e["h"].wait_ge(sem, val)
                e["seen"][sname] = val

    def _commit(self, tok, reads, writes):
        for k in writes:
            self.lastw[k] = tok
            self.readers[k] = []
        for k in reads:
            self.readers.setdefault(k, []).append(tok)

    def op(self, ename, fn, reads=(), writes=()):
        e = self.eng[ename]
        self._wait(ename, self._deps(reads, writes))
        ins = fn(e["h"])
        e["cnt"] += 1
        ins.then_inc(e["sem"], 1)
        self._commit(("s_" + ename, e["sem"], e["cnt"]), reads, writes)

    def mms(self, lst, reads=(), writes=()):
        e = self.eng["pe"]
        self._wait("pe", self._deps(reads, writes))
        ins = None
        for (out, lhsT, rhs, kw) in lst:
            ins = self.nc.tensor.matmul(out, lhsT=lhsT, rhs=rhs, **kw)
        e["cnt"] += 1
        ins.then_inc(e["sem"], 1)
        self._commit(("s_pe", e["sem"], e["cnt"]), reads, writes)

    def dma(self, q, slot, out, in_, reads=(), writes=(), is_out=False):
        if slot.startswith("o_") or slot == "dbg":
            slot = "st_" + "_".join(str(x) for x in (reads[0] if isinstance(reads[0], tuple) else (reads[0],))) if reads else slot
        else:
            slot = "ld_" + "_".join(str(x) for x in (writes[0] if isinstance(writes[0], tuple) else (writes[0],)))
        if slot not in self.dsem:
            self.dsem[slot] = [self.nc.alloc_semaphore(name="d_" + slot), 0]
        ds = self.dsem[slot]
        self._wait(q, self._deps(reads, writes))
        self.eng[q]["h"].dma_start(out=out, in_=in_, allow_slow_non_contiguous=True).then_inc(ds[0], 16)
        ds[1] += 16
        tok = ("d_" + slot, ds[0], ds[1])
        self._commit(tok, reads, writes)
        if is_out:
            self.out_tokens.append(tok)

    def barrier(self):
        toks = [("s_" + n, e["sem"], e["cnt"]) for n, e in self.eng.items() if e["cnt"] > 0]
        toks += [("d_" + k, v[0], v[1]) for k, v in self.dsem.items() if v[1] > 0]
        for n in self.eng:
            e = self.eng[n]
            for (sname, sem, val) in toks:
                if sname == "s_" + n:
                    continue
                if e["seen"].get(sname, 0) < val:
                    e["h"].wait_ge(sem, val)
                    e["seen"][sname] = val
        self.lastw = {}
        self.readers = {}

    def finish(self):
        best = {}
        for (sname, sem, val) in self.out_tokens:
            if val > best.get(sname, (None, 0))[1]:
                best[sname] = (sem, val)
        for sname, (sem, val) in best.items():
            self.nc.sync.wait_ge(sem, val)


def dap(t, off, dims):
    return bass.AP(t, off, [list(d) for d in dims])


class StopBuild(Exception):
    pass


def build(stop=None):
    nc = bass.Bass("TRN2", target_bir_lowering=False)
    dt_in = lambda n, s: nc.dram_tensor(n, s, F32, kind="ExternalInput")
    dt_out = lambda n, s: nc.dram_tensor(n, s, F32, kind="ExternalOutput")
    x_p = dt_in("x_p", [T, D]); x_s = dt_in("x_s", [2, NS, D])
    ck = dt_in("ck", [DEPTH, 2, PAST, 2, H, 64]); cv = dt_in("cv", [DEPTH, 2, PAST, H, 128])
    sre = dt_in("sre", [DEPTH, 2, 64, 64]); sim = dt_in("sim", [DEPTH, 2, 64, 64])
    norm_g = dt_in("norm_g", [DEPTH, D]); w_in = dt_in("w_in", [DEPTH, D, INC])
    qng = dt_in("qng", [DEPTH, 64]); kng = dt_in("kng", [DEPTH, 64])
    lq1 = dt_in("lq1", [DEPTH, 64]); lk1 = dt_in("lk1", [DEPTH, 64]); lq2 = dt_in("lq2", [DEPTH, 64]); lk2 = dt_in("lk2", [DEPTH, 64])
    subg = dt_in("subg", [DEPTH, 128]); w_ap = dt_in("w_ap", [DEPTH, D, D])
    lamr = dt_in("lamr", [DEPTH, 64, 64]); lami = dt_in("lami", [DEPTH, 64, 64]); ldt = dt_in("ldt", [DEPTH, 64])
    bre = dt_in("bre", [DEPTH, 64, 64, 16]); bim = dt_in("bim", [DEPTH, 64, 64, 16])
    cre = dt_in("cre", [DEPTH, 64, 16, 64]); cim = dt_in("cim", [DEPTH, 64, 16, 64])
    dsk = dt_in("dsk", [DEPTH, D]); w_glu = dt_in("w_glu", [DEPTH, D, D]); b_glu = dt_in("b_glu", [DEPTH, D])
    w_sp = dt_in("w_sp", [DEPTH, D, D]); w_o = dt_in("w_o", [DEPTH, D, D])
    c_id = dt_in("c_id", [128, 128]); c_dm = dt_in("c_dm", [128, H, 128])
    c_qa = dt_in("c_qa", [4, H, T]); c_ka = dt_in("c_ka", [4, H, T]); c_par = dt_in("c_par", [128, 2]); c_io = dt_in("c_io", [128, 128])
    y_p = dt_out("y_p", [T, D]); y_s = dt_out("y_s", [2, NS, D])
    nk_p = dt_out("nk_p", [DEPTH, T, 2, H, 64]); nv_p = dt_out("nv_p", [DEPTH, T, H, 128])
    hr_p = dt_out("hr_p", [DEPTH, 64, 64]); hi_p = dt_out("hi_p", [DEPTH, 64, 64])
    nk_s = dt_out("nk_s", [DEPTH, 2, NS, 2, H, 64]); nv_s = dt_out("nv_s", [DEPTH, 2, NS, H, 128])
    hr_s = dt_out("hr_s", [DEPTH, 2, 64, 64]); hi_s = dt_out("hi_s", [DEPTH, 2, 64, 64])
    xmid = nc.dram_tensor("xmid", [NT, D], F32, kind="Internal")

    tr = Tr(nc)
    import contextlib

    ucnt = [0]

    def SBT(es, n, s, d=F32):
        ucnt[0] += 1
        return es.enter_context(nc.sbuf_tensor("%s_%d" % (n, ucnt[0]), s, d))

    def stop_here(name, dumps):
        if stop != name:
            return
        tr.barrier()
        for dn, ap in dumps.items():
            shp = list(ap.shape)
            dtn = nc.dram_tensor("dbg_" + dn, shp, ap.dtype, kind="ExternalOutput")
            tr.dma("sp", "dbg", dtn.ap(), ap, (), [("dbg", dn)], is_out=True)
        raise StopBuild()

    ges = contextlib.ExitStack()
    ps = [ges.enter_context(nc.psum_tensor("ps%d" % i, [128, 512], F32)) for i in range(7)]
    psT = ges.enter_context(nc.psum_tensor("psT", [128, 1024], BF16))
    xnT = SBT(ges, "xnT", [128, 8, NT], BF16)
    ident = SBT(ges, "ident", [128, 128], BF16)
    dmat = SBT(ges, "dmat", [128, H, 128], BF16)
    ones64 = SBT(ges, "ones64", [64, 64], BF16)
    par = SBT(ges, "par", [128, 2])
    epsc = SBT(ges, "epsc", [128, 1])
    sm = SBT(ges, "sm", [128, 16])
    pp = SBT(ges, "pp", [128, 32])
    lv = SBT(ges, "lv", [128, 4, 64])
    subB = SBT(ges, "subB", [128, 128])
    identF = SBT(ges, "identF", [128, 128])
    stg = SBT(ges, "stg", [64, 64]); stg2 = SBT(ges, "stg2", [64, 64]); stg3 = SBT(ges, "stg3", [64, 64])
    prow = SBT(ges, "prow", [32, 128]); dtb = SBT(ges, "dtb", [128, 64])

    def ACT(out, in_, func, reads, writes, **kw):
        tr.op("act", lambda e: e.activation(out=out, in_=in_, func=func, **kw), reads, writes)

    def VTT(out, a, b, op, reads, writes, eng="dve"):
        tr.op(eng, lambda e: e.tensor_tensor(out=out, in0=a, in1=b, op=op), reads, writes)

    def VTS(out, a, s1, s2, op0, op1, reads, writes, eng="dve"):
        if op1 is None:
            tr.op(eng, lambda e: e.tensor_scalar(out=out, in0=a, scalar1=s1, scalar2=None, op0=op0), reads, writes)
        else:
            tr.op(eng, lambda e: e.tensor_scalar(out=out, in0=a, scalar1=s1, scalar2=s2, op0=op0, op1=op1), reads, writes)

    def STT(out, a, s, b, op0, op1, reads, writes, eng="dve"):
        tr.op(eng, lambda e: e.scalar_tensor_tensor(out=out, in0=a, scalar=s, in1=b, op0=op0, op1=op1), reads, writes)

    def VCP(out, in_, reads, writes, eng="dve"):
        tr.op(eng, lambda e: e.tensor_copy(out=out, in_=in_), reads, writes)

    def ACP(out, in_, reads, writes):
        tr.op("act", lambda e: e.copy(out=out, in_=in_), reads, writes)

    def MEMSET(ap, val, writes, eng="dve"):
        tr.op(eng, lambda e: e.memset(ap, val), (), writes)

    def RSUM(out, in_, reads, writes):
        tr.op("dve", lambda e: e.reduce_sum(out=out, in_=in_, axis=AX.X), reads, writes)

    def RSQ(out, in_, scale, n, reads, writes):
        ACT(out, in_, AF.Ln, list(reads) + ["epsc"], writes, scale=scale, bias=epsc[0:n, :])
        ACT(out, out, AF.Exp, writes, writes, scale=-0.5)

    def TRANSP(out, in_, n, reads, writes):
        tr.op("pe", lambda e: e.transpose(out=out, in_=in_, identity=ident[0:n, 0:n]), list(reads) + ["ident"], writes)

    tr.dma("pool", "c0", ident[:], c_id.ap(), (), ["ident"])
    tr.dma("sp", "c1", identF[:], c_id.ap(), (), ["identF"])

    def TRANSPF(out, in_, n, reads, writes):
        tr.op("pe", lambda e: e.transpose(out=out, in_=in_, identity=identF[0:n, 0:n]), list(reads) + ["identF"], writes)

    def load_GP(dst, dkey, tensor, off):
        tr.dma("sp", "stg", stg[:], dap(tensor, off, [[64, 64], [1, 64]]), (), ["stg"])
        TRANSPF(ps[4][0:64, 0:64], stg[:], 64, ["stg"], ["ps4"])
        VCP(dst[0:64, :], ps[4][0:64, 0:64:2], ["ps4"], [dkey])
        VCP(dst[64:128, :], ps[4][0:64, 1:64:2], ["ps4"], [dkey])

    def store_GP(tensor, off, src, skey, okey):
        VCP(stg2[:, 0:64:2], src[0:64, :], [skey], ["stg2"])
        VCP(stg2[:, 1:64:2], src[64:128, :], [skey], ["stg2"])
        TRANSPF(ps[4][0:64, 0:64], stg2[:], 64, ["stg2"], ["ps4"])
        VCP(stg3[:], ps[4][0:64, 0:64], ["ps4"], ["stg3"])
        tr.dma("sp", "o_h", dap(tensor, off, [[64, 64], [1, 64]]), stg3[:], ["stg3"], [okey], is_out=True)
    tr.dma("pool", "c0", dmat[:], c_dm.ap(), (), ["dmat"])
    tr.dma("sp", "c1", par[:], c_par.ap(), (), ["par"])
    MEMSET(ones64[:], 1.0, ["ones64"])
    MEMSET(epsc[:], EPS, ["epsc"])

    tiles = [("p", i, 128, 128 * i) for i in range(16)] + [("s", b, NS, T + NS * b) for b in range(2)]
    blocks = [(512 * i, 512) for i in range(4)] + [(T, 2 * NS)]
    TWO_PI = 2.0 * math.pi

    try:
      for l in range(DEPTH):
        lam_init = 0.8 - 0.6 * math.exp(-0.3 * l)
        with contextlib.ExitStack() as es:
            ngb = SBT(es, "ngb", [128, D]); xt = [SBT(es, "xt%d" % i, [128, D]) for i in range(2)]
            scr = SBT(es, "scr", [128, D]); xnb = SBT(es, "xnb", [128, D], BF16)
            tr.dma("sp", "c1", ngb[:], dap(norm_g, l * D, [[0, 128], [1, D]]), (), ["ngb"])
            MEMSET(prow[:], 0.0, ["prow"])
            tr.dma("sp", "c1", prow[0:1, :].rearrange("p (a b) -> p a b", b=64), dap(qng, l * 64, [[64, 1], [0, 2], [1, 64]]), (), ["prow"])
            tr.dma("sp", "c1", prow[1:2, :].rearrange("p (a b) -> p a b", b=64), dap(kng, l * 64, [[64, 1], [0, 2], [1, 64]]), (), ["prow"])
            tr.dma("sp", "c1", prow[3:11, :], dap(b_glu, l * D, [[128, 8], [1, 128]]), (), ["prow"])
            tr.dma("sp", "c1", prow[11:19, :], dap(dsk, l * D, [[128, 8], [1, 128]]), (), ["prow"])
            TRANSPF(ps[4][:, 0:32], prow[:], 32, ["prow"], ["ps4"])
            VCP(pp[:], ps[4][:, 0:32], ["ps4"], ["pp0", "pp1", "pp2", "pp3", "pp11"])
            VTS(pp[:, 0:1], pp[:, 0:1], 0.125, None, ALU.mult, None, ["pp0"], ["pp0"])
            for j, tnsr in enumerate((lq1, lk1, lq2, lk2)):
                tr.dma("sp", "c1", lv[:, j, :], dap(tnsr, l * 64, [[0, 128], [1, 64]]), (), ["lv%d" % j])
            VTT(lv[:, 0, :], lv[:, 0, :], lv[:, 1, :], ALU.mult, ["lv0", "lv1"], ["lv0"])
            VTT(lv[:, 2, :], lv[:, 2, :], lv[:, 3, :], ALU.mult, ["lv2", "lv3"], ["lv2"])
            RSUM(sm[:, 0:1], lv[:, 0, :], ["lv0"], ["sm0"])
            RSUM(sm[:, 1:2], lv[:, 2, :], ["lv2"], ["sm1"])
            ACT(sm[:, 0:1], sm[:, 0:1], AF.Exp, ["sm0"], ["sm0"])
            ACT(sm[:, 1:2], sm[:, 1:2], AF.Exp, ["sm1"], ["sm1"])
            VTT(sm[:, 0:1], sm[:, 1:2], sm[:, 0:1], ALU.subtract, ["sm0", "sm1"], ["sm0"])
            VTS(pp[:, 2:3], sm[:, 0:1], -lam_init, None, ALU.add, None, ["sm0"], ["pp2"])
            tr.dma("sp", "c1", subB[:], dap(subg, l * 128, [[0, 128], [1, 128]]), (), ["subB"])
            VTS(subB[:], subB[:], 1.0 - lam_init, None, ALU.mult, None, ["subB"], ["subB"])
            for ti, (kind, i, n, c0) in enumerate(tiles):
                xb = xt[ti % 2]; xk = "xt%d" % (ti % 2)
                if l == 0:
                    src = x_p.ap()[128 * i:128 * i + 128, :] if kind == "p" else x_s.ap()[i]
                else:
                    src = xmid.ap()[c0:c0 + n, :]
                tr.dma("sp", xk, xb[0:n, :], src, [("xmid", ti)], [xk])
                VTT(scr[0:n, :], xb[0:n, :], xb[0:n, :], ALU.mult, [xk], ["scr"])
                RSUM(sm[0:n, 2:3], scr[0:n, :], ["scr"], ["sm2"])
                RSQ(sm[0:n, 2:3], sm[0:n, 2:3], 1.0 / D, n, ["sm2"], ["sm2"])
                STT(xnb[0:n, :], xb[0:n, :], sm[0:n, 2:3], ngb[0:n, :], ALU.mult, ALU.mult, [xk, "sm2", "ngb"], ["xnb"])
                for kc in range(8):
                    TRANSP(psT[:, 128 * kc:128 * kc + n], xnb[0:n, 128 * kc:128 * kc + 128], n, ["xnb"], ["psT"])
                VCP(xnT[:, :, c0:c0 + n], psT[:].rearrange("p (k t) -> p k t", t=128)[:, :, 0:n], ["psT"], ["xnT"])
            tr.barrier()
            stop_here("p0_%d" % l, dict(xnT=xnT[:], pp=pp[:], subB=subB[:]))

        yes = contextlib.ExitStack()
        ygT = SBT(yes, "ygT", [128, 8, NT], BF16)

        with contextlib.ExitStack() as es:
            tabCb = SBT(es, "tabCb", [128, 32, 128], BF16); tabSb = SBT(es, "tabSb", [128, 32, 128], BF16)
            tabCl = SBT(es, "tabCl", [128, 2, 32]); tabSl = SBT(es, "tabSl", [128, 2, 32])
            SinT = SBT(es, "SinT", [128, 64, 128], BF16); Cst = SBT(es, "Cst", [128, 64, 128], BF16)
            P = SBT(es, "P", [128, 20, 32])
            PI = SBT(es, "PI", [128, 32], I32)
            Hc = SBT(es, "Hc", [128, 3, 2, 32])
            RH = SBT(es, "RH", [128, 2, 4])
            es2 = contextlib.ExitStack()
            tabC = SBT(es2, "tabC", [128, 32, 128]); tabS = SBT(es2, "tabS", [128, 32, 128])
            es3 = contextlib.ExitStack()
            tmpA = SBT(es3, "tmpA", [128, 32, 128]); tmpB = SBT(es3, "tmpB", [128, 32, 128], I32); io = SBT(es3, "io", [128, 128])
            LR, LI, DT, ER, ANG, RHO, AR, AI, S1, C1, WR, WI, DEN, TA, TB, TC = [P[:, i, :] for i in range(16)]
            pk = lambda i: "P%d" % i
            load_GP(LR, pk(0), lamr, l * 4096)
            load_GP(LI, pk(1), lami, l * 4096)
            tr.dma("sp", "c1", dtb[:], dap(ldt, l * 64, [[0, 128], [1, 64]]), (), ["dtb"])
            VCP(P[0:64, 2, :], dtb[0:64, 0:64:2], ["dtb"], [pk(2)])
            VCP(P[64:128, 2, :], dtb[64:128, 1:64:2], ["dtb"], [pk(2)])
            VTS(TC, DT, 1.0 / 8, None, ALU.mult, None, [pk(2)], [pk(15)])
            VTS(DT, TC, 1.0 / 14, 1.0, ALU.mult, ALU.add, [pk(15)], [pk(2)])
            for kk in range(13, 0, -1):
                VTT(DT, DT, TC, ALU.mult, [pk(2), pk(15)], [pk(2)])
                VTS(DT, DT, 1.0 / kk, 1.0, ALU.mult, ALU.add, [pk(2)], [pk(2)])
            for _ in range(3):
                VTT(DT, DT, DT, ALU.mult, [pk(2)], [pk(2)])
            VTT(ER, LR, DT, ALU.mult, [pk(0), pk(2)], [pk(3)])
            VTT(ANG, LI, DT, ALU.mult, [pk(1), pk(2)], [pk(4)])
            VTS(TA, ANG, 1.0 / TWO_PI, None, ALU.mult, None, [pk(4)], [pk(13)])
            VCP(PI[:], TA, [pk(13)], ["PI"])
            VCP(TA, PI[:], ["PI"], [pk(13)])
            STT(ANG, TA, -TWO_PI, ANG, ALU.mult, ALU.add, [pk(13), pk(4)], [pk(4)])
            VTS(ANG, ANG, -3.14159, 3.14159, ALU.max, ALU.min, [pk(4)], [pk(4)])
            VTS(RHO, ER, 1.0 / 8, 1.0, ALU.mult, ALU.add, [pk(3)], [pk(5)])
            for kk in (7, 6, 5, 4, 3, 2, 1):
                VTT(RHO, RHO, ER, ALU.mult, [pk(5), pk(3)], [pk(5)])
                VTS(RHO, RHO, 1.0 / kk, 1.0, ALU.mult, ALU.add, [pk(5)], [pk(5)])
            ACT(S1, ANG, AF.Sin, [pk(4)], [pk(8)])
            ACT(TB, ANG, AF.Sin, [pk(4)], [pk(14)], scale=0.5)
            VTT(TB, TB, TB, ALU.mult, [pk(14)], [pk(14)])
            VTS(C1, TB, -2.0, 1.0, ALU.mult, ALU.add, [pk(14)], [pk(9)])
            VTT(AR, RHO, C1, ALU.mult, [pk(5), pk(9)], [pk(6)])
            VTT(AI, RHO, S1, ALU.mult, [pk(5), pk(8)], [pk(7)])
            VTS(TA, AR, -1.0, None, ALU.add, None, [pk(6)], [pk(13)])
            VTT(DEN, LR, LR, ALU.mult, [pk(0)], [pk(12)])
            VTT(TB, LI, LI, ALU.mult, [pk(1)], [pk(14)])
            VTT(DEN, DEN, TB, ALU.add, [pk(12), pk(14)], [pk(12)])
            tr.op("dve", lambda e: e.reciprocal(out=DEN, in_=DEN), [pk(12)], [pk(12)])
            VTT(WR, TA, LR, ALU.mult, [pk(13), pk(0)], [pk(10)])
            VTT(TB, AI, LI, ALU.mult, [pk(7), pk(1)], [pk(14)])
            VTT(WR, WR, TB, ALU.add, [pk(10), pk(14)], [pk(10)])
            VTT(WR, WR, DEN, ALU.mult, [pk(10), pk(12)], [pk(10)])
            VTT(WI, AI, LR, ALU.mult, [pk(7), pk(0)], [pk(11)])
            VTT(TB, TA, LI, ALU.mult, [pk(13), pk(1)], [pk(14)])
            VTT(WI, WI, TB, ALU.subtract, [pk(11), pk(14)], [pk(11)])
            VTT(WI, WI, DEN, ALU.mult, [pk(11), pk(12)], [pk(11)])
            tr.dma("sp", "c1", io[:], c_io.ap(), (), ["io"])
            angb = ANG.unsqueeze(2).broadcast_to([128, 32, 128]); iob = io[:].unsqueeze(1).broadcast_to([128, 32, 128])
            VTT(tabS[:], angb, iob, ALU.mult, [pk(4), "io"], ["tabS"])

            def rred(tb, key):
                VTS(tmpA[:], tb[:], 1.0 / TWO_PI, None, ALU.mult, None, [key], ["tmpA"])
                VCP(tmpB[:], tmpA[:], ["tmpA"], ["tmpB"])
                VCP(tmpA[:], tmpB[:], ["tmpB"], ["tmpA"])
                STT(tb[:], tmpA[:], -TWO_PI, tb[:], ALU.mult, ALU.add, ["tmpA", key], [key])
                VTS(tb[:], tb[:], -3.14159, 3.14159, ALU.max, ALU.min, [key], [key])

            rred(tabS, "tabS")
            VTS(tabC[:], tabS[:], math.pi / 2, None, ALU.add, None, ["tabS"], ["tabC"])
            rred(tabC, "tabC")
            ACT(tabS[:], tabS[:], AF.Sin, ["tabS"], ["tabS"])
            ACT(tabC[:], tabC[:], AF.Sin, ["tabC"], ["tabC"])
            ACP(tabCb[:], tabC[:], ["tabC"], ["tabCb"])
            ACP(tabSb[:], tabS[:], ["tabS"], ["tabSb"])
            for j_, col_ in enumerate((127, NS - 1)):
                VCP(tabCl[:, j_, :], tabC[:, :, col_], ["tabC"], ["tabCl"])
                VCP(tabSl[:, j_, :], tabS[:, :, col_], ["tabS"], ["tabSl"])
            tr.barrier()
            es3.close()
            BBw = SBT(es2, "BBw", [128, 64, 128], BF16)
            Bn = SBT(es2, "Bn", [128, 2, 32, 16]); bb = SBT(es2, "bb", [128, 2, 32, 16]); btmp = SBT(es2, "btmp", [128, 32, 16])
            Cn = SBT(es2, "Cn", [128, 2, 8, 64]); Cw = SBT(es2, "Cw", [128, 2, 8, 128], BF16)
            tr.dma("sp", "c1", Bn[:, 0], dap(bre, l * 65536, [[16, 128], [2048, 32], [1, 16]]), (), ["Bn"])
            tr.dma("sp", "c1", Bn[:, 1], dap(bim, l * 65536, [[16, 128], [2048, 32], [1, 16]]), (), ["Bn"])
            wrb = WR.unsqueeze(2).broadcast_to([128, 32, 16]); wib = WI.unsqueeze(2).broadcast_to([128, 32, 16])
            VTT(bb[:, 0], Bn[:, 0], wrb, ALU.mult, ["Bn", pk(10)], ["bb"])
            VTT(btmp[:], Bn[:, 1], wib, ALU.mult, ["Bn", pk(11)], ["btmp"])
            VTT(bb[:, 0], bb[:, 0], btmp[:], ALU.subtract, ["bb", "btmp"], ["bb"])
            VTT(bb[:, 1], Bn[:, 1], wrb, ALU.mult, ["Bn", pk(10)], ["bb"])
            VTT(btmp[:], Bn[:, 0], wib, ALU.mult, ["Bn", pk(11)], ["btmp"])
            VTT(bb[:, 1], bb[:, 1], btmp[:], ALU.add, ["bb", "btmp"], ["bb"])
            MEMSET(BBw[:], 0.0, ["BBw"])
            BBw4 = BBw[:].rearrange("p (q r) c -> p q r c", r=2)
            for gs in range(2):
                for r in range(4):
                    for ri in range(2):
                        VCP(BBw4[64 * gs:64 * gs + 64, r::4, ri, 32 * r + 16 * gs:32 * r + 16 * gs + 16],
                            bb[64 * gs:64 * gs + 64, ri, r::4, :], ["bb", "BBw"], ["BBw"])
            for g4 in range(16):
                for j in range(4):
                    TRANSP(psT[:, 128 * j:128 * j + 128], BBw[:, 4 * g4 + j, :], 128, ["BBw"], ["psT"])
                VCP(SinT[:, 4 * g4:4 * g4 + 4, :], psT[:, 0:512].rearrange("p (a b) -> p a b", b=128), ["psT"], ["SinT"])
            tr.dma("sp", "c1", Cn[:, 0], dap(cre, l * 65536, [[64, 128], [8192, 8], [1, 64]]), (), ["Cn"])
            tr.dma("sp", "c1", Cn[:, 1], dap(cim, l * 65536, [[64, 128], [8192, 8], [1, 64]]), (), ["Cn"])
            VTS(Cn[:, 1], Cn[:, 1], -1.0, None, ALU.mult, None, ["Cn"], ["Cn"])
            for ri in range(2):
                for gs in range(2):
                    VTS(Cw[:, ri, :, 64 * gs:64 * gs + 64], Cn[:, ri], par[:, gs:gs + 1], None, ALU.mult, None, ["Cn", "par"], ["Cw"])
            MEMSET(Cst[:], 0.0, ["Cst"])
            Cst4 = Cst[:].rearrange("p (c r i) m -> p c r i m", r=4, i=2)
            for ri in range(2):
                for c4 in range(2):
                    for j in range(4):
                        TRANSP(psT[:, 128 * j:128 * j + 128], Cw[:, ri, 4 * c4 + j, :], 128, ["Cw"], ["psT"])
                    pv = psT[:, 0:512].rearrange("p (a b) -> p a b", b=128)
                    for r in range(4):
                        VCP(Cst4[:, 4 * c4:4 * c4 + 4, r, ri, 32 * r:32 * r + 32], pv[:, :, 32 * r:32 * r + 32], ["psT", "Cst"], ["Cst"])
            MEMSET(Hc[:, 0], 0.0, ["Hc0"])
            for b in range(2):
                load_GP(Hc[:, 1 + b, 0, :], "Hc%d" % (1 + b), sre, (l * 2 + b) * 4096)
                load_GP(Hc[:, 1 + b, 1, :], "Hc%d" % (1 + b), sim, (l * 2 + b) * 4096)
            tr.barrier()
            stop_here("ssmsetup_%d" % l, dict(P=P[:], tabC=tabC[:], tabS=tabS[:], SinT=SinT[:], Cst=Cst[:], Hc=Hc[:], bb=bb[:]))
            es2.close()
            uTcs = [SBT(es, "uTc%d" % i, [128, NT], BF16) for i in range(2)]
            wus = [SBT(es, "wu%d" % i, [128, 8, 128], BF16) for i in range(2)]
            d0s = [SBT(es, "d0%d" % i, [128, 4, 128]) for i in range(2)]
            W = [SBT(es, "W%d" % i, [128, 4, 128]) for i in range(2)]
            Z = [[SBT(es, "Z%d%d" % (i, j), [128, 4, 128]) for j in range(2)] for i in range(2)]
            T1 = SBT(es, "T1", [128, 4, 128], BF16); T2 = SBT(es, "T2", [128, 4, 128], BF16)
            Q1 = SBT(es, "Q1", [128, 4, 128], BF16); Q2 = SBT(es, "Q2", [128, 4, 128], BF16)
            BUb = [SBT(es, "BUb%d" % i, [128, 4, 128], BF16) for i in range(2)]
            Zb = [SBT(es, "Zb%d" % i, [128, 4, 128], BF16) for i in range(2)]
            Hf = [[SBT(es, "Hf%d%d" % (i, j), [128, 4, 128]) for j in range(2)] for i in range(2)]
            Hb = [[SBT(es, "Hb%d%d" % (i, j), [128, 4, 128], BF16) for j in range(2)] for i in range(2)]
            yts = [SBT(es, "yt%d" % i, [128, 128]) for i in range(2)]
            cs = SBT(es, "cs", [128, 4, 4])

            def load_wu(cc):
                tr.dma("pool", "wu", wus[cc % 2][:], dap(w_in, l * D * INC + 4096 + 128 * cc, [[INC, 128], [INC * 128, 8], [1, 128]]), (), ["wu%d" % (cc % 2)])

            load_wu(0)
            ucnt2 = [0]
            for c in range(8):
                wu = wus[c % 2]; wuk = "wu%d" % (c % 2)
                uTc = uTcs[c % 2]; uk = "uTc%d" % (c % 2)
                d0 = d0s[c % 2]; dk = "d0%d" % (c % 2)
                if c + 1 < 8:
                    load_wu(c + 1)
                for bi, (b0, bn) in enumerate(blocks):
                    u_ = ucnt2[0] % 2; ucnt2[0] += 1
                    pb = ps[5 + u_]; pkk = "ps%d" % (5 + u_)
                    tr.mms([(pb[:, 0:bn], wu[:, kc, :], xnT[:, kc, b0:b0 + bn], dict(start=(kc == 0), stop=(kc == 7))) for kc in range(8)],
                           [wuk, "xnT"], [pkk])
                    ACP(uTc[:, b0:b0 + bn], pb[:, 0:bn], [pkk], [uk])
                VCP(d0[:], RHO[:, 4 * c:4 * c + 4].unsqueeze(2).broadcast_to([128, 4, 128]), [pk(5)], [dk], eng="pool")
                MEMSET(d0[:, :, 0:1], 0.0, [dk], eng="pool")

                def emit_bu(ti):
                    kind, i, nt, c0 = tiles[ti]
                    for ri in range(2):
                        tr.mms([(ps[ri][:, 128 * r:128 * r + nt], SinT[:, (4 * c + r) * 2 + ri, :], uTc[:, c0:c0 + nt], dict(start=True, stop=True)) for r in range(4)],
                               ["SinT", uk], ["ps%d" % ri])

                def emit_in(ti):
                    kind, i, nt, c0 = tiles[ti]
                    par_ = ti % 2
                    slot = 0 if kind == "p" else 1 + i
                    hk = "Hc%d" % slot
                    tcs = tabCb[:, 4 * c:4 * c + 4, 0:nt]; tss = tabSb[:, 4 * c:4 * c + 4, 0:nt]
                    for ri in range(2):
                        ACP(BUb[ri][:, :, 0:nt], ps[ri][:].rearrange("p (r i) -> p r i", i=128)[:, :, 0:nt], ["ps%d" % ri], ["BUb%d" % ri])
                    bur = BUb[0][:, :, 0:nt]; bui = BUb[1][:, :, 0:nt]
                    VTT(T1[:, :, 0:nt], bur, tcs, ALU.mult, ["BUb0", "tabCb"], ["T1"])
                    VTT(T2[:, :, 0:nt], bui, tss, ALU.mult, ["BUb1", "tabSb"], ["T2"])
                    VTT(W[0][:, :, 0:nt], T1[:, :, 0:nt], T2[:, :, 0:nt], ALU.add, ["T1", "T2"], ["W0"])
                    VTT(T1[:, :, 0:nt], bui, tcs, ALU.mult, ["BUb1", "tabCb"], ["T1"])
                    VTT(T2[:, :, 0:nt], bur, tss, ALU.mult, ["BUb0", "tabSb"], ["T2"])
                    VTT(W[1][:, :, 0:nt], T1[:, :, 0:nt], T2[:, :, 0:nt], ALU.subtract, ["T1", "T2"], ["W1"])
                    for ri in range(2):
                        VTT(RH[:, ri, :], Hc[:, slot, ri, 4 * c:4 * c + 4], RHO[:, 4 * c:4 * c + 4], ALU.mult, [hk, pk(5)], ["RH"])
                        VTT(W[ri][:, :, 0:1], W[ri][:, :, 0:1], RH[:, ri, :].unsqueeze(2), ALU.add, ["W%d" % ri, "RH"], ["W%d" % ri])
                    for ri in range(2):
                        Zt = Z[ri][par_]; zk = "Z%d%d" % (ri, par_)
                        if nt == 128:
                            zin = W[ri][:].rearrange("p r i -> p (r i)"); zout = Zt[:].rearrange("p r i -> p (r i)")
                            dd = d0[:].rearrange("p r i -> p (r i)")
                            tr.op("dve", lambda e: e.tensor_tensor_scan(out=zout, data0=dd, data1=zin, initial=0.0, op0=ALU.mult, op1=ALU.add),
                                  ["W%d" % ri, dk], [zk])
                        else:
                            for r in range(4):
                                tr.op("dve", lambda e: e.tensor_tensor_scan(out=Zt[:, r, 0:nt], data0=d0[:, r, 0:nt], data1=W[ri][:, r, 0:nt],
                                                                            initial=0.0, op0=ALU.mult, op1=ALU.add),
                                      ["W%d" % ri, dk], [zk])
                    zrl = Z[0][par_][:, :, nt - 1]; zil = Z[1][par_][:, :, nt - 1]
                    jl_ = 0 if nt == 128 else 1
                    cl = tabCl[:, jl_, 4 * c:4 * c + 4]; sl = tabSl[:, jl_, 4 * c:4 * c + 4]
                    zk0 = "Z0%d" % par_; zk1 = "Z1%d" % par_
                    VTT(cs[:, 0, :], zrl, cl, ALU.mult, [zk0, "tabCl"], ["cs0"])
                    VTT(cs[:, 1, :], zil, sl, ALU.mult, [zk1, "tabSl"], ["cs1"])
                    VTT(Hc[:, slot, 0, 4 * c:4 * c + 4], cs[:, 0, :], cs[:, 1, :], ALU.subtract, ["cs0", "cs1", hk], [hk])
                    VTT(cs[:, 2, :], zrl, sl, ALU.mult, [zk0, "tabSl"], ["cs2"])
                    VTT(cs[:, 3, :], zil, cl, ALU.mult, [zk1, "tabCl"], ["cs3"])
                    VTT(Hc[:, slot, 1, 4 * c:4 * c + 4], cs[:, 2, :], cs[:, 3, :], ALU.add, ["cs2", "cs3", hk], [hk])

                def emit_out(ti):
                    kind, i, nt, c0 = tiles[ti]
                    par_ = ti % 2
                    tcs = tabCb[:, 4 * c:4 * c + 4, 0:nt]; tss = tabSb[:, 4 * c:4 * c + 4, 0:nt]
                    for ri in range(2):
                        ACP(Zb[ri][:, :, 0:nt], Z[ri][par_][:, :, 0:nt], ["Z%d%d" % (ri, par_)], ["Zb%d" % ri])
                    zr = Zb[0][:, :, 0:nt]; zi = Zb[1][:, :, 0:nt]
                    VTT(Q1[:, :, 0:nt], zr, tcs, ALU.mult, ["Zb0", "tabCb"], ["Q1"])
                    VTT(Q2[:, :, 0:nt], zi, tss, ALU.mult, ["Zb1", "tabSb"], ["Q2"])
                    VTT(Hb[0][par_][:, :, 0:nt], Q1[:, :, 0:nt], Q2[:, :, 0:nt], ALU.subtract, ["Q1", "Q2"], ["Hb0%d" % par_])
                    VTT(Q1[:, :, 0:nt], zr, tss, ALU.mult, ["Zb0", "tabSb"], ["Q1"])
                    VTT(Q2[:, :, 0:nt], zi, tcs, ALU.mult, ["Zb1", "tabCb"], ["Q2"])
                    VTT(Hb[1][par_][:, :, 0:nt], Q1[:, :, 0:nt], Q2[:, :, 0:nt], ALU.add, ["Q1", "Q2"], ["Hb1%d" % par_])

                def emit_y(ti):
                    kind, i, nt, c0 = tiles[ti]
                    pb = ps[2 + ti % 2]; pkk = "ps%d" % (2 + ti % 2)
                    yt = yts[ti % 2]; yk = "yt%d" % (ti % 2)
                    par_ = ti % 2
                    lst = []
                    for r in range(4):
                        for ri in range(2):
                            lst.append((pb[:, 0:nt], Cst[:, (4 * c + r) * 2 + ri, :], Hb[ri][par_][:, r, 0:nt], dict(start=(len(lst) == 0), stop=(len(lst) == 7))))
                    tr.mms(lst, ["Cst", "Hb0%d" % par_, "Hb1%d" % par_], [pkk])
                    STT(yt[:, 0:nt], uTc[:, c0:c0 + nt], pp[:, 11 + c:12 + c], pb[:, 0:nt], ALU.mult, ALU.add, [uk, "pp11", pkk], [yk])
                    ACT(ygT[:, c, c0:c0 + nt], yt[:, 0:nt], AF.Gelu, [yk], ["ygT"])

                ntl = len(tiles)
                emit_bu(0)
                for ti in range(ntl):
                    emit_in(ti)
                    if ti + 1 < ntl:
                        emit_bu(ti + 1)
                    emit_out(ti)
                    if ti >= 1:
                        emit_y(ti - 1)
                emit_y(ntl - 1)
            for slot, (tr_, ti_) in enumerate(((hr_p, hi_p), (hr_s, hi_s), (hr_s, hi_s))):
                for ri, tt in enumerate((tr_, ti_)):
                    off = l * 4096 if slot == 0 else (l * 2 + slot - 1) * 4096
                    store_GP(tt, off, Hc[:, slot, ri, :], "Hc%d" % slot, ("oh", l, slot, ri))
            tr.barrier()
            stop_here("ssm_%d" % l, dict(ygT=ygT[:], Hc=Hc[:]))

        oes = contextlib.ExitStack()
        oT = SBT(oes, "oT", [128, 8, NT], BF16)
        with contextlib.ExitStack() as es:
            wbs = [SBT(es, "wb%d" % i, [128, 8, 512], BF16) for i in range(2)]
            Qt = [SBT(es, "Qt%d" % m, [128, NT], BF16) for m in range(2)]
            Kt = [SBT(es, "Kt%d" % m, [128, NT], BF16) for m in range(2)]
            Kp = [[SBT(es, "Kp%d%d" % (b, m), [128, PAST + NS], BF16) for m in range(2)] for b in range(2)]
            Vx = SBT(es, "Vx", [128, 18, 160], BF16)
            Vp = [SBT(es, "Vp%d" % b, [128, 8, 160], BF16) for b in range(2)]
            Gt = SBT(es, "Gt", [128, 18, 128])
            Pb = [SBT(es, "Pb%d" % i, [128, 512], BF16) for i in range(4)]
            sqs = [SBT(es, "sq%d" % i, [128, 512], BF16) for i in range(2)]
            rss = [SBT(es, "rs%d" % i, [128, 512]) for i in range(2)]
            ksts = [SBT(es, "kst%d" % i, [128, 2, 64]) for i in range(4)]
            vsts = [SBT(es, "vst%d" % i, [128, 128]) for i in range(4)]
            sgs = [SBT(es, "sg%d" % i, [128, 128]) for i in range(2)]
            ckbs = [SBT(es, "ckb%d" % i, [128, 8, 128], BF16) for i in range(2)]
            osbs = [SBT(es, "osb%d" % i, [128, 128]) for i in range(4)]
            ots = [SBT(es, "ot%d" % i, [128, 128]) for i in range(4)]
            onbs = [SBT(es, "onb%d" % i, [128, 128], BF16) for i in range(4)]
            fss = [SBT(es, "fs%d" % i, [128, 8]) for i in range(4)]
            onesbd = SBT(es, "onesbd", [128, 128], BF16)
            MEMSET(onesbd[:], 0.0, ["onesbd"])
            MEMSET(onesbd[0:64, 0:64], 1.0, ["onesbd"])
            MEMSET(onesbd[64:128, 64:128], 1.0, ["onesbd"])
            MEMSET(Vx[:, :, 128:129], 1.0, ["Vx"])
            for b in range(2):
                MEMSET(Vp[b][:, :, 128:129], 1.0, ["Vp%d" % b])
            pcount = [0]
            fcount = [0]
            ocnt = [0]
            Ocp = [[SBT(es, "Ocp%d%d" % (i, j), [128, 387]) for j in range(3)] for i in range(2)]
            wbase = l * D * INC

            def load_wb(hh):
                wbt = wbs[hh % 2]; wk = "wb%d" % (hh % 2)
                for j, colo in enumerate((hh * 64, 512 + hh * 64, 1024 + hh * 64, 1536 + hh * 64)):
                    tr.dma("pool", "wb", wbt[:, :, 64 * j:64 * j + 64], dap(w_in, wbase + colo, [[INC, 128], [INC * 128, 8], [1, 64]]), (), [wk])
                tr.dma("pool", "wb", wbt[:, :, 256:384], dap(w_in, wbase + 2048 + hh * 128, [[INC, 128], [INC * 128, 8], [1, 128]]), (), [wk])
                tr.dma("pool", "wb", wbt[:, :, 384:512], dap(w_in, wbase + 3072 + hh * 128, [[INC, 128], [INC * 128, 8], [1, 128]]), (), [wk])

            load_wb(0)
            for h in range(H):
                wb = wbs[h % 2]; wk = "wb%d" % (h % 2)
                for m in range(2):
                    tr.dma("pool", "aug", Qt[m][64:68, 0:T], dap(c_qa, h * T, [[H * T, 4], [1, T]]), (), ["QtA%d" % m])
                    tr.dma("pool", "aug", Kt[m][64:68, 0:T], dap(c_ka, h * T, [[H * T, 4], [1, T]]), (), ["KtA%d" % m])
                    for b in range(2):
                        tr.dma("pool", "aug", Qt[m][64:68, T + NS * b:T + NS * b + NS], dap(c_qa, h * T + PAST, [[H * T, 4], [1, NS]]), (), ["QtA%d" % m])
                        tr.dma("pool", "aug", Kp[b][m][64:68, :], dap(c_ka, h * T, [[H * T, 4], [1, PAST + NS]]), (), ["KpA%d%d" % (b, m)])
                for b in range(2):
                    for m in range(2):
                        tr.dma("pool", "ck", ckbs[b][:, :, 64 * m:64 * m + 64],
                               dap(ck, ((l * 2 + b) * PAST) * 1024 + m * 512 + h * 64, [[1024, 128], [1024 * 128, 8], [1, 64]]), (), ["ckb%d" % b])
                    tr.dma("pool", "cv", Vp[b][:, :, 0:128], dap(cv, ((l * 2 + b) * PAST) * 1024 + h * 128, [[1024, 128], [1024 * 128, 8], [1, 128]]), (), ["Vp%d" % b])
                if h + 1 < H:
                    load_wb(h + 1)
                def emit_vga(ti):
                    kind, i, n, c0 = tiles[ti]
                    u_ = ti % 2
                    pbv = ps[2 + u_]; pkv = "ps%d" % (2 + u_)
                    pbg = ps[0 + u_]; pkg = "ps%d" % (0 + u_)
                    r4 = ti % 4
                    vst = vsts[r4]; vk_ = "vst%d" % r4; sg = sgs[ti % 2]; gk_ = "sg%d" % (ti % 2)
                    tr.mms([(pbv[0:n, 0:128], xnT[:, kc, c0:c0 + n], wb[:, kc, 256:384], dict(start=(kc == 0), stop=(kc == 7))) for kc in range(8)],
                           [wk, "xnT"], [pkv])
                    tr.mms([(pbg[0:n, 0:128], xnT[:, kc, c0:c0 + n], wb[:, kc, 384:512], dict(start=(kc == 0), stop=(kc == 7))) for kc in range(8)],
                           [wk, "xnT"], [pkg])
                    VCP(vst[0:n, :], pbv[0:n, 0:128], [pkv], [vk_])
                    VCP(Vx[0:n, ti, 0:128], pbv[0:n, 0:128], [pkv], ["Vx"])
                    if kind == "p":
                        dst = dap(nv_p, (l * T + c0) * 1024 + h * 128, [[1024, n], [1, 128]])
                    else:
                        dst = dap(nv_s, ((l * 2 + i) * NS) * 1024 + h * 128, [[1024, n], [1, 128]])
                    tr.dma("sp", "o_v", dst, vst[0:n, :], [vk_], [("ov", l, h, ti)], is_out=True)
                    ACT(sg[0:n, :], pbg[0:n, 0:128], AF.Silu, [pkg], [gk_])
                    VTT(Gt[0:n, ti, :], sg[0:n, :], subB[0:n, :], ALU.mult, [gk_, "subB"], ["Gt"])

                vga_next = [0]
                for pr in range(2):
                    isq = pr == 0
                    dsts = Qt if isq else Kt
                    dks = ["Qt0", "Qt1"] if isq else ["Kt0", "Kt1"]
                    gci = 0 if isq else 1
                    for (b0, bn) in blocks:
                        u_ = pcount[0] % 2; pcount[0] += 1
                        pb = ps[5 + u_]; pkk = "ps%d" % (5 + u_)
                        sq = sqs[u_]; rs = rss[u_]; sk_ = "sq%d" % u_; rk_ = "rs%d" % u_
                        tr.mms([(pb[:, 0:bn], wb[:, kc, 128 * pr:128 * pr + 128], xnT[:, kc, b0:b0 + bn], dict(start=(kc == 0), stop=(kc == 7))) for kc in range(8)],
                               [wk, "xnT"], [pkk])
                        ACT(sq[:, 0:bn], pb[:, 0:bn], AF.Square, [pkk], [sk_])
                        tr.mms([(ps[4][:, 0:bn], onesbd[:], sq[:, 0:bn], dict(start=True, stop=True))], [sk_, "onesbd"], ["ps4"])
                        ACT(rs[:, 0:bn], ps[4][:, 0:bn], AF.Ln, ["ps4", "epsc"], [rk_], scale=1.0 / 64, bias=epsc[:, :])
                        ACT(rs[:, 0:bn], rs[:, 0:bn], AF.Exp, [rk_], [rk_], scale=-0.5)
                        for m in range(2):
                            STT(dsts[m][0:64, b0:b0 + bn], pb[64 * m:64 * m + 64, 0:bn], pp[64 * m:64 * m + 64, gci:gci + 1], rs[64 * m:64 * m + 64, 0:bn],
                                ALU.mult, ALU.mult, [pkk, rk_, "pp0", "pp1"], [dks[m]])
                        for _ in range(2):
                            if vga_next[0] < len(tiles):
                                emit_vga(vga_next[0]); vga_next[0] += 1
                while vga_next[0] < len(tiles):
                    emit_vga(vga_next[0]); vga_next[0] += 1
                stop_here("attnB", dict(Qt0=Qt[0][:], Kt0=Kt[0][:], Qt1=Qt[1][:], Kt1=Kt[1][:]))
                for ti, (kind, i, n, c0) in enumerate(tiles):
                    r4 = ti % 4; r2 = ti % 2
                    kst = ksts[r4]; kk_ = "kst%d" % r4; tk_ = "psT"
                    for m in range(2):
                        TRANSP(psT[0:n, 256 + 128 * r2 + 64 * m:256 + 128 * r2 + 64 * m + 64], Kt[m][0:64, c0:c0 + n], 64, ["Kt%d" % m], [tk_])
                    VCP(kst[0:n].rearrange("p m d -> p (m d)"), psT[0:n, 256 + 128 * r2:256 + 128 * r2 + 128], [tk_], [kk_])
                    if kind == "p":
                        dst = dap(nk_p, (l * T + c0) * 1024 + h * 64, [[1024, n], [512, 2], [1, 64]])
                    else:
                        dst = dap(nk_s, ((l * 2 + i) * NS) * 1024 + h * 64, [[1024, n], [512, 2], [1, 64]])
                    tr.dma("sp", "o_k", dst, kst[0:n], [kk_], [("ok", l, h, ti)], is_out=True)
                stop_here("attnD", dict(Gt=Gt[:]))
                for b in range(2):
                    ckb = ckbs[b]
                    for m in range(2):
                        for tt in range(8):
                            TRANSP(psT[0:64, 128 * tt:128 * tt + 128], ckb[:, tt, 64 * m:64 * m + 64], 128, ["ckb%d" % b], ["psT"])
                        VCP(Kp[b][m][0:64, 0:PAST], psT[0:64, :], ["psT"], ["Kp%d%d" % (b, m)])
                        VCP(Kp[b][m][0:64, PAST:PAST + NS], Kt[m][0:64, T + NS * b:T + NS * b + NS], ["Kt%d" % m], ["Kp%d%d" % (b, m)])

                stop_here("attnE", dict(Kp00=Kp[0][0][:]))
                def attend(qcols, qpos0, nq, ktiles, gt_tile):
                    steps = []
                    blocks_ = []
                    for qb0 in range(0, nq, 512):
                        nqb = min(512, nq - qb0)
                        qts = [(qb0 + 128 * i, min(128, nq - qb0 - 128 * i)) for i in range((nqb + 127) // 128)]
                        lastk = {}
                        vis = []
                        for kj, (Kap, Vap, nk, kp0) in enumerate(ktiles):
                            first = None
                            diag = False
                            for qi, (q0, nqi) in enumerate(qts):
                                tpos = qpos0 + q0
                                if kp0 // 64 > (tpos + nqi - 1) // 64:
                                    continue
                                if first is None:
                                    first = qi
                                    diag = (kp0 + nk - 1) > tpos
                                lastk[qi] = kj
                            if first is not None:
                                vis.append((kj, first, diag))
                        blk = dict(qb0=qb0, nqb=nqb, qts=qts, lastk=lastk, started=set(), nsteps=2 * len(vis))
                        blocks_.append(blk)
                        for (kj, first, diag) in vis:
                            for m in range(2):
                                steps.append(dict(blk=blk, kj=kj, first=first, diag=diag, m=m))
                        steps[-1]["last"] = True

                    def emit_S(st):
                        blk = st["blk"]; qts = blk["qts"]; m = st["m"]
                        Kap, Vap, nk, kp0 = ktiles[st["kj"]]
                        c0 = qts[st["first"]][0]; ncol = blk["qb0"] + blk["nqb"] - c0
                        si = pcount[0] % 2; pbi = pcount[0] % 4; pcount[0] += 1
                        S = ps[si]; sk = "ps%d" % si
                        lst = [(S[0:nk, 0:ncol], Kap[m], Qt[m][0:68, qcols + c0:qcols + c0 + ncol], dict(start=True, stop=not st["diag"]))]
                        if st["diag"]:
                            nqd = qts[st["first"]][1]
                            lst.append((S[0:nk, 0:nqd], ident[0:nk, 0:nk], dmat[0:nk, h, 0:nqd], dict(start=False, stop=True)))
                        tr.mms(lst, ["Qt%d" % m, "Kt%d" % m, "Kp0%d" % m, "Kp1%d" % m, "QtA%d" % m, "KtA%d" % m, "KpA0%d" % m, "KpA1%d" % m, "ident", "dmat"], [sk])
                        Pt = Pb[pbi]; pk_ = "Pb%d" % pbi
                        ACT(Pt[0:nk, 0:ncol], S[0:nk, 0:ncol], AF.Exp, [sk], [pk_])
                        st["P"] = (Pt, pk_, c0)

                    def emit_PV(st):
                        blk = st["blk"]; qts = blk["qts"]; m = st["m"]
                        Kap, Vap, nk, kp0 = ktiles[st["kj"]]
                        Pt, pk_, c0 = st["P"]
                        for qi in range(st["first"], len(qts)):
                            q0, nqi = qts[qi]
                            a_ = m * 4 + qi
                            bank = 2 + a_ // 3
                            O = ps[bank][0:nqi, 129 * (a_ % 3):129 * (a_ % 3) + 129]
                            stf = bank not in blk["started"]
                            blk["started"].add(bank)
                            tr.mms([(O, Pt[0:nk, q0 - c0:q0 - c0 + nqi], Vap, dict(start=stf, stop=(blk["lastk"][qi] == st["kj"]), skip_group_check=True))],
                                   [pk_, "Vx", "Vp0", "Vp1"], ["ps%d" % bank])

                    def finalize(blk):
                        qts = blk["qts"]
                        nqt = len(qts)
                        oset = ocnt[0] % 2; ocnt[0] += 1
                        nmax = max(n for (_, n) in qts)
                        used = sorted(set(2 + (m * 4 + qi) // 3 for qi in range(nqt) for m in range(2)))
                        for bank in used:
                            ACP(Ocp[oset][bank - 2][0:nmax, :], ps[bank][0:nmax, 0:387], ["ps%d" % bank], ["ocp%d_%d" % (oset, bank)])
                        Aps = []
                        for qi, (q0, nqi) in enumerate(qts):
                            A = []
                            for m in range(2):
                                a_ = m * 4 + qi
                                bank = 2 + a_ // 3
                                A.append((Ocp[oset][bank - 2][0:nqi, 129 * (a_ % 3):129 * (a_ % 3) + 129], "ocp%d_%d" % (oset, bank)))
                            Aps.append(A)
                        for qi, (q0, n) in enumerate(qts):
                            A = Aps[qi]; fs = fss[qi]; ot = ots[qi]; osb = osbs[qi]
                            fk = "fs%d_" % qi; otk = "ot%d" % qi; osk = "osb%d" % qi
                            tr.op("dve", lambda e: e.reciprocal(out=fs[0:n, 0:1], in_=A[0][0][:, 128:129]), [A[0][1]], [fk + "0"])
                            tr.op("dve", lambda e: e.reciprocal(out=fs[0:n, 1:2], in_=A[1][0][:, 128:129]), [A[1][1]], [fk + "1"])
                            VTT(fs[0:n, 1:2], fs[0:n, 1:2], pp[0:n, 2:3], ALU.mult, [fk + "1", "pp2"], [fk + "1"])
                            VTS(ot[0:n, :], A[1][0][:, 0:128], fs[0:n, 1:2], None, ALU.mult, None, [A[1][1], fk + "1"], [otk])
                            STT(osb[0:n, :], A[0][0][:, 0:128], fs[0:n, 0:1], ot[0:n, :], ALU.mult, ALU.add, [A[0][1], fk + "0", otk], [osk])
                            VTT(ot[0:n, :], osb[0:n, :], osb[0:n, :], ALU.mult, [osk], [otk])
                            RSUM(fs[0:n, 2:3], ot[0:n, :], [otk], [fk + "2"])
                        def stage2():
                            for qi, (q0, n) in enumerate(qts):
                                fs = fss[qi]; fk = "fs%d_" % qi
                                RSQ(fs[0:n, 2:3], fs[0:n, 2:3], 1.0 / 128, n, [fk + "2"], [fk + "2"])
                            for qi, (q0, n) in enumerate(qts):
                                fs = fss[qi]; osb = osbs[qi]; onb = onbs[qi]
                                STT(onb[0:n, :], osb[0:n, :], fs[0:n, 2:3], Gt[0:n, gt_tile(q0), :], ALU.mult, ALU.mult,
                                    ["osb%d" % qi, "fs%d_2" % qi, "Gt"], ["onb%d" % qi])

                        def deferred():
                            for qi, (q0, n) in enumerate(qts):
                                TRANSP(psT[:, 128 * qi:128 * qi + n], onbs[qi][0:n, :], n, ["onb%d" % qi], ["psT"])
                            qa = qts[0][0]; ntot = qts[-1][0] + qts[-1][1] - qa
                            if all(n == 128 for (_, n) in qts):
                                VCP(oT[:, h, qcols + qa:qcols + qa + ntot], psT[:, 0:ntot], ["psT"], ["oT"])
                            else:
                                for qi, (q0, n) in enumerate(qts):
                                    VCP(oT[:, h, qcols + q0:qcols + q0 + n], psT[:, 128 * qi:128 * qi + n], ["psT"], ["oT"])
                        return [[8, stage2], [11, deferred]]

                    pending = []
                    emit_S(steps[0])
                    for i, st in enumerate(steps):
                        if i + 1 < len(steps):
                            emit_S(steps[i + 1])
                        emit_PV(st)
                        for p_ in pending:
                            p_[0] -= 1
                        while pending and pending[0][0] <= 0:
                            pending.pop(0)[1]()
                        if st.get("last"):
                            while pending:
                                pending.pop(0)[1]()
                            pending.extend(finalize(st["blk"]))
                    while pending:
                        pending.pop(0)[1]()

                kts = [([Kt[0][0:68, 128 * j:128 * j + 128], Kt[1][0:68, 128 * j:128 * j + 128]], Vx[:, j, 0:129], 128, 128 * j) for j in range(16)]
                attend(0, 0, T, kts, lambda q0: q0 // 128)
                stop_here("attnF", dict(oT=oT[:]))
                for b in range(2):
                    kts = [([Kp[b][0][0:68, 128 * j:128 * j + 128], Kp[b][1][0:68, 128 * j:128 * j + 128]], Vp[b][:, j, 0:129], 128, 128 * j) for j in range(8)]
                    kts.append(([Kp[b][0][0:68, PAST:PAST + NS], Kp[b][1][0:68, PAST:PAST + NS]], Vx[0:NS, 16 + b, 0:129], NS, PAST))
                    attend(T + NS * b, PAST, NS, kts, lambda q0, b=b: 16 + b)
                stop_here("attnh%d_%d" % (h, l), dict(oT=oT[:], Qt0=Qt[0][:], Kt0=Kt[0][:], Qt1=Qt[1][:], Kt1=Kt[1][:], Vx=Vx[:], Gt=Gt[:], Kp00=Kp[0][0][:], Vp0=Vp[0][:]))
            tr.barrier()
            stop_here("attn_%d" % l, dict(oT=oT[:]))

        with contextlib.ExitStack() as es:
            ysgT = SBT(es, "ysgT", [128, 8, NT], BF16)
            wA = SBT(es, "wA", [128, 8, D], BF16); wB = SBT(es, "wB", [128, 8, D], BF16)
            tmp = [SBT(es, "tmp%d" % i, [128, 512]) for i in range(2)]
            tb16 = SBT(es, "tb16", [128, 512], BF16)
            xr = [SBT(es, "xr%d" % i, [128, D]) for i in range(2)]

            def loadw(dst, key, tnsr, base, rowlen):
                tr.dma("pool", key, dst[:], dap(tnsr, base, [[rowlen, 128], [rowlen * 128, 8], [1, D]]), (), [key])

            def proj(wt, wkey, src, skey, oc, b0, bn, pi):
                pb = ps[pi]
                tr.mms([(pb[:, 0:bn], wt[:, kc, 128 * oc:128 * oc + 128], src[:, kc, b0:b0 + bn], dict(start=(kc == 0), stop=(kc == 7))) for kc in range(8)],
                       [wkey, skey], ["ps%d" % pi])
                return pb[:, 0:bn], "ps%d" % pi

            cnt = [0]
            loadw(wA, "wA", w_glu, l * D * D, D)
            loadw(wB, "wB", w_in, l * D * INC + 5120, INC)
            for oc in range(8):
                for (b0, bn) in blocks:
                    pi = cnt[0] % 2; cnt[0] += 1
                    pa, ka = proj(wA, "wA", ygT, "ygT", oc, b0, bn, pi)
                    ACT(tmp[0][:, 0:bn], pa, AF.Sigmoid, [ka, "pp3"], ["tmp0"], bias=pp[:, 3 + oc:4 + oc])
                    VTT(ysgT[:, oc, b0:b0 + bn], ygT[:, oc, b0:b0 + bn], tmp[0][:, 0:bn], ALU.mult, ["ygT", "tmp0"], [("ysg", oc)])
            for oc in range(8):
                for (b0, bn) in blocks:
                    pi = cnt[0] % 2; cnt[0] += 1
                    pa, ka = proj(wB, "wB", xnT, "xnT", oc, b0, bn, pi)
                    ACT(tb16[:, 0:bn], pa, AF.Silu, [ka], ["tb16"])
                    VTT(ysgT[:, oc, b0:b0 + bn], ysgT[:, oc, b0:b0 + bn], tb16[:, 0:bn], ALU.mult, [("ysg", oc), "tb16"], [("ysg", oc)])
            allysg = [("ysg", oc) for oc in range(8)]
            loadw(wA, "wA", w_sp, l * D * D, D)
            loadw(wB, "wB", w_in, l * D * INC + 7168, INC)
            for oc in range(8):
                for (b0, bn) in blocks:
                    pi = cnt[0] % 2; cnt[0] += 1
                    pg, kg = proj(wB, "wB", xnT, "xnT", oc, b0, bn, 2 + pi)
                    ACT(tmp[0][:, 0:bn], pg, AF.Sigmoid, [kg], ["tmp0"])
                    pb = ps[pi]
                    tr.mms([(pb[:, 0:bn], wA[:, kc, 128 * oc:128 * oc + 128], ysgT[:, kc, b0:b0 + bn], dict(start=(kc == 0), stop=(kc == 7))) for kc in range(8)],
                           ["wA"] + allysg, ["ps%d" % pi])
                    VTT(ygT[:, oc, b0:b0 + bn], pb[:, 0:bn], tmp[0][:, 0:bn], ALU.mult, ["ps%d" % pi, "tmp0"], ["ygT"])
            loadw(wA, "wA", w_ap, l * D * D, D)
            loadw(wB, "wB", w_in, l * D * INC + 6144, INC)
            for oc in range(8):
                for (b0, bn) in blocks:
                    pi = cnt[0] % 2; cnt[0] += 1
                    pg, kg = proj(wB, "wB", xnT, "xnT", oc, b0, bn, 2 + pi)
                    ACT(tmp[0][:, 0:bn], pg, AF.Sigmoid, [kg], ["tmp0"])
                    pa, ka = proj(wA, "wA", oT, "oT", oc, b0, bn, pi)
                    VTT(tmp[1][:, 0:bn], pa, tmp[0][:, 0:bn], ALU.mult, [ka, "tmp0"], ["tmp1"])
                    VTT(ygT[:, oc, b0:b0 + bn], ygT[:, oc, b0:b0 + bn], tmp[1][:, 0:bn], ALU.add, ["ygT", "tmp1"], ["ygT"])
            loadw(wA, "wA", w_o, l * D * D, D)
            for ti, (kind, i, n, c0) in enumerate(tiles):
                xb = xr[ti % 2]; xk = "xr%d" % (ti % 2)
                if l == 0:
                    src = x_p.ap()[128 * i:128 * i + 128, :] if kind == "p" else x_s.ap()[i]
                else:
                    src = xmid.ap()[c0:c0 + n, :]
                tr.dma("sp", xk, xb[0:n, :], src, [("xmid", ti)], [xk])
                for hb in range(2):
                    pi = cnt[0] % 2; cnt[0] += 1
                    pb = ps[pi]
                    tr.mms([(pb[0:n, :], ygT[:, kc, c0:c0 + n], wA[:, kc, 512 * hb:512 * hb + 512], dict(start=(kc == 0), stop=(kc == 7))) for kc in range(8)],
                           ["wA", "ygT"], ["ps%d" % pi])
                    VTT(xb[0:n, 512 * hb:512 * hb + 512], xb[0:n, 512 * hb:512 * hb + 512], pb[0:n, :], ALU.add, [xk, "ps%d" % pi], [xk])
                if l == DEPTH - 1:
                    dst = y_p.ap()[128 * i:128 * i + 128, :] if kind == "p" else y_s.ap()[i]
                    tr.dma("sp", "o_y", dst, xb[0:n, :], [xk], [("oy", ti)], is_out=True)
                else:
                    tr.dma("sp", "o_x", xmid.ap()[c0:c0 + n, :], xb[0:n, :], [xk], [("xmid2", ti)])
            tr.barrier()
            stop_here("merge_%d" % l, dict(mT=ygT[:], ysgT=ysgT[:]))
        oes.close()
        yes.close()

    except StopBuild:
        pass
    tr.finish()
    return nc


_NC = None


def _consts():
    ident = np.eye(128, dtype=np.float32)
    dm = np.zeros((128, H, 128), np.float32)
    s = np.arange(128)[:, None]; t = np.arange(128)[None, :]
    for h in range(H):
        d = np.where(s > t, -2.0 * SLOPES[h] * (s - t), 0.0)
        d = np.where((s // 64) > (t // 64), -30000.0, d)
        dm[:, h, :] = d
    pos = np.arange(T)
    hi = (pos // 128) * 128.0; lo = (pos % 128) * 1.0
    qa = np.zeros((4, H, T), np.float32); ka = np.zeros((4, H, T), np.float32)
    for h in range(H):
        qa[0, h] = -SLOPES[h] * hi; qa[1, h] = -SLOPES[h] * lo; qa[2, h] = 1.0; qa[3, h] = 1.0
        ka[0, h] = 1.0; ka[1, h] = 1.0; ka[2, h] = SLOPES[h] * hi; ka[3, h] = SLOPES[h] * lo
    par = np.zeros((128, 2), np.float32)
    g8 = np.arange(128) // 16
    par[:, 0] = (g8 % 2 == 0); par[:, 1] = (g8 % 2 == 1)
    io = np.tile(np.arange(1, 129, dtype=np.float32)[None, :], (128, 1))
    return dict(c_id=ident, c_dm=dm, c_qa=qa, c_ka=ka, c_par=par, c_io=io)


def kernel(x_prompt, x_sample, cache_k, cache_v, state_ssm_re, state_ssm_im,
           norm_g, w_in, q_norm_g, k_norm_g, lam_q1, lam_k1, lam_q2, lam_k2, subln_g,
           w_attn_proj, ssm_lambda_re, ssm_lambda_im, ssm_log_dt, ssm_b_re, ssm_b_im,
           ssm_c_re, ssm_c_im, ssm_d, w_glu, b_glu, w_ssm_proj, w_out):
    global _NC
    if _NC is None:
        _NC = build()
    nc = _NC
    f = lambda a: np.ascontiguousarray(np.asarray(a, dtype=np.float32))
    shared = dict(norm_g=f(norm_g), w_in=f(w_in), qng=f(q_norm_g), kng=f(k_norm_g), lq1=f(lam_q1), lk1=f(lam_k1),
                  lq2=f(lam_q2), lk2=f(lam_k2), subg=f(subln_g), w_ap=f(w_attn_proj), lamr=f(ssm_lambda_re),
                  lami=f(ssm_lambda_im), ldt=f(ssm_log_dt), bre=f(ssm_b_re), bim=f(ssm_b_im), cre=f(ssm_c_re),
                  cim=f(ssm_c_im), dsk=f(ssm_d), w_glu=f(w_glu), b_glu=f(b_glu), w_sp=f(w_ssm_proj), w_o=f(w_out))
    shared.update(_consts())
    x_prompt = np.asarray(x_prompt); x_sample = np.asarray(x_sample)
    cache_k = np.asarray(cache_k); cache_v = np.asarray(cache_v)
    state_ssm_re = np.asarray(state_ssm_re); state_ssm_im = np.asarray(state_ssm_im)
    in_maps = []
    for c in range(NCORES):
        m = dict(shared)
        m["x_p"] = f(x_prompt[c]); m["x_s"] = f(x_sample[2 * c:2 * c + 2])
        m["ck"] = f(cache_k[:, 2 * c:2 * c + 2]); m["cv"] = f(cache_v[:, 2 * c:2 * c + 2])
        m["sre"] = f(state_ssm_re[:, 2 * c:2 * c + 2]); m["sim"] = f(state_ssm_im[:, 2 * c:2 * c + 2])
        in_maps.append(m)
    res = run_bass_kernel_spmd(nc, in_maps, core_ids=list(range(NCORES)))
    R = res.results
    cat0 = lambda k: np.stack([R[c][k] for c in range(NCORES)], axis=0)
    y_p = cat0("y_p")
    y_s = np.concatenate([R[c]["y_s"] for c in range(NCORES)], axis=0)
    nk_p = np.stack([R[c]["nk_p"] for c in range(NCORES)], axis=1)
    nv_p = np.stack([R[c]["nv_p"] for c in range(NCORES)], axis=1)
    hr_p = np.stack([R[c]["hr_p"] for c in range(NCORES)], axis=1)
    hi_p = np.stack([R[c]["hi_p"] for c in range(NCORES)], axis=1)
    nk_s = np.concatenate([R[c]["nk_s"] for c in range(NCORES)], axis=1)
    nv_s = np.concatenate([R[c]["nv_s"] for c in range(NCORES)], axis=1)
    hr_s = np.concatenate([R[c]["hr_s"] for c in range(NCORES)], axis=1)
    hi_s = np.concatenate([R[c]["hi_s"] for c in range(NCORES)], axis=1)
    return tuple(np.ascontiguousarray(a.astype(np.float32)) for a in (y_p, y_s, nk_p, nv_p, hr_p, hi_p, nk_s, nv_s, hr_s, hi_s))
```

```python
import math
import numpy as np
import concourse.bass as bass
import concourse.mybir as mybir
from concourse.bass_utils import run_bass_kernel_spmd

F32 = mybir.dt.float32
BF16 = mybir.dt.bfloat16
I32 = mybir.dt.int32
ALU = mybir.AluOpType
AF = mybir.ActivationFunctionType
AX = mybir.AxisListType

D = 1024
T = 2048
NS = 16
PAST = 1024
NT = T + 2 * NS
H = 8
DEPTH = 2
EPS = 1e-6
INC = 8192
NCORES = 8
SLOPES = [2.0 ** (-(h + 1)) for h in range(H)]


class Tr:
    def __init__(self, nc):
        self.nc = nc
        self.eng = {}
        for name, h in (("pe", nc.tensor), ("act", nc.scalar), ("dve", nc.vector), ("pool", nc.gpsimd), ("sp", nc.sync)):
            self.eng[name] = dict(h=h, sem=nc.alloc_semaphore(name="s_" + name), cnt=0, seen={})
        self.lastw = {}
        self.readers = {}
        self.dsem = {}
        self.out_tokens = []

    def _deps(self, reads, writes):
        deps = []
        for k in reads:
            if k in self.lastw:
                deps.append(self.lastw[k])
        for k in writes:
            if k in self.lastw:
                deps.append(self.lastw[k])
            deps.extend(self.readers.get(k, ()))
        return deps

    def _wait(self, ename, deps):
        e = self.eng[ename]
        best = {}
        for (sname, sem, val) in deps:
            if ename == "pe" and sname == "s_pe":
                continue
            if val > best.get(sname, (None, 0))[1]:
                best[sname] = (sem, val)
        for sname, (sem, val) in best.items():
            if e["seen"].get(sname, 0) < val:
                e["h"].wait_ge(sem, val)
                e["seen"][sname] = val

    def _commit(self, tok, reads, writes):
        for k in writes:
            self.lastw[k] = tok
            self.readers[k] = []
        for k in reads:
            self.readers.setdefault(k, []).append(tok)

    def op(self, ename, fn, reads=(), writes=()):
        e = self.eng[ename]
        self._wait(ename, self._deps(reads, writes))
        ins = fn(e["h"])
        e["cnt"] += 1
        ins.then_inc(e["sem"], 1)
        self._commit(("s_" + ename, e["sem"], e["cnt"]), reads, writes)

    def mms(self, lst, reads=(), writes=()):
        e = self.eng["pe"]
        self._wait("pe", self._deps(reads, writes))
        ins = None
        for (out, lhsT, rhs, kw) in lst:
            ins = self.nc.tensor.matmul(out, lhsT=lhsT, rhs=rhs, **kw)
        e["cnt"] += 1
        ins.then_inc(e["sem"], 1)
        self._commit(("s_pe", e["sem"], e["cnt"]), reads, writes)

    def dma(self, q, slot, out, in_, reads=(), writes=(), is_out=False):
        if slot.startswith("o_") or slot == "dbg":
            slot = "st_" + "_".join(str(x) for x in (reads[0] if isinstance(reads[0], tuple) else (reads[0],))) if reads else slot
        else:
            slot = "ld_" + "_".join(str(x) for x in (writes[0] if isinstance(writes[0], tuple) else (writes[0],)))
        if slot not in self.dsem:
            self.dsem[slot] = [self.nc.alloc_semaphore(name="d_" + slot), 0]
        ds = self.dsem[slot]
        self._wait(q, self._deps(reads, writes))
        self.eng[q]["h"].dma_start(out=out, in_=in_, allow_slow_non_contiguous=True).then_inc(ds[0], 16)
        ds[1] += 16
        tok = ("d_" + slot, ds[0], ds[1])
        self._commit(tok, reads, writes)
        if is_out:
            self.out_tokens.append(tok)

    def barrier(self):
        toks = [("s_" + n, e["sem"], e["cnt"]) for n, e in self.eng.items() if e["cnt"] > 0]
        toks += [("d_" + k, v[0], v[1]) for k, v in self.dsem.items() if v[1] > 0]
        for n in self.eng:
            e = self.eng[n]
            for (sname, sem, val) in toks:
                if sname == "s_" + n:
                    continue
                if e["seen"].get(sname, 0) < val:
                    e["h"].wait_ge(sem, val)
                    e["seen"][sname] = val
        self.lastw = {}
        self.readers = {}

    def finish(self):
        best = {}
        for (sname, sem, val) in self.out_tokens:
            if val > best.get(sname, (None, 0))[1]:
                best[sname] = (sem, val)
        for sname, (sem, val) in best.items():
            self.nc.sync.wait_ge(sem, val)


def dap(t, off, dims):
    return bass.AP(t, off, [list(d) for d in dims])


class StopBuild(Exception):
    pass


def build(stop=None):
    nc = bass.Bass("TRN2", target_bir_lowering=False)
    dt_in = lambda n, s: nc.dram_tensor(n, s, F32, kind="ExternalInput")
    dt_out = lambda n, s: nc.dram_tensor(n, s, F32, kind="ExternalOutput")
    x_p = dt_in("x_p", [T, D]); x_s = dt_in("x_s", [2, NS, D])
    ck = dt_in("ck", [DEPTH, 2, PAST, 2, H, 64]); cv = dt_in("cv", [DEPTH, 2, PAST, H, 128])
    sre = dt_in("sre", [DEPTH, 2, 64, 64]); sim = dt_in("sim", [DEPTH, 2, 64, 64])
    norm_g = dt_in("norm_g", [DEPTH, D]); w_in = dt_in("w_in", [DEPTH, D, INC])
    qng = dt_in("qng", [DEPTH, 64]); kng = dt_in("kng", [DEPTH, 64])
    lq1 = dt_in("lq1", [DEPTH, 64]); lk1 = dt_in("lk1", [DEPTH, 64]); lq2 = dt_in("lq2", [DEPTH, 64]); lk2 = dt_in("lk2", [DEPTH, 64])
    subg = dt_in("subg", [DEPTH, 128]); w_ap = dt_in("w_ap", [DEPTH, D, D])
    lamr = dt_in("lamr", [DEPTH, 64, 64]); lami = dt_in("lami", [DEPTH, 64, 64]); ldt = dt_in("ldt", [DEPTH, 64])
    bre = dt_in("bre", [DEPTH, 64, 64, 16]); bim = dt_in("bim", [DEPTH, 64, 64, 16])
    cre = dt_in("cre", [DEPTH, 64, 16, 64]); cim = dt_in("cim", [DEPTH, 64, 16, 64])
    dsk = dt_in("dsk", [DEPTH, D]); w_glu = dt_in("w_glu", [DEPTH, D, D]); b_glu = dt_in("b_glu", [DEPTH, D])
    w_sp = dt_in("w_sp", [DEPTH, D, D]); w_o = dt_in("w_o", [DEPTH, D, D])
    c_id = dt_in("c_id", [128, 128]); c_dm = dt_in("c_dm", [128, H, 128])
    c_qa = dt_in("c_qa", [4, H, T]); c_ka = dt_in("c_ka", [4, H, T]); c_par = dt_in("c_par", [128, 2]); c_io = dt_in("c_io", [128, 128])
    y_p = dt_out("y_p", [T, D]); y_s = dt_out("y_s", [2, NS, D])
    nk_p = dt_out("nk_p", [DEPTH, T, 2, H, 64]); nv_p = dt_out("nv_p", [DEPTH, T, H, 128])
    hr_p = dt_out("hr_p", [DEPTH, 64, 64]); hi_p = dt_out("hi_p", [DEPTH, 64, 64])
    nk_s = dt_out("nk_s", [DEPTH, 2, NS, 2, H, 64]); nv_s = dt_out("nv_s", [DEPTH, 2, NS, H, 128])
    hr_s = dt_out("hr_s", [DEPTH, 2, 64, 64]); hi_s = dt_out("hi_s", [DEPTH, 2, 64, 64])
    xmid = nc.dram_tensor("xmid", [NT, D], F32, kind="Internal")

    tr = Tr(nc)
    import contextlib

    ucnt = [0]

    def SBT(es, n, s, d=F32):
        ucnt[0] += 1
        return es.enter_context(nc.sbuf_tensor("%s_%d" % (n, ucnt[0]), s, d))

    def stop_here(name, dumps):
        if stop != name:
            return
        tr.barrier()
        for dn, ap in dumps.items():
            shp = list(ap.shape)
            dtn = nc.dram_tensor("dbg_" + dn, shp, ap.dtype, kind="ExternalOutput")
            tr.dma("sp", "dbg", dtn.ap(), ap, (), [("dbg", dn)], is_out=True)
        raise StopBuild()

    ges = contextlib.ExitStack()
    ps = [ges.enter_context(nc.psum_tensor("ps%d" % i, [128, 512], F32)) for i in range(7)]
    psT = ges.enter_context(nc.psum_tensor("psT", [128, 1024], BF16))
    xnT = SBT(ges, "xnT", [128, 8, NT], BF16)
    ident = SBT(ges, "ident", [128, 128], BF16)
    dmat = SBT(ges, "dmat", [128, H, 128], BF16)
    ones64 = SBT(ges, "ones64", [64, 64], BF16)
    par = SBT(ges, "par", [128, 2])
    epsc = SBT(ges, "epsc", [128, 1])
    sm = SBT(ges, "sm", [128, 16])
    pp = SBT(ges, "pp", [128, 32])
    lv = SBT(ges, "lv", [128, 4, 64])
    subB = SBT(ges, "subB", [128, 128])
    identF = SBT(ges, "identF", [128, 128])
    stg = SBT(ges, "stg", [64, 64]); stg2 = SBT(ges, "stg2", [64, 64]); stg3 = SBT(ges, "stg3", [64, 64])
    prow = SBT(ges, "prow", [32, 128]); dtb = SBT(ges, "dtb", [128, 64])

    def ACT(out, in_, func, reads, writes, **kw):
        tr.op("act", lambda e: e.activation(out=out, in_=in_, func=func, **kw), reads, writes)

    def VTT(out, a, b, op, reads, writes, eng="dve"):
        tr.op(eng, lambda e: e.tensor_tensor(out=out, in0=a, in1=b, op=op), reads, writes)

    def VTS(out, a, s1, s2, op0, op1, reads, writes, eng="dve"):
        if op1 is None:
            tr.op(eng, lambda e: e.tensor_scalar(out=out, in0=a, scalar1=s1, scalar2=None, op0=op0), reads, writes)
        else:
            tr.op(eng, lambda e: e.tensor_scalar(out=out, in0=a, scalar1=s1, scalar2=s2, op0=op0, op1=op1), reads, writes)

    def STT(out, a, s, b, op0, op1, reads, writes, eng="dve"):
        tr.op(eng, lambda e: e.scalar_tensor_tensor(out=out, in0=a, scalar=s, in1=b, op0=op0, op1=op1), reads, writes)

    def VCP(out, in_, reads, writes, eng="dve"):
        tr.op(eng, lambda e: e.tensor_copy(out=out, in_=in_), reads, writes)

    def ACP(out, in_, reads, writes):
        tr.op("act", lambda e: e.copy(out=out, in_=in_), reads, writes)

    def MEMSET(ap, val, writes, eng="dve"):
        tr.op(eng, lambda e: e.memset(ap, val), (), writes)

    def RSUM(out, in_, reads, writes):
        tr.op("dve", lambda e: e.reduce_sum(out=out, in_=in_, axis=AX.X), reads, writes)

    def RSQ(out, in_, scale, n, reads, writes):
        ACT(out, in_, AF.Ln, list(reads) + ["epsc"], writes, scale=scale, bias=epsc[0:n, :])
        ACT(out, out, AF.Exp, writes, writes, scale=-0.5)

    def TRANSP(out, in_, n, reads, writes):
        tr.op("pe", lambda e: e.transpose(out=out, in_=in_, identity=ident[0:n, 0:n]), list(reads) + ["ident"], writes)

    tr.dma("pool", "c0", ident[:], c_id.ap(), (), ["ident"])
    tr.dma("sp", "c1", identF[:], c_id.ap(), (), ["identF"])

    def TRANSPF(out, in_, n, reads, writes):
        tr.op("pe", lambda e: e.transpose(out=out, in_=in_, identity=identF[0:n, 0:n]), list(reads) + ["identF"], writes)

    def load_GP(dst, dkey, tensor, off):
        tr.dma("sp", "stg", stg[:], dap(tensor, off, [[64, 64], [1, 64]]), (), ["stg"])
        TRANSPF(ps[4][0:64, 0:64], stg[:], 64, ["stg"], ["ps4"])
        VCP(dst[0:64, :], ps[4][0:64, 0:64:2], ["ps4"], [dkey])
        VCP(dst[64:128, :], ps[4][0:64, 1:64:2], ["ps4"], [dkey])

    def store_GP(tensor, off, src, skey, okey):
        VCP(stg2[:, 0:64:2], src[0:64, :], [skey], ["stg2"])
        VCP(stg2[:, 1:64:2], src[64:128, :], [skey], ["stg2"])
        TRANSPF(ps[4][0:64, 0:64], stg2[:], 64, ["stg2"], ["ps4"])
        VCP(stg3[:], ps[4][0:64, 0:64], ["ps4"], ["stg3"])
        tr.dma("sp", "o_h", dap(tensor, off, [[64, 64], [1, 64]]), stg3[:], ["stg3"], [okey], is_out=True)
    tr.dma("pool", "c0", dmat[:], c_dm.ap(), (), ["dmat"])
    tr.dma("sp", "c1", par[:], c_par.ap(), (), ["par"])
    MEMSET(ones64[:], 1.0, ["ones64"])
    MEMSET(epsc[:], EPS, ["epsc"])

    tiles = [("p", i, 128, 128 * i) for i in range(16)] + [("s", b, NS, T + NS * b) for b in range(2)]
    blocks = [(512 * i, 512) for i in range(4)] + [(T, 2 * NS)]
    TWO_PI = 2.0 * math.pi

    try:
      for l in range(DEPTH):
        lam_init = 0.8 - 0.6 * math.exp(-0.3 * l)
        with contextlib.ExitStack() as es:
            ngb = SBT(es, "ngb", [128, D]); xt = [SBT(es, "xt%d" % i, [128, D]) for i in range(2)]
            scr = SBT(es, "scr", [128, D]); xnb = SBT(es, "xnb", [128, D], BF16)
            tr.dma("sp", "c1", ngb[:], dap(norm_g, l * D, [[0, 128], [1, D]]), (), ["ngb"])
            MEMSET(prow[:], 0.0, ["prow"])
            tr.dma("sp", "c1", prow[0:1, :].rearrange("p (a b) -> p a b", b=64), dap(qng, l * 64, [[64, 1], [0, 2], [1, 64]]), (), ["prow"])
            tr.dma("sp", "c1", prow[1:2, :].rearrange("p (a b) -> p a b", b=64), dap(kng, l * 64, [[64, 1], [0, 2], [1, 64]]), (), ["prow"])
            tr.dma("sp", "c1", prow[3:11, :], dap(b_glu, l * D, [[128, 8], [1, 128]]), (), ["prow"])
            tr.dma("sp", "c1", prow[11:19, :], dap(dsk, l * D, [[128, 8], [1, 128]]), (), ["prow"])
            TRANSPF(ps[4][:, 0:32], prow[:], 32, ["prow"], ["ps4"])
            VCP(pp[:], ps[4][:, 0:32], ["ps4"], ["pp0", "pp1", "pp2", "pp3", "pp11"])
            VTS(pp[:, 0:1], pp[:, 0:1], 0.125, None, ALU.mult, None, ["pp0"], ["pp0"])
            for j, tnsr in enumerate((lq1, lk1, lq2, lk2)):
                tr.dma("sp", "c1", lv[:, j, :], dap(tnsr, l * 64, [[0, 128], [1, 64]]), (), ["lv%d" % j])
            VTT(lv[:, 0, :], lv[:, 0, :], lv[:, 1, :], ALU.mult, ["lv0", "lv1"], ["lv0"])
            VTT(lv[:, 2, :], lv[:, 2, :], lv[:, 3, :], ALU.mult, ["lv2", "lv3"], ["lv2"])
            RSUM(sm[:, 0:1], lv[:, 0, :], ["lv0"], ["sm0"])
            RSUM(sm[:, 1:2], lv[:, 2, :], ["lv2"], ["sm1"])
            ACT(sm[:, 0:1], sm[:, 0:1], AF.Exp, ["sm0"], ["sm0"])
            ACT(sm[:, 1:2], sm[:, 1:2], AF.Exp, ["sm1"], ["sm1"])
            VTT(sm[:, 0:1], sm[:, 1:2], sm[:, 0:1], ALU.subtract, ["sm0", "sm1"], ["sm0"])
            VTS(pp[:, 2:3], sm[:, 0:1], -lam_init, None, ALU.add, None, ["sm0"], ["pp2"])
            tr.dma("sp", "c1", subB[:], dap(subg, l * 128, [[0, 128], [1, 128]]), (), ["subB"])
            VTS(subB[:], subB[:], 1.0 - lam_init, None, ALU.mult, None, ["subB"], ["subB"])
            for ti, (kind, i, n, c0) in enumerate(tiles):
                xb = xt[ti % 2]; xk = "xt%d" % (ti % 2)
                if l == 0:
                    src = x_p.ap()[128 * i:128 * i + 128, :] if kind == "p" else x_s.ap()[i]
                else:
                    src = xmid.ap()[c0:c0 + n, :]
                tr.dma("sp", xk, xb[0:n, :], src, [("xmid", ti)], [xk])
                VTT(scr[0:n, :], xb[0:n, :], xb[0:n, :], ALU.mult, [xk], ["scr"])
                RSUM(sm[0:n, 2:3], scr[0:n, :], ["scr"], ["sm2"])
                RSQ(sm[0:n, 2:3], sm[0:n, 2:3], 1.0 / D, n, ["sm2"], ["sm2"])
                STT(xnb[0:n, :], xb[0:n, :], sm[0:n, 2:3], ngb[0:n, :], ALU.mult, ALU.mult, [xk, "sm2", "ngb"], ["xnb"])
                for kc in range(8):
                    TRANSP(psT[:, 128 * kc:128 * kc + n], xnb[0:n, 128 * kc:128 * kc + 128], n, ["xnb"], ["psT"])
                VCP(xnT[:, :, c0:c0 + n], psT[:].rearrange("p (k t) -> p k t", t=128)[:, :, 0:n], ["psT"], ["xnT"])
            tr.barrier()
            stop_here("p0_%d" % l, dict(xnT=xnT[:], pp=pp[:], subB=subB[:]))

        yes = contextlib.ExitStack()
        ygT = SBT(yes, "ygT", [128, 8, NT], BF16)

        with contextlib.ExitStack() as es:
            tabCb = SBT(es, "tabCb", [128, 32, 128], BF16); tabSb = SBT(es, "tabSb", [128, 32, 128], BF16)
            tabCl = SBT(es, "tabCl", [128, 2, 32]); tabSl = SBT(es, "tabSl", [128, 2, 32])
            SinT = SBT(es, "SinT", [128, 64, 128], BF16); Cst = SBT(es, "Cst", [128, 64, 128], BF16)
            P = SBT(es, "P", [128, 20, 32])
            PI = SBT(es, "PI", [128, 32], I32)
            Hc = SBT(es, "Hc", [128, 3, 2, 32])
            RH = SBT(es, "RH", [128, 2, 4])
            es2 = contextlib.ExitStack()
            tabC = SBT(es2, "tabC", [128, 32, 128]); tabS = SBT(es2, "tabS", [128, 32, 128])
            es3 = contextlib.ExitStack()
            tmpA = SBT(es3, "tmpA", [128, 32, 128]); tmpB = SBT(es3, "tmpB", [128, 32, 128], I32); io = SBT(es3, "io", [128, 128])
            LR, LI, DT, ER, ANG, RHO, AR, AI, S1, C1, WR, WI, DEN, TA, TB, TC = [P[:, i, :] for i in range(16)]
            pk = lambda i: "P%d" % i
            load_GP(LR, pk(0), lamr, l * 4096)
            load_GP(LI, pk(1), lami, l * 4096)
            tr.dma("sp", "c1", dtb[:], dap(ldt, l * 64, [[0, 128], [1, 64]]), (), ["dtb"])
            VCP(P[0:64, 2, :], dtb[0:64, 0:64:2], ["dtb"], [pk(2)])
            VCP(P[64:128, 2, :], dtb[64:128, 1:64:2], ["dtb"], [pk(2)])
            VTS(TC, DT, 1.0 / 8, None, ALU.mult, None, [pk(2)], [pk(15)])
            VTS(DT, TC, 1.0 / 14, 1.0, ALU.mult, ALU.add, [pk(15)], [pk(2)])
            for kk in range(13, 0, -1):
                VTT(DT, DT, TC, ALU.mult, [pk(2), pk(15)], [pk(2)])
                VTS(DT, DT, 1.0 / kk, 1.0, ALU.mult, ALU.add, [pk(2)], [pk(2)])
            for _ in range(3):
                VTT(DT, DT, DT, ALU.mult, [pk(2)], [pk(2)])
            VTT(ER, LR, DT, ALU.mult, [pk(0), pk(2)], [pk(3)])
            VTT(ANG, LI, DT, ALU.mult, [pk(1), pk(2)], [pk(4)])
            VTS(TA, ANG, 1.0 / TWO_PI, None, ALU.mult, None, [pk(4)], [pk(13)])
            VCP(PI[:], TA, [pk(13)], ["PI"])
            VCP(TA, PI[:], ["PI"], [pk(13)])
            STT(ANG, TA, -TWO_PI, ANG, ALU.mult, ALU.add, [pk(13), pk(4)], [pk(4)])
            VTS(ANG, ANG, -3.14159, 3.14159, ALU.max, ALU.min, [pk(4)], [pk(4)])
            VTS(RHO, ER, 1.0 / 8, 1.0, ALU.mult, ALU.add, [pk(3)], [pk(5)])
            for kk in (7, 6, 5, 4, 3, 2, 1):
                VTT(RHO, RHO, ER, ALU.mult, [pk(5), pk(3)], [pk(5)])
                VTS(RHO, RHO, 1.0 / kk, 1.0, ALU.mult, ALU.add, [pk(5)], [pk(5)])
            ACT(S1, ANG, AF.Sin, [pk(4)], [pk(8)])
            ACT(TB, ANG, AF.Sin, [pk(4)], [pk(14)], scale=0.5)
            VTT(TB, TB, TB, ALU.mult, [pk(14)], [pk(14)])
            VTS(C1, TB, -2.0, 1.0, ALU.mult, ALU.add, [pk(14)], [pk(9)])
            VTT(AR, RHO, C1, ALU.mult, [pk(5), pk(9)], [pk(6)])
            VTT(AI, RHO, S1, ALU.mult, [pk(5), pk(8)], [pk(7)])
            VTS(TA, AR, -1.0, None, ALU.add, None, [pk(6)], [pk(13)])
            VTT(DEN, LR, LR, ALU.mult, [pk(0)], [pk(12)])
            VTT(TB, LI, LI, ALU.mult, [pk(1)], [pk(14)])
            VTT(DEN, DEN, TB, ALU.add, [pk(12), pk(14)], [pk(12)])
            tr.op("dve", lambda e: e.reciprocal(out=DEN, in_=DEN), [pk(12)], [pk(12)])
            VTT(WR, TA, LR, ALU.mult, [pk(13), pk(0)], [pk(10)])
            VTT(TB, AI, LI, ALU.mult, [pk(7), pk(1)], [pk(14)])
            VTT(WR, WR, TB, ALU.add, [pk(10), pk(14)], [pk(10)])
            VTT(WR, WR, DEN, ALU.mult, [pk(10), pk(12)], [pk(10)])
            VTT(WI, AI, LR, ALU.mult, [pk(7), pk(0)], [pk(11)])
            VTT(TB, TA, LI, ALU.mult, [pk(13), pk(1)], [pk(14)])
            VTT(WI, WI, TB, ALU.subtract, [pk(11), pk(14)], [pk(11)])
            VTT(WI, WI, DEN, ALU.mult, [pk(11), pk(12)], [pk(11)])
            tr.dma("sp", "c1", io[:], c_io.ap(), (), ["io"])
            angb = ANG.unsqueeze(2).broadcast_to([128, 32, 128]); iob = io[:].unsqueeze(1).broadcast_to([128, 32, 128])
            VTT(tabS[:], angb, iob, ALU.mult, [pk(4), "io"], ["tabS"])

            def rred(tb, key):
                VTS(tmpA[:], tb[:], 1.0 / TWO_PI, None, ALU.mult, None, [key], ["tmpA"])
                VCP(tmpB[:], tmpA[:], ["tmpA"], ["tmpB"])
                VCP(tmpA[:], tmpB[:], ["tmpB"], ["tmpA"])
                STT(tb[:], tmpA[:], -TWO_PI, tb[:], ALU.mult, ALU.add, ["tmpA", key], [key])
                VTS(tb[:], tb[:], -3.14159, 3.14159, ALU.max, ALU.min, [key], [key])

            rred(tabS, "tabS")
            VTS(tabC[:], tabS[:], math.pi / 2, None, ALU.add, None, ["tabS"], ["tabC"])
            rred(tabC, "tabC")
            ACT(tabS[:], tabS[:], AF.Sin, ["tabS"], ["tabS"])
            ACT(tabC[:], tabC[:], AF.Sin, ["tabC"], ["tabC"])
            ACP(tabCb[:], tabC[:], ["tabC"], ["tabCb"])
            ACP(tabSb[:], tabS[:], ["tabS"], ["tabSb"])
            for j_, col_ in enumerate((127, NS - 1)):
                VCP(tabCl[:, j_, :], tabC[:, :, col_], ["tabC"], ["tabCl"])
                VCP(tabSl[:, j_, :], tabS[:, :, col_], ["tabS"], ["tabSl"])
            tr.barrier()
            es3.close()
            BBw = SBT(es2, "BBw", [128, 64, 128], BF16)
            Bn = SBT(es2, "Bn", [128, 2, 32, 16]); bb = SBT(es2, "bb", [128, 2, 32, 16]); btmp = SBT(es2, "btmp", [128, 32, 16])
            Cn = SBT(es2, "Cn", [128, 2, 8, 64]); Cw = SBT(es2, "Cw", [128, 2, 8, 128], BF16)
            tr.dma("sp", "c1", Bn[:, 0], dap(bre, l * 65536, [[16, 128], [2048, 32], [1, 16]]), (), ["Bn"])
            tr.dma("sp", "c1", Bn[:, 1], dap(bim, l * 65536, [[16, 128], [2048, 32], [1, 16]]), (), ["Bn"])
            wrb = WR.unsqueeze(2).broadcast_to([128, 32, 16]); wib = WI.unsqueeze(2).broadcast_to([128, 32, 16])
            VTT(bb[:, 0], Bn[:, 0], wrb, ALU.mult, ["Bn", pk(10)], ["bb"])
            VTT(btmp[:], Bn[:, 1], wib, ALU.mult, ["Bn", pk(11)], ["btmp"])
            VTT(bb[:, 0], bb[:, 0], btmp[:], ALU.subtract, ["bb", "btmp"], ["bb"])
            VTT(bb[:, 1], Bn[:, 1], wrb, ALU.mult, ["Bn", pk(10)], ["bb"])
            VTT(btmp[:], Bn[:, 0], wib, ALU.mult, ["Bn", pk(11)], ["btmp"])
            VTT(bb[:, 1], bb[:, 1], btmp[:], ALU.add, ["bb", "btmp"], ["bb"])
            MEMSET(BBw[:], 0.0, ["BBw"])
            BBw4 = BBw[:].rearrange("p (q r) c -> p q r c", r=2)
            for gs in range(2):
                for r in range(4):
                    for ri in range(2):
                        VCP(BBw4[64 * gs:64 * gs + 64, r::4, ri, 32 * r + 16 * gs:32 * r + 16 * gs + 16],
                            bb[64 * gs:64 * gs + 64, ri, r::4, :], ["bb", "BBw"], ["BBw"])
            for g4 in range(16):
                for j in range(4):
                    TRANSP(psT[:, 128 * j:128 * j + 128], BBw[:, 4 * g4 + j, :], 128, ["BBw"], ["psT"])
                VCP(SinT[:, 4 * g4:4 * g4 + 4, :], psT[:, 0:512].rearrange("p (a b) -> p a b", b=128), ["psT"], ["SinT"])
            tr.dma("sp", "c1", Cn[:, 0], dap(cre, l * 65536, [[64, 128], [8192, 8], [1, 64]]), (), ["Cn"])
            tr.dma("sp", "c1", Cn[:, 1], dap(cim, l * 65536, [[64, 128], [8192, 8], [1, 64]]), (), ["Cn"])
            VTS(Cn[:, 1], Cn[:, 1], -1.0, None, ALU.mult, None, ["Cn"], ["Cn"])
            for ri in range(2):
                for gs in range(2):
                    VTS(Cw[:, ri, :, 64 * gs:64 * gs + 64], Cn[:, ri], par[:, gs:gs + 1], None, ALU.mult, None, ["Cn", "par"], ["Cw"])
            MEMSET(Cst[:], 0.0, ["Cst"])
            Cst4 = Cst[:].rearrange("p (c r i) m -> p c r i m", r=4, i=2)
            for ri in range(2):
                for c4 in range(2):
                    for j in range(4):
                        TRANSP(psT[:, 128 * j:128 * j + 128], Cw[:, ri, 4 * c4 + j, :], 128, ["Cw"], ["psT"])
                    pv = psT[:, 0:512].rearrange("p (a b) -> p a b", b=128)
                    for r in range(4):
                        VCP(Cst4[:, 4 * c4:4 * c4 + 4, r, ri, 32 * r:32 * r + 32], pv[:, :, 32 * r:32 * r + 32], ["psT", "Cst"], ["Cst"])
            MEMSET(Hc[:, 0], 0.0, ["Hc0"])
            for b in range(2):
                load_GP(Hc[:, 1 + b, 0, :], "Hc%d" % (1 + b), sre, (l * 2 + b) * 4096)
                load_GP(Hc[:, 1 + b, 1, :], "Hc%d" % (1 + b), sim, (l * 2 + b) * 4096)
            tr.barrier()
            stop_here("ssmsetup_%d" % l, dict(P=P[:], tabC=tabC[:], tabS=tabS[:], SinT=SinT[:], Cst=Cst[:], Hc=Hc[:], bb=bb[:]))
            es2.close()
            uTcs = [SBT(es, "uTc%d" % i, [128, NT], BF16) for i in range(2)]
            wus = [SBT(es, "wu%d" % i, [128, 8, 128], BF16) for i in range(2)]
            d0s = [SBT(es, "d0%d" % i, [128, 4, 128]) for i in range(2)]
            W = [SBT(es, "W%d" % i, [128, 4, 128]) for i in range(2)]
            Z = [[SBT(es, "Z%d%d" % (i, j), [128, 4, 128]) for j in range(2)] for i in range(2)]
            T1 = SBT(es, "T1", [128, 4, 128], BF16); T2 = SBT(es, "T2", [128, 4, 128], BF16)
            Q1 = SBT(es, "Q1", [128, 4, 128], BF16); Q2 = SBT(es, "Q2", [128, 4, 128], BF16)
            BUb = [SBT(es, "BUb%d" % i, [128, 4, 128], BF16) for i in range(2)]
            Zb = [SBT(es, "Zb%d" % i, [128, 4, 128], BF16) for i in range(2)]
            Hf = [[SBT(es, "Hf%d%d" % (i, j), [128, 4, 128]) for j in range(2)] for i in range(2)]
            Hb = [[SBT(es, "Hb%d%d" % (i, j), [128, 4, 128], BF16) for j in range(2)] for i in range(2)]
            yts = [SBT(es, "yt%d" % i, [128, 128]) for i in range(2)]
            cs = SBT(es, "cs", [128, 4, 4])

            def load_wu(cc):
                tr.dma("pool", "wu", wus[cc % 2][:], dap(w_in, l * D * INC + 4096 + 128 * cc, [[INC, 128], [INC * 128, 8], [1, 128]]), (), ["wu%d" % (cc % 2)])

            load_wu(0)
            ucnt2 = [0]
            for c in range(8):
                wu = wus[c % 2]; wuk = "wu%d" % (c % 2)
                uTc = uTcs[c % 2]; uk = "uTc%d" % (c % 2)
                d0 = d0s[c % 2]; dk = "d0%d" % (c % 2)
                if c + 1 < 8:
                    load_wu(c + 1)
                for bi, (b0, bn) in enumerate(blocks):
                    u_ = ucnt2[0] % 2; ucnt2[0] += 1
                    pb = ps[5 + u_]; pkk = "ps%d" % (5 + u_)
                    tr.mms([(pb[:, 0:bn], wu[:, kc, :], xnT[:, kc, b0:b0 + bn], dict(start=(kc == 0), stop=(kc == 7))) for kc in range(8)],
                           [wuk, "xnT"], [pkk])
                    ACP(uTc[:, b0:b0 + bn], pb[:, 0:bn], [pkk], [uk])
                VCP(d0[:], RHO[:, 4 * c:4 * c + 4].unsqueeze(2).broadcast_to([128, 4, 128]), [pk(5)], [dk], eng="pool")
                MEMSET(d0[:, :, 0:1], 0.0, [dk], eng="pool")

                def emit_bu(ti):
                    kind, i, nt, c0 = tiles[ti]
                    for ri in range(2):
                        tr.mms([(ps[ri][:, 128 * r:128 * r + nt], SinT[:, (4 * c + r) * 2 + ri, :], uTc[:, c0:c0 + nt], dict(start=True, stop=True)) for r in range(4)],
                               ["SinT", uk], ["ps%d" % ri])

                def emit_cast(ti):
                    kind, i, nt, c0 = tiles[ti]
                    for ri in range(2):
                        ACP(BUb[ri][:, :, 0:nt], ps[ri][:].rearrange("p (r i) -> p r i", i=128)[:, :, 0:nt], ["ps%d" % ri], ["BUb%d" % ri])

                def emit_in(ti):
                    kind, i, nt, c0 = tiles[ti]
                    par_ = ti % 2
                    slot = 0 if kind == "p" else 1 + i
                    hk = "Hc%d" % slot
                    tcs = tabCb[:, 4 * c:4 * c + 4, 0:nt]; tss = tabSb[:, 4 * c:4 * c + 4, 0:nt]
                    bur = BUb[0][:, :, 0:nt]; bui = BUb[1][:, :, 0:nt]
                    VTT(T1[:, :, 0:nt], bur, tcs, ALU.mult, ["BUb0", "tabCb"], ["T1"])
                    VTT(T2[:, :, 0:nt], bui, tss, ALU.mult, ["BUb1", "tabSb"], ["T2"])
                    VTT(W[0][:, :, 0:nt], T1[:, :, 0:nt], T2[:, :, 0:nt], ALU.add, ["T1", "T2"], ["W0"])
                    VTT(T1[:, :, 0:nt], bui, tcs, ALU.mult, ["BUb1", "tabCb"], ["T1"])
                    VTT(T2[:, :, 0:nt], bur, tss, ALU.mult, ["BUb0", "tabSb"], ["T2"])
                    VTT(W[1][:, :, 0:nt], T1[:, :, 0:nt], T2[:, :, 0:nt], ALU.subtract, ["T1", "T2"], ["W1"])
                    for ri in range(2):
                        VTT(RH[:, ri, :], Hc[:, slot, ri, 4 * c:4 * c + 4], RHO[:, 4 * c:4 * c + 4], ALU.mult, [hk, pk(5)], ["RH"])
                        VTT(W[ri][:, :, 0:1], W[ri][:, :, 0:1], RH[:, ri, :].unsqueeze(2), ALU.add, ["W%d" % ri, "RH"], ["W%d" % ri])
                    for ri in range(2):
                        Zt = Z[ri][par_]; zk = "Z%d%d" % (ri, par_)
                        if nt == 128:
                            zin = W[ri][:].rearrange("p r i -> p (r i)"); zout = Zt[:].rearrange("p r i -> p (r i)")
                            dd = d0[:].rearrange("p r i -> p (r i)")
                            tr.op("dve", lambda e: e.tensor_tensor_scan(out=zout, data0=dd, data1=zin, initial=0.0, op0=ALU.mult, op1=ALU.add),
                                  ["W%d" % ri, dk], [zk])
                        else:
                            for r in range(4):
                                tr.op("dve", lambda e: e.tensor_tensor_scan(out=Zt[:, r, 0:nt], data0=d0[:, r, 0:nt], data1=W[ri][:, r, 0:nt],
                                                                            initial=0.0, op0=ALU.mult, op1=ALU.add),
                                      ["W%d" % ri, dk], [zk])
                    zrl = Z[0][par_][:, :, nt - 1]; zil = Z[1][par_][:, :, nt - 1]
                    jl_ = 0 if nt == 128 else 1
                    cl = tabCl[:, jl_, 4 * c:4 * c + 4]; sl = tabSl[:, jl_, 4 * c:4 * c + 4]
                    zk0 = "Z0%d" % par_; zk1 = "Z1%d" % par_
                    VTT(cs[:, 0, :], zrl, cl, ALU.mult, [zk0, "tabCl"], ["cs0"])
                    VTT(cs[:, 1, :], zil, sl, ALU.mult, [zk1, "tabSl"], ["cs1"])
                    VTT(Hc[:, slot, 0, 4 * c:4 * c + 4], cs[:, 0, :], cs[:, 1, :], ALU.subtract, ["cs0", "cs1", hk], [hk])
                    VTT(cs[:, 2, :], zrl, sl, ALU.mult, [zk0, "tabSl"], ["cs2"])
                    VTT(cs[:, 3, :], zil, cl, ALU.mult, [zk1, "tabCl"], ["cs3"])
                    VTT(Hc[:, slot, 1, 4 * c:4 * c + 4], cs[:, 2, :], cs[:, 3, :], ALU.add, ["cs2", "cs3", hk], [hk])

                def emit_out(ti):
                    kind, i, nt, c0 = tiles[ti]
                    par_ = ti % 2
                    tcs = tabCb[:, 4 * c:4 * c + 4, 0:nt]; tss = tabSb[:, 4 * c:4 * c + 4, 0:nt]
                    for ri in range(2):
                        ACP(Zb[ri][:, :, 0:nt], Z[ri][par_][:, :, 0:nt], ["Z%d%d" % (ri, par_)], ["Zb%d" % ri])
                    zr = Zb[0][:, :, 0:nt]; zi = Zb[1][:, :, 0:nt]
                    VTT(Q1[:, :, 0:nt], zr, tcs, ALU.mult, ["Zb0", "tabCb"], ["Q1"])
                    VTT(Q2[:, :, 0:nt], zi, tss, ALU.mult, ["Zb1", "tabSb"], ["Q2"])
                    VTT(Hb[0][par_][:, :, 0:nt], Q1[:, :, 0:nt], Q2[:, :, 0:nt], ALU.subtract, ["Q1", "Q2"], ["Hb0%d" % par_])
                    VTT(Q1[:, :, 0:nt], zr, tss, ALU.mult, ["Zb0", "tabSb"], ["Q1"])
                    VTT(Q2[:, :, 0:nt], zi, tcs, ALU.mult, ["Zb1", "tabCb"], ["Q2"])
                    VTT(Hb[1][par_][:, :, 0:nt], Q1[:, :, 0:nt], Q2[:, :, 0:nt], ALU.add, ["Q1", "Q2"], ["Hb1%d" % par_])

                def emit_y(ti):
                    kind, i, nt, c0 = tiles[ti]
                    pb = ps[2 + ti % 2]; pkk = "ps%d" % (2 + ti % 2)
                    yt = yts[ti % 2]; yk = "yt%d" % (ti % 2)
                    par_ = ti % 2
                    lst = []
                    for r in range(4):
                        for ri in range(2):
                            lst.append((pb[:, 0:nt], Cst[:, (4 * c + r) * 2 + ri, :], Hb[ri][par_][:, r, 0:nt], dict(start=(len(lst) == 0), stop=(len(lst) == 7))))
                    tr.mms(lst, ["Cst", "Hb0%d" % par_, "Hb1%d" % par_], [pkk])
                    STT(yt[:, 0:nt], uTc[:, c0:c0 + nt], pp[:, 11 + c:12 + c], pb[:, 0:nt], ALU.mult, ALU.add, [uk, "pp11", pkk], [yk])
                    ACT(ygT[:, c, c0:c0 + nt], yt[:, 0:nt], AF.Gelu, [yk], ["ygT"])

                ntl = len(tiles)
                emit_bu(0)
                emit_cast(0)
                for ti in range(ntl):
                    emit_in(ti)
                    if ti + 1 < ntl:
                        emit_bu(ti + 1)
                        emit_cast(ti + 1)
                    emit_out(ti)
                    if ti >= 1:
                        emit_y(ti - 1)
                emit_y(ntl - 1)
            for slot, (tr_, ti_) in enumerate(((hr_p, hi_p), (hr_s, hi_s), (hr_s, hi_s))):
                for ri, tt in enumerate((tr_, ti_)):
                    off = l * 4096 if slot == 0 else (l * 2 + slot - 1) * 4096
                    store_GP(tt, off, Hc[:, slot, ri, :], "Hc%d" % slot, ("oh", l, slot, ri))
            tr.barrier()
            stop_here("ssm_%d" % l, dict(ygT=ygT[:], Hc=Hc[:]))

        oes = contextlib.ExitStack()
        oT = SBT(oes, "oT", [128, 8, NT], BF16)
        with contextlib.ExitStack() as es:
            wbs = [SBT(es, "wb%d" % i, [128, 8, 512], BF16) for i in range(2)]
            Qt = [SBT(es, "Qt%d" % m, [128, NT], BF16) for m in range(2)]
            Kt = [SBT(es, "Kt%d" % m, [128, NT], BF16) for m in range(2)]
            Kp = [[SBT(es, "Kp%d%d" % (b, m), [128, PAST + NS], BF16) for m in range(2)] for b in range(2)]
            Vx = SBT(es, "Vx", [128, 18, 160], BF16)
            Vp = [SBT(es, "Vp%d" % b, [128, 8, 160], BF16) for b in range(2)]
            Gt = SBT(es, "Gt", [128, 18, 128])
            Pb = [SBT(es, "Pb%d" % i, [128, 512], BF16) for i in range(4)]
            sqs = [SBT(es, "sq%d" % i, [128, 512], BF16) for i in range(2)]
            rss = [SBT(es, "rs%d" % i, [128, 512]) for i in range(2)]
            ksts = [SBT(es, "kst%d" % i, [128, 2, 64]) for i in range(4)]
            vsts = [SBT(es, "vst%d" % i, [128, 128]) for i in range(4)]
            sgs = [SBT(es, "sg%d" % i, [128, 128]) for i in range(2)]
            ckbs = [SBT(es, "ckb%d" % i, [128, 8, 128], BF16) for i in range(2)]
            osbs = [SBT(es, "osb%d" % i, [128, 128]) for i in range(4)]
            ots = [SBT(es, "ot%d" % i, [128, 128]) for i in range(4)]
            onbs = [SBT(es, "onb%d" % i, [128, 128], BF16) for i in range(4)]
            fss = [SBT(es, "fs%d" % i, [128, 8]) for i in range(4)]
            onesbd = SBT(es, "onesbd", [128, 128], BF16)
            MEMSET(onesbd[:], 0.0, ["onesbd"])
            MEMSET(onesbd[0:64, 0:64], 1.0, ["onesbd"])
            MEMSET(onesbd[64:128, 64:128], 1.0, ["onesbd"])
            MEMSET(Vx[:, :, 128:129], 1.0, ["Vx"])
            for b in range(2):
                MEMSET(Vp[b][:, :, 128:129], 1.0, ["Vp%d" % b])
            pcount = [0]
            fcount = [0]
            ocnt = [0]
            Ocp = [[SBT(es, "Ocp%d%d" % (i, j), [128, 387]) for j in range(3)] for i in range(2)]
            wbase = l * D * INC

            def load_wb(hh):
                wbt = wbs[hh % 2]; wk = "wb%d" % (hh % 2)
                for j, colo in enumerate((hh * 64, 512 + hh * 64, 1024 + hh * 64, 1536 + hh * 64)):
                    tr.dma("pool", "wb", wbt[:, :, 64 * j:64 * j + 64], dap(w_in, wbase + colo, [[INC, 128], [INC * 128, 8], [1, 64]]), (), [wk])
                tr.dma("pool", "wb", wbt[:, :, 256:384], dap(w_in, wbase + 2048 + hh * 128, [[INC, 128], [INC * 128, 8], [1, 128]]), (), [wk])
                tr.dma("pool", "wb", wbt[:, :, 384:512], dap(w_in, wbase + 3072 + hh * 128, [[INC, 128], [INC * 128, 8], [1, 128]]), (), [wk])

            load_wb(0)
            for h in range(H):
                wb = wbs[h % 2]; wk = "wb%d" % (h % 2)
                for m in range(2):
                    tr.dma("pool", "aug", Qt[m][64:68, 0:T], dap(c_qa, h * T, [[H * T, 4], [1, T]]), (), ["QtA%d" % m])
                    tr.dma("pool", "aug", Kt[m][64:68, 0:T], dap(c_ka, h * T, [[H * T, 4], [1, T]]), (), ["KtA%d" % m])
                    for b in range(2):
                        tr.dma("pool", "aug", Qt[m][64:68, T + NS * b:T + NS * b + NS], dap(c_qa, h * T + PAST, [[H * T, 4], [1, NS]]), (), ["QtA%d" % m])
                        tr.dma("pool", "aug", Kp[b][m][64:68, :], dap(c_ka, h * T, [[H * T, 4], [1, PAST + NS]]), (), ["KpA%d%d" % (b, m)])
                for b in range(2):
                    for m in range(2):
                        tr.dma("pool", "ck", ckbs[b][:, :, 64 * m:64 * m + 64],
                               dap(ck, ((l * 2 + b) * PAST) * 1024 + m * 512 + h * 64, [[1024, 128], [1024 * 128, 8], [1, 64]]), (), ["ckb%d" % b])
                    tr.dma("pool", "cv", Vp[b][:, :, 0:128], dap(cv, ((l * 2 + b) * PAST) * 1024 + h * 128, [[1024, 128], [1024 * 128, 8], [1, 128]]), (), ["Vp%d" % b])
                if h + 1 < H:
                    load_wb(h + 1)
                def emit_vga(ti):
                    kind, i, n, c0 = tiles[ti]
                    u_ = ti % 2
                    pbv = ps[2 + u_]; pkv = "ps%d" % (2 + u_)
                    pbg = ps[0 + u_]; pkg = "ps%d" % (0 + u_)
                    r4 = ti % 4
                    vst = vsts[r4]; vk_ = "vst%d" % r4; sg = sgs[ti % 2]; gk_ = "sg%d" % (ti % 2)
                    tr.mms([(pbv[0:n, 0:128], xnT[:, kc, c0:c0 + n], wb[:, kc, 256:384], dict(start=(kc == 0), stop=(kc == 7))) for kc in range(8)],
                           [wk, "xnT"], [pkv])
                    tr.mms([(pbg[0:n, 0:128], xnT[:, kc, c0:c0 + n], wb[:, kc, 384:512], dict(start=(kc == 0), stop=(kc == 7))) for kc in range(8)],
                           [wk, "xnT"], [pkg])
                    VCP(vst[0:n, :], pbv[0:n, 0:128], [pkv], [vk_])
                    VCP(Vx[0:n, ti, 0:128], pbv[0:n, 0:128], [pkv], ["Vx"])
                    if kind == "p":
                        dst = dap(nv_p, (l * T + c0) * 1024 + h * 128, [[1024, n], [1, 128]])
                    else:
                        dst = dap(nv_s, ((l * 2 + i) * NS) * 1024 + h * 128, [[1024, n], [1, 128]])
                    tr.dma("sp", "o_v", dst, vst[0:n, :], [vk_], [("ov", l, h, ti)], is_out=True)
                    ACT(sg[0:n, :], pbg[0:n, 0:128], AF.Silu, [pkg], [gk_])
                    VTT(Gt[0:n, ti, :], sg[0:n, :], subB[0:n, :], ALU.mult, [gk_, "subB"], ["Gt"])

                vga_next = [0]
                for pr in range(2):
                    isq = pr == 0
                    dsts = Qt if isq else Kt
                    dks = ["Qt0", "Qt1"] if isq else ["Kt0", "Kt1"]
                    gci = 0 if isq else 1
                    for (b0, bn) in blocks:
                        u_ = pcount[0] % 2; pcount[0] += 1
                        pb = ps[5 + u_]; pkk = "ps%d" % (5 + u_)
                        sq = sqs[u_]; rs = rss[u_]; sk_ = "sq%d" % u_; rk_ = "rs%d" % u_
                        tr.mms([(pb[:, 0:bn], wb[:, kc, 128 * pr:128 * pr + 128], xnT[:, kc, b0:b0 + bn], dict(start=(kc == 0), stop=(kc == 7))) for kc in range(8)],
                               [wk, "xnT"], [pkk])
                        ACT(sq[:, 0:bn], pb[:, 0:bn], AF.Square, [pkk], [sk_])
                        tr.mms([(ps[4][:, 0:bn], onesbd[:], sq[:, 0:bn], dict(start=True, stop=True))], [sk_, "onesbd"], ["ps4"])
                        ACT(rs[:, 0:bn], ps[4][:, 0:bn], AF.Ln, ["ps4", "epsc"], [rk_], scale=1.0 / 64, bias=epsc[:, :])
                        ACT(rs[:, 0:bn], rs[:, 0:bn], AF.Exp, [rk_], [rk_], scale=-0.5)
                        for m in range(2):
                            STT(dsts[m][0:64, b0:b0 + bn], pb[64 * m:64 * m + 64, 0:bn], pp[64 * m:64 * m + 64, gci:gci + 1], rs[64 * m:64 * m + 64, 0:bn],
                                ALU.mult, ALU.mult, [pkk, rk_, "pp0", "pp1"], [dks[m]])
                        for _ in range(2):
                            if vga_next[0] < len(tiles):
                                emit_vga(vga_next[0]); vga_next[0] += 1
                while vga_next[0] < len(tiles):
                    emit_vga(vga_next[0]); vga_next[0] += 1
                stop_here("attnB", dict(Qt0=Qt[0][:], Kt0=Kt[0][:], Qt1=Qt[1][:], Kt1=Kt[1][:]))
                for ti, (kind, i, n, c0) in enumerate(tiles):
                    r4 = ti % 4; r2 = ti % 2
                    kst = ksts[r4]; kk_ = "kst%d" % r4; tk_ = "psT"
                    for m in range(2):
                        TRANSP(psT[0:n, 256 + 128 * r2 + 64 * m:256 + 128 * r2 + 64 * m + 64], Kt[m][0:64, c0:c0 + n], 64, ["Kt%d" % m], [tk_])
                    VCP(kst[0:n].rearrange("p m d -> p (m d)"), psT[0:n, 256 + 128 * r2:256 + 128 * r2 + 128], [tk_], [kk_])
                    if kind == "p":
                        dst = dap(nk_p, (l * T + c0) * 1024 + h * 64, [[1024, n], [512, 2], [1, 64]])
                    else:
                        dst = dap(nk_s, ((l * 2 + i) * NS) * 1024 + h * 64, [[1024, n], [512, 2], [1, 64]])
                    tr.dma("sp", "o_k", dst, kst[0:n], [kk_], [("ok", l, h, ti)], is_out=True)
                stop_here("attnD", dict(Gt=Gt[:]))
                for b in range(2):
                    ckb = ckbs[b]
                    for m in range(2):
                        for tt in range(8):
                            TRANSP(psT[0:64, 128 * tt:128 * tt + 128], ckb[:, tt, 64 * m:64 * m + 64], 128, ["ckb%d" % b], ["psT"])
                        VCP(Kp[b][m][0:64, 0:PAST], psT[0:64, :], ["psT"], ["Kp%d%d" % (b, m)])
                        VCP(Kp[b][m][0:64, PAST:PAST + NS], Kt[m][0:64, T + NS * b:T + NS * b + NS], ["Kt%d" % m], ["Kp%d%d" % (b, m)])

                stop_here("attnE", dict(Kp00=Kp[0][0][:]))
                pending = []

                def attend(qcols, qpos0, nq, ktiles, gt_tile):
                    steps = []
                    blocks_ = []
                    for qb0 in range(0, nq, 512):
                        nqb = min(512, nq - qb0)
                        qts = [(qb0 + 128 * i, min(128, nq - qb0 - 128 * i)) for i in range((nqb + 127) // 128)]
                        lastk = {}
                        vis = []
                        for kj, (Kap, Vap, nk, kp0) in enumerate(ktiles):
                            first = None
                            diag = False
                            for qi, (q0, nqi) in enumerate(qts):
                                tpos = qpos0 + q0
                                if kp0 // 64 > (tpos + nqi - 1) // 64:
                                    continue
                                if first is None:
                                    first = qi
                                    diag = (kp0 + nk - 1) > tpos
                                lastk[qi] = kj
                            if first is not None:
                                vis.append((kj, first, diag))
                        blk = dict(qb0=qb0, nqb=nqb, qts=qts, lastk=lastk, started=set(), nsteps=2 * len(vis))
                        blocks_.append(blk)
                        for (kj, first, diag) in vis:
                            for m in range(2):
                                steps.append(dict(blk=blk, kj=kj, first=first, diag=diag, m=m))
                        steps[-1]["last"] = True

                    def emit_S(st):
                        blk = st["blk"]; qts = blk["qts"]; m = st["m"]
                        Kap, Vap, nk, kp0 = ktiles[st["kj"]]
                        c0 = qts[st["first"]][0]; ncol = blk["qb0"] + blk["nqb"] - c0
                        si = pcount[0] % 2; pbi = pcount[0] % 4; pcount[0] += 1
                        S = ps[si]; sk = "ps%d" % si
                        lst = [(S[0:nk, 0:ncol], Kap[m], Qt[m][0:68, qcols + c0:qcols + c0 + ncol], dict(start=True, stop=not st["diag"]))]
                        if st["diag"]:
                            nqd = qts[st["first"]][1]
                            lst.append((S[0:nk, 0:nqd], ident[0:nk, 0:nk], dmat[0:nk, h, 0:nqd], dict(start=False, stop=True)))
                        tr.mms(lst, ["Qt%d" % m, "Kt%d" % m, "Kp0%d" % m, "Kp1%d" % m, "QtA%d" % m, "KtA%d" % m, "KpA0%d" % m, "KpA1%d" % m, "ident", "dmat"], [sk])
                        Pt = Pb[pbi]; pk_ = "Pb%d" % pbi
                        ACT(Pt[0:nk, 0:ncol], S[0:nk, 0:ncol], AF.Exp, [sk], [pk_])
                        st["P"] = (Pt, pk_, c0)

                    def emit_PV(st):
                        blk = st["blk"]; qts = blk["qts"]; m = st["m"]
                        Kap, Vap, nk, kp0 = ktiles[st["kj"]]
                        Pt, pk_, c0 = st["P"]
                        for qi in range(st["first"], len(qts)):
                            q0, nqi = qts[qi]
                            a_ = m * 4 + qi
                            bank = 2 + a_ // 3
                            O = ps[bank][0:nqi, 129 * (a_ % 3):129 * (a_ % 3) + 129]
                            stf = bank not in blk["started"]
                            blk["started"].add(bank)
                            tr.mms([(O, Pt[0:nk, q0 - c0:q0 - c0 + nqi], Vap, dict(start=stf, stop=(blk["lastk"][qi] == st["kj"]), skip_group_check=True))],
                                   [pk_, "Vx", "Vp0", "Vp1"], ["ps%d" % bank])

                    def finalize(blk):
                        qts = blk["qts"]
                        nqt = len(qts)
                        oset = ocnt[0] % 2; ocnt[0] += 1
                        nmax = max(n for (_, n) in qts)
                        used = sorted(set(2 + (m * 4 + qi) // 3 for qi in range(nqt) for m in range(2)))
                        for bank in used:
                            ACP(Ocp[oset][bank - 2][0:nmax, :], ps[bank][0:nmax, 0:387], ["ps%d" % bank], ["ocp%d_%d" % (oset, bank)])
                        Aps = []
                        for qi, (q0, nqi) in enumerate(qts):
                            A = []
                            for m in range(2):
                                a_ = m * 4 + qi
                                bank = 2 + a_ // 3
                                A.append((Ocp[oset][bank - 2][0:nqi, 129 * (a_ % 3):129 * (a_ % 3) + 129], "ocp%d_%d" % (oset, bank)))
                            Aps.append(A)
                        for qi, (q0, n) in enumerate(qts):
                            A = Aps[qi]; fs = fss[qi]; ot = ots[qi]; osb = osbs[qi]
                            fk = "fs%d_" % qi; otk = "ot%d" % qi; osk = "osb%d" % qi
                            tr.op("dve", lambda e: e.reciprocal(out=fs[0:n, 0:1], in_=A[0][0][:, 128:129]), [A[0][1]], [fk + "0"])
                            tr.op("dve", lambda e: e.reciprocal(out=fs[0:n, 1:2], in_=A[1][0][:, 128:129]), [A[1][1]], [fk + "1"])
                            VTT(fs[0:n, 1:2], fs[0:n, 1:2], pp[0:n, 2:3], ALU.mult, [fk + "1", "pp2"], [fk + "1"])
                            VTS(ot[0:n, :], A[1][0][:, 0:128], fs[0:n, 1:2], None, ALU.mult, None, [A[1][1], fk + "1"], [otk])
                            STT(osb[0:n, :], A[0][0][:, 0:128], fs[0:n, 0:1], ot[0:n, :], ALU.mult, ALU.add, [A[0][1], fk + "0", otk], [osk])
                            VTT(ot[0:n, :], osb[0:n, :], osb[0:n, :], ALU.mult, [osk], [otk])
                            RSUM(fs[0:n, 2:3], ot[0:n, :], [otk], [fk + "2"])
                        def stage2():
                            for qi, (q0, n) in enumerate(qts):
                                fs = fss[qi]; fk = "fs%d_" % qi
                                RSQ(fs[0:n, 2:3], fs[0:n, 2:3], 1.0 / 128, n, [fk + "2"], [fk + "2"])
                            for qi, (q0, n) in enumerate(qts):
                                fs = fss[qi]; osb = osbs[qi]; onb = onbs[qi]
                                STT(onb[0:n, :], osb[0:n, :], fs[0:n, 2:3], Gt[0:n, gt_tile(q0), :], ALU.mult, ALU.mult,
                                    ["osb%d" % qi, "fs%d_2" % qi, "Gt"], ["onb%d" % qi])

                        def deferred():
                            for qi, (q0, n) in enumerate(qts):
                                TRANSP(psT[:, 128 * qi:128 * qi + n], onbs[qi][0:n, :], n, ["onb%d" % qi], ["psT"])
                            qa = qts[0][0]; ntot = qts[-1][0] + qts[-1][1] - qa
                            if all(n == 128 for (_, n) in qts):
                                VCP(oT[:, h, qcols + qa:qcols + qa + ntot], psT[:, 0:ntot], ["psT"], ["oT"])
                            else:
                                for qi, (q0, n) in enumerate(qts):
                                    VCP(oT[:, h, qcols + q0:qcols + q0 + n], psT[:, 128 * qi:128 * qi + n], ["psT"], ["oT"])
                        return [[8, stage2], [11, deferred]]

                    emit_S(steps[0])
                    for i, st in enumerate(steps):
                        if i + 1 < len(steps):
                            emit_S(steps[i + 1])
                        emit_PV(st)
                        for p_ in pending:
                            p_[0] -= 1
                        while pending and pending[0][0] <= 0:
                            pending.pop(0)[1]()
                        if st.get("last"):
                            while pending:
                                pending.pop(0)[1]()
                            pending.extend(finalize(st["blk"]))

                kts = [([Kt[0][0:68, 128 * j:128 * j + 128], Kt[1][0:68, 128 * j:128 * j + 128]], Vx[:, j, 0:129], 128, 128 * j) for j in range(16)]
                attend(0, 0, T, kts, lambda q0: q0 // 128)
                stop_here("attnF", dict(oT=oT[:]))
                for b in range(2):
                    kts = [([Kp[b][0][0:68, 128 * j:128 * j + 128], Kp[b][1][0:68, 128 * j:128 * j + 128]], Vp[b][:, j, 0:129], 128, 128 * j) for j in range(8)]
                    kts.append(([Kp[b][0][0:68, PAST:PAST + NS], Kp[b][1][0:68, PAST:PAST + NS]], Vx[0:NS, 16 + b, 0:129], NS, PAST))
                    attend(T + NS * b, PAST, NS, kts, lambda q0, b=b: 16 + b)
                while pending:
                    pending.pop(0)[1]()
                stop_here("attnh%d_%d" % (h, l), dict(oT=oT[:], Qt0=Qt[0][:], Kt0=Kt[0][:], Qt1=Qt[1][:], Kt1=Kt[1][:], Vx=Vx[:], Gt=Gt[:], Kp00=Kp[0][0][:], Vp0=Vp[0][:]))
            tr.barrier()
            stop_here("attn_%d" % l, dict(oT=oT[:]))

        with contextlib.ExitStack() as es:
            ysgT = SBT(es, "ysgT", [128, 8, NT], BF16)
            wA = SBT(es, "wA", [128, 8, D], BF16); wB = SBT(es, "wB", [128, 8, D], BF16)
            tmp = [SBT(es, "tmp%d" % i, [128, 512]) for i in range(2)]
            tb16 = SBT(es, "tb16", [128, 512], BF16)
            xr = [SBT(es, "xr%d" % i, [128, D]) for i in range(2)]

            def loadw(dst, key, tnsr, base, rowlen):
                tr.dma("pool", key, dst[:], dap(tnsr, base, [[rowlen, 128], [rowlen * 128, 8], [1, D]]), (), [key])

            def proj(wt, wkey, src, skey, oc, b0, bn, pi):
                pb = ps[pi]
                tr.mms([(pb[:, 0:bn], wt[:, kc, 128 * oc:128 * oc + 128], src[:, kc, b0:b0 + bn], dict(start=(kc == 0), stop=(kc == 7))) for kc in range(8)],
                       [wkey, skey], ["ps%d" % pi])
                return pb[:, 0:bn], "ps%d" % pi

            cnt = [0]
            loadw(wA, "wA", w_glu, l * D * D, D)
            loadw(wB, "wB", w_in, l * D * INC + 5120, INC)
            for oc in range(8):
                for (b0, bn) in blocks:
                    pi = cnt[0] % 2; cnt[0] += 1
                    pa, ka = proj(wA, "wA", ygT, "ygT", oc, b0, bn, pi)
                    ACT(tmp[0][:, 0:bn], pa, AF.Sigmoid, [ka, "pp3"], ["tmp0"], bias=pp[:, 3 + oc:4 + oc])
                    VTT(ysgT[:, oc, b0:b0 + bn], ygT[:, oc, b0:b0 + bn], tmp[0][:, 0:bn], ALU.mult, ["ygT", "tmp0"], [("ysg", oc)])
            for oc in range(8):
                for (b0, bn) in blocks:
                    pi = cnt[0] % 2; cnt[0] += 1
                    pa, ka = proj(wB, "wB", xnT, "xnT", oc, b0, bn, pi)
                    ACT(tb16[:, 0:bn], pa, AF.Silu, [ka], ["tb16"])
                    VTT(ysgT[:, oc, b0:b0 + bn], ysgT[:, oc, b0:b0 + bn], tb16[:, 0:bn], ALU.mult, [("ysg", oc), "tb16"], [("ysg", oc)])
            allysg = [("ysg", oc) for oc in range(8)]
            loadw(wA, "wA", w_sp, l * D * D, D)
            loadw(wB, "wB", w_in, l * D * INC + 7168, INC)
            for oc in range(8):
                for (b0, bn) in blocks:
                    pi = cnt[0] % 2; cnt[0] += 1
                    pg, kg = proj(wB, "wB", xnT, "xnT", oc, b0, bn, 2 + pi)
                    ACT(tmp[0][:, 0:bn], pg, AF.Sigmoid, [kg], ["tmp0"])
                    pb = ps[pi]
                    tr.mms([(pb[:, 0:bn], wA[:, kc, 128 * oc:128 * oc + 128], ysgT[:, kc, b0:b0 + bn], dict(start=(kc == 0), stop=(kc == 7))) for kc in range(8)],
                           ["wA"] + allysg, ["ps%d" % pi])
                    VTT(ygT[:, oc, b0:b0 + bn], pb[:, 0:bn], tmp[0][:, 0:bn], ALU.mult, ["ps%d" % pi, "tmp0"], ["ygT"])
            loadw(wA, "wA", w_ap, l * D * D, D)
            loadw(wB, "wB", w_in, l * D * INC + 6144, INC)
            for oc in range(8):
                for (b0, bn) in blocks:
                    pi = cnt[0] % 2; cnt[0] += 1
                    pg, kg = proj(wB, "wB", xnT, "xnT", oc, b0, bn, 2 + pi)
                    ACT(tmp[0][:, 0:bn], pg, AF.Sigmoid, [kg], ["tmp0"])
                    pa, ka = proj(wA, "wA", oT, "oT", oc, b0, bn, pi)
                    VTT(tmp[1][:, 0:bn], pa, tmp[0][:, 0:bn], ALU.mult, [ka, "tmp0"], ["tmp1"])
                    VTT(ygT[:, oc, b0:b0 + bn], ygT[:, oc, b0:b0 + bn], tmp[1][:, 0:bn], ALU.add, ["ygT", "tmp1"], ["ygT"])
            loadw(wA, "wA", w_o, l * D * D, D)
            for ti, (kind, i, n, c0) in enumerate(tiles):
                xb = xr[ti % 2]; xk = "xr%d" % (ti % 2)
                if l == 0:
                    src = x_p.ap()[128 * i:128 * i + 128, :] if kind == "p" else x_s.ap()[i]
                else:
                    src = xmid.ap()[c0:c0 + n, :]
                tr.dma("sp", xk, xb[0:n, :], src, [("xmid", ti)], [xk])
                for hb in range(2):
                    pi = cnt[0] % 2; cnt[0] += 1
                    pb = ps[pi]
                    tr.mms([(pb[0:n, :], ygT[:, kc, c0:c0 + n], wA[:, kc, 512 * hb:512 * hb + 512], dict(start=(kc == 0), stop=(kc == 7))) for kc in range(8)],
                           ["wA", "ygT"], ["ps%d" % pi])
                    VTT(xb[0:n, 512 * hb:512 * hb + 512], xb[0:n, 512 * hb:512 * hb + 512], pb[0:n, :], ALU.add, [xk, "ps%d" % pi], [xk])
                if l == DEPTH - 1:
                    dst = y_p.ap()[128 * i:128 * i + 128, :] if kind == "p" else y_s.ap()[i]
                    tr.dma("sp", "o_y", dst, xb[0:n, :], [xk], [("oy", ti)], is_out=True)
                else:
                    tr.dma("sp", "o_x", xmid.ap()[c0:c0 + n, :], xb[0:n, :], [xk], [("xmid2", ti)])
            tr.barrier()
            stop_here("merge_%d" % l, dict(mT=ygT[:], ysgT=ysgT[:]))
        oes.close()
        yes.close()

    except StopBuild:
        pass
    tr.finish()
    return nc


_NC = None


def _consts():
    ident = np.eye(128, dtype=np.float32)
    dm = np.zeros((128, H, 128), np.float32)
    s = np.arange(128)[:, None]; t = np.arange(128)[None, :]
    for h in range(H):
        d = np.where(s > t, -2.0 * SLOPES[h] * (s - t), 0.0)
        d = np.where((s // 64) > (t // 64), -30000.0, d)
        dm[:, h, :] = d
    pos = np.arange(T)
    hi = (pos // 128) * 128.0; lo = (pos % 128) * 1.0
    qa = np.zeros((4, H, T), np.float32); ka = np.zeros((4, H, T), np.float32)
    for h in range(H):
        qa[0, h] = -SLOPES[h] * hi; qa[1, h] = -SLOPES[h] * lo; qa[2, h] = 1.0; qa[3, h] = 1.0
        ka[0, h] = 1.0; ka[1, h] = 1.0; ka[2, h] = SLOPES[h] * hi; ka[3, h] = SLOPES[h] * lo
    par = np.zeros((128, 2), np.float32)
    g8 = np.arange(128) // 16
    par[:, 0] = (g8 % 2 == 0); par[:, 1] = (g8 % 2 == 1)
    io = np.tile(np.arange(1, 129, dtype=np.float32)[None, :], (128, 1))
    return dict(c_id=ident, c_dm=dm, c_qa=qa, c_ka=ka, c_par=par, c_io=io)


def kernel(x_prompt, x_sample, cache_k, cache_v, state_ssm_re, state_ssm_im,
           norm_g, w_in, q_norm_g, k_norm_g, lam_q1, lam_k1, lam_q2, lam_k2, subln_g,
           w_attn_proj, ssm_lambda_re, ssm_lambda_im, ssm_log_dt, ssm_b_re, ssm_b_im,
           ssm_c_re, ssm_c_im, ssm_d, w_glu, b_glu, w_ssm_proj, w_out):
    global _NC
    if _NC is None:
        _NC = build()
    nc = _NC
    f = lambda a: np.ascontiguousarray(np.asarray(a, dtype=np.float32))
    shared = dict(norm_g=f(norm_g), w_in=f(w_in), qng=f(q_norm_g), kng=f(k_norm_g), lq1=f(lam_q1), lk1=f(lam_k1),
                  lq2=f(lam_q2), lk2=f(lam_k2), subg=f(subln_g), w_ap=f(w_attn_proj), lamr=f(ssm_lambda_re),
                  lami=f(ssm_lambda_im), ldt=f(ssm_log_dt), bre=f(ssm_b_re), bim=f(ssm_b_im), cre=f(ssm_c_re),
                  cim=f(ssm_c_im), dsk=f(ssm_d), w_glu=f(w_glu), b_glu=f(b_glu), w_sp=f(w_ssm_proj), w_o=f(w_out))
    shared.update(_consts())
    x_prompt = np.asarray(x_prompt); x_sample = np.asarray(x_sample)
    cache_k = np.asarray(cache_k); cache_v = np.asarray(cache_v)
    state_ssm_re = np.asarray(state_ssm_re); state_ssm_im = np.asarray(state_ssm_im)
    in_maps = []
    for c in range(NCORES):
        m = dict(shared)
        m["x_p"] = f(x_prompt[c]); m["x_s"] = f(x_sample[2 * c:2 * c + 2])
        m["ck"] = f(cache_k[:, 2 * c:2 * c + 2]); m["cv"] = f(cache_v[:, 2 * c:2 * c + 2])
        m["sre"] = f(state_ssm_re[:, 2 * c:2 * c + 2]); m["sim"] = f(state_ssm_im[:, 2 * c:2 * c + 2])
        in_maps.append(m)
    res = run_bass_kernel_spmd(nc, in_maps, core_ids=list(range(NCORES)))
    R = res.results
    cat0 = lambda k: np.stack([R[c][k] for c in range(NCORES)], axis=0)
    y_p = cat0("y_p")
    y_s = np.concatenate([R[c]["y_s"] for c in range(NCORES)], axis=0)
    nk_p = np.stack([R[c]["nk_p"] for c in range(NCORES)], axis=1)
    nv_p = np.stack([R[c]["nv_p"] for c in range(NCORES)], axis=1)
    hr_p = np.stack([R[c]["hr_p"] for c in range(NCORES)], axis=1)
    hi_p = np.stack([R[c]["hi_p"] for c in range(NCORES)], axis=1)
    nk_s = np.concatenate([R[c]["nk_s"] for c in range(NCORES)], axis=1)
    nv_s = np.concatenate([R[c]["nv_s"] for c in range(NCORES)], axis=1)
    hr_s = np.concatenate([R[c]["hr_s"] for c in range(NCORES)], axis=1)
    hi_s = np.concatenate([R[c]["hi_s"] for c in range(NCORES)], axis=1)
    return tuple(np.ascontiguousarray(a.astype(np.float32)) for a in (y_p, y_s, nk_p, nv_p, hr_p, hi_p, nk_s, nv_s, hr_s, hi_s))
```

```python
import math
import numpy as np
import concourse.bass as bass
import concourse.mybir as mybir
from concourse.bass_utils import run_bass_kernel_spmd

F32 = mybir.dt.float32
BF16 = mybir.dt.bfloat16
I32 = mybir.dt.int32
ALU = mybir.AluOpType
AF = mybir.ActivationFunctionType
AX = mybir.AxisListType

D = 1024
T = 2048
NS = 16
PAST = 1024
NT = T + 2 * NS
H = 8
DEPTH = 2
EPS = 1e-6
INC = 8192
NCORES = 8
SLOPES = [2.0 ** (-(h + 1)) for h in range(H)]


class Tr:
    def __init__(self, nc):
        self.nc = nc
        self.eng = {}
        for name, h in (("pe", nc.tensor), ("act", nc.scalar), ("dve", nc.vector), ("pool", nc.gpsimd), ("sp", nc.sync)):
            self.eng[name] = dict(h=h, sem=nc.alloc_semaphore(name="s_" + name), cnt=0, seen={})
        self.lastw = {}
        self.readers = {}
        self.dsem = {}
        self.out_tokens = []

    def _deps(self, reads, writes):
        deps = []
        for k in reads:
            if k in self.lastw:
                deps.append(self.lastw[k])
        for k in writes:
            if k in self.lastw:
                deps.append(self.lastw[k])
            deps.extend(self.readers.get(k, ()))
        return deps

    def _wait(self, ename, deps):
        e = self.eng[ename]
        best = {}
        for (sname, sem, val) in deps:
            if ename == "pe" and sname == "s_pe":
                continue
            if val > best.get(sname, (None, 0))[1]:
                best[sname] = (sem, val)
        for sname, (sem, val) in best.items():
            if e["seen"].get(sname, 0) < val:
                e["h"].wait_ge(sem, val)
                e["seen"][sname] = val

    def _commit(self, tok, reads, writes):
        for k in writes:
            self.lastw[k] = tok
            self.readers[k] = []
        for k in reads:
            self.readers.setdefault(k, []).append(tok)

    def op(self, ename, fn, reads=(), writes=()):
        e = self.eng[ename]
        self._wait(ename, self._deps(reads, writes))
        ins = fn(e["h"])
        e["cnt"] += 1
        ins.then_inc(e["sem"], 1)
        self._commit(("s_" + ename, e["sem"], e["cnt"]), reads, writes)

    def mms(self, lst, reads=(), writes=()):
        e = self.eng["pe"]
        self._wait("pe", self._deps(reads, writes))
        ins = None
        for (out, lhsT, rhs, kw) in lst:
            ins = self.nc.tensor.matmul(out, lhsT=lhsT, rhs=rhs, **kw)
        e["cnt"] += 1
        ins.then_inc(e["sem"], 1)
        self._commit(("s_pe", e["sem"], e["cnt"]), reads, writes)

    def dma(self, q, slot, out, in_, reads=(), writes=(), is_out=False):
        if slot.startswith("o_") or slot == "dbg":
            slot = "st_" + "_".join(str(x) for x in (reads[0] if isinstance(reads[0], tuple) else (reads[0],))) if reads else slot
        else:
            slot = "ld_" + "_".join(str(x) for x in (writes[0] if isinstance(writes[0], tuple) else (writes[0],)))
        if slot not in self.dsem:
            self.dsem[slot] = [self.nc.alloc_semaphore(name="d_" + slot), 0]
        ds = self.dsem[slot]
        self._wait(q, self._deps(reads, writes))
        self.eng[q]["h"].dma_start(out=out, in_=in_, allow_slow_non_contiguous=True).then_inc(ds[0], 16)
        ds[1] += 16
        tok = ("d_" + slot, ds[0], ds[1])
        self._commit(tok, reads, writes)
        if is_out:
            self.out_tokens.append(tok)

    def barrier(self):
        toks = [("s_" + n, e["sem"], e["cnt"]) for n, e in self.eng.items() if e["cnt"] > 0]
        toks += [("d_" + k, v[0], v[1]) for k, v in self.dsem.items() if v[1] > 0]
        for n in self.eng:
            e = self.eng[n]
            for (sname, sem, val) in toks:
                if sname == "s_" + n:
                    continue
                if e["seen"].get(sname, 0) < val:
                    e["h"].wait_ge(sem, val)
                    e["seen"][sname] = val
        self.lastw = {}
        self.readers = {}

    def finish(self):
        best = {}
        for (sname, sem, val) in self.out_tokens:
            if val > best.get(sname, (None, 0))[1]:
                best[sname] = (sem, val)
        for sname, (sem, val) in best.items():
            self.nc.sync.wait_ge(sem, val)


def dap(t, off, dims):
    return bass.AP(t, off, [list(d) for d in dims])


class StopBuild(Exception):
    pass


def build(stop=None):
    nc = bass.Bass("TRN2", target_bir_lowering=False)
    dt_in = lambda n, s: nc.dram_tensor(n, s, F32, kind="ExternalInput")
    dt_out = lambda n, s: nc.dram_tensor(n, s, F32, kind="ExternalOutput")
    x_p = dt_in("x_p", [T, D]); x_s = dt_in("x_s", [2, NS, D])
    ck = dt_in("ck", [DEPTH, 2, PAST, 2, H, 64]); cv = dt_in("cv", [DEPTH, 2, PAST, H, 128])
    sre = dt_in("sre", [DEPTH, 2, 64, 64]); sim = dt_in("sim", [DEPTH, 2, 64, 64])
    norm_g = dt_in("norm_g", [DEPTH, D]); w_in = dt_in("w_in", [DEPTH, D, INC])
    qng = dt_in("qng", [DEPTH, 64]); kng = dt_in("kng", [DEPTH, 64])
    lq1 = dt_in("lq1", [DEPTH, 64]); lk1 = dt_in("lk1", [DEPTH, 64]); lq2 = dt_in("lq2", [DEPTH, 64]); lk2 = dt_in("lk2", [DEPTH, 64])
    subg = dt_in("subg", [DEPTH, 128]); w_ap = dt_in("w_ap", [DEPTH, D, D])
    lamr = dt_in("lamr", [DEPTH, 64, 64]); lami = dt_in("lami", [DEPTH, 64, 64]); ldt = dt_in("ldt", [DEPTH, 64])
    bre = dt_in("bre", [DEPTH, 64, 64, 16]); bim = dt_in("bim", [DEPTH, 64, 64, 16])
    cre = dt_in("cre", [DEPTH, 64, 16, 64]); cim = dt_in("cim", [DEPTH, 64, 16, 64])
    dsk = dt_in("dsk", [DEPTH, D]); w_glu = dt_in("w_glu", [DEPTH, D, D]); b_glu = dt_in("b_glu", [DEPTH, D])
    w_sp = dt_in("w_sp", [DEPTH, D, D]); w_o = dt_in("w_o", [DEPTH, D, D])
    c_id = dt_in("c_id", [128, 128]); c_dm = dt_in("c_dm", [128, H, 128])
    c_qa = dt_in("c_qa", [4, H, T]); c_ka = dt_in("c_ka", [4, H, T]); c_par = dt_in("c_par", [128, 2]); c_io = dt_in("c_io", [128, 128])
    y_p = dt_out("y_p", [T, D]); y_s = dt_out("y_s", [2, NS, D])
    nk_p = dt_out("nk_p", [DEPTH, T, 2, H, 64]); nv_p = dt_out("nv_p", [DEPTH, T, H, 128])
    hr_p = dt_out("hr_p", [DEPTH, 64, 64]); hi_p = dt_out("hi_p", [DEPTH, 64, 64])
    nk_s = dt_out("nk_s", [DEPTH, 2, NS, 2, H, 64]); nv_s = dt_out("nv_s", [DEPTH, 2, NS, H, 128])
    hr_s = dt_out("hr_s", [DEPTH, 2, 64, 64]); hi_s = dt_out("hi_s", [DEPTH, 2, 64, 64])
    xmid = nc.dram_tensor("xmid", [NT, D], F32, kind="Internal")

    tr = Tr(nc)
    import contextlib

    ucnt = [0]

    def SBT(es, n, s, d=F32):
        ucnt[0] += 1
        return es.enter_context(nc.sbuf_tensor("%s_%d" % (n, ucnt[0]), s, d))

    def stop_here(name, dumps):
        if stop != name:
            return
        tr.barrier()
        for dn, ap in dumps.items():
            shp = list(ap.shape)
            dtn = nc.dram_tensor("dbg_" + dn, shp, ap.dtype, kind="ExternalOutput")
            tr.dma("sp", "dbg", dtn.ap(), ap, (), [("dbg", dn)], is_out=True)
        raise StopBuild()

    ges = contextlib.ExitStack()
    ps = [ges.enter_context(nc.psum_tensor("ps%d" % i, [128, 512], F32)) for i in range(7)]
    psT = ges.enter_context(nc.psum_tensor("psT", [128, 1024], BF16))
    xnT = SBT(ges, "xnT", [128, 8, NT], BF16)
    ident = SBT(ges, "ident", [128, 128], BF16)
    dmat = SBT(ges, "dmat", [128, H, 128], BF16)
    ones64 = SBT(ges, "ones64", [64, 64], BF16)
    par = SBT(ges, "par", [128, 2])
    epsc = SBT(ges, "epsc", [128, 1])
    sm = SBT(ges, "sm", [128, 16])
    pp = SBT(ges, "pp", [128, 32])
    lv = SBT(ges, "lv", [128, 4, 64])
    subB = SBT(ges, "subB", [128, 128])
    identF = SBT(ges, "identF", [128, 128])
    stg = SBT(ges, "stg", [64, 64]); stg2 = SBT(ges, "stg2", [64, 64]); stg3 = SBT(ges, "stg3", [64, 64])
    prow = SBT(ges, "prow", [32, 128]); dtb = SBT(ges, "dtb", [128, 64])

    def ACT(out, in_, func, reads, writes, **kw):
        tr.op("act", lambda e: e.activation(out=out, in_=in_, func=func, **kw), reads, writes)

    def VTT(out, a, b, op, reads, writes, eng="dve"):
        tr.op(eng, lambda e: e.tensor_tensor(out=out, in0=a, in1=b, op=op), reads, writes)

    def VTS(out, a, s1, s2, op0, op1, reads, writes, eng="dve"):
        if op1 is None:
            tr.op(eng, lambda e: e.tensor_scalar(out=out, in0=a, scalar1=s1, scalar2=None, op0=op0), reads, writes)
        else:
            tr.op(eng, lambda e: e.tensor_scalar(out=out, in0=a, scalar1=s1, scalar2=s2, op0=op0, op1=op1), reads, writes)

    def STT(out, a, s, b, op0, op1, reads, writes, eng="dve"):
        tr.op(eng, lambda e: e.scalar_tensor_tensor(out=out, in0=a, scalar=s, in1=b, op0=op0, op1=op1), reads, writes)

    def VCP(out, in_, reads, writes, eng="dve"):
        tr.op(eng, lambda e: e.tensor_copy(out=out, in_=in_), reads, writes)

    def ACP(out, in_, reads, writes):
        tr.op("act", lambda e: e.copy(out=out, in_=in_), reads, writes)

    def MEMSET(ap, val, writes, eng="dve"):
        tr.op(eng, lambda e: e.memset(ap, val), (), writes)

    def RSUM(out, in_, reads, writes):
        tr.op("dve", lambda e: e.reduce_sum(out=out, in_=in_, axis=AX.X), reads, writes)

    def RSQ(out, in_, scale, n, reads, writes):
        ACT(out, in_, AF.Ln, list(reads) + ["epsc"], writes, scale=scale, bias=epsc[0:n, :])
        ACT(out, out, AF.Exp, writes, writes, scale=-0.5)

    def TRANSP(out, in_, n, reads, writes):
        tr.op("pe", lambda e: e.transpose(out=out, in_=in_, identity=ident[0:n, 0:n]), list(reads) + ["ident"], writes)

    tr.dma("pool", "c0", ident[:], c_id.ap(), (), ["ident"])
    tr.dma("sp", "c1", identF[:], c_id.ap(), (), ["identF"])

    def TRANSPF(out, in_, n, reads, writes):
        tr.op("pe", lambda e: e.transpose(out=out, in_=in_, identity=identF[0:n, 0:n]), list(reads) + ["identF"], writes)

    def load_GP(dst, dkey, tensor, off):
        tr.dma("sp", "stg", stg[:], dap(tensor, off, [[64, 64], [1, 64]]), (), ["stg"])
        TRANSPF(ps[4][0:64, 0:64], stg[:], 64, ["stg"], ["ps4"])
        VCP(dst[0:64, :], ps[4][0:64, 0:64:2], ["ps4"], [dkey])
        VCP(dst[64:128, :], ps[4][0:64, 1:64:2], ["ps4"], [dkey])

    def store_GP(tensor, off, src, skey, okey):
        VCP(stg2[:, 0:64:2], src[0:64, :], [skey], ["stg2"])
        VCP(stg2[:, 1:64:2], src[64:128, :], [skey], ["stg2"])
        TRANSPF(ps[4][0:64, 0:64], stg2[:], 64, ["stg2"], ["ps4"])
        VCP(stg3[:], ps[4][0:64, 0:64], ["ps4"], ["stg3"])
        tr.dma("sp", "o_h", dap(tensor, off, [[64, 64], [1, 64]]), stg3[:], ["stg3"], [okey], is_out=True)
    tr.dma("pool", "c0", dmat[:], c_dm.ap(), (), ["dmat"])
    tr.dma("sp", "c1", par[:], c_par.ap(), (), ["par"])
    MEMSET(ones64[:], 1.0, ["ones64"])
    MEMSET(epsc[:], EPS, ["epsc"])

    tiles = [("p", i, 128, 128 * i) for i in range(16)] + [("s", b, NS, T + NS * b) for b in range(2)]
    blocks = [(512 * i, 512) for i in range(4)] + [(T, 2 * NS)]
    TWO_PI = 2.0 * math.pi

    try:
      for l in range(DEPTH):
        lam_init = 0.8 - 0.6 * math.exp(-0.3 * l)
        with contextlib.ExitStack() as es:
            ngb = SBT(es, "ngb", [128, D]); xt = [SBT(es, "xt%d" % i, [128, D]) for i in range(2)]
            scr = SBT(es, "scr", [128, D]); xnb = SBT(es, "xnb", [128, D], BF16)
            tr.dma("sp", "c1", ngb[:], dap(norm_g, l * D, [[0, 128], [1, D]]), (), ["ngb"])
            MEMSET(prow[:], 0.0, ["prow"])
            tr.dma("sp", "c1", prow[0:1, :].rearrange("p (a b) -> p a b", b=64), dap(qng, l * 64, [[64, 1], [0, 2], [1, 64]]), (), ["prow"])
            tr.dma("sp", "c1", prow[1:2, :].rearrange("p (a b) -> p a b", b=64), dap(kng, l * 64, [[64, 1], [0, 2], [1, 64]]), (), ["prow"])
            tr.dma("sp", "c1", prow[3:11, :], dap(b_glu, l * D, [[128, 8], [1, 128]]), (), ["prow"])
            tr.dma("sp", "c1", prow[11:19, :], dap(dsk, l * D, [[128, 8], [1, 128]]), (), ["prow"])
            TRANSPF(ps[4][:, 0:32], prow[:], 32, ["prow"], ["ps4"])
            VCP(pp[:], ps[4][:, 0:32], ["ps4"], ["pp0", "pp1", "pp2", "pp3", "pp11"])
            VTS(pp[:, 0:1], pp[:, 0:1], 0.125, None, ALU.mult, None, ["pp0"], ["pp0"])
            for j, tnsr in enumerate((lq1, lk1, lq2, lk2)):
                tr.dma("sp", "c1", lv[:, j, :], dap(tnsr, l * 64, [[0, 128], [1, 64]]), (), ["lv%d" % j])
            VTT(lv[:, 0, :], lv[:, 0, :], lv[:, 1, :], ALU.mult, ["lv0", "lv1"], ["lv0"])
            VTT(lv[:, 2, :], lv[:, 2, :], lv[:, 3, :], ALU.mult, ["lv2", "lv3"], ["lv2"])
            RSUM(sm[:, 0:1], lv[:, 0, :], ["lv0"], ["sm0"])
            RSUM(sm[:, 1:2], lv[:, 2, :], ["lv2"], ["sm1"])
            ACT(sm[:, 0:1], sm[:, 0:1], AF.Exp, ["sm0"], ["sm0"])
            ACT(sm[:, 1:2], sm[:, 1:2], AF.Exp, ["sm1"], ["sm1"])
            VTT(sm[:, 0:1], sm[:, 1:2], sm[:, 0:1], ALU.subtract, ["sm0", "sm1"], ["sm0"])
            VTS(pp[:, 2:3], sm[:, 0:1], -lam_init, None, ALU.add, None, ["sm0"], ["pp2"])
            tr.dma("sp", "c1", subB[:], dap(subg, l * 128, [[0, 128], [1, 128]]), (), ["subB"])
            VTS(subB[:], subB[:], 1.0 - lam_init, None, ALU.mult, None, ["subB"], ["subB"])
            for ti, (kind, i, n, c0) in enumerate(tiles):
                xb = xt[ti % 2]; xk = "xt%d" % (ti % 2)
                if l == 0:
                    src = x_p.ap()[128 * i:128 * i + 128, :] if kind == "p" else x_s.ap()[i]
                else:
                    src = xmid.ap()[c0:c0 + n, :]
                tr.dma("sp", xk, xb[0:n, :], src, [("xmid", ti)], [xk])
                VTT(scr[0:n, :], xb[0:n, :], xb[0:n, :], ALU.mult, [xk], ["scr"])
                RSUM(sm[0:n, 2:3], scr[0:n, :], ["scr"], ["sm2"])
                RSQ(sm[0:n, 2:3], sm[0:n, 2:3], 1.0 / D, n, ["sm2"], ["sm2"])
                STT(xnb[0:n, :], xb[0:n, :], sm[0:n, 2:3], ngb[0:n, :], ALU.mult, ALU.mult, [xk, "sm2", "ngb"], ["xnb"])
                for kc in range(8):
                    TRANSP(psT[:, 128 * kc:128 * kc + n], xnb[0:n, 128 * kc:128 * kc + 128], n, ["xnb"], ["psT"])
                VCP(xnT[:, :, c0:c0 + n], psT[:].rearrange("p (k t) -> p k t", t=128)[:, :, 0:n], ["psT"], ["xnT"])
            tr.barrier()
            stop_here("p0_%d" % l, dict(xnT=xnT[:], pp=pp[:], subB=subB[:]))

        yes = contextlib.ExitStack()
        ygT = SBT(yes, "ygT", [128, 8, NT], BF16)

        with contextlib.ExitStack() as es:
            tabCb = SBT(es, "tabCb", [128, 32, 128], BF16); tabSb = SBT(es, "tabSb", [128, 32, 128], BF16)
            tabCl = SBT(es, "tabCl", [128, 2, 32]); tabSl = SBT(es, "tabSl", [128, 2, 32])
            TLa = SBT(es, "TLa", [128, 2, 2, 32]); TLb = SBT(es, "TLb", [128, 2, 2, 32])
            RHO2 = SBT(es, "RHO2", [128, 2, 32])
            SinT = SBT(es, "SinT", [128, 64, 128], BF16); Cst = SBT(es, "Cst", [128, 64, 128], BF16)
            P = SBT(es, "P", [128, 20, 32])
            PI = SBT(es, "PI", [128, 32], I32)
            Hc = SBT(es, "Hc", [128, 3, 2, 32])
            RH = SBT(es, "RH", [128, 2, 4])
            es2 = contextlib.ExitStack()
            tabC = SBT(es2, "tabC", [128, 32, 128]); tabS = SBT(es2, "tabS", [128, 32, 128])
            es3 = contextlib.ExitStack()
            tmpA = SBT(es3, "tmpA", [128, 32, 128]); tmpB = SBT(es3, "tmpB", [128, 32, 128], I32); io = SBT(es3, "io", [128, 128])
            LR, LI, DT, ER, ANG, RHO, AR, AI, S1, C1, WR, WI, DEN, TA, TB, TC = [P[:, i, :] for i in range(16)]
            pk = lambda i: "P%d" % i
            load_GP(LR, pk(0), lamr, l * 4096)
            load_GP(LI, pk(1), lami, l * 4096)
            tr.dma("sp", "c1", dtb[:], dap(ldt, l * 64, [[0, 128], [1, 64]]), (), ["dtb"])
            VCP(P[0:64, 2, :], dtb[0:64, 0:64:2], ["dtb"], [pk(2)])
            VCP(P[64:128, 2, :], dtb[64:128, 1:64:2], ["dtb"], [pk(2)])
            VTS(TC, DT, 1.0 / 8, None, ALU.mult, None, [pk(2)], [pk(15)])
            VTS(DT, TC, 1.0 / 14, 1.0, ALU.mult, ALU.add, [pk(15)], [pk(2)])
            for kk in range(13, 0, -1):
                VTT(DT, DT, TC, ALU.mult, [pk(2), pk(15)], [pk(2)])
                VTS(DT, DT, 1.0 / kk, 1.0, ALU.mult, ALU.add, [pk(2)], [pk(2)])
            for _ in range(3):
                VTT(DT, DT, DT, ALU.mult, [pk(2)], [pk(2)])
            VTT(ER, LR, DT, ALU.mult, [pk(0), pk(2)], [pk(3)])
            VTT(ANG, LI, DT, ALU.mult, [pk(1), pk(2)], [pk(4)])
            VTS(TA, ANG, 1.0 / TWO_PI, None, ALU.mult, None, [pk(4)], [pk(13)])
            VCP(PI[:], TA, [pk(13)], ["PI"])
            VCP(TA, PI[:], ["PI"], [pk(13)])
            STT(ANG, TA, -TWO_PI, ANG, ALU.mult, ALU.add, [pk(13), pk(4)], [pk(4)])
            VTS(ANG, ANG, -3.14159, 3.14159, ALU.max, ALU.min, [pk(4)], [pk(4)])
            VTS(RHO, ER, 1.0 / 8, 1.0, ALU.mult, ALU.add, [pk(3)], [pk(5)])
            for kk in (7, 6, 5, 4, 3, 2, 1):
                VTT(RHO, RHO, ER, ALU.mult, [pk(5), pk(3)], [pk(5)])
                VTS(RHO, RHO, 1.0 / kk, 1.0, ALU.mult, ALU.add, [pk(5)], [pk(5)])
            ACT(S1, ANG, AF.Sin, [pk(4)], [pk(8)])
            ACT(TB, ANG, AF.Sin, [pk(4)], [pk(14)], scale=0.5)
            VTT(TB, TB, TB, ALU.mult, [pk(14)], [pk(14)])
            VTS(C1, TB, -2.0, 1.0, ALU.mult, ALU.add, [pk(14)], [pk(9)])
            VTT(AR, RHO, C1, ALU.mult, [pk(5), pk(9)], [pk(6)])
            VTT(AI, RHO, S1, ALU.mult, [pk(5), pk(8)], [pk(7)])
            VTS(TA, AR, -1.0, None, ALU.add, None, [pk(6)], [pk(13)])
            VTT(DEN, LR, LR, ALU.mult, [pk(0)], [pk(12)])
            VTT(TB, LI, LI, ALU.mult, [pk(1)], [pk(14)])
            VTT(DEN, DEN, TB, ALU.add, [pk(12), pk(14)], [pk(12)])
            tr.op("dve", lambda e: e.reciprocal(out=DEN, in_=DEN), [pk(12)], [pk(12)])
            VTT(WR, TA, LR, ALU.mult, [pk(13), pk(0)], [pk(10)])
            VTT(TB, AI, LI, ALU.mult, [pk(7), pk(1)], [pk(14)])
            VTT(WR, WR, TB, ALU.add, [pk(10), pk(14)], [pk(10)])
            VTT(WR, WR, DEN, ALU.mult, [pk(10), pk(12)], [pk(10)])
            VTT(WI, AI, LR, ALU.mult, [pk(7), pk(0)], [pk(11)])
            VTT(TB, TA, LI, ALU.mult, [pk(13), pk(1)], [pk(14)])
            VTT(WI, WI, TB, ALU.subtract, [pk(11), pk(14)], [pk(11)])
            VTT(WI, WI, DEN, ALU.mult, [pk(11), pk(12)], [pk(11)])
            tr.dma("sp", "c1", io[:], c_io.ap(), (), ["io"])
            angb = ANG.unsqueeze(2).broadcast_to([128, 32, 128]); iob = io[:].unsqueeze(1).broadcast_to([128, 32, 128])
            VTT(tabS[:], angb, iob, ALU.mult, [pk(4), "io"], ["tabS"])

            def rred(tb, key):
                VTS(tmpA[:], tb[:], 1.0 / TWO_PI, None, ALU.mult, None, [key], ["tmpA"])
                VCP(tmpB[:], tmpA[:], ["tmpA"], ["tmpB"])
                VCP(tmpA[:], tmpB[:], ["tmpB"], ["tmpA"])
                STT(tb[:], tmpA[:], -TWO_PI, tb[:], ALU.mult, ALU.add, ["tmpA", key], [key])
                VTS(tb[:], tb[:], -3.14159, 3.14159, ALU.max, ALU.min, [key], [key])

            rred(tabS, "tabS")
            VTS(tabC[:], tabS[:], math.pi / 2, None, ALU.add, None, ["tabS"], ["tabC"])
            rred(tabC, "tabC")
            ACT(tabS[:], tabS[:], AF.Sin, ["tabS"], ["tabS"])
            ACT(tabC[:], tabC[:], AF.Sin, ["tabC"], ["tabC"])
            ACP(tabCb[:], tabC[:], ["tabC"], ["tabCb"])
            ACP(tabSb[:], tabS[:], ["tabS"], ["tabSb"])
            for j_, col_ in enumerate((127, NS - 1)):
                VCP(tabCl[:, j_, :], tabC[:, :, col_], ["tabC"], ["tabCl"])
                VCP(tabSl[:, j_, :], tabS[:, :, col_], ["tabS"], ["tabSl"])
            for j_ in range(2):
                VCP(TLa[:, j_, 0, :], tabCl[:, j_, :], ["tabCl"], ["TLa"])
                VCP(TLa[:, j_, 1, :], tabSl[:, j_, :], ["tabSl"], ["TLa"])
                VCP(TLb[:, j_, 0, :], tabSl[:, j_, :], ["tabSl"], ["TLb"])
                VCP(TLb[:, j_, 1, :], tabCl[:, j_, :], ["tabCl"], ["TLb"])
            for j_ in range(2):
                VCP(RHO2[:, j_, :], RHO, [pk(5)], ["RHO2"])
            tr.barrier()
            es3.close()
            BBw = SBT(es2, "BBw", [128, 64, 128], BF16)
            Bn = SBT(es2, "Bn", [128, 2, 32, 16]); bb = SBT(es2, "bb", [128, 2, 32, 16]); btmp = SBT(es2, "btmp", [128, 32, 16])
            Cn = SBT(es2, "Cn", [128, 2, 8, 64]); Cw = SBT(es2, "Cw", [128, 2, 8, 128], BF16)
            tr.dma("sp", "c1", Bn[:, 0], dap(bre, l * 65536, [[16, 128], [2048, 32], [1, 16]]), (), ["Bn"])
            tr.dma("sp", "c1", Bn[:, 1], dap(bim, l * 65536, [[16, 128], [2048, 32], [1, 16]]), (), ["Bn"])
            wrb = WR.unsqueeze(2).broadcast_to([128, 32, 16]); wib = WI.unsqueeze(2).broadcast_to([128, 32, 16])
            VTT(bb[:, 0], Bn[:, 0], wrb, ALU.mult, ["Bn", pk(10)], ["bb"])
            VTT(btmp[:], Bn[:, 1], wib, ALU.mult, ["Bn", pk(11)], ["btmp"])
            VTT(bb[:, 0], bb[:, 0], btmp[:], ALU.subtract, ["bb", "btmp"], ["bb"])
            VTT(bb[:, 1], Bn[:, 1], wrb, ALU.mult, ["Bn", pk(10)], ["bb"])
            VTT(btmp[:], Bn[:, 0], wib, ALU.mult, ["Bn", pk(11)], ["btmp"])
            VTT(bb[:, 1], bb[:, 1], btmp[:], ALU.add, ["bb", "btmp"], ["bb"])
            MEMSET(BBw[:], 0.0, ["BBw"])
            BBw4 = BBw[:].rearrange("p (q r) c -> p q r c", r=2)
            for gs in range(2):
                for r in range(4):
                    for ri in range(2):
                        VCP(BBw4[64 * gs:64 * gs + 64, r::4, ri, 32 * r + 16 * gs:32 * r + 16 * gs + 16],
                            bb[64 * gs:64 * gs + 64, ri, r::4, :], ["bb", "BBw"], ["BBw"])
            for g4 in range(16):
                for j in range(4):
                    TRANSP(psT[:, 128 * j:128 * j + 128], BBw[:, 4 * g4 + j, :], 128, ["BBw"], ["psT"])
                VCP(SinT[:, 4 * g4:4 * g4 + 4, :], psT[:, 0:512].rearrange("p (a b) -> p a b", b=128), ["psT"], ["SinT"])
            tr.dma("sp", "c1", Cn[:, 0], dap(cre, l * 65536, [[64, 128], [8192, 8], [1, 64]]), (), ["Cn"])
            tr.dma("sp", "c1", Cn[:, 1], dap(cim, l * 65536, [[64, 128], [8192, 8], [1, 64]]), (), ["Cn"])
            VTS(Cn[:, 1], Cn[:, 1], -1.0, None, ALU.mult, None, ["Cn"], ["Cn"])
            for ri in range(2):
                for gs in range(2):
                    VTS(Cw[:, ri, :, 64 * gs:64 * gs + 64], Cn[:, ri], par[:, gs:gs + 1], None, ALU.mult, None, ["Cn", "par"], ["Cw"])
            MEMSET(Cst[:], 0.0, ["Cst"])
            Cst4 = Cst[:].rearrange("p (c r i) m -> p c r i m", r=4, i=2)
            for ri in range(2):
                for c4 in range(2):
                    for j in range(4):
                        TRANSP(psT[:, 128 * j:128 * j + 128], Cw[:, ri, 4 * c4 + j, :], 128, ["Cw"], ["psT"])
                    pv = psT[:, 0:512].rearrange("p (a b) -> p a b", b=128)
                    for r in range(4):
                        VCP(Cst4[:, 4 * c4:4 * c4 + 4, r, ri, 32 * r:32 * r + 32], pv[:, :, 32 * r:32 * r + 32], ["psT", "Cst"], ["Cst"])
            MEMSET(Hc[:, 0], 0.0, ["Hc0"])
            for b in range(2):
                load_GP(Hc[:, 1 + b, 0, :], "Hc%d" % (1 + b), sre, (l * 2 + b) * 4096)
                load_GP(Hc[:, 1 + b, 1, :], "Hc%d" % (1 + b), sim, (l * 2 + b) * 4096)
            tr.barrier()
            stop_here("ssmsetup_%d" % l, dict(P=P[:], tabC=tabC[:], tabS=tabS[:], SinT=SinT[:], Cst=Cst[:], Hc=Hc[:], bb=bb[:]))
            es2.close()
            uTcs = [SBT(es, "uTc%d" % i, [128, NT], BF16) for i in range(2)]
            wus = [SBT(es, "wu%d" % i, [128, 8, 128], BF16) for i in range(2)]
            d0s = [SBT(es, "d0%d" % i, [128, 4, 128]) for i in range(2)]
            W2 = SBT(es, "W2", [128, 2, 4, 128])
            W = [W2[:, i] for i in range(2)]
            Z2 = [SBT(es, "Z2%d" % j, [128, 2, 4, 128]) for j in range(2)]
            Z = [[Z2[j][:, i] for j in range(2)] for i in range(2)]
            RH2 = SBT(es, "RH2", [128, 2, 4]); cs2 = SBT(es, "cs2", [128, 2, 2, 4])
            T1 = SBT(es, "T1", [128, 4, 128], BF16); T2 = SBT(es, "T2", [128, 4, 128], BF16)
            Q1 = SBT(es, "Q1", [128, 4, 128], BF16); Q2 = SBT(es, "Q2", [128, 4, 128], BF16)
            BUb = [SBT(es, "BUb%d" % i, [128, 4, 128], BF16) for i in range(2)]
            Zb = [SBT(es, "Zb%d" % i, [128, 4, 128], BF16) for i in range(2)]
            Hf = [[SBT(es, "Hf%d%d" % (i, j), [128, 4, 128]) for j in range(2)] for i in range(2)]
            Hb = [[SBT(es, "Hb%d%d" % (i, j), [128, 4, 128], BF16) for j in range(2)] for i in range(2)]
            yts = [SBT(es, "yt%d" % i, [128, 128]) for i in range(2)]
            cs = SBT(es, "cs", [128, 4, 4])

            def load_wu(cc):
                tr.dma("pool", "wu", wus[cc % 2][:], dap(w_in, l * D * INC + 4096 + 128 * cc, [[INC, 128], [INC * 128, 8], [1, 128]]), (), ["wu%d" % (cc % 2)])

            load_wu(0)
            ucnt2 = [0]
            for c in range(8):
                wu = wus[c % 2]; wuk = "wu%d" % (c % 2)
                uTc = uTcs[c % 2]; uk = "uTc%d" % (c % 2)
                d0 = d0s[c % 2]; dk = "d0%d" % (c % 2)
                if c + 1 < 8:
                    load_wu(c + 1)
                for bi, (b0, bn) in enumerate(blocks):
                    u_ = ucnt2[0] % 2; ucnt2[0] += 1
                    pb = ps[5 + u_]; pkk = "ps%d" % (5 + u_)
                    tr.mms([(pb[:, 0:bn], wu[:, kc, :], xnT[:, kc, b0:b0 + bn], dict(start=(kc == 0), stop=(kc == 7))) for kc in range(8)],
                           [wuk, "xnT"], [pkk])
                    ACP(uTc[:, b0:b0 + bn], pb[:, 0:bn], [pkk], [uk])
                VCP(d0[:], RHO[:, 4 * c:4 * c + 4].unsqueeze(2).broadcast_to([128, 4, 128]), [pk(5)], [dk], eng="pool")
                MEMSET(d0[:, :, 0:1], 0.0, [dk], eng="pool")

                def emit_bu(ti):
                    kind, i, nt, c0 = tiles[ti]
                    for ri in range(2):
                        tr.mms([(ps[ri][:, 128 * r:128 * r + nt], SinT[:, (4 * c + r) * 2 + ri, :], uTc[:, c0:c0 + nt], dict(start=True, stop=True)) for r in range(4)],
                               ["SinT", uk], ["ps%d" % ri])

                def emit_cast(ti):
                    kind, i, nt, c0 = tiles[ti]
                    for ri in range(2):
                        ACP(BUb[ri][:, :, 0:nt], ps[ri][:].rearrange("p (r i) -> p r i", i=128)[:, :, 0:nt], ["ps%d" % ri], ["BUb%d" % ri])

                def emit_in(ti):
                    kind, i, nt, c0 = tiles[ti]
                    par_ = ti % 2
                    slot = 0 if kind == "p" else 1 + i
                    hk = "Hc%d" % slot
                    tcs = tabCb[:, 4 * c:4 * c + 4, 0:nt]; tss = tabSb[:, 4 * c:4 * c + 4, 0:nt]
                    bur = BUb[0][:, :, 0:nt]; bui = BUb[1][:, :, 0:nt]
                    VTT(T1[:, :, 0:nt], bur, tcs, ALU.mult, ["BUb0", "tabCb"], ["T1"])
                    VTT(T2[:, :, 0:nt], bui, tss, ALU.mult, ["BUb1", "tabSb"], ["T2"])
                    VTT(W[0][:, :, 0:nt], T1[:, :, 0:nt], T2[:, :, 0:nt], ALU.add, ["T1", "T2"], ["W0"])
                    VTT(T1[:, :, 0:nt], bui, tcs, ALU.mult, ["BUb1", "tabCb"], ["T1"])
                    VTT(T2[:, :, 0:nt], bur, tss, ALU.mult, ["BUb0", "tabSb"], ["T2"])
                    VTT(W[1][:, :, 0:nt], T1[:, :, 0:nt], T2[:, :, 0:nt], ALU.subtract, ["T1", "T2"], ["W1"])
                    VTT(RH2[:], Hc[:, slot, :, 4 * c:4 * c + 4], RHO2[:, :, 4 * c:4 * c + 4], ALU.mult, [hk, "RHO2"], ["RH2"])
                    VTT(W2[:, :, :, 0:1], W2[:, :, :, 0:1], RH2[:].unsqueeze(3), ALU.add, ["W0", "W1", "RH2"], ["W0", "W1"])
                    for ri in range(2):
                        Zt = Z[ri][par_]; zk = "Z%d%d" % (ri, par_)
                        if nt == 128:
                            zin = W[ri].rearrange("p r i -> p (r i)"); zout = Zt.rearrange("p r i -> p (r i)")
                            dd = d0[:].rearrange("p r i -> p (r i)")
                            tr.op("dve", lambda e: e.tensor_tensor_scan(out=zout, data0=dd, data1=zin, initial=0.0, op0=ALU.mult, op1=ALU.add),
                                  ["W%d" % ri, dk], [zk])
                        else:
                            for r in range(4):
                                tr.op("dve", lambda e: e.tensor_tensor_scan(out=Zt[:, r, 0:nt], data0=d0[:, r, 0:nt], data1=W[ri][:, r, 0:nt],
                                                                            initial=0.0, op0=ALU.mult, op1=ALU.add),
                                      ["W%d" % ri, dk], [zk])
                    jl_ = 0 if nt == 128 else 1
                    zl = Z2[par_][:, :, :, nt - 1]
                    zk0 = "Z0%d" % par_; zk1 = "Z1%d" % par_
                    VTT(cs2[:, 0], zl, TLa[:, jl_, :, 4 * c:4 * c + 4], ALU.mult, [zk0, zk1, "TLa"], ["cs2a"])
                    VTT(cs2[:, 1], zl, TLb[:, jl_, :, 4 * c:4 * c + 4], ALU.mult, [zk0, zk1, "TLb"], ["cs2b"])
                    VTT(Hc[:, slot, 0, 4 * c:4 * c + 4], cs2[:, 0, 0, :], cs2[:, 0, 1, :], ALU.subtract, ["cs2a", hk], [hk])
                    VTT(Hc[:, slot, 1, 4 * c:4 * c + 4], cs2[:, 1, 0, :], cs2[:, 1, 1, :], ALU.add, ["cs2b", hk], [hk])

                def emit_out(ti):
                    kind, i, nt, c0 = tiles[ti]
                    par_ = ti % 2
                    tcs = tabCb[:, 4 * c:4 * c + 4, 0:nt]; tss = tabSb[:, 4 * c:4 * c + 4, 0:nt]
                    for ri in range(2):
                        ACP(Zb[ri][:, :, 0:nt], Z[ri][par_][:, :, 0:nt], ["Z%d%d" % (ri, par_)], ["Zb%d" % ri])
                    zr = Zb[0][:, :, 0:nt]; zi = Zb[1][:, :, 0:nt]
                    VTT(Q1[:, :, 0:nt], zr, tcs, ALU.mult, ["Zb0", "tabCb"], ["Q1"])
                    VTT(Q2[:, :, 0:nt], zi, tss, ALU.mult, ["Zb1", "tabSb"], ["Q2"])
                    VTT(Hb[0][par_][:, :, 0:nt], Q1[:, :, 0:nt], Q2[:, :, 0:nt], ALU.subtract, ["Q1", "Q2"], ["Hb0%d" % par_])
                    VTT(Q1[:, :, 0:nt], zr, tss, ALU.mult, ["Zb0", "tabSb"], ["Q1"])
                    VTT(Q2[:, :, 0:nt], zi, tcs, ALU.mult, ["Zb1", "tabCb"], ["Q2"])
                    VTT(Hb[1][par_][:, :, 0:nt], Q1[:, :, 0:nt], Q2[:, :, 0:nt], ALU.add, ["Q1", "Q2"], ["Hb1%d" % par_])

                def emit_y(ti):
                    kind, i, nt, c0 = tiles[ti]
                    pb = ps[2 + ti % 2]; pkk = "ps%d" % (2 + ti % 2)
                    yt = yts[ti % 2]; yk = "yt%d" % (ti % 2)
                    par_ = ti % 2
                    lst = []
                    for r in range(4):
                        for ri in range(2):
                            lst.append((pb[:, 0:nt], Cst[:, (4 * c + r) * 2 + ri, :], Hb[ri][par_][:, r, 0:nt], dict(start=(len(lst) == 0), stop=(len(lst) == 7))))
                    tr.mms(lst, ["Cst", "Hb0%d" % par_, "Hb1%d" % par_], [pkk])
                    STT(yt[:, 0:nt], uTc[:, c0:c0 + nt], pp[:, 11 + c:12 + c], pb[:, 0:nt], ALU.mult, ALU.add, [uk, "pp11", pkk], [yk])
                    ACT(ygT[:, c, c0:c0 + nt], yt[:, 0:nt], AF.Gelu, [yk], ["ygT"])

                ntl = len(tiles)
                emit_bu(0)
                emit_cast(0)
                for ti in range(ntl):
                    emit_in(ti)
                    if ti + 1 < ntl:
                        emit_bu(ti + 1)
                        emit_cast(ti + 1)
                    emit_out(ti)
                    if ti >= 1:
                        emit_y(ti - 1)
                emit_y(ntl - 1)
            for slot, (tr_, ti_) in enumerate(((hr_p, hi_p), (hr_s, hi_s), (hr_s, hi_s))):
                for ri, tt in enumerate((tr_, ti_)):
                    off = l * 4096 if slot == 0 else (l * 2 + slot - 1) * 4096
                    store_GP(tt, off, Hc[:, slot, ri, :], "Hc%d" % slot, ("oh", l, slot, ri))
            tr.barrier()
            stop_here("ssm_%d" % l, dict(ygT=ygT[:], Hc=Hc[:]))

        oes = contextlib.ExitStack()
        oT = SBT(oes, "oT", [128, 8, NT], BF16)
        with contextlib.ExitStack() as es:
            wbs = [SBT(es, "wb%d" % i, [128, 8, 512], BF16) for i in range(2)]
            Qt = [SBT(es, "Qt%d" % m, [128, NT], BF16) for m in range(2)]
            Kt = [SBT(es, "Kt%d" % m, [128, NT], BF16) for m in range(2)]
            Kp = [[SBT(es, "Kp%d%d" % (b, m), [128, PAST + NS], BF16) for m in range(2)] for b in range(2)]
            Vx = SBT(es, "Vx", [128, 18, 160], BF16)
            Vp = [SBT(es, "Vp%d" % b, [128, 8, 160], BF16) for b in range(2)]
            Gt = SBT(es, "Gt", [128, 18, 128])
            Pb = [SBT(es, "Pb%d" % i, [128, 512], BF16) for i in range(4)]
            sqs = [SBT(es, "sq%d" % i, [128, 512], BF16) for i in range(2)]
            rss = [SBT(es, "rs%d" % i, [128, 512]) for i in range(2)]
            ksts = [SBT(es, "kst%d" % i, [128, 2, 64]) for i in range(4)]
            vsts = [SBT(es, "vst%d" % i, [128, 128]) for i in range(4)]
            sgs = [SBT(es, "sg%d" % i, [128, 128]) for i in range(2)]
            ckbs = [SBT(es, "ckb%d" % i, [128, 8, 128], BF16) for i in range(2)]
            osbs = [SBT(es, "osb%d" % i, [128, 128]) for i in range(4)]
            ots = [SBT(es, "ot%d" % i, [128, 128]) for i in range(4)]
            onbs = [SBT(es, "onb%d" % i, [128, 128], BF16) for i in range(4)]
            fss = [SBT(es, "fs%d" % i, [128, 8]) for i in range(4)]
            onesbd = SBT(es, "onesbd", [128, 128], BF16)
            MEMSET(onesbd[:], 0.0, ["onesbd"])
            MEMSET(onesbd[0:64, 0:64], 1.0, ["onesbd"])
            MEMSET(onesbd[64:128, 64:128], 1.0, ["onesbd"])
            MEMSET(Vx[:, :, 128:129], 1.0, ["Vx"])
            for b in range(2):
                MEMSET(Vp[b][:, :, 128:129], 1.0, ["Vp%d" % b])
            pcount = [0]
            fcount = [0]
            ocnt = [0]
            Ocp = [[SBT(es, "Ocp%d%d" % (i, j), [128, 387]) for j in range(3)] for i in range(2)]
            wbase = l * D * INC

            def load_wb(hh):
                wbt = wbs[hh % 2]; wk = "wb%d" % (hh % 2)
                for j, colo in enumerate((hh * 64, 512 + hh * 64, 1024 + hh * 64, 1536 + hh * 64)):
                    tr.dma("pool", "wb", wbt[:, :, 64 * j:64 * j + 64], dap(w_in, wbase + colo, [[INC, 128], [INC * 128, 8], [1, 64]]), (), [wk])
                tr.dma("pool", "wb", wbt[:, :, 256:384], dap(w_in, wbase + 2048 + hh * 128, [[INC, 128], [INC * 128, 8], [1, 128]]), (), [wk])
                tr.dma("pool", "wb", wbt[:, :, 384:512], dap(w_in, wbase + 3072 + hh * 128, [[INC, 128], [INC * 128, 8], [1, 128]]), (), [wk])

            load_wb(0)
            for h in range(H):
                wb = wbs[h % 2]; wk = "wb%d" % (h % 2)
                for m in range(2):
                    tr.dma("pool", "aug", Qt[m][64:68, 0:T], dap(c_qa, h * T, [[H * T, 4], [1, T]]), (), ["QtA%d" % m])
                    tr.dma("pool", "aug", Kt[m][64:68, 0:T], dap(c_ka, h * T, [[H * T, 4], [1, T]]), (), ["KtA%d" % m])
                    for b in range(2):
                        tr.dma("pool", "aug", Qt[m][64:68, T + NS * b:T + NS * b + NS], dap(c_qa, h * T + PAST, [[H * T, 4], [1, NS]]), (), ["QtA%d" % m])
                        tr.dma("pool", "aug", Kp[b][m][64:68, :], dap(c_ka, h * T, [[H * T, 4], [1, PAST + NS]]), (), ["KpA%d%d" % (b, m)])
                for b in range(2):
                    for m in range(2):
                        tr.dma("pool", "ck", ckbs[b][:, :, 64 * m:64 * m + 64],
                               dap(ck, ((l * 2 + b) * PAST) * 1024 + m * 512 + h * 64, [[1024, 128], [1024 * 128, 8], [1, 64]]), (), ["ckb%d" % b])
                    tr.dma("pool", "cv", Vp[b][:, :, 0:128], dap(cv, ((l * 2 + b) * PAST) * 1024 + h * 128, [[1024, 128], [1024 * 128, 8], [1, 128]]), (), ["Vp%d" % b])
                if h + 1 < H:
                    load_wb(h + 1)
                def emit_vga(ti):
                    kind, i, n, c0 = tiles[ti]
                    u_ = ti % 2
                    pbv = ps[2 + u_]; pkv = "ps%d" % (2 + u_)
                    pbg = ps[0 + u_]; pkg = "ps%d" % (0 + u_)
                    r4 = ti % 4
                    vst = vsts[r4]; vk_ = "vst%d" % r4; sg = sgs[ti % 2]; gk_ = "sg%d" % (ti % 2)
                    tr.mms([(pbv[0:n, 0:128], xnT[:, kc, c0:c0 + n], wb[:, kc, 256:384], dict(start=(kc == 0), stop=(kc == 7))) for kc in range(8)],
                           [wk, "xnT"], [pkv])
                    tr.mms([(pbg[0:n, 0:128], xnT[:, kc, c0:c0 + n], wb[:, kc, 384:512], dict(start=(kc == 0), stop=(kc == 7))) for kc in range(8)],
                           [wk, "xnT"], [pkg])
                    VCP(vst[0:n, :], pbv[0:n, 0:128], [pkv], [vk_])
                    VCP(Vx[0:n, ti, 0:128], pbv[0:n, 0:128], [pkv], ["Vx"])
                    if kind == "p":
                        dst = dap(nv_p, (l * T + c0) * 1024 + h * 128, [[1024, n], [1, 128]])
                    else:
                        dst = dap(nv_s, ((l * 2 + i) * NS) * 1024 + h * 128, [[1024, n], [1, 128]])
                    tr.dma("sp", "o_v", dst, vst[0:n, :], [vk_], [("ov", l, h, ti)], is_out=True)
                    ACT(sg[0:n, :], pbg[0:n, 0:128], AF.Silu, [pkg], [gk_])
                    VTT(Gt[0:n, ti, :], sg[0:n, :], subB[0:n, :], ALU.mult, [gk_, "subB"], ["Gt"])

                vga_next = [0]
                for pr in range(2):
                    isq = pr == 0
                    dsts = Qt if isq else Kt
                    dks = ["Qt0", "Qt1"] if isq else ["Kt0", "Kt1"]
                    gci = 0 if isq else 1
                    for (b0, bn) in blocks:
                        u_ = pcount[0] % 2; pcount[0] += 1
                        pb = ps[5 + u_]; pkk = "ps%d" % (5 + u_)
                        sq = sqs[u_]; rs = rss[u_]; sk_ = "sq%d" % u_; rk_ = "rs%d" % u_
                        tr.mms([(pb[:, 0:bn], wb[:, kc, 128 * pr:128 * pr + 128], xnT[:, kc, b0:b0 + bn], dict(start=(kc == 0), stop=(kc == 7))) for kc in range(8)],
                               [wk, "xnT"], [pkk])
                        ACT(sq[:, 0:bn], pb[:, 0:bn], AF.Square, [pkk], [sk_])
                        tr.mms([(ps[4][:, 0:bn], onesbd[:], sq[:, 0:bn], dict(start=True, stop=True))], [sk_, "onesbd"], ["ps4"])
                        ACT(rs[:, 0:bn], ps[4][:, 0:bn], AF.Ln, ["ps4", "epsc"], [rk_], scale=1.0 / 64, bias=epsc[:, :])
                        ACT(rs[:, 0:bn], rs[:, 0:bn], AF.Exp, [rk_], [rk_], scale=-0.5)
                        for m in range(2):
                            STT(dsts[m][0:64, b0:b0 + bn], pb[64 * m:64 * m + 64, 0:bn], pp[64 * m:64 * m + 64, gci:gci + 1], rs[64 * m:64 * m + 64, 0:bn],
                                ALU.mult, ALU.mult, [pkk, rk_, "pp0", "pp1"], [dks[m]])
                        for _ in range(2):
                            if vga_next[0] < len(tiles):
                                emit_vga(vga_next[0]); vga_next[0] += 1
                while vga_next[0] < len(tiles):
                    emit_vga(vga_next[0]); vga_next[0] += 1
                stop_here("attnB", dict(Qt0=Qt[0][:], Kt0=Kt[0][:], Qt1=Qt[1][:], Kt1=Kt[1][:]))
                for ti, (kind, i, n, c0) in enumerate(tiles):
                    r4 = ti % 4; r2 = ti % 2
                    kst = ksts[r4]; kk_ = "kst%d" % r4; tk_ = "psT"
                    for m in range(2):
                        TRANSP(psT[0:n, 256 + 128 * r2 + 64 * m:256 + 128 * r2 + 64 * m + 64], Kt[m][0:64, c0:c0 + n], 64, ["Kt%d" % m], [tk_])
                    VCP(kst[0:n].rearrange("p m d -> p (m d)"), psT[0:n, 256 + 128 * r2:256 + 128 * r2 + 128], [tk_], [kk_])
                    if kind == "p":
                        dst = dap(nk_p, (l * T + c0) * 1024 + h * 64, [[1024, n], [512, 2], [1, 64]])
                    else:
                        dst = dap(nk_s, ((l * 2 + i) * NS) * 1024 + h * 64, [[1024, n], [512, 2], [1, 64]])
                    tr.dma("sp", "o_k", dst, kst[0:n], [kk_], [("ok", l, h, ti)], is_out=True)
                stop_here("attnD", dict(Gt=Gt[:]))
                for b in range(2):
                    ckb = ckbs[b]
                    for m in range(2):
                        for tt in range(8):
                            TRANSP(psT[0:64, 128 * tt:128 * tt + 128], ckb[:, tt, 64 * m:64 * m + 64], 128, ["ckb%d" % b], ["psT"])
                        VCP(Kp[b][m][0:64, 0:PAST], psT[0:64, :], ["psT"], ["Kp%d%d" % (b, m)])
                        VCP(Kp[b][m][0:64, PAST:PAST + NS], Kt[m][0:64, T + NS * b:T + NS * b + NS], ["Kt%d" % m], ["Kp%d%d" % (b, m)])

                stop_here("attnE", dict(Kp00=Kp[0][0][:]))
                pending = []

                def attend(qcols, qpos0, nq, ktiles, gt_tile):
                    steps = []
                    blocks_ = []
                    for qb0 in range(0, nq, 512):
                        nqb = min(512, nq - qb0)
                        qts = [(qb0 + 128 * i, min(128, nq - qb0 - 128 * i)) for i in range((nqb + 127) // 128)]
                        lastk = {}
                        vis = []
                        for kj, (Kap, Vap, nk, kp0) in enumerate(ktiles):
                            first = None
                            diag = False
                            for qi, (q0, nqi) in enumerate(qts):
                                tpos = qpos0 + q0
                                if kp0 // 64 > (tpos + nqi - 1) // 64:
                                    continue
                                if first is None:
                                    first = qi
                                    diag = (kp0 + nk - 1) > tpos
                                lastk[qi] = kj
                            if first is not None:
                                vis.append((kj, first, diag))
                        blk = dict(qb0=qb0, nqb=nqb, qts=qts, lastk=lastk, started=set(), nsteps=2 * len(vis))
                        blocks_.append(blk)
                        for (kj, first, diag) in vis:
                            for m in range(2):
                                steps.append(dict(blk=blk, kj=kj, first=first, diag=diag, m=m))
                        steps[-1]["last"] = True

                    def emit_S(st):
                        blk = st["blk"]; qts = blk["qts"]; m = st["m"]
                        Kap, Vap, nk, kp0 = ktiles[st["kj"]]
                        c0 = qts[st["first"]][0]; ncol = blk["qb0"] + blk["nqb"] - c0
                        si = pcount[0] % 2; pbi = pcount[0] % 4; pcount[0] += 1
                        S = ps[si]; sk = "ps%d" % si
                        lst = [(S[0:nk, 0:ncol], Kap[m], Qt[m][0:68, qcols + c0:qcols + c0 + ncol], dict(start=True, stop=not st["diag"]))]
                        if st["diag"]:
                            nqd = qts[st["first"]][1]
                            lst.append((S[0:nk, 0:nqd], ident[0:nk, 0:nk], dmat[0:nk, h, 0:nqd], dict(start=False, stop=True)))
                        tr.mms(lst, ["Qt%d" % m, "Kt%d" % m, "Kp0%d" % m, "Kp1%d" % m, "QtA%d" % m, "KtA%d" % m, "KpA0%d" % m, "KpA1%d" % m, "ident", "dmat"], [sk])
                        Pt = Pb[pbi]; pk_ = "Pb%d" % pbi
                        ACT(Pt[0:nk, 0:ncol], S[0:nk, 0:ncol], AF.Exp, [sk], [pk_])
                        st["P"] = (Pt, pk_, c0)

                    def emit_PV(st):
                        blk = st["blk"]; qts = blk["qts"]; m = st["m"]
                        Kap, Vap, nk, kp0 = ktiles[st["kj"]]
                        Pt, pk_, c0 = st["P"]
                        for qi in range(st["first"], len(qts)):
                            q0, nqi = qts[qi]
                            a_ = m * 4 + qi
                            bank = 2 + a_ // 3
                            O = ps[bank][0:nqi, 129 * (a_ % 3):129 * (a_ % 3) + 129]
                            stf = bank not in blk["started"]
                            blk["started"].add(bank)
                            tr.mms([(O, Pt[0:nk, q0 - c0:q0 - c0 + nqi], Vap, dict(start=stf, stop=(blk["lastk"][qi] == st["kj"]), skip_group_check=True))],
                                   [pk_, "Vx", "Vp0", "Vp1"], ["ps%d" % bank])

                    def finalize(blk):
                        qts = blk["qts"]
                        nqt = len(qts)
                        oset = ocnt[0] % 2; ocnt[0] += 1
                        nmax = max(n for (_, n) in qts)
                        used = sorted(set(2 + (m * 4 + qi) // 3 for qi in range(nqt) for m in range(2)))
                        for bank in used:
                            ACP(Ocp[oset][bank - 2][0:nmax, :], ps[bank][0:nmax, 0:387], ["ps%d" % bank], ["ocp%d_%d" % (oset, bank)])
                        Aps = []
                        for qi, (q0, nqi) in enumerate(qts):
                            A = []
                            for m in range(2):
                                a_ = m * 4 + qi
                                bank = 2 + a_ // 3
                                A.append((Ocp[oset][bank - 2][0:nqi, 129 * (a_ % 3):129 * (a_ % 3) + 129], "ocp%d_%d" % (oset, bank)))
                            Aps.append(A)
                        for qi, (q0, n) in enumerate(qts):
                            A = Aps[qi]; fs = fss[qi]; ot = ots[qi]; osb = osbs[qi]
                            fk = "fs%d_" % qi; otk = "ot%d" % qi; osk = "osb%d" % qi
                            tr.op("dve", lambda e: e.reciprocal(out=fs[0:n, 0:1], in_=A[0][0][:, 128:129]), [A[0][1]], [fk + "0"])
                            tr.op("dve", lambda e: e.reciprocal(out=fs[0:n, 1:2], in_=A[1][0][:, 128:129]), [A[1][1]], [fk + "1"])
                            VTT(fs[0:n, 1:2], fs[0:n, 1:2], pp[0:n, 2:3], ALU.mult, [fk + "1", "pp2"], [fk + "1"])
                            VTS(ot[0:n, :], A[1][0][:, 0:128], fs[0:n, 1:2], None, ALU.mult, None, [A[1][1], fk + "1"], [otk])
                            STT(osb[0:n, :], A[0][0][:, 0:128], fs[0:n, 0:1], ot[0:n, :], ALU.mult, ALU.add, [A[0][1], fk + "0", otk], [osk])
                            VTT(ot[0:n, :], osb[0:n, :], osb[0:n, :], ALU.mult, [osk], [otk])
                            RSUM(fs[0:n, 2:3], ot[0:n, :], [otk], [fk + "2"])
                        def stage2():
                            for qi, (q0, n) in enumerate(qts):
                                fs = fss[qi]; fk = "fs%d_" % qi
                                RSQ(fs[0:n, 2:3], fs[0:n, 2:3], 1.0 / 128, n, [fk + "2"], [fk + "2"])
                            for qi, (q0, n) in enumerate(qts):
                                fs = fss[qi]; osb = osbs[qi]; onb = onbs[qi]
                                STT(onb[0:n, :], osb[0:n, :], fs[0:n, 2:3], Gt[0:n, gt_tile(q0), :], ALU.mult, ALU.mult,
                                    ["osb%d" % qi, "fs%d_2" % qi, "Gt"], ["onb%d" % qi])

                        def deferred():
                            for qi, (q0, n) in enumerate(qts):
                                TRANSP(psT[:, 128 * qi:128 * qi + n], onbs[qi][0:n, :], n, ["onb%d" % qi], ["psT"])
                            qa = qts[0][0]; ntot = qts[-1][0] + qts[-1][1] - qa
                            if all(n == 128 for (_, n) in qts):
                                VCP(oT[:, h, qcols + qa:qcols + qa + ntot], psT[:, 0:ntot], ["psT"], ["oT"])
                            else:
                                for qi, (q0, n) in enumerate(qts):
                                    VCP(oT[:, h, qcols + q0:qcols + q0 + n], psT[:, 128 * qi:128 * qi + n], ["psT"], ["oT"])
                        return [[8, stage2], [11, deferred]]

                    emit_S(steps[0])
                    for i, st in enumerate(steps):
                        if i + 1 < len(steps):
                            emit_S(steps[i + 1])
                        emit_PV(st)
                        for p_ in pending:
                            p_[0] -= 1
                        while pending and pending[0][0] <= 0:
                            pending.pop(0)[1]()
                        if st.get("last"):
                            while pending:
                                pending.pop(0)[1]()
                            pending.extend(finalize(st["blk"]))

                kts = [([Kt[0][0:68, 128 * j:128 * j + 128], Kt[1][0:68, 128 * j:128 * j + 128]], Vx[:, j, 0:129], 128, 128 * j) for j in range(16)]
                attend(0, 0, T, kts, lambda q0: q0 // 128)
                stop_here("attnF", dict(oT=oT[:]))
                for b in range(2):
                    kts = [([Kp[b][0][0:68, 128 * j:128 * j + 128], Kp[b][1][0:68, 128 * j:128 * j + 128]], Vp[b][:, j, 0:129], 128, 128 * j) for j in range(8)]
                    kts.append(([Kp[b][0][0:68, PAST:PAST + NS], Kp[b][1][0:68, PAST:PAST + NS]], Vx[0:NS, 16 + b, 0:129], NS, PAST))
                    attend(T + NS * b, PAST, NS, kts, lambda q0, b=b: 16 + b)
                while pending:
                    pending.pop(0)[1]()
                stop_here("attnh%d_%d" % (h, l), dict(oT=oT[:], Qt0=Qt[0][:], Kt0=Kt[0][:], Qt1=Qt[1][:], Kt1=Kt[1][:], Vx=Vx[:], Gt=Gt[:], Kp00=Kp[0][0][:], Vp0=Vp[0][:]))
            tr.barrier()
            stop_here("attn_%d" % l, dict(oT=oT[:]))

        with contextlib.ExitStack() as es:
            ysgT = SBT(es, "ysgT", [128, 8, NT], BF16)
            wA = SBT(es, "wA", [128, 8, D], BF16); wB = SBT(es, "wB", [128, 8, D], BF16)
            tmp = [SBT(es, "tmp%d" % i, [128, 512]) for i in range(2)]
            tb16 = SBT(es, "tb16", [128, 512], BF16)
            xr = [SBT(es, "xr%d" % i, [128, D]) for i in range(2)]

            def loadw(dst, key, tnsr, base, rowlen):
                tr.dma("pool", key, dst[:], dap(tnsr, base, [[rowlen, 128], [rowlen * 128, 8], [1, D]]), (), [key])

            def proj(wt, wkey, src, skey, oc, b0, bn, pi):
                pb = ps[pi]
                tr.mms([(pb[:, 0:bn], wt[:, kc, 128 * oc:128 * oc + 128], src[:, kc, b0:b0 + bn], dict(start=(kc == 0), stop=(kc == 7))) for kc in range(8)],
                       [wkey, skey], ["ps%d" % pi])
                return pb[:, 0:bn], "ps%d" % pi

            cnt = [0]
            loadw(wA, "wA", w_glu, l * D * D, D)
            loadw(wB, "wB", w_in, l * D * INC + 5120, INC)
            for oc in range(8):
                for (b0, bn) in blocks:
                    pi = cnt[0] % 2; cnt[0] += 1
                    pa, ka = proj(wA, "wA", ygT, "ygT", oc, b0, bn, pi)
                    ACT(tmp[0][:, 0:bn], pa, AF.Sigmoid, [ka, "pp3"], ["tmp0"], bias=pp[:, 3 + oc:4 + oc])
                    VTT(ysgT[:, oc, b0:b0 + bn], ygT[:, oc, b0:b0 + bn], tmp[0][:, 0:bn], ALU.mult, ["ygT", "tmp0"], [("ysg", oc)])
            for oc in range(8):
                for (b0, bn) in blocks:
                    pi = cnt[0] % 2; cnt[0] += 1
                    pa, ka = proj(wB, "wB", xnT, "xnT", oc, b0, bn, pi)
                    ACT(tb16[:, 0:bn], pa, AF.Silu, [ka], ["tb16"])
                    VTT(ysgT[:, oc, b0:b0 + bn], ysgT[:, oc, b0:b0 + bn], tb16[:, 0:bn], ALU.mult, [("ysg", oc), "tb16"], [("ysg", oc)])
            allysg = [("ysg", oc) for oc in range(8)]
            loadw(wA, "wA", w_sp, l * D * D, D)
            loadw(wB, "wB", w_in, l * D * INC + 7168, INC)
            for oc in range(8):
                for (b0, bn) in blocks:
                    pi = cnt[0] % 2; cnt[0] += 1
                    pg, kg = proj(wB, "wB", xnT, "xnT", oc, b0, bn, 2 + pi)
                    ACT(tmp[0][:, 0:bn], pg, AF.Sigmoid, [kg], ["tmp0"])
                    pb = ps[pi]
                    tr.mms([(pb[:, 0:bn], wA[:, kc, 128 * oc:128 * oc + 128], ysgT[:, kc, b0:b0 + bn], dict(start=(kc == 0), stop=(kc == 7))) for kc in range(8)],
                           ["wA"] + allysg, ["ps%d" % pi])
                    VTT(ygT[:, oc, b0:b0 + bn], pb[:, 0:bn], tmp[0][:, 0:bn], ALU.mult, ["ps%d" % pi, "tmp0"], ["ygT"])
            loadw(wA, "wA", w_ap, l * D * D, D)
            loadw(wB, "wB", w_in, l * D * INC + 6144, INC)
            for oc in range(8):
                for (b0, bn) in blocks:
                    pi = cnt[0] % 2; cnt[0] += 1
                    pg, kg = proj(wB, "wB", xnT, "xnT", oc, b0, bn, 2 + pi)
                    ACT(tmp[0][:, 0:bn], pg, AF.Sigmoid, [kg], ["tmp0"])
                    pa, ka = proj(wA, "wA", oT, "oT", oc, b0, bn, pi)
                    VTT(tmp[1][:, 0:bn], pa, tmp[0][:, 0:bn], ALU.mult, [ka, "tmp0"], ["tmp1"])
                    VTT(ygT[:, oc, b0:b0 + bn], ygT[:, oc, b0:b0 + bn], tmp[1][:, 0:bn], ALU.add, ["ygT", "tmp1"], ["ygT"])
            loadw(wA, "wA", w_o, l * D * D, D)
            for ti, (kind, i, n, c0) in enumerate(tiles):
                xb = xr[ti % 2]; xk = "xr%d" % (ti % 2)
                if l == 0:
                    src = x_p.ap()[128 * i:128 * i + 128, :] if kind == "p" else x_s.ap()[i]
                else:
                    src = xmid.ap()[c0:c0 + n, :]
                tr.dma("sp", xk, xb[0:n, :], src, [("xmid", ti)], [xk])
                for hb in range(2):
                    pi = cnt[0] % 2; cnt[0] += 1
                    pb = ps[pi]
                    tr.mms([(pb[0:n, :], ygT[:, kc, c0:c0 + n], wA[:, kc, 512 * hb:512 * hb + 512], dict(start=(kc == 0), stop=(kc == 7))) for kc in range(8)],
                           ["wA", "ygT"], ["ps%d" % pi])
                    VTT(xb[0:n, 512 * hb:512 * hb + 512], xb[0:n, 512 * hb:512 * hb + 512], pb[0:n, :], ALU.add, [xk, "ps%d" % pi], [xk])
                if l == DEPTH - 1:
                    dst = y_p.ap()[128 * i:128 * i + 128, :] if kind == "p" else y_s.ap()[i]
                    tr.dma("sp", "o_y", dst, xb[0:n, :], [xk], [("oy", ti)], is_out=True)
                else:
                    tr.dma("sp", "o_x", xmid.ap()[c0:c0 + n, :], xb[0:n, :], [xk], [("xmid2", ti)])
            tr.barrier()
            stop_here("merge_%d" % l, dict(mT=ygT[:], ysgT=ysgT[:]))
        oes.close()
        yes.close()

    except StopBuild:
        pass
    tr.finish()
    return nc


_NC = None


def _consts():
    ident = np.eye(128, dtype=np.float32)
    dm = np.zeros((128, H, 128), np.float32)
    s = np.arange(128)[:, None]; t = np.arange(128)[None, :]
    for h in range(H):
        d = np.where(s > t, -2.0 * SLOPES[h] * (s - t), 0.0)
        d = np.where((s // 64) > (t // 64), -30000.0, d)
        dm[:, h, :] = d
    pos = np.arange(T)
    hi = (pos // 128) * 128.0; lo = (pos % 128) * 1.0
    qa = np.zeros((4, H, T), np.float32); ka = np.zeros((4, H, T), np.float32)
    for h in range(H):
        qa[0, h] = -SLOPES[h] * hi; qa[1, h] = -SLOPES[h] * lo; qa[2, h] = 1.0; qa[3, h] = 1.0
        ka[0, h] = 1.0; ka[1, h] = 1.0; ka[2, h] = SLOPES[h] * hi; ka[3, h] = SLOPES[h] * lo
    par = np.zeros((128, 2), np.float32)
    g8 = np.arange(128) // 16
    par[:, 0] = (g8 % 2 == 0); par[:, 1] = (g8 % 2 == 1)
    io = np.tile(np.arange(1, 129, dtype=np.float32)[None, :], (128, 1))
    return dict(c_id=ident, c_dm=dm, c_qa=qa, c_ka=ka, c_par=par, c_io=io)


def kernel(x_prompt, x_sample, cache_k, cache_v, state_ssm_re, state_ssm_im,
           norm_g, w_in, q_norm_g, k_norm_g, lam_q1, lam_k1, lam_q2, lam_k2, subln_g,
           w_attn_proj, ssm_lambda_re, ssm_lambda_im, ssm_log_dt, ssm_b_re, ssm_b_im,
           ssm_c_re, ssm_c_im, ssm_d, w_glu, b_glu, w_ssm_proj, w_out):
    global _NC
    if _NC is None:
        _NC = build()
    nc = _NC
    f = lambda a: np.ascontiguousarray(np.asarray(a, dtype=np.float32))
    shared = dict(norm_g=f(norm_g), w_in=f(w_in), qng=f(q_norm_g), kng=f(k_norm_g), lq1=f(lam_q1), lk1=f(lam_k1),
                  lq2=f(lam_q2), lk2=f(lam_k2), subg=f(subln_g), w_ap=f(w_attn_proj), lamr=f(ssm_lambda_re),
                  lami=f(ssm_lambda_im), ldt=f(ssm_log_dt), bre=f(ssm_b_re), bim=f(ssm_b_im), cre=f(ssm_c_re),
                  cim=f(ssm_c_im), dsk=f(ssm_d), w_glu=f(w_glu), b_glu=f(b_glu), w_sp=f(w_ssm_proj), w_o=f(w_out))
    shared.update(_consts())
    x_prompt = np.asarray(x_prompt); x_sample = np.asarray(x_sample)
    cache_k = np.asarray(cache_k); cache_v = np.asarray(cache_v)
    state_ssm_re = np.asarray(state_ssm_re); state_ssm_im = np.asarray(state_ssm_im)
    in_maps = []
    for c in range(NCORES):
        m = dict(shared)
        m["x_p"] = f(x_prompt[c]); m["x_s"] = f(x_sample[2 * c:2 * c + 2])
        m["ck"] = f(cache_k[:, 2 * c:2 * c + 2]); m["cv"] = f(cache_v[:, 2 * c:2 * c + 2])
        m["sre"] = f(state_ssm_re[:, 2 * c:2 * c + 2]); m["sim"] = f(state_ssm_im[:, 2 * c:2 * c + 2])
        in_maps.append(m)
    res = run_bass_kernel_spmd(nc, in_maps, core_ids=list(range(NCORES)))
    R = res.results
    cat0 = lambda k: np.stack([R[c][k] for c in range(NCORES)], axis=0)
    y_p = cat0("y_p")
    y_s = np.concatenate([R[c]["y_s"] for c in range(NCORES)], axis=0)
    nk_p = np.stack([R[c]["nk_p"] for c in range(NCORES)], axis=1)
    nv_p = np.stack([R[c]["nv_p"] for c in range(NCORES)], axis=1)
    hr_p = np.stack([R[c]["hr_p"] for c in range(NCORES)], axis=1)
    hi_p = np.stack([R[c]["hi_p"] for c in range(NCORES)], axis=1)
    nk_s = np.concatenate([R[c]["nk_s"] for c in range(NCORES)], axis=1)
    nv_s = np.concatenate([R[c]["nv_s"] for c in range(NCORES)], axis=1)
    hr_s = np.concatenate([R[c]["hr_s"] for c in range(NCORES)], axis=1)
    hi_s = np.concatenate([R[c]["hi_s"] for c in range(NCORES)], axis=1)
    return tuple(np.ascontiguousarray(a.astype(np.float32)) for a in (y_p, y_s, nk_p, nv_p, hr_p, hi_p, nk_s, nv_s, hr_s, hi_s))
```

```python
import math
import numpy as np
import concourse.bass as bass
import concourse.mybir as mybir
from concourse.bass_utils import run_bass_kernel_spmd

F32 = mybir.dt.float32
BF16 = mybir.dt.bfloat16
I32 = mybir.dt.int32
ALU = mybir.AluOpType
AF = mybir.ActivationFunctionType
AX = mybir.AxisListType

D = 1024
T = 2048
NS = 16
PAST = 1024
NT = T + 2 * NS
H = 8
DEPTH = 2
EPS = 1e-6
INC = 8192
NCORES = 8
SLOPES = [2.0 ** (-(h + 1)) for h in range(H)]


class Tr:
    def __init__(self, nc):
        self.nc = nc
        self.eng = {}
        for name, h in (("pe", nc.tensor), ("act", nc.scalar), ("dve", nc.vector), ("pool", nc.gpsimd), ("sp", nc.sync)):
            self.eng[name] = dict(h=h, sem=nc.alloc_semaphore(name="s_" + name), cnt=0, seen={})
        self.lastw = {}
        self.readers = {}
        self.dsem = {}
        self.out_tokens = []

    def _deps(self, reads, writes):
        deps = []
        for k in reads:
            if k in self.lastw:
                deps.append(self.lastw[k])
        for k in writes:
            if k in self.lastw:
                deps.append(self.lastw[k])
            deps.extend(self.readers.get(k, ()))
        return deps

    def _wait(self, ename, deps):
        e = self.eng[ename]
        best = {}
        for (sname, sem, val) in deps:
            if ename == "pe" and sname == "s_pe":
                continue
            if val > best.get(sname, (None, 0))[1]:
                best[sname] = (sem, val)
        for sname, (sem, val) in best.items():
            if e["seen"].get(sname, 0) < val:
                e["h"].wait_ge(sem, val)
                e["seen"][sname] = val

    def _commit(self, tok, reads, writes):
        for k in writes:
            self.lastw[k] = tok
            self.readers[k] = []
        for k in reads:
            self.readers.setdefault(k, []).append(tok)

    def op(self, ename, fn, reads=(), writes=()):
        e = self.eng[ename]
        self._wait(ename, self._deps(reads, writes))
        ins = fn(e["h"])
        e["cnt"] += 1
        ins.then_inc(e["sem"], 1)
        self._commit(("s_" + ename, e["sem"], e["cnt"]), reads, writes)

    def mms(self, lst, reads=(), writes=()):
        e = self.eng["pe"]
        self._wait("pe", self._deps(reads, writes))
        ins = None
        for (out, lhsT, rhs, kw) in lst:
            ins = self.nc.tensor.matmul(out, lhsT=lhsT, rhs=rhs, **kw)
        e["cnt"] += 1
        ins.then_inc(e["sem"], 1)
        self._commit(("s_pe", e["sem"], e["cnt"]), reads, writes)

    def dma(self, q, slot, out, in_, reads=(), writes=(), is_out=False):
        if slot.startswith("o_") or slot == "dbg":
            slot = "st_" + "_".join(str(x) for x in (reads[0] if isinstance(reads[0], tuple) else (reads[0],))) if reads else slot
        else:
            slot = "ld_" + "_".join(str(x) for x in (writes[0] if isinstance(writes[0], tuple) else (writes[0],)))
        if slot not in self.dsem:
            self.dsem[slot] = [self.nc.alloc_semaphore(name="d_" + slot), 0]
        ds = self.dsem[slot]
        self._wait(q, self._deps(reads, writes))
        self.eng[q]["h"].dma_start(out=out, in_=in_, allow_slow_non_contiguous=True).then_inc(ds[0], 16)
        ds[1] += 16
        tok = ("d_" + slot, ds[0], ds[1])
        self._commit(tok, reads, writes)
        if is_out:
            self.out_tokens.append(tok)

    def barrier(self):
        toks = [("s_" + n, e["sem"], e["cnt"]) for n, e in self.eng.items() if e["cnt"] > 0]
        toks += [("d_" + k, v[0], v[1]) for k, v in self.dsem.items() if v[1] > 0]
        for n in self.eng:
            e = self.eng[n]
            for (sname, sem, val) in toks:
                if sname == "s_" + n:
                    continue
                if e["seen"].get(sname, 0) < val:
                    e["h"].wait_ge(sem, val)
                    e["seen"][sname] = val
        self.lastw = {}
        self.readers = {}

    def finish(self):
        best = {}
        for (sname, sem, val) in self.out_tokens:
            if val > best.get(sname, (None, 0))[1]:
                best[sname] = (sem, val)
        for sname, (sem, val) in best.items():
            self.nc.sync.wait_ge(sem, val)


def dap(t, off, dims):
    return bass.AP(t, off, [list(d) for d in dims])


class StopBuild(Exception):
    pass


def build(stop=None):
    nc = bass.Bass("TRN2", target_bir_lowering=False)
    dt_in = lambda n, s: nc.dram_tensor(n, s, F32, kind="ExternalInput")
    dt_out = lambda n, s: nc.dram_tensor(n, s, F32, kind="ExternalOutput")
    x_p = dt_in("x_p", [T, D]); x_s = dt_in("x_s", [2, NS, D])
    ck = dt_in("ck", [DEPTH, 2, PAST, 2, H, 64]); cv = dt_in("cv", [DEPTH, 2, PAST, H, 128])
    sre = dt_in("sre", [DEPTH, 2, 64, 64]); sim = dt_in("sim", [DEPTH, 2, 64, 64])
    norm_g = dt_in("norm_g", [DEPTH, D]); w_in = dt_in("w_in", [DEPTH, D, INC])
    qng = dt_in("qng", [DEPTH, 64]); kng = dt_in("kng", [DEPTH, 64])
    lq1 = dt_in("lq1", [DEPTH, 64]); lk1 = dt_in("lk1", [DEPTH, 64]); lq2 = dt_in("lq2", [DEPTH, 64]); lk2 = dt_in("lk2", [DEPTH, 64])
    subg = dt_in("subg", [DEPTH, 128]); w_ap = dt_in("w_ap", [DEPTH, D, D])
    lamr = dt_in("lamr", [DEPTH, 64, 64]); lami = dt_in("lami", [DEPTH, 64, 64]); ldt = dt_in("ldt", [DEPTH, 64])
    bre = dt_in("bre", [DEPTH, 64, 64, 16]); bim = dt_in("bim", [DEPTH, 64, 64, 16])
    cre = dt_in("cre", [DEPTH, 64, 16, 64]); cim = dt_in("cim", [DEPTH, 64, 16, 64])
    dsk = dt_in("dsk", [DEPTH, D]); w_glu = dt_in("w_glu", [DEPTH, D, D]); b_glu = dt_in("b_glu", [DEPTH, D])
    w_sp = dt_in("w_sp", [DEPTH, D, D]); w_o = dt_in("w_o", [DEPTH, D, D])
    c_id = dt_in("c_id", [128, 128]); c_dm = dt_in("c_dm", [128, H, 128])
    c_qa = dt_in("c_qa", [4, H, T]); c_ka = dt_in("c_ka", [4, H, T]); c_par = dt_in("c_par", [128, 2]); c_io = dt_in("c_io", [128, 128])
    y_p = dt_out("y_p", [T, D]); y_s = dt_out("y_s", [2, NS, D])
    nk_p = dt_out("nk_p", [DEPTH, T, 2, H, 64]); nv_p = dt_out("nv_p", [DEPTH, T, H, 128])
    hr_p = dt_out("hr_p", [DEPTH, 64, 64]); hi_p = dt_out("hi_p", [DEPTH, 64, 64])
    nk_s = dt_out("nk_s", [DEPTH, 2, NS, 2, H, 64]); nv_s = dt_out("nv_s", [DEPTH, 2, NS, H, 128])
    hr_s = dt_out("hr_s", [DEPTH, 2, 64, 64]); hi_s = dt_out("hi_s", [DEPTH, 2, 64, 64])
    xmid = nc.dram_tensor("xmid", [NT, D], F32, kind="Internal")

    tr = Tr(nc)
    import contextlib

    ucnt = [0]

    def SBT(es, n, s, d=F32):
        ucnt[0] += 1
        return es.enter_context(nc.sbuf_tensor("%s_%d" % (n, ucnt[0]), s, d))

    def stop_here(name, dumps):
        if stop != name:
            return
        tr.barrier()
        for dn, ap in dumps.items():
            shp = list(ap.shape)
            dtn = nc.dram_tensor("dbg_" + dn, shp, ap.dtype, kind="ExternalOutput")
            tr.dma("sp", "dbg", dtn.ap(), ap, (), [("dbg", dn)], is_out=True)
        raise StopBuild()

    ges = contextlib.ExitStack()
    ps = [ges.enter_context(nc.psum_tensor("ps%d" % i, [128, 512], F32)) for i in range(7)]
    psT = ges.enter_context(nc.psum_tensor("psT", [128, 1024], BF16))
    xnT = SBT(ges, "xnT", [128, 8, NT], BF16)
    ident = SBT(ges, "ident", [128, 128], BF16)
    dmat = SBT(ges, "dmat", [128, H, 128], BF16)
    ones64 = SBT(ges, "ones64", [64, 64], BF16)
    par = SBT(ges, "par", [128, 2])
    epsc = SBT(ges, "epsc", [128, 1])
    sm = SBT(ges, "sm", [128, 16])
    pp = SBT(ges, "pp", [128, 32])
    lv = SBT(ges, "lv", [128, 4, 64])
    subB = SBT(ges, "subB", [128, 128])
    identF = SBT(ges, "identF", [128, 128])
    stg = SBT(ges, "stg", [64, 64]); stg2 = SBT(ges, "stg2", [64, 64]); stg3 = SBT(ges, "stg3", [64, 64])
    prow = SBT(ges, "prow", [32, 128]); dtb = SBT(ges, "dtb", [128, 64])

    def ACT(out, in_, func, reads, writes, **kw):
        tr.op("act", lambda e: e.activation(out=out, in_=in_, func=func, **kw), reads, writes)

    def VTT(out, a, b, op, reads, writes, eng="dve"):
        tr.op(eng, lambda e: e.tensor_tensor(out=out, in0=a, in1=b, op=op), reads, writes)

    def VTS(out, a, s1, s2, op0, op1, reads, writes, eng="dve"):
        if op1 is None:
            tr.op(eng, lambda e: e.tensor_scalar(out=out, in0=a, scalar1=s1, scalar2=None, op0=op0), reads, writes)
        else:
            tr.op(eng, lambda e: e.tensor_scalar(out=out, in0=a, scalar1=s1, scalar2=s2, op0=op0, op1=op1), reads, writes)

    def STT(out, a, s, b, op0, op1, reads, writes, eng="dve"):
        tr.op(eng, lambda e: e.scalar_tensor_tensor(out=out, in0=a, scalar=s, in1=b, op0=op0, op1=op1), reads, writes)

    def VCP(out, in_, reads, writes, eng="dve"):
        tr.op(eng, lambda e: e.tensor_copy(out=out, in_=in_), reads, writes)

    def ACP(out, in_, reads, writes):
        tr.op("act", lambda e: e.copy(out=out, in_=in_), reads, writes)

    def MEMSET(ap, val, writes, eng="dve"):
        tr.op(eng, lambda e: e.memset(ap, val), (), writes)

    def RSUM(out, in_, reads, writes):
        tr.op("dve", lambda e: e.reduce_sum(out=out, in_=in_, axis=AX.X), reads, writes)

    def RSQ(out, in_, scale, n, reads, writes):
        ACT(out, in_, AF.Ln, list(reads) + ["epsc"], writes, scale=scale, bias=epsc[0:n, :])
        ACT(out, out, AF.Exp, writes, writes, scale=-0.5)

    def TRANSP(out, in_, n, reads, writes):
        tr.op("pe", lambda e: e.transpose(out=out, in_=in_, identity=ident[0:n, 0:n]), list(reads) + ["ident"], writes)

    tr.dma("pool", "c0", ident[:], c_id.ap(), (), ["ident"])
    tr.dma("sp", "c1", identF[:], c_id.ap(), (), ["identF"])

    def TRANSPF(out, in_, n, reads, writes):
        tr.op("pe", lambda e: e.transpose(out=out, in_=in_, identity=identF[0:n, 0:n]), list(reads) + ["identF"], writes)

    def load_GP(dst, dkey, tensor, off):
        tr.dma("sp", "stg", stg[:], dap(tensor, off, [[64, 64], [1, 64]]), (), ["stg"])
        TRANSPF(ps[4][0:64, 0:64], stg[:], 64, ["stg"], ["ps4"])
        VCP(dst[0:64, :], ps[4][0:64, 0:64:2], ["ps4"], [dkey])
        VCP(dst[64:128, :], ps[4][0:64, 1:64:2], ["ps4"], [dkey])

    def store_GP(tensor, off, src, skey, okey):
        VCP(stg2[:, 0:64:2], src[0:64, :], [skey], ["stg2"])
        VCP(stg2[:, 1:64:2], src[64:128, :], [skey], ["stg2"])
        TRANSPF(ps[4][0:64, 0:64], stg2[:], 64, ["stg2"], ["ps4"])
        VCP(stg3[:], ps[4][0:64, 0:64], ["ps4"], ["stg3"])
        tr.dma("sp", "o_h", dap(tensor, off, [[64, 64], [1, 64]]), stg3[:], ["stg3"], [okey], is_out=True)
    tr.dma("pool", "c0", dmat[:], c_dm.ap(), (), ["dmat"])
    tr.dma("sp", "c1", par[:], c_par.ap(), (), ["par"])
    MEMSET(ones64[:], 1.0, ["ones64"])
    MEMSET(epsc[:], EPS, ["epsc"])

    tiles = [("p", i, 128, 128 * i) for i in range(16)] + [("s", b, NS, T + NS * b) for b in range(2)]
    blocks = [(512 * i, 512) for i in range(4)] + [(T, 2 * NS)]
    TWO_PI = 2.0 * math.pi

    try:
      for l in range(DEPTH):
        lam_init = 0.8 - 0.6 * math.exp(-0.3 * l)
        with contextlib.ExitStack() as es:
            ngb = SBT(es, "ngb", [128, D]); xt = [SBT(es, "xt%d" % i, [128, D]) for i in range(2)]
            scr = SBT(es, "scr", [128, D]); xnb = SBT(es, "xnb", [128, D], BF16)
            tr.dma("sp", "c1", ngb[:], dap(norm_g, l * D, [[0, 128], [1, D]]), (), ["ngb"])
            MEMSET(prow[:], 0.0, ["prow"])
            tr.dma("sp", "c1", prow[0:1, :].rearrange("p (a b) -> p a b", b=64), dap(qng, l * 64, [[64, 1], [0, 2], [1, 64]]), (), ["prow"])
            tr.dma("sp", "c1", prow[1:2, :].rearrange("p (a b) -> p a b", b=64), dap(kng, l * 64, [[64, 1], [0, 2], [1, 64]]), (), ["prow"])
            tr.dma("sp", "c1", prow[3:11, :], dap(b_glu, l * D, [[128, 8], [1, 128]]), (), ["prow"])
            tr.dma("sp", "c1", prow[11:19, :], dap(dsk, l * D, [[128, 8], [1, 128]]), (), ["prow"])
            TRANSPF(ps[4][:, 0:32], prow[:], 32, ["prow"], ["ps4"])
            VCP(pp[:], ps[4][:, 0:32], ["ps4"], ["pp0", "pp1", "pp2", "pp3", "pp11"])
            VTS(pp[:, 0:1], pp[:, 0:1], 0.125, None, ALU.mult, None, ["pp0"], ["pp0"])
            for j, tnsr in enumerate((lq1, lk1, lq2, lk2)):
                tr.dma("sp", "c1", lv[:, j, :], dap(tnsr, l * 64, [[0, 128], [1, 64]]), (), ["lv%d" % j])
            VTT(lv[:, 0, :], lv[:, 0, :], lv[:, 1, :], ALU.mult, ["lv0", "lv1"], ["lv0"])
            VTT(lv[:, 2, :], lv[:, 2, :], lv[:, 3, :], ALU.mult, ["lv2", "lv3"], ["lv2"])
            RSUM(sm[:, 0:1], lv[:, 0, :], ["lv0"], ["sm0"])
            RSUM(sm[:, 1:2], lv[:, 2, :], ["lv2"], ["sm1"])
            ACT(sm[:, 0:1], sm[:, 0:1], AF.Exp, ["sm0"], ["sm0"])
            ACT(sm[:, 1:2], sm[:, 1:2], AF.Exp, ["sm1"], ["sm1"])
            VTT(sm[:, 0:1], sm[:, 1:2], sm[:, 0:1], ALU.subtract, ["sm0", "sm1"], ["sm0"])
            VTS(pp[:, 2:3], sm[:, 0:1], -lam_init, None, ALU.add, None, ["sm0"], ["pp2"])
            tr.dma("sp", "c1", subB[:], dap(subg, l * 128, [[0, 128], [1, 128]]), (), ["subB"])
            VTS(subB[:], subB[:], 1.0 - lam_init, None, ALU.mult, None, ["subB"], ["subB"])
            for ti, (kind, i, n, c0) in enumerate(tiles):
                xb = xt[ti % 2]; xk = "xt%d" % (ti % 2)
                if l == 0:
                    src = x_p.ap()[128 * i:128 * i + 128, :] if kind == "p" else x_s.ap()[i]
                else:
                    src = xmid.ap()[c0:c0 + n, :]
                tr.dma("sp", xk, xb[0:n, :], src, [("xmid", ti)], [xk])
                VTT(scr[0:n, :], xb[0:n, :], xb[0:n, :], ALU.mult, [xk], ["scr"])
                RSUM(sm[0:n, 2:3], scr[0:n, :], ["scr"], ["sm2"])
                RSQ(sm[0:n, 2:3], sm[0:n, 2:3], 1.0 / D, n, ["sm2"], ["sm2"])
                STT(xnb[0:n, :], xb[0:n, :], sm[0:n, 2:3], ngb[0:n, :], ALU.mult, ALU.mult, [xk, "sm2", "ngb"], ["xnb"])
                for kc in range(8):
                    TRANSP(psT[:, 128 * kc:128 * kc + n], xnb[0:n, 128 * kc:128 * kc + 128], n, ["xnb"], ["psT"])
                VCP(xnT[:, :, c0:c0 + n], psT[:].rearrange("p (k t) -> p k t", t=128)[:, :, 0:n], ["psT"], ["xnT"])
            tr.barrier()
            stop_here("p0_%d" % l, dict(xnT=xnT[:], pp=pp[:], subB=subB[:]))

        yes = contextlib.ExitStack()
        ygT = SBT(yes, "ygT", [128, 8, NT], BF16)

        with contextlib.ExitStack() as es:
            tabCb = SBT(es, "tabCb", [128, 32, 128], BF16); tabSb = SBT(es, "tabSb", [128, 32, 128], BF16)
            tabCl = SBT(es, "tabCl", [128, 2, 32]); tabSl = SBT(es, "tabSl", [128, 2, 32])
            TLa = SBT(es, "TLa", [128, 2, 2, 32]); TLb = SBT(es, "TLb", [128, 2, 2, 32])
            RHO2 = SBT(es, "RHO2", [128, 2, 32])
            SinT = SBT(es, "SinT", [128, 64, 128], BF16); Cst = SBT(es, "Cst", [128, 64, 128], BF16)
            P = SBT(es, "P", [128, 20, 32])
            PI = SBT(es, "PI", [128, 32], I32)
            Hc = SBT(es, "Hc", [128, 3, 2, 32])
            RH = SBT(es, "RH", [128, 2, 4])
            es2 = contextlib.ExitStack()
            tabC = SBT(es2, "tabC", [128, 32, 128]); tabS = SBT(es2, "tabS", [128, 32, 128])
            es3 = contextlib.ExitStack()
            tmpA = SBT(es3, "tmpA", [128, 32, 128]); tmpB = SBT(es3, "tmpB", [128, 32, 128], I32); io = SBT(es3, "io", [128, 128])
            LR, LI, DT, ER, ANG, RHO, AR, AI, S1, C1, WR, WI, DEN, TA, TB, TC = [P[:, i, :] for i in range(16)]
            pk = lambda i: "P%d" % i
            load_GP(LR, pk(0), lamr, l * 4096)
            load_GP(LI, pk(1), lami, l * 4096)
            tr.dma("sp", "c1", dtb[:], dap(ldt, l * 64, [[0, 128], [1, 64]]), (), ["dtb"])
            VCP(P[0:64, 2, :], dtb[0:64, 0:64:2], ["dtb"], [pk(2)])
            VCP(P[64:128, 2, :], dtb[64:128, 1:64:2], ["dtb"], [pk(2)])
            VTS(TC, DT, 1.0 / 8, None, ALU.mult, None, [pk(2)], [pk(15)])
            VTS(DT, TC, 1.0 / 14, 1.0, ALU.mult, ALU.add, [pk(15)], [pk(2)])
            for kk in range(13, 0, -1):
                VTT(DT, DT, TC, ALU.mult, [pk(2), pk(15)], [pk(2)])
                VTS(DT, DT, 1.0 / kk, 1.0, ALU.mult, ALU.add, [pk(2)], [pk(2)])
            for _ in range(3):
                VTT(DT, DT, DT, ALU.mult, [pk(2)], [pk(2)])
            VTT(ER, LR, DT, ALU.mult, [pk(0), pk(2)], [pk(3)])
            VTT(ANG, LI, DT, ALU.mult, [pk(1), pk(2)], [pk(4)])
            VTS(TA, ANG, 1.0 / TWO_PI, None, ALU.mult, None, [pk(4)], [pk(13)])
            VCP(PI[:], TA, [pk(13)], ["PI"])
            VCP(TA, PI[:], ["PI"], [pk(13)])
            STT(ANG, TA, -TWO_PI, ANG, ALU.mult, ALU.add, [pk(13), pk(4)], [pk(4)])
            VTS(ANG, ANG, -3.14159, 3.14159, ALU.max, ALU.min, [pk(4)], [pk(4)])
            VTS(RHO, ER, 1.0 / 8, 1.0, ALU.mult, ALU.add, [pk(3)], [pk(5)])
            for kk in (7, 6, 5, 4, 3, 2, 1):
                VTT(RHO, RHO, ER, ALU.mult, [pk(5), pk(3)], [pk(5)])
                VTS(RHO, RHO, 1.0 / kk, 1.0, ALU.mult, ALU.add, [pk(5)], [pk(5)])
            ACT(S1, ANG, AF.Sin, [pk(4)], [pk(8)])
            ACT(TB, ANG, AF.Sin, [pk(4)], [pk(14)], scale=0.5)
            VTT(TB, TB, TB, ALU.mult, [pk(14)], [pk(14)])
            VTS(C1, TB, -2.0, 1.0, ALU.mult, ALU.add, [pk(14)], [pk(9)])
            VTT(AR, RHO, C1, ALU.mult, [pk(5), pk(9)], [pk(6)])
            VTT(AI, RHO, S1, ALU.mult, [pk(5), pk(8)], [pk(7)])
            VTS(TA, AR, -1.0, None, ALU.add, None, [pk(6)], [pk(13)])
            VTT(DEN, LR, LR, ALU.mult, [pk(0)], [pk(12)])
            VTT(TB, LI, LI, ALU.mult, [pk(1)], [pk(14)])
            VTT(DEN, DEN, TB, ALU.add, [pk(12), pk(14)], [pk(12)])
            tr.op("dve", lambda e: e.reciprocal(out=DEN, in_=DEN), [pk(12)], [pk(12)])
            VTT(WR, TA, LR, ALU.mult, [pk(13), pk(0)], [pk(10)])
            VTT(TB, AI, LI, ALU.mult, [pk(7), pk(1)], [pk(14)])
            VTT(WR, WR, TB, ALU.add, [pk(10), pk(14)], [pk(10)])
            VTT(WR, WR, DEN, ALU.mult, [pk(10), pk(12)], [pk(10)])
            VTT(WI, AI, LR, ALU.mult, [pk(7), pk(0)], [pk(11)])
            VTT(TB, TA, LI, ALU.mult, [pk(13), pk(1)], [pk(14)])
            VTT(WI, WI, TB, ALU.subtract, [pk(11), pk(14)], [pk(11)])
            VTT(WI, WI, DEN, ALU.mult, [pk(11), pk(12)], [pk(11)])
            tr.dma("sp", "c1", io[:], c_io.ap(), (), ["io"])
            angb = ANG.unsqueeze(2).broadcast_to([128, 32, 128]); iob = io[:].unsqueeze(1).broadcast_to([128, 32, 128])
            VTT(tabS[:], angb, iob, ALU.mult, [pk(4), "io"], ["tabS"])

            def rred(tb, key):
                VTS(tmpA[:], tb[:], 1.0 / TWO_PI, None, ALU.mult, None, [key], ["tmpA"])
                VCP(tmpB[:], tmpA[:], ["tmpA"], ["tmpB"])
                VCP(tmpA[:], tmpB[:], ["tmpB"], ["tmpA"])
                STT(tb[:], tmpA[:], -TWO_PI, tb[:], ALU.mult, ALU.add, ["tmpA", key], [key])
                VTS(tb[:], tb[:], -3.14159, 3.14159, ALU.max, ALU.min, [key], [key])

            rred(tabS, "tabS")
            VTS(tabC[:], tabS[:], math.pi / 2, None, ALU.add, None, ["tabS"], ["tabC"])
            rred(tabC, "tabC")
            ACT(tabS[:], tabS[:], AF.Sin, ["tabS"], ["tabS"])
            ACT(tabC[:], tabC[:], AF.Sin, ["tabC"], ["tabC"])
            ACP(tabCb[:], tabC[:], ["tabC"], ["tabCb"])
            ACP(tabSb[:], tabS[:], ["tabS"], ["tabSb"])
            for j_, col_ in enumerate((127, NS - 1)):
                VCP(tabCl[:, j_, :], tabC[:, :, col_], ["tabC"], ["tabCl"])
                VCP(tabSl[:, j_, :], tabS[:, :, col_], ["tabS"], ["tabSl"])
            for j_ in range(2):
                VCP(TLa[:, j_, 0, :], tabCl[:, j_, :], ["tabCl"], ["TLa"])
                VCP(TLa[:, j_, 1, :], tabSl[:, j_, :], ["tabSl"], ["TLa"])
                VCP(TLb[:, j_, 0, :], tabSl[:, j_, :], ["tabSl"], ["TLb"])
                VCP(TLb[:, j_, 1, :], tabCl[:, j_, :], ["tabCl"], ["TLb"])
            for j_ in range(2):
                VCP(RHO2[:, j_, :], RHO, [pk(5)], ["RHO2"])
            tr.barrier()
            es3.close()
            BBw = SBT(es2, "BBw", [128, 64, 128], BF16)
            Bn = SBT(es2, "Bn", [128, 2, 32, 16]); bb = SBT(es2, "bb", [128, 2, 32, 16]); btmp = SBT(es2, "btmp", [128, 32, 16])
            Cn = SBT(es2, "Cn", [128, 2, 8, 64]); Cw = SBT(es2, "Cw", [128, 2, 8, 128], BF16)
            tr.dma("sp", "c1", Bn[:, 0], dap(bre, l * 65536, [[16, 128], [2048, 32], [1, 16]]), (), ["Bn"])
            tr.dma("sp", "c1", Bn[:, 1], dap(bim, l * 65536, [[16, 128], [2048, 32], [1, 16]]), (), ["Bn"])
            wrb = WR.unsqueeze(2).broadcast_to([128, 32, 16]); wib = WI.unsqueeze(2).broadcast_to([128, 32, 16])
            VTT(bb[:, 0], Bn[:, 0], wrb, ALU.mult, ["Bn", pk(10)], ["bb"])
            VTT(btmp[:], Bn[:, 1], wib, ALU.mult, ["Bn", pk(11)], ["btmp"])
            VTT(bb[:, 0], bb[:, 0], btmp[:], ALU.subtract, ["bb", "btmp"], ["bb"])
            VTT(bb[:, 1], Bn[:, 1], wrb, ALU.mult, ["Bn", pk(10)], ["bb"])
            VTT(btmp[:], Bn[:, 0], wib, ALU.mult, ["Bn", pk(11)], ["btmp"])
            VTT(bb[:, 1], bb[:, 1], btmp[:], ALU.add, ["bb", "btmp"], ["bb"])
            MEMSET(BBw[:], 0.0, ["BBw"])
            BBw4 = BBw[:].rearrange("p (q r) c -> p q r c", r=2)
            for gs in range(2):
                for r in range(4):
                    for ri in range(2):
                        VCP(BBw4[64 * gs:64 * gs + 64, r::4, ri, 32 * r + 16 * gs:32 * r + 16 * gs + 16],
                            bb[64 * gs:64 * gs + 64, ri, r::4, :], ["bb", "BBw"], ["BBw"])
            for g4 in range(16):
                for j in range(4):
                    TRANSP(psT[:, 128 * j:128 * j + 128], BBw[:, 4 * g4 + j, :], 128, ["BBw"], ["psT"])
                VCP(SinT[:, 4 * g4:4 * g4 + 4, :], psT[:, 0:512].rearrange("p (a b) -> p a b", b=128), ["psT"], ["SinT"])
            tr.dma("sp", "c1", Cn[:, 0], dap(cre, l * 65536, [[64, 128], [8192, 8], [1, 64]]), (), ["Cn"])
            tr.dma("sp", "c1", Cn[:, 1], dap(cim, l * 65536, [[64, 128], [8192, 8], [1, 64]]), (), ["Cn"])
            VTS(Cn[:, 1], Cn[:, 1], -1.0, None, ALU.mult, None, ["Cn"], ["Cn"])
            for ri in range(2):
                for gs in range(2):
                    VTS(Cw[:, ri, :, 64 * gs:64 * gs + 64], Cn[:, ri], par[:, gs:gs + 1], None, ALU.mult, None, ["Cn", "par"], ["Cw"])
            MEMSET(Cst[:], 0.0, ["Cst"])
            Cst4 = Cst[:].rearrange("p (c r i) m -> p c r i m", r=4, i=2)
            for ri in range(2):
                for c4 in range(2):
                    for j in range(4):
                        TRANSP(psT[:, 128 * j:128 * j + 128], Cw[:, ri, 4 * c4 + j, :], 128, ["Cw"], ["psT"])
                    pv = psT[:, 0:512].rearrange("p (a b) -> p a b", b=128)
                    for r in range(4):
                        VCP(Cst4[:, 4 * c4:4 * c4 + 4, r, ri, 32 * r:32 * r + 32], pv[:, :, 32 * r:32 * r + 32], ["psT", "Cst"], ["Cst"])
            MEMSET(Hc[:, 0], 0.0, ["Hc0"])
            for b in range(2):
                load_GP(Hc[:, 1 + b, 0, :], "Hc%d" % (1 + b), sre, (l * 2 + b) * 4096)
                load_GP(Hc[:, 1 + b, 1, :], "Hc%d" % (1 + b), sim, (l * 2 + b) * 4096)
            tr.barrier()
            stop_here("ssmsetup_%d" % l, dict(P=P[:], tabC=tabC[:], tabS=tabS[:], SinT=SinT[:], Cst=Cst[:], Hc=Hc[:], bb=bb[:]))
            es2.close()
            uTcs = [SBT(es, "uTc%d" % i, [128, NT], BF16) for i in range(2)]
            wus = [SBT(es, "wu%d" % i, [128, 8, 128], BF16) for i in range(2)]
            d0s = [SBT(es, "d0%d" % i, [128, 4, 128]) for i in range(2)]
            W2 = SBT(es, "W2", [128, 2, 4, 128])
            W = [W2[:, i] for i in range(2)]
            Z2 = [SBT(es, "Z2%d" % j, [128, 2, 4, 128]) for j in range(2)]
            Z = [[Z2[j][:, i] for j in range(2)] for i in range(2)]
            RH2 = SBT(es, "RH2", [128, 2, 4]); cs2 = SBT(es, "cs2", [128, 2, 2, 4])
            T1 = SBT(es, "T1", [128, 4, 128], BF16); T2 = SBT(es, "T2", [128, 4, 128], BF16)
            Q1 = SBT(es, "Q1", [128, 4, 128], BF16); Q2 = SBT(es, "Q2", [128, 4, 128], BF16)
            BUb = [SBT(es, "BUb%d" % i, [128, 4, 128], BF16) for i in range(2)]
            Zb = [SBT(es, "Zb%d" % i, [128, 4, 128], BF16) for i in range(2)]
            Hf = [[SBT(es, "Hf%d%d" % (i, j), [128, 4, 128]) for j in range(2)] for i in range(2)]
            Hb = [[SBT(es, "Hb%d%d" % (i, j), [128, 4, 128], BF16) for j in range(2)] for i in range(2)]
            yts = [SBT(es, "yt%d" % i, [128, 128]) for i in range(2)]
            cs = SBT(es, "cs", [128, 4, 4])

            def load_wu(cc):
                tr.dma("pool", "wu", wus[cc % 2][:], dap(w_in, l * D * INC + 4096 + 128 * cc, [[INC, 128], [INC * 128, 8], [1, 128]]), (), ["wu%d" % (cc % 2)])

            load_wu(0)
            ucnt2 = [0]
            for c in range(8):
                wu = wus[c % 2]; wuk = "wu%d" % (c % 2)
                uTc = uTcs[c % 2]; uk = "uTc%d" % (c % 2)
                d0 = d0s[c % 2]; dk = "d0%d" % (c % 2)
                if c + 1 < 8:
                    load_wu(c + 1)
                for bi, (b0, bn) in enumerate(blocks):
                    u_ = ucnt2[0] % 2; ucnt2[0] += 1
                    pb = ps[5 + u_]; pkk = "ps%d" % (5 + u_)
                    tr.mms([(pb[:, 0:bn], wu[:, kc, :], xnT[:, kc, b0:b0 + bn], dict(start=(kc == 0), stop=(kc == 7))) for kc in range(8)],
                           [wuk, "xnT"], [pkk])
                    ACP(uTc[:, b0:b0 + bn], pb[:, 0:bn], [pkk], [uk])
                VCP(d0[:], RHO[:, 4 * c:4 * c + 4].unsqueeze(2).broadcast_to([128, 4, 128]), [pk(5)], [dk], eng="pool")
                MEMSET(d0[:, :, 0:1], 0.0, [dk], eng="pool")

                def emit_bu(ti):
                    kind, i, nt, c0 = tiles[ti]
                    for ri in range(2):
                        tr.mms([(ps[ri][:, 128 * r:128 * r + nt], SinT[:, (4 * c + r) * 2 + ri, :], uTc[:, c0:c0 + nt], dict(start=True, stop=True)) for r in range(4)],
                               ["SinT", uk], ["ps%d" % ri])

                def emit_cast(ti):
                    kind, i, nt, c0 = tiles[ti]
                    for ri in range(2):
                        ACP(BUb[ri][:, :, 0:nt], ps[ri][:].rearrange("p (r i) -> p r i", i=128)[:, :, 0:nt], ["ps%d" % ri], ["BUb%d" % ri])

                def emit_in(ti):
                    kind, i, nt, c0 = tiles[ti]
                    par_ = ti % 2
                    slot = 0 if kind == "p" else 1 + i
                    hk = "Hc%d" % slot
                    tcs = tabCb[:, 4 * c:4 * c + 4, 0:nt]; tss = tabSb[:, 4 * c:4 * c + 4, 0:nt]
                    bur = BUb[0][:, :, 0:nt]; bui = BUb[1][:, :, 0:nt]
                    VTT(T1[:, :, 0:nt], bur, tcs, ALU.mult, ["BUb0", "tabCb"], ["T1"])
                    VTT(T2[:, :, 0:nt], bui, tss, ALU.mult, ["BUb1", "tabSb"], ["T2"])
                    VTT(W[0][:, :, 0:nt], T1[:, :, 0:nt], T2[:, :, 0:nt], ALU.add, ["T1", "T2"], ["W0"])
                    VTT(T1[:, :, 0:nt], bui, tcs, ALU.mult, ["BUb1", "tabCb"], ["T1"])
                    VTT(T2[:, :, 0:nt], bur, tss, ALU.mult, ["BUb0", "tabSb"], ["T2"])
                    VTT(W[1][:, :, 0:nt], T1[:, :, 0:nt], T2[:, :, 0:nt], ALU.subtract, ["T1", "T2"], ["W1"])
                    VTT(RH2[:], Hc[:, slot, :, 4 * c:4 * c + 4], RHO2[:, :, 4 * c:4 * c + 4], ALU.mult, [hk, "RHO2"], ["RH2"])
                    VTT(W2[:, :, :, 0:1], W2[:, :, :, 0:1], RH2[:].unsqueeze(3), ALU.add, ["W0", "W1", "RH2"], ["W0", "W1"])
                    for ri in range(2):
                        Zt = Z[ri][par_]; zk = "Z%d%d" % (ri, par_)
                        if nt == 128:
                            zin = W[ri].rearrange("p r i -> p (r i)"); zout = Zt.rearrange("p r i -> p (r i)")
                            dd = d0[:].rearrange("p r i -> p (r i)")
                            tr.op("dve", lambda e: e.tensor_tensor_scan(out=zout, data0=dd, data1=zin, initial=0.0, op0=ALU.mult, op1=ALU.add),
                                  ["W%d" % ri, dk], [zk])
                        else:
                            for r in range(4):
                                tr.op("dve", lambda e: e.tensor_tensor_scan(out=Zt[:, r, 0:nt], data0=d0[:, r, 0:nt], data1=W[ri][:, r, 0:nt],
                                                                            initial=0.0, op0=ALU.mult, op1=ALU.add),
                                      ["W%d" % ri, dk], [zk])
                    jl_ = 0 if nt == 128 else 1
                    zl = Z2[par_][:, :, :, nt - 1]
                    zk0 = "Z0%d" % par_; zk1 = "Z1%d" % par_
                    VTT(cs2[:, 0], zl, TLa[:, jl_, :, 4 * c:4 * c + 4], ALU.mult, [zk0, zk1, "TLa"], ["cs2a"])
                    VTT(cs2[:, 1], zl, TLb[:, jl_, :, 4 * c:4 * c + 4], ALU.mult, [zk0, zk1, "TLb"], ["cs2b"])
                    VTT(Hc[:, slot, 0, 4 * c:4 * c + 4], cs2[:, 0, 0, :], cs2[:, 0, 1, :], ALU.subtract, ["cs2a", hk], [hk])
                    VTT(Hc[:, slot, 1, 4 * c:4 * c + 4], cs2[:, 1, 0, :], cs2[:, 1, 1, :], ALU.add, ["cs2b", hk], [hk])

                def emit_out(ti):
                    kind, i, nt, c0 = tiles[ti]
                    par_ = ti % 2
                    tcs = tabCb[:, 4 * c:4 * c + 4, 0:nt]; tss = tabSb[:, 4 * c:4 * c + 4, 0:nt]
                    for ri in range(2):
                        ACP(Zb[ri][:, :, 0:nt], Z[ri][par_][:, :, 0:nt], ["Z%d%d" % (ri, par_)], ["Zb%d" % ri])
                    zr = Zb[0][:, :, 0:nt]; zi = Zb[1][:, :, 0:nt]
                    VTT(Q1[:, :, 0:nt], zr, tcs, ALU.mult, ["Zb0", "tabCb"], ["Q1"])
                    VTT(Q2[:, :, 0:nt], zi, tss, ALU.mult, ["Zb1", "tabSb"], ["Q2"])
                    VTT(Hb[0][par_][:, :, 0:nt], Q1[:, :, 0:nt], Q2[:, :, 0:nt], ALU.subtract, ["Q1", "Q2"], ["Hb0%d" % par_])
                    VTT(Q1[:, :, 0:nt], zr, tss, ALU.mult, ["Zb0", "tabSb"], ["Q1"])
                    VTT(Q2[:, :, 0:nt], zi, tcs, ALU.mult, ["Zb1", "tabCb"], ["Q2"])
                    VTT(Hb[1][par_][:, :, 0:nt], Q1[:, :, 0:nt], Q2[:, :, 0:nt], ALU.add, ["Q1", "Q2"], ["Hb1%d" % par_])

                def emit_y(ti):
                    kind, i, nt, c0 = tiles[ti]
                    pb = ps[2 + ti % 2]; pkk = "ps%d" % (2 + ti % 2)
                    yt = yts[ti % 2]; yk = "yt%d" % (ti % 2)
                    par_ = ti % 2
                    lst = []
                    for r in range(4):
                        for ri in range(2):
                            lst.append((pb[:, 0:nt], Cst[:, (4 * c + r) * 2 + ri, :], Hb[ri][par_][:, r, 0:nt], dict(start=(len(lst) == 0), stop=(len(lst) == 7))))
                    tr.mms(lst, ["Cst", "Hb0%d" % par_, "Hb1%d" % par_], [pkk])
                    STT(yt[:, 0:nt], uTc[:, c0:c0 + nt], pp[:, 11 + c:12 + c], pb[:, 0:nt], ALU.mult, ALU.add, [uk, "pp11", pkk], [yk])
                    ACT(ygT[:, c, c0:c0 + nt], yt[:, 0:nt], AF.Gelu, [yk], ["ygT"])

                ntl = len(tiles)
                emit_bu(0)
                emit_cast(0)
                for ti in range(ntl):
                    emit_in(ti)
                    if ti + 1 < ntl:
                        emit_bu(ti + 1)
                        emit_cast(ti + 1)
                    emit_out(ti)
                    if ti >= 1:
                        emit_y(ti - 1)
                emit_y(ntl - 1)
            for slot, (tr_, ti_) in enumerate(((hr_p, hi_p), (hr_s, hi_s), (hr_s, hi_s))):
                for ri, tt in enumerate((tr_, ti_)):
                    off = l * 4096 if slot == 0 else (l * 2 + slot - 1) * 4096
                    store_GP(tt, off, Hc[:, slot, ri, :], "Hc%d" % slot, ("oh", l, slot, ri))
            tr.barrier()
            stop_here("ssm_%d" % l, dict(ygT=ygT[:], Hc=Hc[:]))

        oes = contextlib.ExitStack()
        oT = SBT(oes, "oT", [128, 8, NT], BF16)
        with contextlib.ExitStack() as es:
            wbs = [SBT(es, "wb%d" % i, [128, 8, 512], BF16) for i in range(2)]
            Qt = [SBT(es, "Qt%d" % m, [128, NT], BF16) for m in range(2)]
            Kt = [SBT(es, "Kt%d" % m, [128, NT], BF16) for m in range(2)]
            Kp = [[SBT(es, "Kp%d%d" % (b, m), [128, PAST + NS], BF16) for m in range(2)] for b in range(2)]
            Vx = SBT(es, "Vx", [128, 18, 160], BF16)
            Vp = [SBT(es, "Vp%d" % b, [128, 8, 160], BF16) for b in range(2)]
            Gt = SBT(es, "Gt", [128, 18, 128])
            Pb = [SBT(es, "Pb%d" % i, [128, 512], BF16) for i in range(4)]
            sqs = [SBT(es, "sq%d" % i, [128, 512], BF16) for i in range(2)]
            rss = [SBT(es, "rs%d" % i, [128, 512]) for i in range(2)]
            ksts = [SBT(es, "kst%d" % i, [128, 2, 64]) for i in range(4)]
            vsts = [SBT(es, "vst%d" % i, [128, 128]) for i in range(4)]
            sgs = [SBT(es, "sg%d" % i, [128, 128]) for i in range(2)]
            ckbs = [SBT(es, "ckb%d" % i, [128, 8, 128], BF16) for i in range(2)]
            osbs = [SBT(es, "osb%d" % i, [128, 128]) for i in range(4)]
            ots = [SBT(es, "ot%d" % i, [128, 128]) for i in range(4)]
            onbs = [SBT(es, "onb%d" % i, [128, 128], BF16) for i in range(4)]
            fss = [SBT(es, "fs%d" % i, [128, 8]) for i in range(4)]
            onesbd = SBT(es, "onesbd", [128, 128], BF16)
            MEMSET(onesbd[:], 0.0, ["onesbd"])
            MEMSET(onesbd[0:64, 0:64], 1.0, ["onesbd"])
            MEMSET(onesbd[64:128, 64:128], 1.0, ["onesbd"])
            MEMSET(Vx[:, :, 128:129], 1.0, ["Vx"])
            for b in range(2):
                MEMSET(Vp[b][:, :, 128:129], 1.0, ["Vp%d" % b])
            pcount = [0]
            fcount = [0]
            ocnt = [0]
            Ocp = [[SBT(es, "Ocp%d%d" % (i, j), [128, 387]) for j in range(3)] for i in range(2)]
            wbase = l * D * INC

            def load_wb(hh):
                wbt = wbs[hh % 2]; wk = "wb%d" % (hh % 2)
                for j, colo in enumerate((hh * 64, 512 + hh * 64, 1024 + hh * 64, 1536 + hh * 64)):
                    tr.dma("pool", "wb", wbt[:, :, 64 * j:64 * j + 64], dap(w_in, wbase + colo, [[INC, 128], [INC * 128, 8], [1, 64]]), (), [wk])
                tr.dma("pool", "wb", wbt[:, :, 256:384], dap(w_in, wbase + 2048 + hh * 128, [[INC, 128], [INC * 128, 8], [1, 128]]), (), [wk])
                tr.dma("pool", "wb", wbt[:, :, 384:512], dap(w_in, wbase + 3072 + hh * 128, [[INC, 128], [INC * 128, 8], [1, 128]]), (), [wk])

            load_wb(0)
            for h in range(H):
                wb = wbs[h % 2]; wk = "wb%d" % (h % 2)
                for m in range(2):
                    tr.dma("pool", "aug", Qt[m][64:68, 0:T], dap(c_qa, h * T, [[H * T, 4], [1, T]]), (), ["QtA%d" % m])
                    tr.dma("pool", "aug", Kt[m][64:68, 0:T], dap(c_ka, h * T, [[H * T, 4], [1, T]]), (), ["KtA%d" % m])
                    for b in range(2):
                        tr.dma("pool", "aug", Qt[m][64:68, T + NS * b:T + NS * b + NS], dap(c_qa, h * T + PAST, [[H * T, 4], [1, NS]]), (), ["QtA%d" % m])
                        tr.dma("pool", "aug", Kp[b][m][64:68, :], dap(c_ka, h * T, [[H * T, 4], [1, PAST + NS]]), (), ["KpA%d%d" % (b, m)])
                for b in range(2):
                    for m in range(2):
                        tr.dma("pool", "ck", ckbs[b][:, :, 64 * m:64 * m + 64],
                               dap(ck, ((l * 2 + b) * PAST) * 1024 + m * 512 + h * 64, [[1024, 128], [1024 * 128, 8], [1, 64]]), (), ["ckb%d" % b])
                    tr.dma("pool", "cv", Vp[b][:, :, 0:128], dap(cv, ((l * 2 + b) * PAST) * 1024 + h * 128, [[1024, 128], [1024 * 128, 8], [1, 128]]), (), ["Vp%d" % b])
                if h + 1 < H:
                    load_wb(h + 1)
                def emit_vga(ti):
                    kind, i, n, c0 = tiles[ti]
                    u_ = ti % 2
                    pbv = ps[2 + u_]; pkv = "ps%d" % (2 + u_)
                    pbg = ps[0 + u_]; pkg = "ps%d" % (0 + u_)
                    r4 = ti % 4
                    vst = vsts[r4]; vk_ = "vst%d" % r4; sg = sgs[ti % 2]; gk_ = "sg%d" % (ti % 2)
                    tr.mms([(pbv[0:n, 0:128], xnT[:, kc, c0:c0 + n], wb[:, kc, 256:384], dict(start=(kc == 0), stop=(kc == 7))) for kc in range(8)],
                           [wk, "xnT"], [pkv])
                    tr.mms([(pbg[0:n, 0:128], xnT[:, kc, c0:c0 + n], wb[:, kc, 384:512], dict(start=(kc == 0), stop=(kc == 7))) for kc in range(8)],
                           [wk, "xnT"], [pkg])
                    VCP(vst[0:n, :], pbv[0:n, 0:128], [pkv], [vk_])
                    VCP(Vx[0:n, ti, 0:128], pbv[0:n, 0:128], [pkv], ["Vx"])
                    if kind == "p":
                        dst = dap(nv_p, (l * T + c0) * 1024 + h * 128, [[1024, n], [1, 128]])
                    else:
                        dst = dap(nv_s, ((l * 2 + i) * NS) * 1024 + h * 128, [[1024, n], [1, 128]])
                    tr.dma("sp", "o_v", dst, vst[0:n, :], [vk_], [("ov", l, h, ti)], is_out=True)
                    ACT(sg[0:n, :], pbg[0:n, 0:128], AF.Silu, [pkg], [gk_])
                    VTT(Gt[0:n, ti, :], sg[0:n, :], subB[0:n, :], ALU.mult, [gk_, "subB"], ["Gt"])

                vga_next = [0]
                for pr in range(2):
                    isq = pr == 0
                    dsts = Qt if isq else Kt
                    dks = ["Qt0", "Qt1"] if isq else ["Kt0", "Kt1"]
                    gci = 0 if isq else 1
                    for (b0, bn) in blocks:
                        u_ = pcount[0] % 2; pcount[0] += 1
                        pb = ps[5 + u_]; pkk = "ps%d" % (5 + u_)
                        sq = sqs[u_]; rs = rss[u_]; sk_ = "sq%d" % u_; rk_ = "rs%d" % u_
                        tr.mms([(pb[:, 0:bn], wb[:, kc, 128 * pr:128 * pr + 128], xnT[:, kc, b0:b0 + bn], dict(start=(kc == 0), stop=(kc == 7))) for kc in range(8)],
                               [wk, "xnT"], [pkk])
                        ACT(sq[:, 0:bn], pb[:, 0:bn], AF.Square, [pkk], [sk_])
                        tr.mms([(ps[4][:, 0:bn], onesbd[:], sq[:, 0:bn], dict(start=True, stop=True))], [sk_, "onesbd"], ["ps4"])
                        ACT(rs[:, 0:bn], ps[4][:, 0:bn], AF.Ln, ["ps4", "epsc"], [rk_], scale=1.0 / 64, bias=epsc[:, :])
                        ACT(rs[:, 0:bn], rs[:, 0:bn], AF.Exp, [rk_], [rk_], scale=-0.5)
                        for m in range(2):
                            STT(dsts[m][0:64, b0:b0 + bn], pb[64 * m:64 * m + 64, 0:bn], pp[64 * m:64 * m + 64, gci:gci + 1], rs[64 * m:64 * m + 64, 0:bn],
                                ALU.mult, ALU.mult, [pkk, rk_, "pp0", "pp1"], [dks[m]])
                        for _ in range(2):
                            if vga_next[0] < len(tiles):
                                emit_vga(vga_next[0]); vga_next[0] += 1
                while vga_next[0] < len(tiles):
                    emit_vga(vga_next[0]); vga_next[0] += 1
                stop_here("attnB", dict(Qt0=Qt[0][:], Kt0=Kt[0][:], Qt1=Qt[1][:], Kt1=Kt[1][:]))
                bg = []

                def mk_newk(ti):
                    def f():
                        kind, i, n, c0 = tiles[ti]
                        r4 = ti % 4; r2 = ti % 2
                        kst = ksts[r4]; kk_ = "kst%d" % r4; tk_ = "psT"
                        for m in range(2):
                            TRANSP(psT[0:n, 256 + 128 * r2 + 64 * m:256 + 128 * r2 + 64 * m + 64], Kt[m][0:64, c0:c0 + n], 64, ["Kt%d" % m], [tk_])
                        VCP(kst[0:n].rearrange("p m d -> p (m d)"), psT[0:n, 256 + 128 * r2:256 + 128 * r2 + 128], [tk_], [kk_])
                        if kind == "p":
                            dst = dap(nk_p, (l * T + c0) * 1024 + h * 64, [[1024, n], [512, 2], [1, 64]])
                        else:
                            dst = dap(nk_s, ((l * 2 + i) * NS) * 1024 + h * 64, [[1024, n], [512, 2], [1, 64]])
                        tr.dma("sp", "o_k", dst, kst[0:n], [kk_], [("ok", l, h, ti)], is_out=True)
                    return f

                def mk_cache(b, m):
                    def f():
                        ckb = ckbs[b]
                        for tt in range(8):
                            TRANSP(psT[0:64, 128 * tt:128 * tt + 128], ckb[:, tt, 64 * m:64 * m + 64], 128, ["ckb%d" % b], ["psT"])
                        VCP(Kp[b][m][0:64, 0:PAST], psT[0:64, :], ["psT"], ["Kp%d%d" % (b, m)])
                        VCP(Kp[b][m][0:64, PAST:PAST + NS], Kt[m][0:64, T + NS * b:T + NS * b + NS], ["Kt%d" % m], ["Kp%d%d" % (b, m)])
                    return f

                for b in range(2):
                    for m in range(2):
                        bg.append(mk_cache(b, m))
                for ti in range(len(tiles)):
                    bg.append(mk_newk(ti))
                stop_here("attnD", dict(Gt=Gt[:]))
                stop_here("attnE", dict(Kp00=Kp[0][0][:]))
                pending = []

                def attend(qcols, qpos0, nq, ktiles, gt_tile):
                    steps = []
                    blocks_ = []
                    for qb0 in range(0, nq, 512):
                        nqb = min(512, nq - qb0)
                        qts = [(qb0 + 128 * i, min(128, nq - qb0 - 128 * i)) for i in range((nqb + 127) // 128)]
                        lastk = {}
                        vis = []
                        for kj, (Kap, Vap, nk, kp0) in enumerate(ktiles):
                            first = None
                            diag = False
                            for qi, (q0, nqi) in enumerate(qts):
                                tpos = qpos0 + q0
                                if kp0 // 64 > (tpos + nqi - 1) // 64:
                                    continue
                                if first is None:
                                    first = qi
                                    diag = (kp0 + nk - 1) > tpos
                                lastk[qi] = kj
                            if first is not None:
                                vis.append((kj, first, diag))
                        blk = dict(qb0=qb0, nqb=nqb, qts=qts, lastk=lastk, started=set(), nsteps=2 * len(vis))
                        blocks_.append(blk)
                        for (kj, first, diag) in vis:
                            for m in range(2):
                                steps.append(dict(blk=blk, kj=kj, first=first, diag=diag, m=m))
                        steps[-1]["last"] = True

                    def emit_S(st):
                        blk = st["blk"]; qts = blk["qts"]; m = st["m"]
                        Kap, Vap, nk, kp0 = ktiles[st["kj"]]
                        c0 = qts[st["first"]][0]; ncol = blk["qb0"] + blk["nqb"] - c0
                        si = pcount[0] % 2; pbi = pcount[0] % 4; pcount[0] += 1
                        S = ps[si]; sk = "ps%d" % si
                        lst = [(S[0:nk, 0:ncol], Kap[m], Qt[m][0:68, qcols + c0:qcols + c0 + ncol], dict(start=True, stop=not st["diag"]))]
                        if st["diag"]:
                            nqd = qts[st["first"]][1]
                            lst.append((S[0:nk, 0:nqd], ident[0:nk, 0:nk], dmat[0:nk, h, 0:nqd], dict(start=False, stop=True)))
                        tr.mms(lst, ["Qt%d" % m, "Kt%d" % m, "Kp0%d" % m, "Kp1%d" % m, "QtA%d" % m, "KtA%d" % m, "KpA0%d" % m, "KpA1%d" % m, "ident", "dmat"], [sk])
                        Pt = Pb[pbi]; pk_ = "Pb%d" % pbi
                        ACT(Pt[0:nk, 0:ncol], S[0:nk, 0:ncol], AF.Exp, [sk], [pk_])
                        st["P"] = (Pt, pk_, c0)

                    def emit_PV(st):
                        blk = st["blk"]; qts = blk["qts"]; m = st["m"]
                        Kap, Vap, nk, kp0 = ktiles[st["kj"]]
                        Pt, pk_, c0 = st["P"]
                        for qi in range(st["first"], len(qts)):
                            q0, nqi = qts[qi]
                            a_ = m * 4 + qi
                            bank = 2 + a_ // 3
                            O = ps[bank][0:nqi, 129 * (a_ % 3):129 * (a_ % 3) + 129]
                            stf = bank not in blk["started"]
                            blk["started"].add(bank)
                            tr.mms([(O, Pt[0:nk, q0 - c0:q0 - c0 + nqi], Vap, dict(start=stf, stop=(blk["lastk"][qi] == st["kj"]), skip_group_check=True))],
                                   [pk_, "Vx", "Vp0", "Vp1"], ["ps%d" % bank])

                    def finalize(blk):
                        qts = blk["qts"]
                        nqt = len(qts)
                        oset = ocnt[0] % 2; ocnt[0] += 1
                        nmax = max(n for (_, n) in qts)
                        used = sorted(set(2 + (m * 4 + qi) // 3 for qi in range(nqt) for m in range(2)))
                        for bank in used:
                            ACP(Ocp[oset][bank - 2][0:nmax, :], ps[bank][0:nmax, 0:387], ["ps%d" % bank], ["ocp%d_%d" % (oset, bank)])
                        Aps = []
                        for qi, (q0, nqi) in enumerate(qts):
                            A = []
                            for m in range(2):
                                a_ = m * 4 + qi
                                bank = 2 + a_ // 3
                                A.append((Ocp[oset][bank - 2][0:nqi, 129 * (a_ % 3):129 * (a_ % 3) + 129], "ocp%d_%d" % (oset, bank)))
                            Aps.append(A)
                        for qi, (q0, n) in enumerate(qts):
                            A = Aps[qi]; fs = fss[qi]; ot = ots[qi]; osb = osbs[qi]
                            fk = "fs%d_" % qi; otk = "ot%d" % qi; osk = "osb%d" % qi
                            tr.op("dve", lambda e: e.reciprocal(out=fs[0:n, 0:1], in_=A[0][0][:, 128:129]), [A[0][1]], [fk + "0"])
                            tr.op("dve", lambda e: e.reciprocal(out=fs[0:n, 1:2], in_=A[1][0][:, 128:129]), [A[1][1]], [fk + "1"])
                            VTT(fs[0:n, 1:2], fs[0:n, 1:2], pp[0:n, 2:3], ALU.mult, [fk + "1", "pp2"], [fk + "1"])
                            VTS(ot[0:n, :], A[1][0][:, 0:128], fs[0:n, 1:2], None, ALU.mult, None, [A[1][1], fk + "1"], [otk])
                            STT(osb[0:n, :], A[0][0][:, 0:128], fs[0:n, 0:1], ot[0:n, :], ALU.mult, ALU.add, [A[0][1], fk + "0", otk], [osk])
                            VTT(ot[0:n, :], osb[0:n, :], osb[0:n, :], ALU.mult, [osk], [otk])
                            RSUM(fs[0:n, 2:3], ot[0:n, :], [otk], [fk + "2"])
                        def stage2():
                            for qi, (q0, n) in enumerate(qts):
                                fs = fss[qi]; fk = "fs%d_" % qi
                                RSQ(fs[0:n, 2:3], fs[0:n, 2:3], 1.0 / 128, n, [fk + "2"], [fk + "2"])
                            for qi, (q0, n) in enumerate(qts):
                                fs = fss[qi]; osb = osbs[qi]; onb = onbs[qi]
                                STT(onb[0:n, :], osb[0:n, :], fs[0:n, 2:3], Gt[0:n, gt_tile(q0), :], ALU.mult, ALU.mult,
                                    ["osb%d" % qi, "fs%d_2" % qi, "Gt"], ["onb%d" % qi])

                        def deferred():
                            for qi, (q0, n) in enumerate(qts):
                                TRANSP(psT[:, 128 * qi:128 * qi + n], onbs[qi][0:n, :], n, ["onb%d" % qi], ["psT"])
                            qa = qts[0][0]; ntot = qts[-1][0] + qts[-1][1] - qa
                            if all(n == 128 for (_, n) in qts):
                                VCP(oT[:, h, qcols + qa:qcols + qa + ntot], psT[:, 0:ntot], ["psT"], ["oT"])
                            else:
                                for qi, (q0, n) in enumerate(qts):
                                    VCP(oT[:, h, qcols + q0:qcols + q0 + n], psT[:, 128 * qi:128 * qi + n], ["psT"], ["oT"])
                        return [[8, stage2], [11, deferred]]

                    emit_S(steps[0])
                    for i, st in enumerate(steps):
                        if i + 1 < len(steps):
                            emit_S(steps[i + 1])
                        emit_PV(st)
                        if bg and i >= 2:
                            bg.pop(0)()
                        for p_ in pending:
                            p_[0] -= 1
                        while pending and pending[0][0] <= 0:
                            pending.pop(0)[1]()
                        if st.get("last"):
                            while pending:
                                pending.pop(0)[1]()
                            pending.extend(finalize(st["blk"]))
                    while bg:
                        bg.pop(0)()

                kts = [([Kt[0][0:68, 128 * j:128 * j + 128], Kt[1][0:68, 128 * j:128 * j + 128]], Vx[:, j, 0:129], 128, 128 * j) for j in range(16)]
                attend(0, 0, T, kts, lambda q0: q0 // 128)
                stop_here("attnF", dict(oT=oT[:]))
                for b in range(2):
                    kts = [([Kp[b][0][0:68, 128 * j:128 * j + 128], Kp[b][1][0:68, 128 * j:128 * j + 128]], Vp[b][:, j, 0:129], 128, 128 * j) for j in range(8)]
                    kts.append(([Kp[b][0][0:68, PAST:PAST + NS], Kp[b][1][0:68, PAST:PAST + NS]], Vx[0:NS, 16 + b, 0:129], NS, PAST))
                    attend(T + NS * b, PAST, NS, kts, lambda q0, b=b: 16 + b)
                while pending:
                    pending.pop(0)[1]()
                stop_here("attnh%d_%d" % (h, l), dict(oT=oT[:], Qt0=Qt[0][:], Kt0=Kt[0][:], Qt1=Qt[1][:], Kt1=Kt[1][:], Vx=Vx[:], Gt=Gt[:], Kp00=Kp[0][0][:], Vp0=Vp[0][:]))
            tr.barrier()
            stop_here("attn_%d" % l, dict(oT=oT[:]))

        with contextlib.ExitStack() as es:
            ysgT = SBT(es, "ysgT", [128, 8, NT], BF16)
            wA = SBT(es, "wA", [128, 8, D], BF16); wB = SBT(es, "wB", [128, 8, D], BF16)
            tmp = [SBT(es, "tmp%d" % i, [128, 512]) for i in range(2)]
            tb16 = SBT(es, "tb16", [128, 512], BF16)
            xr = [SBT(es, "xr%d" % i, [128, D]) for i in range(2)]

            def loadw(dst, key, tnsr, base, rowlen):
                tr.dma("pool", key, dst[:], dap(tnsr, base, [[rowlen, 128], [rowlen * 128, 8], [1, D]]), (), [key])

            def proj(wt, wkey, src, skey, oc, b0, bn, pi):
                pb = ps[pi]
                tr.mms([(pb[:, 0:bn], wt[:, kc, 128 * oc:128 * oc + 128], src[:, kc, b0:b0 + bn], dict(start=(kc == 0), stop=(kc == 7))) for kc in range(8)],
                       [wkey, skey], ["ps%d" % pi])
                return pb[:, 0:bn], "ps%d" % pi

            cnt = [0]
            loadw(wA, "wA", w_glu, l * D * D, D)
            loadw(wB, "wB", w_in, l * D * INC + 5120, INC)
            for oc in range(8):
                for (b0, bn) in blocks:
                    pi = cnt[0] % 2; cnt[0] += 1
                    pa, ka = proj(wA, "wA", ygT, "ygT", oc, b0, bn, pi)
                    ACT(tmp[0][:, 0:bn], pa, AF.Sigmoid, [ka, "pp3"], ["tmp0"], bias=pp[:, 3 + oc:4 + oc])
                    VTT(ysgT[:, oc, b0:b0 + bn], ygT[:, oc, b0:b0 + bn], tmp[0][:, 0:bn], ALU.mult, ["ygT", "tmp0"], [("ysg", oc)])
            for oc in range(8):
                for (b0, bn) in blocks:
                    pi = cnt[0] % 2; cnt[0] += 1
                    pa, ka = proj(wB, "wB", xnT, "xnT", oc, b0, bn, pi)
                    ACT(tb16[:, 0:bn], pa, AF.Silu, [ka], ["tb16"])
                    VTT(ysgT[:, oc, b0:b0 + bn], ysgT[:, oc, b0:b0 + bn], tb16[:, 0:bn], ALU.mult, [("ysg", oc), "tb16"], [("ysg", oc)])
            allysg = [("ysg", oc) for oc in range(8)]
            loadw(wA, "wA", w_sp, l * D * D, D)
            loadw(wB, "wB", w_in, l * D * INC + 7168, INC)
            for oc in range(8):
                for (b0, bn) in blocks:
                    pi = cnt[0] % 2; cnt[0] += 1
                    pg, kg = proj(wB, "wB", xnT, "xnT", oc, b0, bn, 2 + pi)
                    ACT(tmp[0][:, 0:bn], pg, AF.Sigmoid, [kg], ["tmp0"])
                    pb = ps[pi]
                    tr.mms([(pb[:, 0:bn], wA[:, kc, 128 * oc:128 * oc + 128], ysgT[:, kc, b0:b0 + bn], dict(start=(kc == 0), stop=(kc == 7))) for kc in range(8)],
                           ["wA"] + allysg, ["ps%d" % pi])
                    VTT(ygT[:, oc, b0:b0 + bn], pb[:, 0:bn], tmp[0][:, 0:bn], ALU.mult, ["ps%d" % pi, "tmp0"], ["ygT"])
            loadw(wA, "wA", w_ap, l * D * D, D)
            loadw(wB, "wB", w_in, l * D * INC + 6144, INC)
            for oc in range(8):
                for (b0, bn) in blocks:
                    pi = cnt[0] % 2; cnt[0] += 1
                    pg, kg = proj(wB, "wB", xnT, "xnT", oc, b0, bn, 2 + pi)
                    ACT(tmp[0][:, 0:bn], pg, AF.Sigmoid, [kg], ["tmp0"])
                    pa, ka = proj(wA, "wA", oT, "oT", oc, b0, bn, pi)
                    VTT(tmp[1][:, 0:bn], pa, tmp[0][:, 0:bn], ALU.mult, [ka, "tmp0"], ["tmp1"])
                    VTT(ygT[:, oc, b0:b0 + bn], ygT[:, oc, b0:b0 + bn], tmp[1][:, 0:bn], ALU.add, ["ygT", "tmp1"], ["ygT"])
            loadw(wA, "wA", w_o, l * D * D, D)
            for ti, (kind, i, n, c0) in enumerate(tiles):
                xb = xr[ti % 2]; xk = "xr%d" % (ti % 2)
                if l == 0:
                    src = x_p.ap()[128 * i:128 * i + 128, :] if kind == "p" else x_s.ap()[i]
                else:
                    src = xmid.ap()[c0:c0 + n, :]
                tr.dma("sp", xk, xb[0:n, :], src, [("xmid", ti)], [xk])
                for hb in range(2):
                    pi = cnt[0] % 2; cnt[0] += 1
                    pb = ps[pi]
                    tr.mms([(pb[0:n, :], ygT[:, kc, c0:c0 + n], wA[:, kc, 512 * hb:512 * hb + 512], dict(start=(kc == 0), stop=(kc == 7))) for kc in range(8)],
                           ["wA", "ygT"], ["ps%d" % pi])
                    VTT(xb[0:n, 512 * hb:512 * hb + 512], xb[0:n, 512 * hb:512 * hb + 512], pb[0:n, :], ALU.add, [xk, "ps%d" % pi], [xk])
                if l == DEPTH - 1:
                    dst = y_p.ap()[128 * i:128 * i + 128, :] if kind == "p" else y_s.ap()[i]
                    tr.dma("sp", "o_y", dst, xb[0:n, :], [xk], [("oy", ti)], is_out=True)
                else:
                    tr.dma("sp", "o_x", xmid.ap()[c0:c0 + n, :], xb[0:n, :], [xk], [("xmid2", ti)])
            tr.barrier()
            stop_here("merge_%d" % l, dict(mT=ygT[:], ysgT=ysgT[:]))
        oes.close()
        yes.close()

    except StopBuild:
        pass
    tr.finish()
    return nc


_NC = None


def _consts():
    ident = np.eye(128, dtype=np.float32)
    dm = np.zeros((128, H, 128), np.float32)
    s = np.arange(128)[:, None]; t = np.arange(128)[None, :]
    for h in range(H):
        d = np.where(s > t, -2.0 * SLOPES[h] * (s - t), 0.0)
        d = np.where((s // 64) > (t // 64), -30000.0, d)
        dm[:, h, :] = d
    pos = np.arange(T)
    hi = (pos // 128) * 128.0; lo = (pos % 128) * 1.0
    qa = np.zeros((4, H, T), np.float32); ka = np.zeros((4, H, T), np.float32)
    for h in range(H):
        qa[0, h] = -SLOPES[h] * hi; qa[1, h] = -SLOPES[h] * lo; qa[2, h] = 1.0; qa[3, h] = 1.0
        ka[0, h] = 1.0; ka[1, h] = 1.0; ka[2, h] = SLOPES[h] * hi; ka[3, h] = SLOPES[h] * lo
    par = np.zeros((128, 2), np.float32)
    g8 = np.arange(128) // 16
    par[:, 0] = (g8 % 2 == 0); par[:, 1] = (g8 % 2 == 1)
    io = np.tile(np.arange(1, 129, dtype=np.float32)[None, :], (128, 1))
    return dict(c_id=ident, c_dm=dm, c_qa=qa, c_ka=ka, c_par=par, c_io=io)


def kernel(x_prompt, x_sample, cache_k, cache_v, state_ssm_re, state_ssm_im,
           norm_g, w_in, q_norm_g, k_norm_g, lam_q1, lam_k1, lam_q2, lam_k2, subln_g,
           w_attn_proj, ssm_lambda_re, ssm_lambda_im, ssm_log_dt, ssm_b_re, ssm_b_im,
           ssm_c_re, ssm_c_im, ssm_d, w_glu, b_glu, w_ssm_proj, w_out):
    global _NC
    if _NC is None:
        _NC = build()
    nc = _NC
    f = lambda a: np.ascontiguousarray(np.asarray(a, dtype=np.float32))
    shared = dict(norm_g=f(norm_g), w_in=f(w_in), qng=f(q_norm_g), kng=f(k_norm_g), lq1=f(lam_q1), lk1=f(lam_k1),
                  lq2=f(lam_q2), lk2=f(lam_k2), subg=f(subln_g), w_ap=f(w_attn_proj), lamr=f(ssm_lambda_re),
                  lami=f(ssm_lambda_im), ldt=f(ssm_log_dt), bre=f(ssm_b_re), bim=f(ssm_b_im), cre=f(ssm_c_re),
                  cim=f(ssm_c_im), dsk=f(ssm_d), w_glu=f(w_glu), b_glu=f(b_glu), w_sp=f(w_ssm_proj), w_o=f(w_out))
    shared.update(_consts())
    x_prompt = np.asarray(x_prompt); x_sample = np.asarray(x_sample)
    cache_k = np.asarray(cache_k); cache_v = np.asarray(cache_v)
    state_ssm_re = np.asarray(state_ssm_re); state_ssm_im = np.asarray(state_ssm_im)
    in_maps = []
    for c in range(NCORES):
        m = dict(shared)
        m["x_p"] = f(x_prompt[c]); m["x_s"] = f(x_sample[2 * c:2 * c + 2])
        m["ck"] = f(cache_k[:, 2 * c:2 * c + 2]); m["cv"] = f(cache_v[:, 2 * c:2 * c + 2])
        m["sre"] = f(state_ssm_re[:, 2 * c:2 * c + 2]); m["sim"] = f(state_ssm_im[:, 2 * c:2 * c + 2])
        in_maps.append(m)
    res = run_bass_kernel_spmd(nc, in_maps, core_ids=list(range(NCORES)))
    R = res.results
    cat0 = lambda k: np.stack([R[c][k] for c in range(NCORES)], axis=0)
    y_p = cat0("y_p")
    y_s = np.concatenate([R[c]["y_s"] for c in range(NCORES)], axis=0)
    nk_p = np.stack([R[c]["nk_p"] for c in range(NCORES)], axis=1)
    nv_p = np.stack([R[c]["nv_p"] for c in range(NCORES)], axis=1)
    hr_p = np.stack([R[c]["hr_p"] for c in range(NCORES)], axis=1)
    hi_p = np.stack([R[c]["hi_p"] for c in range(NCORES)], axis=1)
    nk_s = np.concatenate([R[c]["nk_s"] for c in range(NCORES)], axis=1)
    nv_s = np.concatenate([R[c]["nv_s"] for c in range(NCORES)], axis=1)
    hr_s = np.concatenate([R[c]["hr_s"] for c in range(NCORES)], axis=1)
    hi_s = np.concatenate([R[c]["hi_s"] for c in range(NCORES)], axis=1)
    return tuple(np.ascontiguousarray(a.astype(np.float32)) for a in (y_p, y_s, nk_p, nv_p, hr_p, hi_p, nk_s, nv_s, hr_s, hi_s))
```

```python
import math
import numpy as np
import concourse.bass as bass
import concourse.mybir as mybir
from concourse.bass_utils import run_bass_kernel_spmd

F32 = mybir.dt.float32
BF16 = mybir.dt.bfloat16
I32 = mybir.dt.int32
ALU = mybir.AluOpType
AF = mybir.ActivationFunctionType
AX = mybir.AxisListType

D = 1024
T = 2048
NS = 16
PAST = 1024
NT = T + 2 * NS
H = 8
DEPTH = 2
EPS = 1e-6
INC = 8192
NCORES = 8
SLOPES = [2.0 ** (-(h + 1)) for h in range(H)]


class Tr:
    def __init__(self, nc):
        self.nc = nc
        self.eng = {}
        for name, h in (("pe", nc.tensor), ("act", nc.scalar), ("dve", nc.vector), ("pool", nc.gpsimd), ("sp", nc.sync)):
            self.eng[name] = dict(h=h, sem=nc.alloc_semaphore(name="s_" + name), cnt=0, seen={})
        self.lastw = {}
        self.readers = {}
        self.dsem = {}
        self.out_tokens = []

    def _deps(self, reads, writes):
        deps = []
        for k in reads:
            if k in self.lastw:
                deps.append(self.lastw[k])
        for k in writes:
            if k in self.lastw:
                deps.append(self.lastw[k])
            deps.extend(self.readers.get(k, ()))
        return deps

    def _wait(self, ename, deps):
        e = self.eng[ename]
        best = {}
        for (sname, sem, val) in deps:
            if ename == "pe" and sname == "s_pe":
                continue
            if val > best.get(sname, (None, 0))[1]:
                best[sname] = (sem, val)
        for sname, (sem, val) in best.items():
            if e["seen"].get(sname, 0) < val:
                e["h"].wait_ge(sem, val)
                e["seen"][sname] = val

    def _commit(self, tok, reads, writes):
        for k in writes:
            self.lastw[k] = tok
            self.readers[k] = []
        for k in reads:
            self.readers.setdefault(k, []).append(tok)

    def op(self, ename, fn, reads=(), writes=()):
        e = self.eng[ename]
        self._wait(ename, self._deps(reads, writes))
        ins = fn(e["h"])
        e["cnt"] += 1
        ins.then_inc(e["sem"], 1)
        self._commit(("s_" + ename, e["sem"], e["cnt"]), reads, writes)

    def mms(self, lst, reads=(), writes=()):
        e = self.eng["pe"]
        self._wait("pe", self._deps(reads, writes))
        ins = None
        for (out, lhsT, rhs, kw) in lst:
            ins = self.nc.tensor.matmul(out, lhsT=lhsT, rhs=rhs, **kw)
        e["cnt"] += 1
        ins.then_inc(e["sem"], 1)
        self._commit(("s_pe", e["sem"], e["cnt"]), reads, writes)

    def dma(self, q, slot, out, in_, reads=(), writes=(), is_out=False):
        if slot.startswith("o_") or slot == "dbg":
            slot = "st_" + "_".join(str(x) for x in (reads[0] if isinstance(reads[0], tuple) else (reads[0],))) if reads else slot
        else:
            slot = "ld_" + "_".join(str(x) for x in (writes[0] if isinstance(writes[0], tuple) else (writes[0],)))
        if slot not in self.dsem:
            self.dsem[slot] = [self.nc.alloc_semaphore(name="d_" + slot), 0]
        ds = self.dsem[slot]
        self._wait(q, self._deps(reads, writes))
        self.eng[q]["h"].dma_start(out=out, in_=in_, allow_slow_non_contiguous=True).then_inc(ds[0], 16)
        ds[1] += 16
        tok = ("d_" + slot, ds[0], ds[1])
        self._commit(tok, reads, writes)
        if is_out:
            self.out_tokens.append(tok)

    def barrier(self):
        toks = [("s_" + n, e["sem"], e["cnt"]) for n, e in self.eng.items() if e["cnt"] > 0]
        toks += [("d_" + k, v[0], v[1]) for k, v in self.dsem.items() if v[1] > 0]
        for n in self.eng:
            e = self.eng[n]
            for (sname, sem, val) in toks:
                if sname == "s_" + n:
                    continue
                if e["seen"].get(sname, 0) < val:
                    e["h"].wait_ge(sem, val)
                    e["seen"][sname] = val
        self.lastw = {}
        self.readers = {}

    def finish(self):
        best = {}
        for (sname, sem, val) in self.out_tokens:
            if val > best.get(sname, (None, 0))[1]:
                best[sname] = (sem, val)
        for sname, (sem, val) in best.items():
            self.nc.sync.wait_ge(sem, val)


def dap(t, off, dims):
    return bass.AP(t, off, [list(d) for d in dims])


class StopBuild(Exception):
    pass


def build(stop=None):
    nc = bass.Bass("TRN2", target_bir_lowering=False)
    dt_in = lambda n, s: nc.dram_tensor(n, s, F32, kind="ExternalInput")
    dt_out = lambda n, s: nc.dram_tensor(n, s, F32, kind="ExternalOutput")
    x_p = dt_in("x_p", [T, D]); x_s = dt_in("x_s", [2, NS, D])
    ck = dt_in("ck", [DEPTH, 2, PAST, 2, H, 64]); cv = dt_in("cv", [DEPTH, 2, PAST, H, 128])
    sre = dt_in("sre", [DEPTH, 2, 64, 64]); sim = dt_in("sim", [DEPTH, 2, 64, 64])
    norm_g = dt_in("norm_g", [DEPTH, D]); w_in = dt_in("w_in", [DEPTH, D, INC])
    qng = dt_in("qng", [DEPTH, 64]); kng = dt_in("kng", [DEPTH, 64])
    lq1 = dt_in("lq1", [DEPTH, 64]); lk1 = dt_in("lk1", [DEPTH, 64]); lq2 = dt_in("lq2", [DEPTH, 64]); lk2 = dt_in("lk2", [DEPTH, 64])
    subg = dt_in("subg", [DEPTH, 128]); w_ap = dt_in("w_ap", [DEPTH, D, D])
    lamr = dt_in("lamr", [DEPTH, 64, 64]); lami = dt_in("lami", [DEPTH, 64, 64]); ldt = dt_in("ldt", [DEPTH, 64])
    bre = dt_in("bre", [DEPTH, 64, 64, 16]); bim = dt_in("bim", [DEPTH, 64, 64, 16])
    cre = dt_in("cre", [DEPTH, 64, 16, 64]); cim = dt_in("cim", [DEPTH, 64, 16, 64])
    dsk = dt_in("dsk", [DEPTH, D]); w_glu = dt_in("w_glu", [DEPTH, D, D]); b_glu = dt_in("b_glu", [DEPTH, D])
    w_sp = dt_in("w_sp", [DEPTH, D, D]); w_o = dt_in("w_o", [DEPTH, D, D])
    c_id = dt_in("c_id", [128, 128]); c_dm = dt_in("c_dm", [128, H, 128])
    c_qa = dt_in("c_qa", [4, H, T]); c_ka = dt_in("c_ka", [4, H, T]); c_par = dt_in("c_par", [128, 2]); c_io = dt_in("c_io", [128, 128])
    y_p = dt_out("y_p", [T, D]); y_s = dt_out("y_s", [2, NS, D])
    nk_p = dt_out("nk_p", [DEPTH, T, 2, H, 64]); nv_p = dt_out("nv_p", [DEPTH, T, H, 128])
    hr_p = dt_out("hr_p", [DEPTH, 64, 64]); hi_p = dt_out("hi_p", [DEPTH, 64, 64])
    nk_s = dt_out("nk_s", [DEPTH, 2, NS, 2, H, 64]); nv_s = dt_out("nv_s", [DEPTH, 2, NS, H, 128])
    hr_s = dt_out("hr_s", [DEPTH, 2, 64, 64]); hi_s = dt_out("hi_s", [DEPTH, 2, 64, 64])
    xmid = nc.dram_tensor("xmid", [NT, D], F32, kind="Internal")

    tr = Tr(nc)
    import contextlib

    ucnt = [0]

    def SBT(es, n, s, d=F32):
        ucnt[0] += 1
        return es.enter_context(nc.sbuf_tensor("%s_%d" % (n, ucnt[0]), s, d))

    def stop_here(name, dumps):
        if stop != name:
            return
        tr.barrier()
        for dn, ap in dumps.items():
            shp = list(ap.shape)
            dtn = nc.dram_tensor("dbg_" + dn, shp, ap.dtype, kind="ExternalOutput")
            tr.dma("sp", "dbg", dtn.ap(), ap, (), [("dbg", dn)], is_out=True)
        raise StopBuild()

    ges = contextlib.ExitStack()
    ps = [ges.enter_context(nc.psum_tensor("ps%d" % i, [128, 512], F32)) for i in range(7)]
    psT = ges.enter_context(nc.psum_tensor("psT", [128, 1024], BF16))
    xnT = SBT(ges, "xnT", [128, 8, NT], BF16)
    ident = SBT(ges, "ident", [128, 128], BF16)
    dmat = SBT(ges, "dmat", [128, H, 128], BF16)
    ones64 = SBT(ges, "ones64", [64, 64], BF16)
    par = SBT(ges, "par", [128, 2])
    epsc = SBT(ges, "epsc", [128, 1])
    sm = SBT(ges, "sm", [128, 16])
    pp = SBT(ges, "pp", [128, 32])
    lv = SBT(ges, "lv", [128, 4, 64])
    subB = SBT(ges, "subB", [128, 128])
    identF = SBT(ges, "identF", [128, 128])
    stg = SBT(ges, "stg", [64, 64]); stg2 = SBT(ges, "stg2", [64, 64]); stg3 = SBT(ges, "stg3", [64, 64])
    prow = SBT(ges, "prow", [32, 128]); dtb = SBT(ges, "dtb", [128, 64])

    def ACT(out, in_, func, reads, writes, **kw):
        tr.op("act", lambda e: e.activation(out=out, in_=in_, func=func, **kw), reads, writes)

    def VTT(out, a, b, op, reads, writes, eng="dve"):
        tr.op(eng, lambda e: e.tensor_tensor(out=out, in0=a, in1=b, op=op), reads, writes)

    def VTS(out, a, s1, s2, op0, op1, reads, writes, eng="dve"):
        if op1 is None:
            tr.op(eng, lambda e: e.tensor_scalar(out=out, in0=a, scalar1=s1, scalar2=None, op0=op0), reads, writes)
        else:
            tr.op(eng, lambda e: e.tensor_scalar(out=out, in0=a, scalar1=s1, scalar2=s2, op0=op0, op1=op1), reads, writes)

    def STT(out, a, s, b, op0, op1, reads, writes, eng="dve"):
        tr.op(eng, lambda e: e.scalar_tensor_tensor(out=out, in0=a, scalar=s, in1=b, op0=op0, op1=op1), reads, writes)

    def VCP(out, in_, reads, writes, eng="dve"):
        tr.op(eng, lambda e: e.tensor_copy(out=out, in_=in_), reads, writes)

    def ACP(out, in_, reads, writes):
        tr.op("act", lambda e: e.copy(out=out, in_=in_), reads, writes)

    def MEMSET(ap, val, writes, eng="dve"):
        tr.op(eng, lambda e: e.memset(ap, val), (), writes)

    def RSUM(out, in_, reads, writes):
        tr.op("dve", lambda e: e.reduce_sum(out=out, in_=in_, axis=AX.X), reads, writes)

    def RSQ(out, in_, scale, n, reads, writes):
        ACT(out, in_, AF.Ln, list(reads) + ["epsc"], writes, scale=scale, bias=epsc[0:n, :])
        ACT(out, out, AF.Exp, writes, writes, scale=-0.5)

    def TRANSP(out, in_, n, reads, writes):
        tr.op("pe", lambda e: e.transpose(out=out, in_=in_, identity=ident[0:n, 0:n]), list(reads) + ["ident"], writes)

    tr.dma("pool", "c0", ident[:], c_id.ap(), (), ["ident"])
    tr.dma("sp", "c1", identF[:], c_id.ap(), (), ["identF"])

    def TRANSPF(out, in_, n, reads, writes):
        tr.op("pe", lambda e: e.transpose(out=out, in_=in_, identity=identF[0:n, 0:n]), list(reads) + ["identF"], writes)

    def load_GP(dst, dkey, tensor, off):
        tr.dma("sp", "stg", stg[:], dap(tensor, off, [[64, 64], [1, 64]]), (), ["stg"])
        TRANSPF(ps[4][0:64, 0:64], stg[:], 64, ["stg"], ["ps4"])
        VCP(dst[0:64, :], ps[4][0:64, 0:64:2], ["ps4"], [dkey])
        VCP(dst[64:128, :], ps[4][0:64, 1:64:2], ["ps4"], [dkey])

    def store_GP(tensor, off, src, skey, okey):
        VCP(stg2[:, 0:64:2], src[0:64, :], [skey], ["stg2"])
        VCP(stg2[:, 1:64:2], src[64:128, :], [skey], ["stg2"])
        TRANSPF(ps[4][0:64, 0:64], stg2[:], 64, ["stg2"], ["ps4"])
        VCP(stg3[:], ps[4][0:64, 0:64], ["ps4"], ["stg3"])
        tr.dma("sp", "o_h", dap(tensor, off, [[64, 64], [1, 64]]), stg3[:], ["stg3"], [okey], is_out=True)
    tr.dma("pool", "c0", dmat[:], c_dm.ap(), (), ["dmat"])
    tr.dma("sp", "c1", par[:], c_par.ap(), (), ["par"])
    MEMSET(ones64[:], 1.0, ["ones64"])
    MEMSET(epsc[:], EPS, ["epsc"])

    tiles = [("p", i, 128, 128 * i) for i in range(16)] + [("s", b, NS, T + NS * b) for b in range(2)]
    blocks = [(512 * i, 512) for i in range(4)] + [(T, 2 * NS)]
    TWO_PI = 2.0 * math.pi

    try:
      for l in range(DEPTH):
        lam_init = 0.8 - 0.6 * math.exp(-0.3 * l)
        with contextlib.ExitStack() as es:
            ngb = SBT(es, "ngb", [128, D]); xt = [SBT(es, "xt%d" % i, [128, D]) for i in range(2)]
            scr = SBT(es, "scr", [128, D]); xnb = SBT(es, "xnb", [128, D], BF16)
            tr.dma("sp", "c1", ngb[:], dap(norm_g, l * D, [[0, 128], [1, D]]), (), ["ngb"])
            MEMSET(prow[:], 0.0, ["prow"])
            tr.dma("sp", "c1", prow[0:1, :].rearrange("p (a b) -> p a b", b=64), dap(qng, l * 64, [[64, 1], [0, 2], [1, 64]]), (), ["prow"])
            tr.dma("sp", "c1", prow[1:2, :].rearrange("p (a b) -> p a b", b=64), dap(kng, l * 64, [[64, 1], [0, 2], [1, 64]]), (), ["prow"])
            tr.dma("sp", "c1", prow[3:11, :], dap(b_glu, l * D, [[128, 8], [1, 128]]), (), ["prow"])
            tr.dma("sp", "c1", prow[11:19, :], dap(dsk, l * D, [[128, 8], [1, 128]]), (), ["prow"])
            TRANSPF(ps[4][:, 0:32], prow[:], 32, ["prow"], ["ps4"])
            VCP(pp[:], ps[4][:, 0:32], ["ps4"], ["pp0", "pp1", "pp2", "pp3", "pp11"])
            VTS(pp[:, 0:1], pp[:, 0:1], 0.125, None, ALU.mult, None, ["pp0"], ["pp0"])
            for j, tnsr in enumerate((lq1, lk1, lq2, lk2)):
                tr.dma("sp", "c1", lv[:, j, :], dap(tnsr, l * 64, [[0, 128], [1, 64]]), (), ["lv%d" % j])
            VTT(lv[:, 0, :], lv[:, 0, :], lv[:, 1, :], ALU.mult, ["lv0", "lv1"], ["lv0"])
            VTT(lv[:, 2, :], lv[:, 2, :], lv[:, 3, :], ALU.mult, ["lv2", "lv3"], ["lv2"])
            RSUM(sm[:, 0:1], lv[:, 0, :], ["lv0"], ["sm0"])
            RSUM(sm[:, 1:2], lv[:, 2, :], ["lv2"], ["sm1"])
            ACT(sm[:, 0:1], sm[:, 0:1], AF.Exp, ["sm0"], ["sm0"])
            ACT(sm[:, 1:2], sm[:, 1:2], AF.Exp, ["sm1"], ["sm1"])
            VTT(sm[:, 0:1], sm[:, 1:2], sm[:, 0:1], ALU.subtract, ["sm0", "sm1"], ["sm0"])
            VTS(pp[:, 2:3], sm[:, 0:1], -lam_init, None, ALU.add, None, ["sm0"], ["pp2"])
            tr.dma("sp", "c1", subB[:], dap(subg, l * 128, [[0, 128], [1, 128]]), (), ["subB"])
            VTS(subB[:], subB[:], 1.0 - lam_init, None, ALU.mult, None, ["subB"], ["subB"])
            for ti, (kind, i, n, c0) in enumerate(tiles):
                xb = xt[ti % 2]; xk = "xt%d" % (ti % 2)
                if l == 0:
                    src = x_p.ap()[128 * i:128 * i + 128, :] if kind == "p" else x_s.ap()[i]
                else:
                    src = xmid.ap()[c0:c0 + n, :]
                tr.dma("sp", xk, xb[0:n, :], src, [("xmid", ti)], [xk])
                VTT(scr[0:n, :], xb[0:n, :], xb[0:n, :], ALU.mult, [xk], ["scr"])
                RSUM(sm[0:n, 2:3], scr[0:n, :], ["scr"], ["sm2"])
                RSQ(sm[0:n, 2:3], sm[0:n, 2:3], 1.0 / D, n, ["sm2"], ["sm2"])
                STT(xnb[0:n, :], xb[0:n, :], sm[0:n, 2:3], ngb[0:n, :], ALU.mult, ALU.mult, [xk, "sm2", "ngb"], ["xnb"])
                for kc in range(8):
                    TRANSP(psT[:, 128 * kc:128 * kc + n], xnb[0:n, 128 * kc:128 * kc + 128], n, ["xnb"], ["psT"])
                VCP(xnT[:, :, c0:c0 + n], psT[:].rearrange("p (k t) -> p k t", t=128)[:, :, 0:n], ["psT"], ["xnT"])
            tr.barrier()
            stop_here("p0_%d" % l, dict(xnT=xnT[:], pp=pp[:], subB=subB[:]))

        yes = contextlib.ExitStack()
        ygT = SBT(yes, "ygT", [128, 8, NT], BF16)

        with contextlib.ExitStack() as es:
            tabCb = SBT(es, "tabCb", [128, 32, 128], BF16); tabSb = SBT(es, "tabSb", [128, 32, 128], BF16)
            tabCl = SBT(es, "tabCl", [128, 2, 32]); tabSl = SBT(es, "tabSl", [128, 2, 32])
            TLa = SBT(es, "TLa", [128, 2, 2, 32]); TLb = SBT(es, "TLb", [128, 2, 2, 32])
            RHO2 = SBT(es, "RHO2", [128, 2, 32])
            SinT = SBT(es, "SinT", [128, 64, 128], BF16); Cst = SBT(es, "Cst", [128, 64, 128], BF16)
            P = SBT(es, "P", [128, 20, 32])
            PI = SBT(es, "PI", [128, 32], I32)
            Hc = SBT(es, "Hc", [128, 3, 2, 32])
            RH = SBT(es, "RH", [128, 2, 4])
            es2 = contextlib.ExitStack()
            tabC = SBT(es2, "tabC", [128, 32, 128]); tabS = SBT(es2, "tabS", [128, 32, 128])
            es3 = contextlib.ExitStack()
            tmpA = SBT(es3, "tmpA", [128, 32, 128]); tmpB = SBT(es3, "tmpB", [128, 32, 128], I32); io = SBT(es3, "io", [128, 128])
            LR, LI, DT, ER, ANG, RHO, AR, AI, S1, C1, WR, WI, DEN, TA, TB, TC = [P[:, i, :] for i in range(16)]
            pk = lambda i: "P%d" % i
            load_GP(LR, pk(0), lamr, l * 4096)
            load_GP(LI, pk(1), lami, l * 4096)
            tr.dma("sp", "c1", dtb[:], dap(ldt, l * 64, [[0, 128], [1, 64]]), (), ["dtb"])
            VCP(P[0:64, 2, :], dtb[0:64, 0:64:2], ["dtb"], [pk(2)])
            VCP(P[64:128, 2, :], dtb[64:128, 1:64:2], ["dtb"], [pk(2)])
            VTS(TC, DT, 1.0 / 8, None, ALU.mult, None, [pk(2)], [pk(15)])
            VTS(DT, TC, 1.0 / 14, 1.0, ALU.mult, ALU.add, [pk(15)], [pk(2)])
            for kk in range(13, 0, -1):
                VTT(DT, DT, TC, ALU.mult, [pk(2), pk(15)], [pk(2)])
                VTS(DT, DT, 1.0 / kk, 1.0, ALU.mult, ALU.add, [pk(2)], [pk(2)])
            for _ in range(3):
                VTT(DT, DT, DT, ALU.mult, [pk(2)], [pk(2)])
            VTT(ER, LR, DT, ALU.mult, [pk(0), pk(2)], [pk(3)])
            VTT(ANG, LI, DT, ALU.mult, [pk(1), pk(2)], [pk(4)])
            VTS(TA, ANG, 1.0 / TWO_PI, None, ALU.mult, None, [pk(4)], [pk(13)])
            VCP(PI[:], TA, [pk(13)], ["PI"])
            VCP(TA, PI[:], ["PI"], [pk(13)])
            STT(ANG, TA, -TWO_PI, ANG, ALU.mult, ALU.add, [pk(13), pk(4)], [pk(4)])
            VTS(ANG, ANG, -3.14159, 3.14159, ALU.max, ALU.min, [pk(4)], [pk(4)])
            VTS(RHO, ER, 1.0 / 8, 1.0, ALU.mult, ALU.add, [pk(3)], [pk(5)])
            for kk in (7, 6, 5, 4, 3, 2, 1):
                VTT(RHO, RHO, ER, ALU.mult, [pk(5), pk(3)], [pk(5)])
                VTS(RHO, RHO, 1.0 / kk, 1.0, ALU.mult, ALU.add, [pk(5)], [pk(5)])
            ACT(S1, ANG, AF.Sin, [pk(4)], [pk(8)])
            ACT(TB, ANG, AF.Sin, [pk(4)], [pk(14)], scale=0.5)
            VTT(TB, TB, TB, ALU.mult, [pk(14)], [pk(14)])
            VTS(C1, TB, -2.0, 1.0, ALU.mult, ALU.add, [pk(14)], [pk(9)])
            VTT(AR, RHO, C1, ALU.mult, [pk(5), pk(9)], [pk(6)])
            VTT(AI, RHO, S1, ALU.mult, [pk(5), pk(8)], [pk(7)])
            VTS(TA, AR, -1.0, None, ALU.add, None, [pk(6)], [pk(13)])
            VTT(DEN, LR, LR, ALU.mult, [pk(0)], [pk(12)])
            VTT(TB, LI, LI, ALU.mult, [pk(1)], [pk(14)])
            VTT(DEN, DEN, TB, ALU.add, [pk(12), pk(14)], [pk(12)])
            tr.op("dve", lambda e: e.reciprocal(out=DEN, in_=DEN), [pk(12)], [pk(12)])
            VTT(WR, TA, LR, ALU.mult, [pk(13), pk(0)], [pk(10)])
            VTT(TB, AI, LI, ALU.mult, [pk(7), pk(1)], [pk(14)])
            VTT(WR, WR, TB, ALU.add, [pk(10), pk(14)], [pk(10)])
            VTT(WR, WR, DEN, ALU.mult, [pk(10), pk(12)], [pk(10)])
            VTT(WI, AI, LR, ALU.mult, [pk(7), pk(0)], [pk(11)])
            VTT(TB, TA, LI, ALU.mult, [pk(13), pk(1)], [pk(14)])
            VTT(WI, WI, TB, ALU.subtract, [pk(11), pk(14)], [pk(11)])
            VTT(WI, WI, DEN, ALU.mult, [pk(11), pk(12)], [pk(11)])
            tr.dma("sp", "c1", io[:], c_io.ap(), (), ["io"])
            angb = ANG.unsqueeze(2).broadcast_to([128, 32, 128]); iob = io[:].unsqueeze(1).broadcast_to([128, 32, 128])
            VTT(tabS[:], angb, iob, ALU.mult, [pk(4), "io"], ["tabS"])

            def rred(tb, key):
                VTS(tmpA[:], tb[:], 1.0 / TWO_PI, None, ALU.mult, None, [key], ["tmpA"])
                VCP(tmpB[:], tmpA[:], ["tmpA"], ["tmpB"])
                VCP(tmpA[:], tmpB[:], ["tmpB"], ["tmpA"])
                STT(tb[:], tmpA[:], -TWO_PI, tb[:], ALU.mult, ALU.add, ["tmpA", key], [key])
                VTS(tb[:], tb[:], -3.14159, 3.14159, ALU.max, ALU.min, [key], [key])

            rred(tabS, "tabS")
            VTS(tabC[:], tabS[:], math.pi / 2, None, ALU.add, None, ["tabS"], ["tabC"])
            rred(tabC, "tabC")
            ACT(tabS[:], tabS[:], AF.Sin, ["tabS"], ["tabS"])
            ACT(tabC[:], tabC[:], AF.Sin, ["tabC"], ["tabC"])
            ACP(tabCb[:], tabC[:], ["tabC"], ["tabCb"])
            ACP(tabSb[:], tabS[:], ["tabS"], ["tabSb"])
            for j_, col_ in enumerate((127, NS - 1)):
                VCP(tabCl[:, j_, :], tabC[:, :, col_], ["tabC"], ["tabCl"])
                VCP(tabSl[:, j_, :], tabS[:, :, col_], ["tabS"], ["tabSl"])
            for j_ in range(2):
                VCP(TLa[:, j_, 0, :], tabCl[:, j_, :], ["tabCl"], ["TLa"])
                VCP(TLa[:, j_, 1, :], tabSl[:, j_, :], ["tabSl"], ["TLa"])
                VCP(TLb[:, j_, 0, :], tabSl[:, j_, :], ["tabSl"], ["TLb"])
                VCP(TLb[:, j_, 1, :], tabCl[:, j_, :], ["tabCl"], ["TLb"])
            for j_ in range(2):
                VCP(RHO2[:, j_, :], RHO, [pk(5)], ["RHO2"])
            tr.barrier()
            es3.close()
            BBw = SBT(es2, "BBw", [128, 64, 128], BF16)
            Bn = SBT(es2, "Bn", [128, 2, 32, 16]); bb = SBT(es2, "bb", [128, 2, 32, 16]); btmp = SBT(es2, "btmp", [128, 32, 16])
            Cn = SBT(es2, "Cn", [128, 2, 8, 64]); Cw = SBT(es2, "Cw", [128, 2, 8, 128], BF16)
            tr.dma("sp", "c1", Bn[:, 0], dap(bre, l * 65536, [[16, 128], [2048, 32], [1, 16]]), (), ["Bn"])
            tr.dma("sp", "c1", Bn[:, 1], dap(bim, l * 65536, [[16, 128], [2048, 32], [1, 16]]), (), ["Bn"])
            wrb = WR.unsqueeze(2).broadcast_to([128, 32, 16]); wib = WI.unsqueeze(2).broadcast_to([128, 32, 16])
            VTT(bb[:, 0], Bn[:, 0], wrb, ALU.mult, ["Bn", pk(10)], ["bb"])
            VTT(btmp[:], Bn[:, 1], wib, ALU.mult, ["Bn", pk(11)], ["btmp"])
            VTT(bb[:, 0], bb[:, 0], btmp[:], ALU.subtract, ["bb", "btmp"], ["bb"])
            VTT(bb[:, 1], Bn[:, 1], wrb, ALU.mult, ["Bn", pk(10)], ["bb"])
            VTT(btmp[:], Bn[:, 0], wib, ALU.mult, ["Bn", pk(11)], ["btmp"])
            VTT(bb[:, 1], bb[:, 1], btmp[:], ALU.add, ["bb", "btmp"], ["bb"])
            MEMSET(BBw[:], 0.0, ["BBw"])
            BBw4 = BBw[:].rearrange("p (q r) c -> p q r c", r=2)
            for gs in range(2):
                for r in range(4):
                    for ri in range(2):
                        VCP(BBw4[64 * gs:64 * gs + 64, r::4, ri, 32 * r + 16 * gs:32 * r + 16 * gs + 16],
                            bb[64 * gs:64 * gs + 64, ri, r::4, :], ["bb", "BBw"], ["BBw"])
            for g4 in range(16):
                for j in range(4):
                    TRANSP(psT[:, 128 * j:128 * j + 128], BBw[:, 4 * g4 + j, :], 128, ["BBw"], ["psT"])
                VCP(SinT[:, 4 * g4:4 * g4 + 4, :], psT[:, 0:512].rearrange("p (a b) -> p a b", b=128), ["psT"], ["SinT"])
            tr.dma("sp", "c1", Cn[:, 0], dap(cre, l * 65536, [[64, 128], [8192, 8], [1, 64]]), (), ["Cn"])
            tr.dma("sp", "c1", Cn[:, 1], dap(cim, l * 65536, [[64, 128], [8192, 8], [1, 64]]), (), ["Cn"])
            VTS(Cn[:, 1], Cn[:, 1], -1.0, None, ALU.mult, None, ["Cn"], ["Cn"])
            for ri in range(2):
                for gs in range(2):
                    VTS(Cw[:, ri, :, 64 * gs:64 * gs + 64], Cn[:, ri], par[:, gs:gs + 1], None, ALU.mult, None, ["Cn", "par"], ["Cw"])
            MEMSET(Cst[:], 0.0, ["Cst"])
            Cst4 = Cst[:].rearrange("p (c r i) m -> p c r i m", r=4, i=2)
            for ri in range(2):
                for c4 in range(2):
                    for j in range(4):
                        TRANSP(psT[:, 128 * j:128 * j + 128], Cw[:, ri, 4 * c4 + j, :], 128, ["Cw"], ["psT"])
                    pv = psT[:, 0:512].rearrange("p (a b) -> p a b", b=128)
                    for r in range(4):
                        VCP(Cst4[:, 4 * c4:4 * c4 + 4, r, ri, 32 * r:32 * r + 32], pv[:, :, 32 * r:32 * r + 32], ["psT", "Cst"], ["Cst"])
            MEMSET(Hc[:, 0], 0.0, ["Hc0"])
            for b in range(2):
                load_GP(Hc[:, 1 + b, 0, :], "Hc%d" % (1 + b), sre, (l * 2 + b) * 4096)
                load_GP(Hc[:, 1 + b, 1, :], "Hc%d" % (1 + b), sim, (l * 2 + b) * 4096)
            tr.barrier()
            stop_here("ssmsetup_%d" % l, dict(P=P[:], tabC=tabC[:], tabS=tabS[:], SinT=SinT[:], Cst=Cst[:], Hc=Hc[:], bb=bb[:]))
            es2.close()
            uTcs = [SBT(es, "uTc%d" % i, [128, NT], BF16) for i in range(2)]
            wus = [SBT(es, "wu%d" % i, [128, 8, 128], BF16) for i in range(2)]
            d0s = [SBT(es, "d0%d" % i, [128, 4, 128]) for i in range(2)]
            W2 = SBT(es, "W2", [128, 2, 4, 128])
            W = [W2[:, i] for i in range(2)]
            Z2 = [SBT(es, "Z2%d" % j, [128, 2, 4, 128]) for j in range(2)]
            Z = [[Z2[j][:, i] for j in range(2)] for i in range(2)]
            RH2 = SBT(es, "RH2", [128, 2, 4]); cs2 = SBT(es, "cs2", [128, 2, 2, 4])
            T1 = SBT(es, "T1", [128, 4, 128], BF16); T2 = SBT(es, "T2", [128, 4, 128], BF16)
            Q1 = SBT(es, "Q1", [128, 4, 128], BF16); Q2 = SBT(es, "Q2", [128, 4, 128], BF16)
            BUb = [SBT(es, "BUb%d" % i, [128, 4, 128], BF16) for i in range(2)]
            Zb = [SBT(es, "Zb%d" % i, [128, 4, 128], BF16) for i in range(2)]
            Hf = [[SBT(es, "Hf%d%d" % (i, j), [128, 4, 128]) for j in range(2)] for i in range(2)]
            Hb = [[SBT(es, "Hb%d%d" % (i, j), [128, 4, 128], BF16) for j in range(2)] for i in range(2)]
            yts = [SBT(es, "yt%d" % i, [128, 128]) for i in range(2)]
            cs = SBT(es, "cs", [128, 4, 4])

            def load_wu(cc):
                tr.dma("pool", "wu", wus[cc % 2][:], dap(w_in, l * D * INC + 4096 + 128 * cc, [[INC, 128], [INC * 128, 8], [1, 128]]), (), ["wu%d" % (cc % 2)])

            load_wu(0)
            ucnt2 = [0]
            for c in range(8):
                wu = wus[c % 2]; wuk = "wu%d" % (c % 2)
                uTc = uTcs[c % 2]; uk = "uTc%d" % (c % 2)
                d0 = d0s[c % 2]; dk = "d0%d" % (c % 2)
                if c + 1 < 8:
                    load_wu(c + 1)
                for bi, (b0, bn) in enumerate(blocks):
                    u_ = ucnt2[0] % 2; ucnt2[0] += 1
                    pb = ps[5 + u_]; pkk = "ps%d" % (5 + u_)
                    tr.mms([(pb[:, 0:bn], wu[:, kc, :], xnT[:, kc, b0:b0 + bn], dict(start=(kc == 0), stop=(kc == 7))) for kc in range(8)],
                           [wuk, "xnT"], [pkk])
                    ACP(uTc[:, b0:b0 + bn], pb[:, 0:bn], [pkk], [uk])
                VCP(d0[:], RHO[:, 4 * c:4 * c + 4].unsqueeze(2).broadcast_to([128, 4, 128]), [pk(5)], [dk], eng="pool")
                MEMSET(d0[:, :, 0:1], 0.0, [dk], eng="pool")

                def emit_bu(ti):
                    kind, i, nt, c0 = tiles[ti]
                    for ri in range(2):
                        tr.mms([(ps[ri][:, 128 * r:128 * r + nt], SinT[:, (4 * c + r) * 2 + ri, :], uTc[:, c0:c0 + nt], dict(start=True, stop=True)) for r in range(4)],
                               ["SinT", uk], ["ps%d" % ri])

                def emit_cast(ti):
                    kind, i, nt, c0 = tiles[ti]
                    for ri in range(2):
                        ACP(BUb[ri][:, :, 0:nt], ps[ri][:].rearrange("p (r i) -> p r i", i=128)[:, :, 0:nt], ["ps%d" % ri], ["BUb%d" % ri])

                def emit_in(ti):
                    kind, i, nt, c0 = tiles[ti]
                    par_ = ti % 2
                    slot = 0 if kind == "p" else 1 + i
                    hk = "Hc%d" % slot
                    tcs = tabCb[:, 4 * c:4 * c + 4, 0:nt]; tss = tabSb[:, 4 * c:4 * c + 4, 0:nt]
                    bur = BUb[0][:, :, 0:nt]; bui = BUb[1][:, :, 0:nt]
                    VTT(T1[:, :, 0:nt], bur, tcs, ALU.mult, ["BUb0", "tabCb"], ["T1"])
                    VTT(T2[:, :, 0:nt], bui, tss, ALU.mult, ["BUb1", "tabSb"], ["T2"])
                    VTT(W[0][:, :, 0:nt], T1[:, :, 0:nt], T2[:, :, 0:nt], ALU.add, ["T1", "T2"], ["W0"])
                    VTT(T1[:, :, 0:nt], bui, tcs, ALU.mult, ["BUb1", "tabCb"], ["T1"])
                    VTT(T2[:, :, 0:nt], bur, tss, ALU.mult, ["BUb0", "tabSb"], ["T2"])
                    VTT(W[1][:, :, 0:nt], T1[:, :, 0:nt], T2[:, :, 0:nt], ALU.subtract, ["T1", "T2"], ["W1"])
                    VTT(RH2[:], Hc[:, slot, :, 4 * c:4 * c + 4], RHO2[:, :, 4 * c:4 * c + 4], ALU.mult, [hk, "RHO2"], ["RH2"])
                    VTT(W2[:, :, :, 0:1], W2[:, :, :, 0:1], RH2[:].unsqueeze(3), ALU.add, ["W0", "W1", "RH2"], ["W0", "W1"])
                    for ri in range(2):
                        Zt = Z[ri][par_]; zk = "Z%d%d" % (ri, par_)
                        if nt == 128:
                            zin = W[ri].rearrange("p r i -> p (r i)"); zout = Zt.rearrange("p r i -> p (r i)")
                            dd = d0[:].rearrange("p r i -> p (r i)")
                            tr.op("dve", lambda e: e.tensor_tensor_scan(out=zout, data0=dd, data1=zin, initial=0.0, op0=ALU.mult, op1=ALU.add),
                                  ["W%d" % ri, dk], [zk])
                        else:
                            for r in range(4):
                                tr.op("dve", lambda e: e.tensor_tensor_scan(out=Zt[:, r, 0:nt], data0=d0[:, r, 0:nt], data1=W[ri][:, r, 0:nt],
                                                                            initial=0.0, op0=ALU.mult, op1=ALU.add),
                                      ["W%d" % ri, dk], [zk])
                    jl_ = 0 if nt == 128 else 1
                    zl = Z2[par_][:, :, :, nt - 1]
                    zk0 = "Z0%d" % par_; zk1 = "Z1%d" % par_
                    VTT(cs2[:, 0], zl, TLa[:, jl_, :, 4 * c:4 * c + 4], ALU.mult, [zk0, zk1, "TLa"], ["cs2a"])
                    VTT(cs2[:, 1], zl, TLb[:, jl_, :, 4 * c:4 * c + 4], ALU.mult, [zk0, zk1, "TLb"], ["cs2b"])
                    VTT(Hc[:, slot, 0, 4 * c:4 * c + 4], cs2[:, 0, 0, :], cs2[:, 0, 1, :], ALU.subtract, ["cs2a", hk], [hk])
                    VTT(Hc[:, slot, 1, 4 * c:4 * c + 4], cs2[:, 1, 0, :], cs2[:, 1, 1, :], ALU.add, ["cs2b", hk], [hk])

                def emit_out(ti):
                    kind, i, nt, c0 = tiles[ti]
                    par_ = ti % 2
                    tcs = tabCb[:, 4 * c:4 * c + 4, 0:nt]; tss = tabSb[:, 4 * c:4 * c + 4, 0:nt]
                    for ri in range(2):
                        ACP(Zb[ri][:, :, 0:nt], Z[ri][par_][:, :, 0:nt], ["Z%d%d" % (ri, par_)], ["Zb%d" % ri])
                    zr = Zb[0][:, :, 0:nt]; zi = Zb[1][:, :, 0:nt]
                    VTT(Q1[:, :, 0:nt], zr, tcs, ALU.mult, ["Zb0", "tabCb"], ["Q1"])
                    VTT(Q2[:, :, 0:nt], zi, tss, ALU.mult, ["Zb1", "tabSb"], ["Q2"])
                    VTT(Hb[0][par_][:, :, 0:nt], Q1[:, :, 0:nt], Q2[:, :, 0:nt], ALU.subtract, ["Q1", "Q2"], ["Hb0%d" % par_])
                    VTT(Q1[:, :, 0:nt], zr, tss, ALU.mult, ["Zb0", "tabSb"], ["Q1"])
                    VTT(Q2[:, :, 0:nt], zi, tcs, ALU.mult, ["Zb1", "tabCb"], ["Q2"])
                    VTT(Hb[1][par_][:, :, 0:nt], Q1[:, :, 0:nt], Q2[:, :, 0:nt], ALU.add, ["Q1", "Q2"], ["Hb1%d" % par_])

                def emit_y(ti):
                    kind, i, nt, c0 = tiles[ti]
                    pb = ps[2 + ti % 2]; pkk = "ps%d" % (2 + ti % 2)
                    yt = yts[ti % 2]; yk = "yt%d" % (ti % 2)
                    par_ = ti % 2
                    lst = []
                    for r in range(4):
                        for ri in range(2):
                            lst.append((pb[:, 0:nt], Cst[:, (4 * c + r) * 2 + ri, :], Hb[ri][par_][:, r, 0:nt], dict(start=(len(lst) == 0), stop=(len(lst) == 7))))
                    tr.mms(lst, ["Cst", "Hb0%d" % par_, "Hb1%d" % par_], [pkk])
                    STT(yt[:, 0:nt], uTc[:, c0:c0 + nt], pp[:, 11 + c:12 + c], pb[:, 0:nt], ALU.mult, ALU.add, [uk, "pp11", pkk], [yk])
                    ACT(ygT[:, c, c0:c0 + nt], yt[:, 0:nt], AF.Gelu, [yk], ["ygT"])

                ntl = len(tiles)
                emit_bu(0)
                emit_cast(0)
                for ti in range(ntl):
                    emit_in(ti)
                    if ti + 1 < ntl:
                        emit_bu(ti + 1)
                        emit_cast(ti + 1)
                    emit_out(ti)
                    if ti >= 1:
                        emit_y(ti - 1)
                emit_y(ntl - 1)
            for slot, (tr_, ti_) in enumerate(((hr_p, hi_p), (hr_s, hi_s), (hr_s, hi_s))):
                for ri, tt in enumerate((tr_, ti_)):
                    off = l * 4096 if slot == 0 else (l * 2 + slot - 1) * 4096
                    store_GP(tt, off, Hc[:, slot, ri, :], "Hc%d" % slot, ("oh", l, slot, ri))
            tr.barrier()
            stop_here("ssm_%d" % l, dict(ygT=ygT[:], Hc=Hc[:]))

        oes = contextlib.ExitStack()
        oT = SBT(oes, "oT", [128, 8, NT], BF16)
        with contextlib.ExitStack() as es:
            wbs = [SBT(es, "wb%d" % i, [128, 8, 512], BF16) for i in range(2)]
            Qt = [SBT(es, "Qt%d" % m, [128, NT], BF16) for m in range(2)]
            Kt = [SBT(es, "Kt%d" % m, [128, NT], BF16) for m in range(2)]
            Kp = [[SBT(es, "Kp%d%d" % (b, m), [128, PAST + NS], BF16) for m in range(2)] for b in range(2)]
            Vx = SBT(es, "Vx", [128, 18, 160], BF16)
            Vp = [SBT(es, "Vp%d" % b, [128, 8, 160], BF16) for b in range(2)]
            Gt = SBT(es, "Gt", [128, 18, 128])
            Pb = [SBT(es, "Pb%d" % i, [128, 512], BF16) for i in range(4)]
            sqs = [SBT(es, "sq%d" % i, [128, 512], BF16) for i in range(2)]
            rss = [SBT(es, "rs%d" % i, [128, 512]) for i in range(2)]
            ksts = [SBT(es, "kst%d" % i, [128, 2, 64]) for i in range(4)]
            vsts = [SBT(es, "vst%d" % i, [128, 128]) for i in range(4)]
            sgs = [SBT(es, "sg%d" % i, [128, 128]) for i in range(2)]
            ckbs = [SBT(es, "ckb%d" % i, [128, 8, 128], BF16) for i in range(2)]
            osbs = [SBT(es, "osb%d" % i, [128, 128]) for i in range(4)]
            ots = [SBT(es, "ot%d" % i, [128, 128]) for i in range(4)]
            onbs = [SBT(es, "onb%d" % i, [128, 128], BF16) for i in range(4)]
            fss = [SBT(es, "fs%d" % i, [128, 8]) for i in range(4)]
            onesbd = SBT(es, "onesbd", [128, 128], BF16)
            MEMSET(onesbd[:], 0.0, ["onesbd"])
            MEMSET(onesbd[0:64, 0:64], 1.0, ["onesbd"])
            MEMSET(onesbd[64:128, 64:128], 1.0, ["onesbd"])
            MEMSET(Vx[:, :, 128:129], 1.0, ["Vx"])
            for b in range(2):
                MEMSET(Vp[b][:, :, 128:129], 1.0, ["Vp%d" % b])
            pcount = [0]
            fcount = [0]
            ocnt = [0]
            Ocp = [[SBT(es, "Ocp%d%d" % (i, j), [128, 387]) for j in range(3)] for i in range(2)]
            wbase = l * D * INC

            def load_wb(hh):
                wbt = wbs[hh % 2]; wk = "wb%d" % (hh % 2)
                for j, colo in enumerate((hh * 64, 512 + hh * 64, 1024 + hh * 64, 1536 + hh * 64)):
                    tr.dma("pool", "wb", wbt[:, :, 64 * j:64 * j + 64], dap(w_in, wbase + colo, [[INC, 128], [INC * 128, 8], [1, 64]]), (), [wk])
                tr.dma("pool", "wb", wbt[:, :, 256:384], dap(w_in, wbase + 2048 + hh * 128, [[INC, 128], [INC * 128, 8], [1, 128]]), (), [wk])
                tr.dma("pool", "wb", wbt[:, :, 384:512], dap(w_in, wbase + 3072 + hh * 128, [[INC, 128], [INC * 128, 8], [1, 128]]), (), [wk])

            load_wb(0)
            for h in range(H):
                wb = wbs[h % 2]; wk = "wb%d" % (h % 2)
                for m in range(2):
                    tr.dma("pool", "aug", Qt[m][64:68, 0:T], dap(c_qa, h * T, [[H * T, 4], [1, T]]), (), ["QtA%d" % m])
                    tr.dma("pool", "aug", Kt[m][64:68, 0:T], dap(c_ka, h * T, [[H * T, 4], [1, T]]), (), ["KtA%d" % m])
                    for b in range(2):
                        tr.dma("pool", "aug", Qt[m][64:68, T + NS * b:T + NS * b + NS], dap(c_qa, h * T + PAST, [[H * T, 4], [1, NS]]), (), ["QtA%d" % m])
                        tr.dma("pool", "aug", Kp[b][m][64:68, :], dap(c_ka, h * T, [[H * T, 4], [1, PAST + NS]]), (), ["KpA%d%d" % (b, m)])
                for b in range(2):
                    for m in range(2):
                        tr.dma("pool", "ck", ckbs[b][:, :, 64 * m:64 * m + 64],
                               dap(ck, ((l * 2 + b) * PAST) * 1024 + m * 512 + h * 64, [[1024, 128], [1024 * 128, 8], [1, 64]]), (), ["ckb%d" % b])
                    tr.dma("pool", "cv", Vp[b][:, :, 0:128], dap(cv, ((l * 2 + b) * PAST) * 1024 + h * 128, [[1024, 128], [1024 * 128, 8], [1, 128]]), (), ["Vp%d" % b])
                if h + 1 < H:
                    load_wb(h + 1)
                def emit_vga(ti):
                    kind, i, n, c0 = tiles[ti]
                    u_ = ti % 2
                    pbv = ps[2 + u_]; pkv = "ps%d" % (2 + u_)
                    pbg = ps[0 + u_]; pkg = "ps%d" % (0 + u_)
                    r4 = ti % 4
                    vst = vsts[r4]; vk_ = "vst%d" % r4; sg = sgs[ti % 2]; gk_ = "sg%d" % (ti % 2)
                    tr.mms([(pbv[0:n, 0:128], xnT[:, kc, c0:c0 + n], wb[:, kc, 256:384], dict(start=(kc == 0), stop=(kc == 7))) for kc in range(8)],
                           [wk, "xnT"], [pkv])
                    tr.mms([(pbg[0:n, 0:128], xnT[:, kc, c0:c0 + n], wb[:, kc, 384:512], dict(start=(kc == 0), stop=(kc == 7))) for kc in range(8)],
                           [wk, "xnT"], [pkg])
                    VCP(vst[0:n, :], pbv[0:n, 0:128], [pkv], [vk_])
                    VCP(Vx[0:n, ti, 0:128], pbv[0:n, 0:128], [pkv], ["Vx"])
                    if kind == "p":
                        dst = dap(nv_p, (l * T + c0) * 1024 + h * 128, [[1024, n], [1, 128]])
                    else:
                        dst = dap(nv_s, ((l * 2 + i) * NS) * 1024 + h * 128, [[1024, n], [1, 128]])
                    tr.dma("sp", "o_v", dst, vst[0:n, :], [vk_], [("ov", l, h, ti)], is_out=True)
                    ACT(sg[0:n, :], pbg[0:n, 0:128], AF.Silu, [pkg], [gk_])
                    VTT(Gt[0:n, ti, :], sg[0:n, :], subB[0:n, :], ALU.mult, [gk_, "subB"], ["Gt"])

                vga_next = [0]
                for pr in range(2):
                    isq = pr == 0
                    dsts = Qt if isq else Kt
                    dks = ["Qt0", "Qt1"] if isq else ["Kt0", "Kt1"]
                    gci = 0 if isq else 1
                    for (b0, bn) in blocks:
                        u_ = pcount[0] % 2; pcount[0] += 1
                        pb = ps[5 + u_]; pkk = "ps%d" % (5 + u_)
                        sq = sqs[u_]; rs = rss[u_]; sk_ = "sq%d" % u_; rk_ = "rs%d" % u_
                        tr.mms([(pb[:, 0:bn], wb[:, kc, 128 * pr:128 * pr + 128], xnT[:, kc, b0:b0 + bn], dict(start=(kc == 0), stop=(kc == 7))) for kc in range(8)],
                               [wk, "xnT"], [pkk])
                        ACT(sq[:, 0:bn], pb[:, 0:bn], AF.Square, [pkk], [sk_])
                        tr.mms([(ps[4][:, 0:bn], onesbd[:], sq[:, 0:bn], dict(start=True, stop=True))], [sk_, "onesbd"], ["ps4"])
                        ACT(rs[:, 0:bn], ps[4][:, 0:bn], AF.Ln, ["ps4", "epsc"], [rk_], scale=1.0 / 64, bias=epsc[:, :])
                        ACT(rs[:, 0:bn], rs[:, 0:bn], AF.Exp, [rk_], [rk_], scale=-0.5)
                        for m in range(2):
                            STT(dsts[m][0:64, b0:b0 + bn], pb[64 * m:64 * m + 64, 0:bn], pp[64 * m:64 * m + 64, gci:gci + 1], rs[64 * m:64 * m + 64, 0:bn],
                                ALU.mult, ALU.mult, [pkk, rk_, "pp0", "pp1"], [dks[m]])
                while vga_next[0] < len(tiles):
                    emit_vga(vga_next[0]); vga_next[0] += 1
                stop_here("attnB", dict(Qt0=Qt[0][:], Kt0=Kt[0][:], Qt1=Qt[1][:], Kt1=Kt[1][:]))
                bg = []

                def mk_newk(ti):
                    def f():
                        kind, i, n, c0 = tiles[ti]
                        r4 = ti % 4; r2 = ti % 2
                        kst = ksts[r4]; kk_ = "kst%d" % r4; tk_ = "psT"
                        for m in range(2):
                            TRANSP(psT[0:n, 256 + 128 * r2 + 64 * m:256 + 128 * r2 + 64 * m + 64], Kt[m][0:64, c0:c0 + n], 64, ["Kt%d" % m], [tk_])
                        VCP(kst[0:n].rearrange("p m d -> p (m d)"), psT[0:n, 256 + 128 * r2:256 + 128 * r2 + 128], [tk_], [kk_])
                        if kind == "p":
                            dst = dap(nk_p, (l * T + c0) * 1024 + h * 64, [[1024, n], [512, 2], [1, 64]])
                        else:
                            dst = dap(nk_s, ((l * 2 + i) * NS) * 1024 + h * 64, [[1024, n], [512, 2], [1, 64]])
                        tr.dma("sp", "o_k", dst, kst[0:n], [kk_], [("ok", l, h, ti)], is_out=True)
                    return f

                def mk_cache(b, m):
                    def f():
                        ckb = ckbs[b]
                        for tt in range(8):
                            TRANSP(psT[0:64, 128 * tt:128 * tt + 128], ckb[:, tt, 64 * m:64 * m + 64], 128, ["ckb%d" % b], ["psT"])
                        VCP(Kp[b][m][0:64, 0:PAST], psT[0:64, :], ["psT"], ["Kp%d%d" % (b, m)])
                        VCP(Kp[b][m][0:64, PAST:PAST + NS], Kt[m][0:64, T + NS * b:T + NS * b + NS], ["Kt%d" % m], ["Kp%d%d" % (b, m)])
                    return f

                for b in range(2):
                    for m in range(2):
                        bg.append(mk_cache(b, m))
                for ti in range(len(tiles)):
                    bg.append(mk_newk(ti))
                stop_here("attnD", dict(Gt=Gt[:]))
                stop_here("attnE", dict(Kp00=Kp[0][0][:]))
                pending = []

                def attend(qcols, qpos0, nq, ktiles, gt_tile):
                    steps = []
                    blocks_ = []
                    for qb0 in range(0, nq, 512):
                        nqb = min(512, nq - qb0)
                        qts = [(qb0 + 128 * i, min(128, nq - qb0 - 128 * i)) for i in range((nqb + 127) // 128)]
                        lastk = {}
                        vis = []
                        for kj, (Kap, Vap, nk, kp0) in enumerate(ktiles):
                            first = None
                            diag = False
                            for qi, (q0, nqi) in enumerate(qts):
                                tpos = qpos0 + q0
                                if kp0 // 64 > (tpos + nqi - 1) // 64:
                                    continue
                                if first is None:
                                    first = qi
                                    diag = (kp0 + nk - 1) > tpos
                                lastk[qi] = kj
                            if first is not None:
                                vis.append((kj, first, diag))
                        blk = dict(qb0=qb0, nqb=nqb, qts=qts, lastk=lastk, started=set(), nsteps=2 * len(vis))
                        blocks_.append(blk)
                        for (kj, first, diag) in vis:
                            for m in range(2):
                                steps.append(dict(blk=blk, kj=kj, first=first, diag=diag, m=m))
                        steps[-1]["last"] = True

                    def emit_S(st):
                        blk = st["blk"]; qts = blk["qts"]; m = st["m"]
                        Kap, Vap, nk, kp0 = ktiles[st["kj"]]
                        c0 = qts[st["first"]][0]; ncol = blk["qb0"] + blk["nqb"] - c0
                        si = pcount[0] % 2; pbi = pcount[0] % 4; pcount[0] += 1
                        S = ps[si]; sk = "ps%d" % si
                        lst = [(S[0:nk, 0:ncol], Kap[m], Qt[m][0:68, qcols + c0:qcols + c0 + ncol], dict(start=True, stop=not st["diag"]))]
                        if st["diag"]:
                            nqd = qts[st["first"]][1]
                            lst.append((S[0:nk, 0:nqd], ident[0:nk, 0:nk], dmat[0:nk, h, 0:nqd], dict(start=False, stop=True)))
                        tr.mms(lst, ["Qt%d" % m, "Kt%d" % m, "Kp0%d" % m, "Kp1%d" % m, "QtA%d" % m, "KtA%d" % m, "KpA0%d" % m, "KpA1%d" % m, "ident", "dmat"], [sk])
                        Pt = Pb[pbi]; pk_ = "Pb%d" % pbi
                        ACT(Pt[0:nk, 0:ncol], S[0:nk, 0:ncol], AF.Exp, [sk], [pk_])
                        st["P"] = (Pt, pk_, c0)

                    def emit_PV(st):
                        blk = st["blk"]; qts = blk["qts"]; m = st["m"]
                        Kap, Vap, nk, kp0 = ktiles[st["kj"]]
                        Pt, pk_, c0 = st["P"]
                        for qi in range(st["first"], len(qts)):
                            q0, nqi = qts[qi]
                            a_ = m * 4 + qi
                            bank = 2 + a_ // 3
                            O = ps[bank][0:nqi, 129 * (a_ % 3):129 * (a_ % 3) + 129]
                            stf = bank not in blk["started"]
                            blk["started"].add(bank)
                            tr.mms([(O, Pt[0:nk, q0 - c0:q0 - c0 + nqi], Vap, dict(start=stf, stop=(blk["lastk"][qi] == st["kj"]), skip_group_check=True))],
                                   [pk_, "Vx", "Vp0", "Vp1"], ["ps%d" % bank])

                    def finalize(blk):
                        qts = blk["qts"]
                        nqt = len(qts)
                        oset = ocnt[0] % 2; ocnt[0] += 1
                        nmax = max(n for (_, n) in qts)
                        used = sorted(set(2 + (m * 4 + qi) // 3 for qi in range(nqt) for m in range(2)))
                        for bank in used:
                            ACP(Ocp[oset][bank - 2][0:nmax, :], ps[bank][0:nmax, 0:387], ["ps%d" % bank], ["ocp%d_%d" % (oset, bank)])
                        Aps = []
                        for qi, (q0, nqi) in enumerate(qts):
                            A = []
                            for m in range(2):
                                a_ = m * 4 + qi
                                bank = 2 + a_ // 3
                                A.append((Ocp[oset][bank - 2][0:nqi, 129 * (a_ % 3):129 * (a_ % 3) + 129], "ocp%d_%d" % (oset, bank)))
                            Aps.append(A)
                        for qi, (q0, n) in enumerate(qts):
                            A = Aps[qi]; fs = fss[qi]; ot = ots[qi]; osb = osbs[qi]
                            fk = "fs%d_" % qi; otk = "ot%d" % qi; osk = "osb%d" % qi
                            tr.op("dve", lambda e: e.reciprocal(out=fs[0:n, 0:1], in_=A[0][0][:, 128:129]), [A[0][1]], [fk + "0"])
                            tr.op("dve", lambda e: e.reciprocal(out=fs[0:n, 1:2], in_=A[1][0][:, 128:129]), [A[1][1]], [fk + "1"])
                            VTT(fs[0:n, 1:2], fs[0:n, 1:2], pp[0:n, 2:3], ALU.mult, [fk + "1", "pp2"], [fk + "1"])
                            VTS(ot[0:n, :], A[1][0][:, 0:128], fs[0:n, 1:2], None, ALU.mult, None, [A[1][1], fk + "1"], [otk])
                            STT(osb[0:n, :], A[0][0][:, 0:128], fs[0:n, 0:1], ot[0:n, :], ALU.mult, ALU.add, [A[0][1], fk + "0", otk], [osk])
                            VTT(ot[0:n, :], osb[0:n, :], osb[0:n, :], ALU.mult, [osk], [otk])
                            RSUM(fs[0:n, 2:3], ot[0:n, :], [otk], [fk + "2"])
                        def stage2():
                            for qi, (q0, n) in enumerate(qts):
                                fs = fss[qi]; fk = "fs%d_" % qi
                                RSQ(fs[0:n, 2:3], fs[0:n, 2:3], 1.0 / 128, n, [fk + "2"], [fk + "2"])
                            for qi, (q0, n) in enumerate(qts):
                                fs = fss[qi]; osb = osbs[qi]; onb = onbs[qi]
                                STT(onb[0:n, :], osb[0:n, :], fs[0:n, 2:3], Gt[0:n, gt_tile(q0), :], ALU.mult, ALU.mult,
                                    ["osb%d" % qi, "fs%d_2" % qi, "Gt"], ["onb%d" % qi])

                        def deferred():
                            for qi, (q0, n) in enumerate(qts):
                                TRANSP(psT[:, 128 * qi:128 * qi + n], onbs[qi][0:n, :], n, ["onb%d" % qi], ["psT"])
                            qa = qts[0][0]; ntot = qts[-1][0] + qts[-1][1] - qa
                            if all(n == 128 for (_, n) in qts):
                                VCP(oT[:, h, qcols + qa:qcols + qa + ntot], psT[:, 0:ntot], ["psT"], ["oT"])
                            else:
                                for qi, (q0, n) in enumerate(qts):
                                    VCP(oT[:, h, qcols + q0:qcols + q0 + n], psT[:, 128 * qi:128 * qi + n], ["psT"], ["oT"])
                        return [[8, stage2], [11, deferred]]

                    emit_S(steps[0])
                    for i, st in enumerate(steps):
                        if i + 1 < len(steps):
                            emit_S(steps[i + 1])
                        emit_PV(st)
                        if bg and i >= 2:
                            bg.pop(0)()
                        for p_ in pending:
                            p_[0] -= 1
                        while pending and pending[0][0] <= 0:
                            pending.pop(0)[1]()
                        if st.get("last"):
                            while pending:
                                pending.pop(0)[1]()
                            pending.extend(finalize(st["blk"]))
                    while bg:
                        bg.pop(0)()

                kts = [([Kt[0][0:68, 128 * j:128 * j + 128], Kt[1][0:68, 128 * j:128 * j + 128]], Vx[:, j, 0:129], 128, 128 * j) for j in range(16)]
                attend(0, 0, T, kts, lambda q0: q0 // 128)
                stop_here("attnF", dict(oT=oT[:]))
                for b in range(2):
                    kts = [([Kp[b][0][0:68, 128 * j:128 * j + 128], Kp[b][1][0:68, 128 * j:128 * j + 128]], Vp[b][:, j, 0:129], 128, 128 * j) for j in range(8)]
                    kts.append(([Kp[b][0][0:68, PAST:PAST + NS], Kp[b][1][0:68, PAST:PAST + NS]], Vx[0:NS, 16 + b, 0:129], NS, PAST))
                    attend(T + NS * b, PAST, NS, kts, lambda q0, b=b: 16 + b)
                while pending:
                    pending.pop(0)[1]()
                stop_here("attnh%d_%d" % (h, l), dict(oT=oT[:], Qt0=Qt[0][:], Kt0=Kt[0][:], Qt1=Qt[1][:], Kt1=Kt[1][:], Vx=Vx[:], Gt=Gt[:], Kp00=Kp[0][0][:], Vp0=Vp[0][:]))
            tr.barrier()
            stop_here("attn_%d" % l, dict(oT=oT[:]))

        with contextlib.ExitStack() as es:
            ysgT = SBT(es, "ysgT", [128, 8, NT], BF16)
            wA = SBT(es, "wA", [128, 8, D], BF16); wB = SBT(es, "wB", [128, 8, D], BF16)
            tmp = [SBT(es, "tmp%d" % i, [128, 512]) for i in range(2)]
            tb16 = SBT(es, "tb16", [128, 512], BF16)
            xr = [SBT(es, "xr%d" % i, [128, D]) for i in range(2)]

            def loadw(dst, key, tnsr, base, rowlen):
                tr.dma("pool", key, dst[:], dap(tnsr, base, [[rowlen, 128], [rowlen * 128, 8], [1, D]]), (), [key])

            def proj(wt, wkey, src, skey, oc, b0, bn, pi):
                pb = ps[pi]
                tr.mms([(pb[:, 0:bn], wt[:, kc, 128 * oc:128 * oc + 128], src[:, kc, b0:b0 + bn], dict(start=(kc == 0), stop=(kc == 7))) for kc in range(8)],
                       [wkey, skey], ["ps%d" % pi])
                return pb[:, 0:bn], "ps%d" % pi

            cnt = [0]
            loadw(wA, "wA", w_glu, l * D * D, D)
            loadw(wB, "wB", w_in, l * D * INC + 5120, INC)
            for oc in range(8):
                for (b0, bn) in blocks:
                    pi = cnt[0] % 2; cnt[0] += 1
                    pa, ka = proj(wA, "wA", ygT, "ygT", oc, b0, bn, pi)
                    ACT(tmp[0][:, 0:bn], pa, AF.Sigmoid, [ka, "pp3"], ["tmp0"], bias=pp[:, 3 + oc:4 + oc])
                    VTT(ysgT[:, oc, b0:b0 + bn], ygT[:, oc, b0:b0 + bn], tmp[0][:, 0:bn], ALU.mult, ["ygT", "tmp0"], [("ysg", oc)])
            for oc in range(8):
                for (b0, bn) in blocks:
                    pi = cnt[0] % 2; cnt[0] += 1
                    pa, ka = proj(wB, "wB", xnT, "xnT", oc, b0, bn, pi)
                    ACT(tb16[:, 0:bn], pa, AF.Silu, [ka], ["tb16"])
                    VTT(ysgT[:, oc, b0:b0 + bn], ysgT[:, oc, b0:b0 + bn], tb16[:, 0:bn], ALU.mult, [("ysg", oc), "tb16"], [("ysg", oc)])
            allysg = [("ysg", oc) for oc in range(8)]
            loadw(wA, "wA", w_sp, l * D * D, D)
            loadw(wB, "wB", w_in, l * D * INC + 7168, INC)
            for oc in range(8):
                for (b0, bn) in blocks:
                    pi = cnt[0] % 2; cnt[0] += 1
                    pg, kg = proj(wB, "wB", xnT, "xnT", oc, b0, bn, 2 + pi)
                    ACT(tmp[0][:, 0:bn], pg, AF.Sigmoid, [kg], ["tmp0"])
                    pb = ps[pi]
                    tr.mms([(pb[:, 0:bn], wA[:, kc, 128 * oc:128 * oc + 128], ysgT[:, kc, b0:b0 + bn], dict(start=(kc == 0), stop=(kc == 7))) for kc in range(8)],
                           ["wA"] + allysg, ["ps%d" % pi])
                    VTT(ygT[:, oc, b0:b0 + bn], pb[:, 0:bn], tmp[0][:, 0:bn], ALU.mult, ["ps%d" % pi, "tmp0"], ["ygT"])
            loadw(wA, "wA", w_ap, l * D * D, D)
            loadw(wB, "wB", w_in, l * D * INC + 6144, INC)
            for oc in range(8):
                for (b0, bn) in blocks:
                    pi = cnt[0] % 2; cnt[0] += 1
                    pg, kg = proj(wB, "wB", xnT, "xnT", oc, b0, bn, 2 + pi)
                    ACT(tmp[0][:, 0:bn], pg, AF.Sigmoid, [kg], ["tmp0"])
                    pa, ka = proj(wA, "wA", oT, "oT", oc, b0, bn, pi)
                    VTT(tmp[1][:, 0:bn], pa, tmp[0][:, 0:bn], ALU.mult, [ka, "tmp0"], ["tmp1"])
                    VTT(ygT[:, oc, b0:b0 + bn], ygT[:, oc, b0:b0 + bn], tmp[1][:, 0:bn], ALU.add, ["ygT", "tmp1"], ["ygT"])
            loadw(wA, "wA", w_o, l * D * D, D)
            for ti, (kind, i, n, c0) in enumerate(tiles):
                xb = xr[ti % 2]; xk = "xr%d" % (ti % 2)
                if l == 0:
                    src = x_p.ap()[128 * i:128 * i + 128, :] if kind == "p" else x_s.ap()[i]
                else:
                    src = xmid.ap()[c0:c0 + n, :]
                tr.dma("sp", xk, xb[0:n, :], src, [("xmid", ti)], [xk])
                for hb in range(2):
                    pi = cnt[0] % 2; cnt[0] += 1
                    pb = ps[pi]
                    tr.mms([(pb[0:n, :], ygT[:, kc, c0:c0 + n], wA[:, kc, 512 * hb:512 * hb + 512], dict(start=(kc == 0), stop=(kc == 7))) for kc in range(8)],
                           ["wA", "ygT"], ["ps%d" % pi])
                    VTT(xb[0:n, 512 * hb:512 * hb + 512], xb[0:n, 512 * hb:512 * hb + 512], pb[0:n, :], ALU.add, [xk, "ps%d" % pi], [xk])
                if l == DEPTH - 1:
                    dst = y_p.ap()[128 * i:128 * i + 128, :] if kind == "p" else y_s.ap()[i]
                    tr.dma("sp", "o_y", dst, xb[0:n, :], [xk], [("oy", ti)], is_out=True)
                else:
                    tr.dma("sp", "o_x", xmid.ap()[c0:c0 + n, :], xb[0:n, :], [xk], [("xmid2", ti)])
            tr.barrier()
            stop_here("merge_%d" % l, dict(mT=ygT[:], ysgT=ysgT[:]))
        oes.close()
        yes.close()

    except StopBuild:
        pass
    tr.finish()
    return nc


_NC = None


def _consts():
    ident = np.eye(128, dtype=np.float32)
    dm = np.zeros((128, H, 128), np.float32)
    s = np.arange(128)[:, None]; t = np.arange(128)[None, :]
    for h in range(H):
        d = np.where(s > t, -2.0 * SLOPES[h] * (s - t), 0.0)
        d = np.where((s // 64) > (t // 64), -30000.0, d)
        dm[:, h, :] = d
    pos = np.arange(T)
    hi = (pos // 128) * 128.0; lo = (pos % 128) * 1.0
    qa = np.zeros((4, H, T), np.float32); ka = np.zeros((4, H, T), np.float32)
    for h in range(H):
        qa[0, h] = -SLOPES[h] * hi; qa[1, h] = -SLOPES[h] * lo; qa[2, h] = 1.0; qa[3, h] = 1.0
        ka[0, h] = 1.0; ka[1, h] = 1.0; ka[2, h] = SLOPES[h] * hi; ka[3, h] = SLOPES[h] * lo
    par = np.zeros((128, 2), np.float32)
    g8 = np.arange(128) // 16
    par[:, 0] = (g8 % 2 == 0); par[:, 1] = (g8 % 2 == 1)
    io = np.tile(np.arange(1, 129, dtype=np.float32)[None, :], (128, 1))
    return dict(c_id=ident, c_dm=dm, c_qa=qa, c_ka=ka, c_par=par, c_io=io)


def kernel(x_prompt, x_sample, cache_k, cache_v, state_ssm_re, state_ssm_im,
           norm_g, w_in, q_norm_g, k_norm_g, lam_q1, lam_k1, lam_q2, lam_k2, subln_g,
           w_attn_proj, ssm_lambda_re, ssm_lambda_im, ssm_log_dt, ssm_b_re, ssm_b_im,
           ssm_c_re, ssm_c_im, ssm_d, w_glu, b_glu, w_ssm_proj, w_out):
    global _NC
    if _NC is None:
        _NC = build()
    nc = _NC
    f = lambda a: np.ascontiguousarray(np.asarray(a, dtype=np.float32))
    shared = dict(norm_g=f(norm_g), w_in=f(w_in), qng=f(q_norm_g), kng=f(k_norm_g), lq1=f(lam_q1), lk1=f(lam_k1),
                  lq2=f(lam_q2), lk2=f(lam_k2), subg=f(subln_g), w_ap=f(w_attn_proj), lamr=f(ssm_lambda_re),
                  lami=f(ssm_lambda_im), ldt=f(ssm_log_dt), bre=f(ssm_b_re), bim=f(ssm_b_im), cre=f(ssm_c_re),
                  cim=f(ssm_c_im), dsk=f(ssm_d), w_glu=f(w_glu), b_glu=f(b_glu), w_sp=f(w_ssm_proj), w_o=f(w_out))
    shared.update(_consts())
    x_prompt = np.asarray(x_prompt); x_sample = np.asarray(x_sample)
    cache_k = np.asarray(cache_k); cache_v = np.asarray(cache_v)
    state_ssm_re = np.asarray(state_ssm_re); state_ssm_im = np.asarray(state_ssm_im)
    in_maps = []
    for c in range(NCORES):
        m = dict(shared)
        m["x_p"] = f(x_prompt[c]); m["x_s"] = f(x_sample[2 * c:2 * c + 2])
        m["ck"] = f(cache_k[:, 2 * c:2 * c + 2]); m["cv"] = f(cache_v[:, 2 * c:2 * c + 2])
        m["sre"] = f(state_ssm_re[:, 2 * c:2 * c + 2]); m["sim"] = f(state_ssm_im[:, 2 * c:2 * c + 2])
        in_maps.append(m)
    res = run_bass_kernel_spmd(nc, in_maps, core_ids=list(range(NCORES)))
    R = res.results
    cat0 = lambda k: np.stack([R[c][k] for c in range(NCORES)], axis=0)
    y_p = cat0("y_p")
    y_s = np.concatenate([R[c]["y_s"] for c in range(NCORES)], axis=0)
    nk_p = np.stack([R[c]["nk_p"] for c in range(NCORES)], axis=1)
    nv_p = np.stack([R[c]["nv_p"] for c in range(NCORES)], axis=1)
    hr_p = np.stack([R[c]["hr_p"] for c in range(NCORES)], axis=1)
    hi_p = np.stack([R[c]["hi_p"] for c in range(NCORES)], axis=1)
    nk_s = np.concatenate([R[c]["nk_s"] for c in range(NCORES)], axis=1)
    nv_s = np.concatenate([R[c]["nv_s"] for c in range(NCORES)], axis=1)
    hr_s = np.concatenate([R[c]["hr_s"] for c in range(NCORES)], axis=1)
    hi_s = np.concatenate([R[c]["hi_s"] for c in range(NCORES)], axis=1)
    return tuple(np.ascontiguousarray(a.astype(np.float32)) for a in (y_p, y_s, nk_p, nv_p, hr_p, hi_p, nk_s, nv_s, hr_s, hi_s))
```

```python
import math
import numpy as np
import concourse.bass as bass
import concourse.mybir as mybir
from concourse.bass_utils import run_bass_kernel_spmd

F32 = mybir.dt.float32
BF16 = mybir.dt.bfloat16
I32 = mybir.dt.int32
ALU = mybir.AluOpType
AF = mybir.ActivationFunctionType
AX = mybir.AxisListType

D = 1024
T = 2048
NS = 16
PAST = 1024
NT = T + 2 * NS
H = 8
DEPTH = 2
EPS = 1e-6
INC = 8192
NCORES = 8
SLOPES = [2.0 ** (-(h + 1)) for h in range(H)]


class Tr:
    def __init__(self, nc):
        self.nc = nc
        self.eng = {}
        for name, h in (("pe", nc.tensor), ("act", nc.scalar), ("dve", nc.vector), ("pool", nc.gpsimd), ("sp", nc.sync)):
            self.eng[name] = dict(h=h, sem=nc.alloc_semaphore(name="s_" + name), cnt=0, seen={})
        self.lastw = {}
        self.readers = {}
        self.dsem = {}
        self.out_tokens = []

    def _deps(self, reads, writes):
        deps = []
        for k in reads:
            if k in self.lastw:
                deps.append(self.lastw[k])
        for k in writes:
            if k in self.lastw:
                deps.append(self.lastw[k])
            deps.extend(self.readers.get(k, ()))
        return deps

    def _wait(self, ename, deps):
        e = self.eng[ename]
        best = {}
        for (sname, sem, val) in deps:
            if ename == "pe" and sname == "s_pe":
                continue
            if val > best.get(sname, (None, 0))[1]:
                best[sname] = (sem, val)
        for sname, (sem, val) in best.items():
            if e["seen"].get(sname, 0) < val:
                e["h"].wait_ge(sem, val)
                e["seen"][sname] = val

    def _commit(self, tok, reads, writes):
        for k in writes:
            self.lastw[k] = tok
            self.readers[k] = []
        for k in reads:
            self.readers.setdefault(k, []).append(tok)

    def op(self, ename, fn, reads=(), writes=()):
        e = self.eng[ename]
        self._wait(ename, self._deps(reads, writes))
        ins = fn(e["h"])
        e["cnt"] += 1
        ins.then_inc(e["sem"], 1)
        self._commit(("s_" + ename, e["sem"], e["cnt"]), reads, writes)

    def mms(self, lst, reads=(), writes=()):
        e = self.eng["pe"]
        self._wait("pe", self._deps(reads, writes))
        ins = None
        for (out, lhsT, rhs, kw) in lst:
            ins = self.nc.tensor.matmul(out, lhsT=lhsT, rhs=rhs, **kw)
        e["cnt"] += 1
        ins.then_inc(e["sem"], 1)
        self._commit(("s_pe", e["sem"], e["cnt"]), reads, writes)

    def dma(self, q, slot, out, in_, reads=(), writes=(), is_out=False):
        if slot.startswith("o_") or slot == "dbg":
            slot = "st_" + "_".join(str(x) for x in (reads[0] if isinstance(reads[0], tuple) else (reads[0],))) if reads else slot
        else:
            slot = "ld_" + "_".join(str(x) for x in (writes[0] if isinstance(writes[0], tuple) else (writes[0],)))
        if slot not in self.dsem:
            self.dsem[slot] = [self.nc.alloc_semaphore(name="d_" + slot), 0]
        ds = self.dsem[slot]
        self._wait(q, self._deps(reads, writes))
        self.eng[q]["h"].dma_start(out=out, in_=in_, allow_slow_non_contiguous=True).then_inc(ds[0], 16)
        ds[1] += 16
        tok = ("d_" + slot, ds[0], ds[1])
        self._commit(tok, reads, writes)
        if is_out:
            self.out_tokens.append(tok)

    def barrier(self):
        toks = [("s_" + n, e["sem"], e["cnt"]) for n, e in self.eng.items() if e["cnt"] > 0]
        toks += [("d_" + k, v[0], v[1]) for k, v in self.dsem.items() if v[1] > 0]
        for n in self.eng:
            e = self.eng[n]
            for (sname, sem, val) in toks:
                if sname == "s_" + n:
                    continue
                if e["seen"].get(sname, 0) < val:
                    e["h"].wait_ge(sem, val)
                    e["seen"][sname] = val
        self.lastw = {}
        self.readers = {}

    def finish(self):
        best = {}
        for (sname, sem, val) in self.out_tokens:
            if val > best.get(sname, (None, 0))[1]:
                best[sname] = (sem, val)
        for sname, (sem, val) in best.items():
            self.nc.sync.wait_ge(sem, val)


def dap(t, off, dims):
    return bass.AP(t, off, [list(d) for d in dims])


class StopBuild(Exception):
    pass


def build(stop=None):
    nc = bass.Bass("TRN2", target_bir_lowering=False)
    dt_in = lambda n, s: nc.dram_tensor(n, s, F32, kind="ExternalInput")
    dt_out = lambda n, s: nc.dram_tensor(n, s, F32, kind="ExternalOutput")
    x_p = dt_in("x_p", [T, D]); x_s = dt_in("x_s", [2, NS, D])
    ck = dt_in("ck", [DEPTH, 2, PAST, 2, H, 64]); cv = dt_in("cv", [DEPTH, 2, PAST, H, 128])
    sre = dt_in("sre", [DEPTH, 2, 64, 64]); sim = dt_in("sim", [DEPTH, 2, 64, 64])
    norm_g = dt_in("norm_g", [DEPTH, D]); w_in = dt_in("w_in", [DEPTH, D, INC])
    qng = dt_in("qng", [DEPTH, 64]); kng = dt_in("kng", [DEPTH, 64])
    lq1 = dt_in("lq1", [DEPTH, 64]); lk1 = dt_in("lk1", [DEPTH, 64]); lq2 = dt_in("lq2", [DEPTH, 64]); lk2 = dt_in("lk2", [DEPTH, 64])
    subg = dt_in("subg", [DEPTH, 128]); w_ap = dt_in("w_ap", [DEPTH, D, D])
    lamr = dt_in("lamr", [DEPTH, 64, 64]); lami = dt_in("lami", [DEPTH, 64, 64]); ldt = dt_in("ldt", [DEPTH, 64])
    bre = dt_in("bre", [DEPTH, 64, 64, 16]); bim = dt_in("bim", [DEPTH, 64, 64, 16])
    cre = dt_in("cre", [DEPTH, 64, 16, 64]); cim = dt_in("cim", [DEPTH, 64, 16, 64])
    dsk = dt_in("dsk", [DEPTH, D]); w_glu = dt_in("w_glu", [DEPTH, D, D]); b_glu = dt_in("b_glu", [DEPTH, D])
    w_sp = dt_in("w_sp", [DEPTH, D, D]); w_o = dt_in("w_o", [DEPTH, D, D])
    c_id = dt_in("c_id", [128, 128]); c_dm = dt_in("c_dm", [128, H, 128])
    c_qa = dt_in("c_qa", [4, H, T]); c_ka = dt_in("c_ka", [4, H, T]); c_par = dt_in("c_par", [128, 2]); c_io = dt_in("c_io", [128, 128])
    y_p = dt_out("y_p", [T, D]); y_s = dt_out("y_s", [2, NS, D])
    nk_p = dt_out("nk_p", [DEPTH, T, 2, H, 64]); nv_p = dt_out("nv_p", [DEPTH, T, H, 128])
    hr_p = dt_out("hr_p", [DEPTH, 64, 64]); hi_p = dt_out("hi_p", [DEPTH, 64, 64])
    nk_s = dt_out("nk_s", [DEPTH, 2, NS, 2, H, 64]); nv_s = dt_out("nv_s", [DEPTH, 2, NS, H, 128])
    hr_s = dt_out("hr_s", [DEPTH, 2, 64, 64]); hi_s = dt_out("hi_s", [DEPTH, 2, 64, 64])
    xmid = nc.dram_tensor("xmid", [NT, D], F32, kind="Internal")

    tr = Tr(nc)
    import contextlib

    ucnt = [0]

    def SBT(es, n, s, d=F32):
        ucnt[0] += 1
        return es.enter_context(nc.sbuf_tensor("%s_%d" % (n, ucnt[0]), s, d))

    def stop_here(name, dumps):
        if stop != name:
            return
        tr.barrier()
        for dn, ap in dumps.items():
            shp = list(ap.shape)
            dtn = nc.dram_tensor("dbg_" + dn, shp, ap.dtype, kind="ExternalOutput")
            tr.dma("sp", "dbg", dtn.ap(), ap, (), [("dbg", dn)], is_out=True)
        raise StopBuild()

    ges = contextlib.ExitStack()
    ps = [ges.enter_context(nc.psum_tensor("ps%d" % i, [128, 512], F32)) for i in range(7)]
    psT = ges.enter_context(nc.psum_tensor("psT", [128, 1024], BF16))
    xnT = SBT(ges, "xnT", [128, 8, NT], BF16)
    ident = SBT(ges, "ident", [128, 128], BF16)
    dmat = SBT(ges, "dmat", [128, H, 128], BF16)
    ones64 = SBT(ges, "ones64", [64, 64], BF16)
    par = SBT(ges, "par", [128, 2])
    epsc = SBT(ges, "epsc", [128, 1])
    sm = SBT(ges, "sm", [128, 16])
    pp = SBT(ges, "pp", [128, 32])
    lv = SBT(ges, "lv", [128, 4, 64])
    subB = SBT(ges, "subB", [128, 128])
    identF = SBT(ges, "identF", [128, 128])
    stg = SBT(ges, "stg", [64, 64]); stg2 = SBT(ges, "stg2", [64, 64]); stg3 = SBT(ges, "stg3", [64, 64])
    prow = SBT(ges, "prow", [32, 128]); dtb = SBT(ges, "dtb", [128, 64])

    def ACT(out, in_, func, reads, writes, **kw):
        tr.op("act", lambda e: e.activation(out=out, in_=in_, func=func, **kw), reads, writes)

    def VTT(out, a, b, op, reads, writes, eng="dve"):
        tr.op(eng, lambda e: e.tensor_tensor(out=out, in0=a, in1=b, op=op), reads, writes)

    def VTS(out, a, s1, s2, op0, op1, reads, writes, eng="dve"):
        if op1 is None:
            tr.op(eng, lambda e: e.tensor_scalar(out=out, in0=a, scalar1=s1, scalar2=None, op0=op0), reads, writes)
        else:
            tr.op(eng, lambda e: e.tensor_scalar(out=out, in0=a, scalar1=s1, scalar2=s2, op0=op0, op1=op1), reads, writes)

    def STT(out, a, s, b, op0, op1, reads, writes, eng="dve"):
        tr.op(eng, lambda e: e.scalar_tensor_tensor(out=out, in0=a, scalar=s, in1=b, op0=op0, op1=op1), reads, writes)

    def VCP(out, in_, reads, writes, eng="dve"):
        tr.op(eng, lambda e: e.tensor_copy(out=out, in_=in_), reads, writes)

    def ACP(out, in_, reads, writes):
        tr.op("act", lambda e: e.copy(out=out, in_=in_), reads, writes)

    def MEMSET(ap, val, writes, eng="dve"):
        tr.op(eng, lambda e: e.memset(ap, val), (), writes)

    def RSUM(out, in_, reads, writes):
        tr.op("dve", lambda e: e.reduce_sum(out=out, in_=in_, axis=AX.X), reads, writes)

    def RSQ(out, in_, scale, n, reads, writes):
        ACT(out, in_, AF.Ln, list(reads) + ["epsc"], writes, scale=scale, bias=epsc[0:n, :])
        ACT(out, out, AF.Exp, writes, writes, scale=-0.5)

    def TRANSP(out, in_, n, reads, writes):
        tr.op("pe", lambda e: e.transpose(out=out, in_=in_, identity=ident[0:n, 0:n]), list(reads) + ["ident"], writes)

    tr.dma("pool", "c0", ident[:], c_id.ap(), (), ["ident"])
    tr.dma("sp", "c1", identF[:], c_id.ap(), (), ["identF"])

    def TRANSPF(out, in_, n, reads, writes):
        tr.op("pe", lambda e: e.transpose(out=out, in_=in_, identity=identF[0:n, 0:n]), list(reads) + ["identF"], writes)

    def load_GP(dst, dkey, tensor, off):
        tr.dma("sp", "stg", stg[:], dap(tensor, off, [[64, 64], [1, 64]]), (), ["stg"])
        TRANSPF(ps[4][0:64, 0:64], stg[:], 64, ["stg"], ["ps4"])
        VCP(dst[0:64, :], ps[4][0:64, 0:64:2], ["ps4"], [dkey])
        VCP(dst[64:128, :], ps[4][0:64, 1:64:2], ["ps4"], [dkey])

    def store_GP(tensor, off, src, skey, okey):
        VCP(stg2[:, 0:64:2], src[0:64, :], [skey], ["stg2"])
        VCP(stg2[:, 1:64:2], src[64:128, :], [skey], ["stg2"])
        TRANSPF(ps[4][0:64, 0:64], stg2[:], 64, ["stg2"], ["ps4"])
        VCP(stg3[:], ps[4][0:64, 0:64], ["ps4"], ["stg3"])
        tr.dma("sp", "o_h", dap(tensor, off, [[64, 64], [1, 64]]), stg3[:], ["stg3"], [okey], is_out=True)
    tr.dma("pool", "c0", dmat[:], c_dm.ap(), (), ["dmat"])
    tr.dma("sp", "c1", par[:], c_par.ap(), (), ["par"])
    MEMSET(ones64[:], 1.0, ["ones64"])
    MEMSET(epsc[:], EPS, ["epsc"])

    tiles = [("p", i, 128, 128 * i) for i in range(16)] + [("s", b, NS, T + NS * b) for b in range(2)]
    blocks = [(512 * i, 512) for i in range(4)] + [(T, 2 * NS)]
    TWO_PI = 2.0 * math.pi

    try:
      for l in range(DEPTH):
        lam_init = 0.8 - 0.6 * math.exp(-0.3 * l)
        with contextlib.ExitStack() as es:
            ngb = SBT(es, "ngb", [128, D]); xt = [SBT(es, "xt%d" % i, [128, D]) for i in range(2)]
            scr = SBT(es, "scr", [128, D]); xnb = SBT(es, "xnb", [128, D], BF16)
            tr.dma("sp", "c1", ngb[:], dap(norm_g, l * D, [[0, 128], [1, D]]), (), ["ngb"])
            MEMSET(prow[:], 0.0, ["prow"])
            tr.dma("sp", "c1", prow[0:1, :].rearrange("p (a b) -> p a b", b=64), dap(qng, l * 64, [[64, 1], [0, 2], [1, 64]]), (), ["prow"])
            tr.dma("sp", "c1", prow[1:2, :].rearrange("p (a b) -> p a b", b=64), dap(kng, l * 64, [[64, 1], [0, 2], [1, 64]]), (), ["prow"])
            tr.dma("sp", "c1", prow[3:11, :], dap(b_glu, l * D, [[128, 8], [1, 128]]), (), ["prow"])
            tr.dma("sp", "c1", prow[11:19, :], dap(dsk, l * D, [[128, 8], [1, 128]]), (), ["prow"])
            TRANSPF(ps[4][:, 0:32], prow[:], 32, ["prow"], ["ps4"])
            VCP(pp[:], ps[4][:, 0:32], ["ps4"], ["pp0", "pp1", "pp2", "pp3", "pp11"])
            VTS(pp[:, 0:1], pp[:, 0:1], 0.125, None, ALU.mult, None, ["pp0"], ["pp0"])
            for j, tnsr in enumerate((lq1, lk1, lq2, lk2)):
                tr.dma("sp", "c1", lv[:, j, :], dap(tnsr, l * 64, [[0, 128], [1, 64]]), (), ["lv%d" % j])
            VTT(lv[:, 0, :], lv[:, 0, :], lv[:, 1, :], ALU.mult, ["lv0", "lv1"], ["lv0"])
            VTT(lv[:, 2, :], lv[:, 2, :], lv[:, 3, :], ALU.mult, ["lv2", "lv3"], ["lv2"])
            RSUM(sm[:, 0:1], lv[:, 0, :], ["lv0"], ["sm0"])
            RSUM(sm[:, 1:2], lv[:, 2, :], ["lv2"], ["sm1"])
            ACT(sm[:, 0:1], sm[:, 0:1], AF.Exp, ["sm0"], ["sm0"])
            ACT(sm[:, 1:2], sm[:, 1:2], AF.Exp, ["sm1"], ["sm1"])
            VTT(sm[:, 0:1], sm[:, 1:2], sm[:, 0:1], ALU.subtract, ["sm0", "sm1"], ["sm0"])
            VTS(pp[:, 2:3], sm[:, 0:1], -lam_init, None, ALU.add, None, ["sm0"], ["pp2"])
            tr.dma("sp", "c1", subB[:], dap(subg, l * 128, [[0, 128], [1, 128]]), (), ["subB"])
            VTS(subB[:], subB[:], 1.0 - lam_init, None, ALU.mult, None, ["subB"], ["subB"])
            for ti, (kind, i, n, c0) in enumerate(tiles):
                xb = xt[ti % 2]; xk = "xt%d" % (ti % 2)
                if l == 0:
                    src = x_p.ap()[128 * i:128 * i + 128, :] if kind == "p" else x_s.ap()[i]
                else:
                    src = xmid.ap()[c0:c0 + n, :]
                tr.dma("sp", xk, xb[0:n, :], src, [("xmid", ti)], [xk])
                VTT(scr[0:n, :], xb[0:n, :], xb[0:n, :], ALU.mult, [xk], ["scr"])
                RSUM(sm[0:n, 2:3], scr[0:n, :], ["scr"], ["sm2"])
                RSQ(sm[0:n, 2:3], sm[0:n, 2:3], 1.0 / D, n, ["sm2"], ["sm2"])
                STT(xnb[0:n, :], xb[0:n, :], sm[0:n, 2:3], ngb[0:n, :], ALU.mult, ALU.mult, [xk, "sm2", "ngb"], ["xnb"])
                for kc in range(8):
                    TRANSP(psT[:, 128 * kc:128 * kc + n], xnb[0:n, 128 * kc:128 * kc + 128], n, ["xnb"], ["psT"])
                VCP(xnT[:, :, c0:c0 + n], psT[:].rearrange("p (k t) -> p k t", t=128)[:, :, 0:n], ["psT"], ["xnT"])
            tr.barrier()
            stop_here("p0_%d" % l, dict(xnT=xnT[:], pp=pp[:], subB=subB[:]))

        yes = contextlib.ExitStack()
        ygT = SBT(yes, "ygT", [128, 8, NT], BF16)

        with contextlib.ExitStack() as es:
            tabCb = SBT(es, "tabCb", [128, 32, 128], BF16); tabSb = SBT(es, "tabSb", [128, 32, 128], BF16)
            tabCl = SBT(es, "tabCl", [128, 2, 32]); tabSl = SBT(es, "tabSl", [128, 2, 32])
            TLa = SBT(es, "TLa", [128, 2, 2, 32]); TLb = SBT(es, "TLb", [128, 2, 2, 32])
            RHO2 = SBT(es, "RHO2", [128, 2, 32])
            Dd = SBT(es, "Dd", [128, 8, 128], BF16)
            for c_ in range(8):
                VTS(Dd[:, c_, :], identF[:], pp[:, 11 + c_:12 + c_], None, ALU.mult, None, ["identF", "pp11"], ["Dd"])
            SinT = SBT(es, "SinT", [128, 64, 128], BF16); Cst = SBT(es, "Cst", [128, 64, 128], BF16)
            P = SBT(es, "P", [128, 20, 32])
            PI = SBT(es, "PI", [128, 32], I32)
            Hc = SBT(es, "Hc", [128, 3, 2, 32])
            RH = SBT(es, "RH", [128, 2, 4])
            es2 = contextlib.ExitStack()
            tabC = SBT(es2, "tabC", [128, 32, 128]); tabS = SBT(es2, "tabS", [128, 32, 128])
            es3 = contextlib.ExitStack()
            tmpA = SBT(es3, "tmpA", [128, 32, 128]); tmpB = SBT(es3, "tmpB", [128, 32, 128], I32); io = SBT(es3, "io", [128, 128])
            LR, LI, DT, ER, ANG, RHO, AR, AI, S1, C1, WR, WI, DEN, TA, TB, TC = [P[:, i, :] for i in range(16)]
            pk = lambda i: "P%d" % i
            load_GP(LR, pk(0), lamr, l * 4096)
            load_GP(LI, pk(1), lami, l * 4096)
            tr.dma("sp", "c1", dtb[:], dap(ldt, l * 64, [[0, 128], [1, 64]]), (), ["dtb"])
            VCP(P[0:64, 2, :], dtb[0:64, 0:64:2], ["dtb"], [pk(2)])
            VCP(P[64:128, 2, :], dtb[64:128, 1:64:2], ["dtb"], [pk(2)])
            VTS(TC, DT, 1.0 / 8, None, ALU.mult, None, [pk(2)], [pk(15)])
            VTS(DT, TC, 1.0 / 14, 1.0, ALU.mult, ALU.add, [pk(15)], [pk(2)])
            for kk in range(13, 0, -1):
                VTT(DT, DT, TC, ALU.mult, [pk(2), pk(15)], [pk(2)])
                VTS(DT, DT, 1.0 / kk, 1.0, ALU.mult, ALU.add, [pk(2)], [pk(2)])
            for _ in range(3):
                VTT(DT, DT, DT, ALU.mult, [pk(2)], [pk(2)])
            VTT(ER, LR, DT, ALU.mult, [pk(0), pk(2)], [pk(3)])
            VTT(ANG, LI, DT, ALU.mult, [pk(1), pk(2)], [pk(4)])
            VTS(TA, ANG, 1.0 / TWO_PI, None, ALU.mult, None, [pk(4)], [pk(13)])
            VCP(PI[:], TA, [pk(13)], ["PI"])
            VCP(TA, PI[:], ["PI"], [pk(13)])
            STT(ANG, TA, -TWO_PI, ANG, ALU.mult, ALU.add, [pk(13), pk(4)], [pk(4)])
            VTS(ANG, ANG, -3.14159, 3.14159, ALU.max, ALU.min, [pk(4)], [pk(4)])
            VTS(RHO, ER, 1.0 / 8, 1.0, ALU.mult, ALU.add, [pk(3)], [pk(5)])
            for kk in (7, 6, 5, 4, 3, 2, 1):
                VTT(RHO, RHO, ER, ALU.mult, [pk(5), pk(3)], [pk(5)])
                VTS(RHO, RHO, 1.0 / kk, 1.0, ALU.mult, ALU.add, [pk(5)], [pk(5)])
            ACT(S1, ANG, AF.Sin, [pk(4)], [pk(8)])
            ACT(TB, ANG, AF.Sin, [pk(4)], [pk(14)], scale=0.5)
            VTT(TB, TB, TB, ALU.mult, [pk(14)], [pk(14)])
            VTS(C1, TB, -2.0, 1.0, ALU.mult, ALU.add, [pk(14)], [pk(9)])
            VTT(AR, RHO, C1, ALU.mult, [pk(5), pk(9)], [pk(6)])
            VTT(AI, RHO, S1, ALU.mult, [pk(5), pk(8)], [pk(7)])
            VTS(TA, AR, -1.0, None, ALU.add, None, [pk(6)], [pk(13)])
            VTT(DEN, LR, LR, ALU.mult, [pk(0)], [pk(12)])
            VTT(TB, LI, LI, ALU.mult, [pk(1)], [pk(14)])
            VTT(DEN, DEN, TB, ALU.add, [pk(12), pk(14)], [pk(12)])
            tr.op("dve", lambda e: e.reciprocal(out=DEN, in_=DEN), [pk(12)], [pk(12)])
            VTT(WR, TA, LR, ALU.mult, [pk(13), pk(0)], [pk(10)])
            VTT(TB, AI, LI, ALU.mult, [pk(7), pk(1)], [pk(14)])
            VTT(WR, WR, TB, ALU.add, [pk(10), pk(14)], [pk(10)])
            VTT(WR, WR, DEN, ALU.mult, [pk(10), pk(12)], [pk(10)])
            VTT(WI, AI, LR, ALU.mult, [pk(7), pk(0)], [pk(11)])
            VTT(TB, TA, LI, ALU.mult, [pk(13), pk(1)], [pk(14)])
            VTT(WI, WI, TB, ALU.subtract, [pk(11), pk(14)], [pk(11)])
            VTT(WI, WI, DEN, ALU.mult, [pk(11), pk(12)], [pk(11)])
            tr.dma("sp", "c1", io[:], c_io.ap(), (), ["io"])
            angb = ANG.unsqueeze(2).broadcast_to([128, 32, 128]); iob = io[:].unsqueeze(1).broadcast_to([128, 32, 128])
            VTT(tabS[:], angb, iob, ALU.mult, [pk(4), "io"], ["tabS"])

            def rred(tb, key):
                VTS(tmpA[:], tb[:], 1.0 / TWO_PI, None, ALU.mult, None, [key], ["tmpA"])
                VCP(tmpB[:], tmpA[:], ["tmpA"], ["tmpB"])
                VCP(tmpA[:], tmpB[:], ["tmpB"], ["tmpA"])
                STT(tb[:], tmpA[:], -TWO_PI, tb[:], ALU.mult, ALU.add, ["tmpA", key], [key])
                VTS(tb[:], tb[:], -3.14159, 3.14159, ALU.max, ALU.min, [key], [key])

            rred(tabS, "tabS")
            VTS(tabC[:], tabS[:], math.pi / 2, None, ALU.add, None, ["tabS"], ["tabC"])
            rred(tabC, "tabC")
            ACT(tabS[:], tabS[:], AF.Sin, ["tabS"], ["tabS"])
            ACT(tabC[:], tabC[:], AF.Sin, ["tabC"], ["tabC"])
            ACP(tabCb[:], tabC[:], ["tabC"], ["tabCb"])
            ACP(tabSb[:], tabS[:], ["tabS"], ["tabSb"])
            for j_, col_ in enumerate((127, NS - 1)):
                VCP(tabCl[:, j_, :], tabC[:, :, col_], ["tabC"], ["tabCl"])
                VCP(tabSl[:, j_, :], tabS[:, :, col_], ["tabS"], ["tabSl"])
            for j_ in range(2):
                VCP(TLa[:, j_, 0, :], tabCl[:, j_, :], ["tabCl"], ["TLa"])
                VCP(TLa[:, j_, 1, :], tabSl[:, j_, :], ["tabSl"], ["TLa"])
                VCP(TLb[:, j_, 0, :], tabSl[:, j_, :], ["tabSl"], ["TLb"])
                VCP(TLb[:, j_, 1, :], tabCl[:, j_, :], ["tabCl"], ["TLb"])
            for j_ in range(2):
                VCP(RHO2[:, j_, :], RHO, [pk(5)], ["RHO2"])
            tr.barrier()
            es3.close()
            BBw = SBT(es2, "BBw", [128, 64, 128], BF16)
            Bn = SBT(es2, "Bn", [128, 2, 32, 16]); bb = SBT(es2, "bb", [128, 2, 32, 16]); btmp = SBT(es2, "btmp", [128, 32, 16])
            Cn = SBT(es2, "Cn", [128, 2, 8, 64]); Cw = SBT(es2, "Cw", [128, 2, 8, 128], BF16)
            tr.dma("sp", "c1", Bn[:, 0], dap(bre, l * 65536, [[16, 128], [2048, 32], [1, 16]]), (), ["Bn"])
            tr.dma("sp", "c1", Bn[:, 1], dap(bim, l * 65536, [[16, 128], [2048, 32], [1, 16]]), (), ["Bn"])
            wrb = WR.unsqueeze(2).broadcast_to([128, 32, 16]); wib = WI.unsqueeze(2).broadcast_to([128, 32, 16])
            VTT(bb[:, 0], Bn[:, 0], wrb, ALU.mult, ["Bn", pk(10)], ["bb"])
            VTT(btmp[:], Bn[:, 1], wib, ALU.mult, ["Bn", pk(11)], ["btmp"])
            VTT(bb[:, 0], bb[:, 0], btmp[:], ALU.subtract, ["bb", "btmp"], ["bb"])
            VTT(bb[:, 1], Bn[:, 1], wrb, ALU.mult, ["Bn", pk(10)], ["bb"])
            VTT(btmp[:], Bn[:, 0], wib, ALU.mult, ["Bn", pk(11)], ["btmp"])
            VTT(bb[:, 1], bb[:, 1], btmp[:], ALU.add, ["bb", "btmp"], ["bb"])
            MEMSET(BBw[:], 0.0, ["BBw"])
            BBw4 = BBw[:].rearrange("p (q r) c -> p q r c", r=2)
            for gs in range(2):
                for r in range(4):
                    for ri in range(2):
                        VCP(BBw4[64 * gs:64 * gs + 64, r::4, ri, 32 * r + 16 * gs:32 * r + 16 * gs + 16],
                            bb[64 * gs:64 * gs + 64, ri, r::4, :], ["bb", "BBw"], ["BBw"])
            for g4 in range(16):
                for j in range(4):
                    TRANSP(psT[:, 128 * j:128 * j + 128], BBw[:, 4 * g4 + j, :], 128, ["BBw"], ["psT"])
                VCP(SinT[:, 4 * g4:4 * g4 + 4, :], psT[:, 0:512].rearrange("p (a b) -> p a b", b=128), ["psT"], ["SinT"])
            tr.dma("sp", "c1", Cn[:, 0], dap(cre, l * 65536, [[64, 128], [8192, 8], [1, 64]]), (), ["Cn"])
            tr.dma("sp", "c1", Cn[:, 1], dap(cim, l * 65536, [[64, 128], [8192, 8], [1, 64]]), (), ["Cn"])
            VTS(Cn[:, 1], Cn[:, 1], -1.0, None, ALU.mult, None, ["Cn"], ["Cn"])
            for ri in range(2):
                for gs in range(2):
                    VTS(Cw[:, ri, :, 64 * gs:64 * gs + 64], Cn[:, ri], par[:, gs:gs + 1], None, ALU.mult, None, ["Cn", "par"], ["Cw"])
            MEMSET(Cst[:], 0.0, ["Cst"])
            Cst4 = Cst[:].rearrange("p (c r i) m -> p c r i m", r=4, i=2)
            for ri in range(2):
                for c4 in range(2):
                    for j in range(4):
                        TRANSP(psT[:, 128 * j:128 * j + 128], Cw[:, ri, 4 * c4 + j, :], 128, ["Cw"], ["psT"])
                    pv = psT[:, 0:512].rearrange("p (a b) -> p a b", b=128)
                    for r in range(4):
                        VCP(Cst4[:, 4 * c4:4 * c4 + 4, r, ri, 32 * r:32 * r + 32], pv[:, :, 32 * r:32 * r + 32], ["psT", "Cst"], ["Cst"])
            MEMSET(Hc[:, 0], 0.0, ["Hc0"])
            for b in range(2):
                load_GP(Hc[:, 1 + b, 0, :], "Hc%d" % (1 + b), sre, (l * 2 + b) * 4096)
                load_GP(Hc[:, 1 + b, 1, :], "Hc%d" % (1 + b), sim, (l * 2 + b) * 4096)
            tr.barrier()
            stop_here("ssmsetup_%d" % l, dict(P=P[:], tabC=tabC[:], tabS=tabS[:], SinT=SinT[:], Cst=Cst[:], Hc=Hc[:], bb=bb[:]))
            es2.close()
            uTcs = [SBT(es, "uTc%d" % i, [128, NT], BF16) for i in range(2)]
            wus = [SBT(es, "wu%d" % i, [128, 8, 128], BF16) for i in range(2)]
            d0s = [SBT(es, "d0%d" % i, [128, 4, 128]) for i in range(2)]
            W2 = SBT(es, "W2", [128, 2, 4, 128])
            W = [W2[:, i] for i in range(2)]
            Z2 = [SBT(es, "Z2%d" % j, [128, 2, 4, 128]) for j in range(2)]
            Z = [[Z2[j][:, i] for j in range(2)] for i in range(2)]
            RH2 = SBT(es, "RH2", [128, 2, 4]); cs2 = SBT(es, "cs2", [128, 2, 2, 4])
            T1 = SBT(es, "T1", [128, 4, 128], BF16); T2 = SBT(es, "T2", [128, 4, 128], BF16)
            Q1 = SBT(es, "Q1", [128, 4, 128], BF16); Q2 = SBT(es, "Q2", [128, 4, 128], BF16)
            BUb = [SBT(es, "BUb%d" % i, [128, 4, 128], BF16) for i in range(2)]
            Zb = [SBT(es, "Zb%d" % i, [128, 4, 128], BF16) for i in range(2)]
            Hf = [[SBT(es, "Hf%d%d" % (i, j), [128, 4, 128]) for j in range(2)] for i in range(2)]
            Hb = [[SBT(es, "Hb%d%d" % (i, j), [128, 4, 128], BF16) for j in range(2)] for i in range(2)]
            yts = [SBT(es, "yt%d" % i, [128, 128]) for i in range(2)]
            cs = SBT(es, "cs", [128, 4, 4])

            def load_wu(cc):
                tr.dma("pool", "wu", wus[cc % 2][:], dap(w_in, l * D * INC + 4096 + 128 * cc, [[INC, 128], [INC * 128, 8], [1, 128]]), (), ["wu%d" % (cc % 2)])

            load_wu(0)
            ucnt2 = [0]
            for c in range(8):
                wu = wus[c % 2]; wuk = "wu%d" % (c % 2)
                uTc = uTcs[c % 2]; uk = "uTc%d" % (c % 2)
                d0 = d0s[c % 2]; dk = "d0%d" % (c % 2)
                if c + 1 < 8:
                    load_wu(c + 1)
                for bi, (b0, bn) in enumerate(blocks):
                    u_ = ucnt2[0] % 2; ucnt2[0] += 1
                    pb = ps[5 + u_]; pkk = "ps%d" % (5 + u_)
                    tr.mms([(pb[:, 0:bn], wu[:, kc, :], xnT[:, kc, b0:b0 + bn], dict(start=(kc == 0), stop=(kc == 7))) for kc in range(8)],
                           [wuk, "xnT"], [pkk])
                    ACP(uTc[:, b0:b0 + bn], pb[:, 0:bn], [pkk], [uk])
                VCP(d0[:], RHO[:, 4 * c:4 * c + 4].unsqueeze(2).broadcast_to([128, 4, 128]), [pk(5)], [dk], eng="pool")
                MEMSET(d0[:, :, 0:1], 0.0, [dk], eng="pool")

                def emit_bu(ti):
                    kind, i, nt, c0 = tiles[ti]
                    for ri in range(2):
                        tr.mms([(ps[ri][:, 128 * r:128 * r + nt], SinT[:, (4 * c + r) * 2 + ri, :], uTc[:, c0:c0 + nt], dict(start=True, stop=True)) for r in range(4)],
                               ["SinT", uk], ["ps%d" % ri])

                def emit_cast(ti):
                    kind, i, nt, c0 = tiles[ti]
                    for ri in range(2):
                        ACP(BUb[ri][:, :, 0:nt], ps[ri][:].rearrange("p (r i) -> p r i", i=128)[:, :, 0:nt], ["ps%d" % ri], ["BUb%d" % ri])

                def emit_in(ti):
                    kind, i, nt, c0 = tiles[ti]
                    par_ = ti % 2
                    slot = 0 if kind == "p" else 1 + i
                    hk = "Hc%d" % slot
                    tcs = tabCb[:, 4 * c:4 * c + 4, 0:nt]; tss = tabSb[:, 4 * c:4 * c + 4, 0:nt]
                    bur = BUb[0][:, :, 0:nt]; bui = BUb[1][:, :, 0:nt]
                    VTT(T1[:, :, 0:nt], bur, tcs, ALU.mult, ["BUb0", "tabCb"], ["T1"])
                    VTT(T2[:, :, 0:nt], bui, tss, ALU.mult, ["BUb1", "tabSb"], ["T2"])
                    VTT(W[0][:, :, 0:nt], T1[:, :, 0:nt], T2[:, :, 0:nt], ALU.add, ["T1", "T2"], ["W0"])
                    VTT(T1[:, :, 0:nt], bui, tcs, ALU.mult, ["BUb1", "tabCb"], ["T1"])
                    VTT(T2[:, :, 0:nt], bur, tss, ALU.mult, ["BUb0", "tabSb"], ["T2"])
                    VTT(W[1][:, :, 0:nt], T1[:, :, 0:nt], T2[:, :, 0:nt], ALU.subtract, ["T1", "T2"], ["W1"])
                    VTT(RH2[:], Hc[:, slot, :, 4 * c:4 * c + 4], RHO2[:, :, 4 * c:4 * c + 4], ALU.mult, [hk, "RHO2"], ["RH2"])
                    VTT(W2[:, :, :, 0:1], W2[:, :, :, 0:1], RH2[:].unsqueeze(3), ALU.add, ["W0", "W1", "RH2"], ["W0", "W1"])
                    for ri in range(2):
                        Zt = Z[ri][par_]; zk = "Z%d%d" % (ri, par_)
                        if nt == 128:
                            zin = W[ri].rearrange("p r i -> p (r i)"); zout = Zt.rearrange("p r i -> p (r i)")
                            dd = d0[:].rearrange("p r i -> p (r i)")
                            tr.op("dve", lambda e: e.tensor_tensor_scan(out=zout, data0=dd, data1=zin, initial=0.0, op0=ALU.mult, op1=ALU.add),
                                  ["W%d" % ri, dk], [zk])
                        else:
                            for r in range(4):
                                tr.op("dve", lambda e: e.tensor_tensor_scan(out=Zt[:, r, 0:nt], data0=d0[:, r, 0:nt], data1=W[ri][:, r, 0:nt],
                                                                            initial=0.0, op0=ALU.mult, op1=ALU.add),
                                      ["W%d" % ri, dk], [zk])
                    jl_ = 0 if nt == 128 else 1
                    zl = Z2[par_][:, :, :, nt - 1]
                    zk0 = "Z0%d" % par_; zk1 = "Z1%d" % par_
                    VTT(cs2[:, 0], zl, TLa[:, jl_, :, 4 * c:4 * c + 4], ALU.mult, [zk0, zk1, "TLa"], ["cs2a"])
                    VTT(cs2[:, 1], zl, TLb[:, jl_, :, 4 * c:4 * c + 4], ALU.mult, [zk0, zk1, "TLb"], ["cs2b"])
                    VTT(Hc[:, slot, 0, 4 * c:4 * c + 4], cs2[:, 0, 0, :], cs2[:, 0, 1, :], ALU.subtract, ["cs2a", hk], [hk])
                    VTT(Hc[:, slot, 1, 4 * c:4 * c + 4], cs2[:, 1, 0, :], cs2[:, 1, 1, :], ALU.add, ["cs2b", hk], [hk])

                def emit_out(ti):
                    kind, i, nt, c0 = tiles[ti]
                    par_ = ti % 2
                    tcs = tabCb[:, 4 * c:4 * c + 4, 0:nt]; tss = tabSb[:, 4 * c:4 * c + 4, 0:nt]
                    for ri in range(2):
                        ACP(Zb[ri][:, :, 0:nt], Z[ri][par_][:, :, 0:nt], ["Z%d%d" % (ri, par_)], ["Zb%d" % ri])
                    zr = Zb[0][:, :, 0:nt]; zi = Zb[1][:, :, 0:nt]
                    VTT(Q1[:, :, 0:nt], zr, tcs, ALU.mult, ["Zb0", "tabCb"], ["Q1"])
                    VTT(Q2[:, :, 0:nt], zi, tss, ALU.mult, ["Zb1", "tabSb"], ["Q2"])
                    VTT(Hb[0][par_][:, :, 0:nt], Q1[:, :, 0:nt], Q2[:, :, 0:nt], ALU.subtract, ["Q1", "Q2"], ["Hb0%d" % par_])
                    VTT(Q1[:, :, 0:nt], zr, tss, ALU.mult, ["Zb0", "tabSb"], ["Q1"])
                    VTT(Q2[:, :, 0:nt], zi, tcs, ALU.mult, ["Zb1", "tabCb"], ["Q2"])
                    VTT(Hb[1][par_][:, :, 0:nt], Q1[:, :, 0:nt], Q2[:, :, 0:nt], ALU.add, ["Q1", "Q2"], ["Hb1%d" % par_])

                def emit_y(ti):
                    kind, i, nt, c0 = tiles[ti]
                    pb = ps[2 + ti % 2]; pkk = "ps%d" % (2 + ti % 2)
                    yt = yts[ti % 2]; yk = "yt%d" % (ti % 2)
                    par_ = ti % 2
                    lst = []
                    for r in range(4):
                        for ri in range(2):
                            lst.append((pb[:, 0:nt], Cst[:, (4 * c + r) * 2 + ri, :], Hb[ri][par_][:, r, 0:nt], dict(start=(len(lst) == 0), stop=(len(lst) == 7))))
                    lst[-1][3]["stop"] = False
                    lst.append((pb[:, 0:nt], Dd[:, c, :], uTc[:, c0:c0 + nt], dict(start=False, stop=True)))
                    tr.mms(lst, ["Cst", "Hb0%d" % par_, "Hb1%d" % par_, "Dd", uk], [pkk])
                    ACT(ygT[:, c, c0:c0 + nt], pb[:, 0:nt], AF.Gelu, [pkk], ["ygT"])

                ntl = len(tiles)
                emit_bu(0)
                emit_cast(0)
                for ti in range(ntl):
                    emit_in(ti)
                    if ti + 1 < ntl:
                        emit_bu(ti + 1)
                        emit_cast(ti + 1)
                    emit_out(ti)
                    if ti >= 1:
                        emit_y(ti - 1)
                emit_y(ntl - 1)
            for slot, (tr_, ti_) in enumerate(((hr_p, hi_p), (hr_s, hi_s), (hr_s, hi_s))):
                for ri, tt in enumerate((tr_, ti_)):
                    off = l * 4096 if slot == 0 else (l * 2 + slot - 1) * 4096
                    store_GP(tt, off, Hc[:, slot, ri, :], "Hc%d" % slot, ("oh", l, slot, ri))
            tr.barrier()
            stop_here("ssm_%d" % l, dict(ygT=ygT[:], Hc=Hc[:]))

        oes = contextlib.ExitStack()
        oT = SBT(oes, "oT", [128, 8, NT], BF16)
        with contextlib.ExitStack() as es:
            wbs = [SBT(es, "wb%d" % i, [128, 8, 512], BF16) for i in range(2)]
            Qt = [SBT(es, "Qt%d" % m, [128, NT], BF16) for m in range(2)]
            Kt = [SBT(es, "Kt%d" % m, [128, NT], BF16) for m in range(2)]
            Kp = [[SBT(es, "Kp%d%d" % (b, m), [128, PAST + NS], BF16) for m in range(2)] for b in range(2)]
            Vx = SBT(es, "Vx", [128, 18, 160], BF16)
            Vp = [SBT(es, "Vp%d" % b, [128, 8, 160], BF16) for b in range(2)]
            Gt = SBT(es, "Gt", [128, 18, 128])
            Pb = [SBT(es, "Pb%d" % i, [128, 512], BF16) for i in range(4)]
            sqs = [SBT(es, "sq%d" % i, [128, 512], BF16) for i in range(2)]
            rss = [SBT(es, "rs%d" % i, [128, 512]) for i in range(2)]
            ksts = [SBT(es, "kst%d" % i, [128, 2, 64]) for i in range(4)]
            vsts = [SBT(es, "vst%d" % i, [128, 128]) for i in range(4)]
            sgs = [SBT(es, "sg%d" % i, [128, 128]) for i in range(2)]
            ckbs = [SBT(es, "ckb%d" % i, [128, 8, 128], BF16) for i in range(2)]
            osbs = [SBT(es, "osb%d" % i, [128, 128]) for i in range(4)]
            ots = [SBT(es, "ot%d" % i, [128, 128]) for i in range(4)]
            onbs = [SBT(es, "onb%d" % i, [128, 128], BF16) for i in range(4)]
            fss = [SBT(es, "fs%d" % i, [128, 8]) for i in range(4)]
            onesbd = SBT(es, "onesbd", [128, 128], BF16)
            MEMSET(onesbd[:], 0.0, ["onesbd"])
            MEMSET(onesbd[0:64, 0:64], 1.0, ["onesbd"])
            MEMSET(onesbd[64:128, 64:128], 1.0, ["onesbd"])
            MEMSET(Vx[:, :, 128:129], 1.0, ["Vx"])
            for b in range(2):
                MEMSET(Vp[b][:, :, 128:129], 1.0, ["Vp%d" % b])
            pcount = [0]
            fcount = [0]
            ocnt = [0]
            Ocp = [[SBT(es, "Ocp%d%d" % (i, j), [128, 387]) for j in range(3)] for i in range(2)]
            wbase = l * D * INC

            def load_wb(hh):
                wbt = wbs[hh % 2]; wk = "wb%d" % (hh % 2)
                for j, colo in enumerate((hh * 64, 512 + hh * 64, 1024 + hh * 64, 1536 + hh * 64)):
                    tr.dma("pool", "wb", wbt[:, :, 64 * j:64 * j + 64], dap(w_in, wbase + colo, [[INC, 128], [INC * 128, 8], [1, 64]]), (), [wk])
                tr.dma("pool", "wb", wbt[:, :, 256:384], dap(w_in, wbase + 2048 + hh * 128, [[INC, 128], [INC * 128, 8], [1, 128]]), (), [wk])
                tr.dma("pool", "wb", wbt[:, :, 384:512], dap(w_in, wbase + 3072 + hh * 128, [[INC, 128], [INC * 128, 8], [1, 128]]), (), [wk])

            load_wb(0)
            for h in range(H):
                wb = wbs[h % 2]; wk = "wb%d" % (h % 2)
                for m in range(2):
                    tr.dma("pool", "aug", Qt[m][64:68, 0:T], dap(c_qa, h * T, [[H * T, 4], [1, T]]), (), ["QtA%d" % m])
                    tr.dma("pool", "aug", Kt[m][64:68, 0:T], dap(c_ka, h * T, [[H * T, 4], [1, T]]), (), ["KtA%d" % m])
                    for b in range(2):
                        tr.dma("pool", "aug", Qt[m][64:68, T + NS * b:T + NS * b + NS], dap(c_qa, h * T + PAST, [[H * T, 4], [1, NS]]), (), ["QtA%d" % m])
                        tr.dma("pool", "aug", Kp[b][m][64:68, :], dap(c_ka, h * T, [[H * T, 4], [1, PAST + NS]]), (), ["KpA%d%d" % (b, m)])
                for b in range(2):
                    for m in range(2):
                        tr.dma("pool", "ck", ckbs[b][:, :, 64 * m:64 * m + 64],
                               dap(ck, ((l * 2 + b) * PAST) * 1024 + m * 512 + h * 64, [[1024, 128], [1024 * 128, 8], [1, 64]]), (), ["ckb%d" % b])
                    tr.dma("pool", "cv", Vp[b][:, :, 0:128], dap(cv, ((l * 2 + b) * PAST) * 1024 + h * 128, [[1024, 128], [1024 * 128, 8], [1, 128]]), (), ["Vp%d" % b])
                if h + 1 < H:
                    load_wb(h + 1)
                def emit_vga(ti):
                    kind, i, n, c0 = tiles[ti]
                    u_ = ti % 2
                    pbv = ps[2 + u_]; pkv = "ps%d" % (2 + u_)
                    pbg = ps[0 + u_]; pkg = "ps%d" % (0 + u_)
                    r4 = ti % 4
                    vst = vsts[r4]; vk_ = "vst%d" % r4; sg = sgs[ti % 2]; gk_ = "sg%d" % (ti % 2)
                    tr.mms([(pbv[0:n, 0:128], xnT[:, kc, c0:c0 + n], wb[:, kc, 256:384], dict(start=(kc == 0), stop=(kc == 7))) for kc in range(8)],
                           [wk, "xnT"], [pkv])
                    tr.mms([(pbg[0:n, 0:128], xnT[:, kc, c0:c0 + n], wb[:, kc, 384:512], dict(start=(kc == 0), stop=(kc == 7))) for kc in range(8)],
                           [wk, "xnT"], [pkg])
                    VCP(vst[0:n, :], pbv[0:n, 0:128], [pkv], [vk_])
                    VCP(Vx[0:n, ti, 0:128], pbv[0:n, 0:128], [pkv], ["Vx"])
                    if kind == "p":
                        dst = dap(nv_p, (l * T + c0) * 1024 + h * 128, [[1024, n], [1, 128]])
                    else:
                        dst = dap(nv_s, ((l * 2 + i) * NS) * 1024 + h * 128, [[1024, n], [1, 128]])
                    tr.dma("sp", "o_v", dst, vst[0:n, :], [vk_], [("ov", l, h, ti)], is_out=True)
                    ACT(sg[0:n, :], pbg[0:n, 0:128], AF.Silu, [pkg], [gk_])
                    VTT(Gt[0:n, ti, :], sg[0:n, :], subB[0:n, :], ALU.mult, [gk_, "subB"], ["Gt"])

                vga_next = [0]
                for pr in range(2):
                    isq = pr == 0
                    dsts = Qt if isq else Kt
                    dks = ["Qt0", "Qt1"] if isq else ["Kt0", "Kt1"]
                    gci = 0 if isq else 1
                    for (b0, bn) in blocks:
                        u_ = pcount[0] % 2; pcount[0] += 1
                        pb = ps[5 + u_]; pkk = "ps%d" % (5 + u_)
                        sq = sqs[u_]; rs = rss[u_]; sk_ = "sq%d" % u_; rk_ = "rs%d" % u_
                        tr.mms([(pb[:, 0:bn], wb[:, kc, 128 * pr:128 * pr + 128], xnT[:, kc, b0:b0 + bn], dict(start=(kc == 0), stop=(kc == 7))) for kc in range(8)],
                               [wk, "xnT"], [pkk])
                        ACT(sq[:, 0:bn], pb[:, 0:bn], AF.Square, [pkk], [sk_])
                        tr.mms([(ps[4][:, 0:bn], onesbd[:], sq[:, 0:bn], dict(start=True, stop=True))], [sk_, "onesbd"], ["ps4"])
                        ACT(rs[:, 0:bn], ps[4][:, 0:bn], AF.Ln, ["ps4", "epsc"], [rk_], scale=1.0 / 64, bias=epsc[:, :])
                        ACT(rs[:, 0:bn], rs[:, 0:bn], AF.Exp, [rk_], [rk_], scale=-0.5)
                        for m in range(2):
                            STT(dsts[m][0:64, b0:b0 + bn], pb[64 * m:64 * m + 64, 0:bn], pp[64 * m:64 * m + 64, gci:gci + 1], rs[64 * m:64 * m + 64, 0:bn],
                                ALU.mult, ALU.mult, [pkk, rk_, "pp0", "pp1"], [dks[m]])
                while vga_next[0] < len(tiles):
                    emit_vga(vga_next[0]); vga_next[0] += 1
                stop_here("attnB", dict(Qt0=Qt[0][:], Kt0=Kt[0][:], Qt1=Qt[1][:], Kt1=Kt[1][:]))
                bg = []

                def mk_newk(ti):
                    def f():
                        kind, i, n, c0 = tiles[ti]
                        r4 = ti % 4; r2 = ti % 2
                        kst = ksts[r4]; kk_ = "kst%d" % r4; tk_ = "psT"
                        for m in range(2):
                            TRANSP(psT[0:n, 256 + 128 * r2 + 64 * m:256 + 128 * r2 + 64 * m + 64], Kt[m][0:64, c0:c0 + n], 64, ["Kt%d" % m], [tk_])
                        VCP(kst[0:n].rearrange("p m d -> p (m d)"), psT[0:n, 256 + 128 * r2:256 + 128 * r2 + 128], [tk_], [kk_])
                        if kind == "p":
                            dst = dap(nk_p, (l * T + c0) * 1024 + h * 64, [[1024, n], [512, 2], [1, 64]])
                        else:
                            dst = dap(nk_s, ((l * 2 + i) * NS) * 1024 + h * 64, [[1024, n], [512, 2], [1, 64]])
                        tr.dma("sp", "o_k", dst, kst[0:n], [kk_], [("ok", l, h, ti)], is_out=True)
                    return f

                def mk_cache(b, m):
                    def f():
                        ckb = ckbs[b]
                        for tt in range(8):
                            TRANSP(psT[0:64, 128 * tt:128 * tt + 128], ckb[:, tt, 64 * m:64 * m + 64], 128, ["ckb%d" % b], ["psT"])
                        VCP(Kp[b][m][0:64, 0:PAST], psT[0:64, :], ["psT"], ["Kp%d%d" % (b, m)])
                        VCP(Kp[b][m][0:64, PAST:PAST + NS], Kt[m][0:64, T + NS * b:T + NS * b + NS], ["Kt%d" % m], ["Kp%d%d" % (b, m)])
                    return f

                for b in range(2):
                    for m in range(2):
                        bg.append(mk_cache(b, m))
                for ti in range(len(tiles)):
                    bg.append(mk_newk(ti))
                stop_here("attnD", dict(Gt=Gt[:]))
                stop_here("attnE", dict(Kp00=Kp[0][0][:]))
                pending = []

                def attend(qcols, qpos0, nq, ktiles, gt_tile):
                    steps = []
                    blocks_ = []
                    for qb0 in range(0, nq, 512):
                        nqb = min(512, nq - qb0)
                        qts = [(qb0 + 128 * i, min(128, nq - qb0 - 128 * i)) for i in range((nqb + 127) // 128)]
                        lastk = {}
                        vis = []
                        for kj, (Kap, Vap, nk, kp0) in enumerate(ktiles):
                            first = None
                            diag = False
                            for qi, (q0, nqi) in enumerate(qts):
                                tpos = qpos0 + q0
                                if kp0 // 64 > (tpos + nqi - 1) // 64:
                                    continue
                                if first is None:
                                    first = qi
                                    diag = (kp0 + nk - 1) > tpos
                                lastk[qi] = kj
                            if first is not None:
                                vis.append((kj, first, diag))
                        blk = dict(qb0=qb0, nqb=nqb, qts=qts, lastk=lastk, started=set(), nsteps=2 * len(vis))
                        blocks_.append(blk)
                        for (kj, first, diag) in vis:
                            for m in range(2):
                                steps.append(dict(blk=blk, kj=kj, first=first, diag=diag, m=m))
                        steps[-1]["last"] = True

                    def emit_S(st):
                        blk = st["blk"]; qts = blk["qts"]; m = st["m"]
                        Kap, Vap, nk, kp0 = ktiles[st["kj"]]
                        c0 = qts[st["first"]][0]; ncol = blk["qb0"] + blk["nqb"] - c0
                        si = pcount[0] % 2; pbi = pcount[0] % 4; pcount[0] += 1
                        S = ps[si]; sk = "ps%d" % si
                        lst = [(S[0:nk, 0:ncol], Kap[m], Qt[m][0:68, qcols + c0:qcols + c0 + ncol], dict(start=True, stop=not st["diag"]))]
                        if st["diag"]:
                            nqd = qts[st["first"]][1]
                            lst.append((S[0:nk, 0:nqd], ident[0:nk, 0:nk], dmat[0:nk, h, 0:nqd], dict(start=False, stop=True)))
                        tr.mms(lst, ["Qt%d" % m, "Kt%d" % m, "Kp0%d" % m, "Kp1%d" % m, "QtA%d" % m, "KtA%d" % m, "KpA0%d" % m, "KpA1%d" % m, "ident", "dmat"], [sk])
                        Pt = Pb[pbi]; pk_ = "Pb%d" % pbi
                        ACT(Pt[0:nk, 0:ncol], S[0:nk, 0:ncol], AF.Exp, [sk], [pk_])
                        st["P"] = (Pt, pk_, c0)

                    def emit_PV(st):
                        blk = st["blk"]; qts = blk["qts"]; m = st["m"]
                        Kap, Vap, nk, kp0 = ktiles[st["kj"]]
                        Pt, pk_, c0 = st["P"]
                        for qi in range(st["first"], len(qts)):
                            q0, nqi = qts[qi]
                            a_ = m * 4 + qi
                            bank = 2 + a_ // 3
                            O = ps[bank][0:nqi, 129 * (a_ % 3):129 * (a_ % 3) + 129]
                            stf = bank not in blk["started"]
                            blk["started"].add(bank)
                            tr.mms([(O, Pt[0:nk, q0 - c0:q0 - c0 + nqi], Vap, dict(start=stf, stop=(blk["lastk"][qi] == st["kj"]), skip_group_check=True))],
                                   [pk_, "Vx", "Vp0", "Vp1"], ["ps%d" % bank])

                    def finalize(blk):
                        qts = blk["qts"]
                        nqt = len(qts)
                        oset = ocnt[0] % 2; ocnt[0] += 1
                        nmax = max(n for (_, n) in qts)
                        used = sorted(set(2 + (m * 4 + qi) // 3 for qi in range(nqt) for m in range(2)))
                        for bank in used:
                            ACP(Ocp[oset][bank - 2][0:nmax, :], ps[bank][0:nmax, 0:387], ["ps%d" % bank], ["ocp%d_%d" % (oset, bank)])
                        Aps = []
                        for qi, (q0, nqi) in enumerate(qts):
                            A = []
                            for m in range(2):
                                a_ = m * 4 + qi
                                bank = 2 + a_ // 3
                                A.append((Ocp[oset][bank - 2][0:nqi, 129 * (a_ % 3):129 * (a_ % 3) + 129], "ocp%d_%d" % (oset, bank)))
                            Aps.append(A)
                        for qi, (q0, n) in enumerate(qts):
                            A = Aps[qi]; fs = fss[qi]; ot = ots[qi]; osb = osbs[qi]
                            fk = "fs%d_" % qi; otk = "ot%d" % qi; osk = "osb%d" % qi
                            tr.op("dve", lambda e: e.reciprocal(out=fs[0:n, 0:1], in_=A[0][0][:, 128:129]), [A[0][1]], [fk + "0"])
                            tr.op("dve", lambda e: e.reciprocal(out=fs[0:n, 1:2], in_=A[1][0][:, 128:129]), [A[1][1]], [fk + "1"])
                            VTT(fs[0:n, 1:2], fs[0:n, 1:2], pp[0:n, 2:3], ALU.mult, [fk + "1", "pp2"], [fk + "1"])
                            VTS(ot[0:n, :], A[1][0][:, 0:128], fs[0:n, 1:2], None, ALU.mult, None, [A[1][1], fk + "1"], [otk])
                            STT(osb[0:n, :], A[0][0][:, 0:128], fs[0:n, 0:1], ot[0:n, :], ALU.mult, ALU.add, [A[0][1], fk + "0", otk], [osk])
                            VTT(ot[0:n, :], osb[0:n, :], osb[0:n, :], ALU.mult, [osk], [otk])
                            RSUM(fs[0:n, 2:3], ot[0:n, :], [otk], [fk + "2"])
                        def stage2():
                            for qi, (q0, n) in enumerate(qts):
                                fs = fss[qi]; fk = "fs%d_" % qi
                                RSQ(fs[0:n, 2:3], fs[0:n, 2:3], 1.0 / 128, n, [fk + "2"], [fk + "2"])
                            for qi, (q0, n) in enumerate(qts):
                                fs = fss[qi]; osb = osbs[qi]; onb = onbs[qi]
                                STT(onb[0:n, :], osb[0:n, :], fs[0:n, 2:3], Gt[0:n, gt_tile(q0), :], ALU.mult, ALU.mult,
                                    ["osb%d" % qi, "fs%d_2" % qi, "Gt"], ["onb%d" % qi])

                        def deferred():
                            for qi, (q0, n) in enumerate(qts):
                                TRANSP(psT[:, 128 * qi:128 * qi + n], onbs[qi][0:n, :], n, ["onb%d" % qi], ["psT"])
                            qa = qts[0][0]; ntot = qts[-1][0] + qts[-1][1] - qa
                            if all(n == 128 for (_, n) in qts):
                                VCP(oT[:, h, qcols + qa:qcols + qa + ntot], psT[:, 0:ntot], ["psT"], ["oT"])
                            else:
                                for qi, (q0, n) in enumerate(qts):
                                    VCP(oT[:, h, qcols + q0:qcols + q0 + n], psT[:, 128 * qi:128 * qi + n], ["psT"], ["oT"])
                        return [[8, stage2], [11, deferred]]

                    emit_S(steps[0])
                    for i, st in enumerate(steps):
                        if i + 1 < len(steps):
                            emit_S(steps[i + 1])
                        emit_PV(st)
                        if bg and i >= 2:
                            bg.pop(0)()
                        for p_ in pending:
                            p_[0] -= 1
                        while pending and pending[0][0] <= 0:
                            pending.pop(0)[1]()
                        if st.get("last"):
                            while pending:
                                pending.pop(0)[1]()
                            pending.extend(finalize(st["blk"]))
                    while bg:
                        bg.pop(0)()

                kts = [([Kt[0][0:68, 128 * j:128 * j + 128], Kt[1][0:68, 128 * j:128 * j + 128]], Vx[:, j, 0:129], 128, 128 * j) for j in range(16)]
                attend(0, 0, T, kts, lambda q0: q0 // 128)
                stop_here("attnF", dict(oT=oT[:]))
                for b in range(2):
                    kts = [([Kp[b][0][0:68, 128 * j:128 * j + 128], Kp[b][1][0:68, 128 * j:128 * j + 128]], Vp[b][:, j, 0:129], 128, 128 * j) for j in range(8)]
                    kts.append(([Kp[b][0][0:68, PAST:PAST + NS], Kp[b][1][0:68, PAST:PAST + NS]], Vx[0:NS, 16 + b, 0:129], NS, PAST))
                    attend(T + NS * b, PAST, NS, kts, lambda q0, b=b: 16 + b)
                while pending:
                    pending.pop(0)[1]()
                stop_here("attnh%d_%d" % (h, l), dict(oT=oT[:], Qt0=Qt[0][:], Kt0=Kt[0][:], Qt1=Qt[1][:], Kt1=Kt[1][:], Vx=Vx[:], Gt=Gt[:], Kp00=Kp[0][0][:], Vp0=Vp[0][:]))
            tr.barrier()
            stop_here("attn_%d" % l, dict(oT=oT[:]))

        with contextlib.ExitStack() as es:
            ysgT = SBT(es, "ysgT", [128, 8, NT], BF16)
            wA = SBT(es, "wA", [128, 8, D], BF16); wB = SBT(es, "wB", [128, 8, D], BF16)
            tmp = [SBT(es, "tmp%d" % i, [128, 512]) for i in range(2)]
            tb16 = SBT(es, "tb16", [128, 512], BF16)
            xr = [SBT(es, "xr%d" % i, [128, D]) for i in range(2)]

            def loadw(dst, key, tnsr, base, rowlen):
                tr.dma("pool", key, dst[:], dap(tnsr, base, [[rowlen, 128], [rowlen * 128, 8], [1, D]]), (), [key])

            def proj(wt, wkey, src, skey, oc, b0, bn, pi):
                pb = ps[pi]
                tr.mms([(pb[:, 0:bn], wt[:, kc, 128 * oc:128 * oc + 128], src[:, kc, b0:b0 + bn], dict(start=(kc == 0), stop=(kc == 7))) for kc in range(8)],
                       [wkey, skey], ["ps%d" % pi])
                return pb[:, 0:bn], "ps%d" % pi

            cnt = [0]
            loadw(wA, "wA", w_glu, l * D * D, D)
            loadw(wB, "wB", w_in, l * D * INC + 5120, INC)
            for oc in range(8):
                for (b0, bn) in blocks:
                    pi = cnt[0] % 2; cnt[0] += 1
                    pa, ka = proj(wA, "wA", ygT, "ygT", oc, b0, bn, pi)
                    ACT(tmp[0][:, 0:bn], pa, AF.Sigmoid, [ka, "pp3"], ["tmp0"], bias=pp[:, 3 + oc:4 + oc])
                    VTT(ysgT[:, oc, b0:b0 + bn], ygT[:, oc, b0:b0 + bn], tmp[0][:, 0:bn], ALU.mult, ["ygT", "tmp0"], [("ysg", oc)])
            for oc in range(8):
                for (b0, bn) in blocks:
                    pi = cnt[0] % 2; cnt[0] += 1
                    pa, ka = proj(wB, "wB", xnT, "xnT", oc, b0, bn, pi)
                    ACT(tb16[:, 0:bn], pa, AF.Silu, [ka], ["tb16"])
                    VTT(ysgT[:, oc, b0:b0 + bn], ysgT[:, oc, b0:b0 + bn], tb16[:, 0:bn], ALU.mult, [("ysg", oc), "tb16"], [("ysg", oc)])
            allysg = [("ysg", oc) for oc in range(8)]
            loadw(wA, "wA", w_sp, l * D * D, D)
            loadw(wB, "wB", w_in, l * D * INC + 7168, INC)
            for oc in range(8):
                for (b0, bn) in blocks:
                    pi = cnt[0] % 2; cnt[0] += 1
                    pg, kg = proj(wB, "wB", xnT, "xnT", oc, b0, bn, 2 + pi)
                    ACT(tmp[0][:, 0:bn], pg, AF.Sigmoid, [kg], ["tmp0"])
                    pb = ps[pi]
                    tr.mms([(pb[:, 0:bn], wA[:, kc, 128 * oc:128 * oc + 128], ysgT[:, kc, b0:b0 + bn], dict(start=(kc == 0), stop=(kc == 7))) for kc in range(8)],
                           ["wA"] + allysg, ["ps%d" % pi])
                    VTT(ygT[:, oc, b0:b0 + bn], pb[:, 0:bn], tmp[0][:, 0:bn], ALU.mult, ["ps%d" % pi, "tmp0"], ["ygT"])
            loadw(wA, "wA", w_ap, l * D * D, D)
            loadw(wB, "wB", w_in, l * D * INC + 6144, INC)
            for oc in range(8):
                for (b0, bn) in blocks:
                    pi = cnt[0] % 2; cnt[0] += 1
                    pg, kg = proj(wB, "wB", xnT, "xnT", oc, b0, bn, 2 + pi)
                    ACT(tmp[0][:, 0:bn], pg, AF.Sigmoid, [kg], ["tmp0"])
                    pa, ka = proj(wA, "wA", oT, "oT", oc, b0, bn, pi)
                    VTT(tmp[1][:, 0:bn], pa, tmp[0][:, 0:bn], ALU.mult, [ka, "tmp0"], ["tmp1"])
                    VTT(ygT[:, oc, b0:b0 + bn], ygT[:, oc, b0:b0 + bn], tmp[1][:, 0:bn], ALU.add, ["ygT", "tmp1"], ["ygT"])
            loadw(wA, "wA", w_o, l * D * D, D)
            for ti, (kind, i, n, c0) in enumerate(tiles):
                xb = xr[ti % 2]; xk = "xr%d" % (ti % 2)
                if l == 0:
                    src = x_p.ap()[128 * i:128 * i + 128, :] if kind == "p" else x_s.ap()[i]
                else:
                    src = xmid.ap()[c0:c0 + n, :]
                tr.dma("sp", xk, xb[0:n, :], src, [("xmid", ti)], [xk])
                for hb in range(2):
                    pi = cnt[0] % 2; cnt[0] += 1
                    pb = ps[pi]
                    tr.mms([(pb[0:n, :], ygT[:, kc, c0:c0 + n], wA[:, kc, 512 * hb:512 * hb + 512], dict(start=(kc == 0), stop=(kc == 7))) for kc in range(8)],
                           ["wA", "ygT"], ["ps%d" % pi])
                    VTT(xb[0:n, 512 * hb:512 * hb + 512], xb[0:n, 512 * hb:512 * hb + 512], pb[0:n, :], ALU.add, [xk, "ps%d" % pi], [xk])
                if l == DEPTH - 1:
                    dst = y_p.ap()[128 * i:128 * i + 128, :] if kind == "p" else y_s.ap()[i]
                    tr.dma("sp", "o_y", dst, xb[0:n, :], [xk], [("oy", ti)], is_out=True)
                else:
                    tr.dma("sp", "o_x", xmid.ap()[c0:c0 + n, :], xb[0:n, :], [xk], [("xmid2", ti)])
            tr.barrier()
            stop_here("merge_%d" % l, dict(mT=ygT[:], ysgT=ysgT[:]))
        oes.close()
        yes.close()

    except StopBuild:
        pass
    tr.finish()
    return nc


_NC = None


def _consts():
    ident = np.eye(128, dtype=np.float32)
    dm = np.zeros((128, H, 128), np.float32)
    s = np.arange(128)[:, None]; t = np.arange(128)[None, :]
    for h in range(H):
        d = np.where(s > t, -2.0 * SLOPES[h] * (s - t), 0.0)
        d = np.where((s // 64) > (t // 64), -30000.0, d)
        dm[:, h, :] = d
    pos = np.arange(T)
    hi = (pos // 128) * 128.0; lo = (pos % 128) * 1.0
    qa = np.zeros((4, H, T), np.float32); ka = np.zeros((4, H, T), np.float32)
    for h in range(H):
        qa[0, h] = -SLOPES[h] * hi; qa[1, h] = -SLOPES[h] * lo; qa[2, h] = 1.0; qa[3, h] = 1.0
        ka[0, h] = 1.0; ka[1, h] = 1.0; ka[2, h] = SLOPES[h] * hi; ka[3, h] = SLOPES[h] * lo
    par = np.zeros((128, 2), np.float32)
    g8 = np.arange(128) // 16
    par[:, 0] = (g8 % 2 == 0); par[:, 1] = (g8 % 2 == 1)
    io = np.tile(np.arange(1, 129, dtype=np.float32)[None, :], (128, 1))
    return dict(c_id=ident, c_dm=dm, c_qa=qa, c_ka=ka, c_par=par, c_io=io)


def kernel(x_prompt, x_sample, cache_k, cache_v, state_ssm_re, state_ssm_im,
           norm_g, w_in, q_norm_g, k_norm_g, lam_q1, lam_k1, lam_q2, lam_k2, subln_g,
           w_attn_proj, ssm_lambda_re, ssm_lambda_im, ssm_log_dt, ssm_b_re, ssm_b_im,
           ssm_c_re, ssm_c_im, ssm_d, w_glu, b_glu, w_ssm_proj, w_out):
    global _NC
    if _NC is None:
        _NC = build()
    nc = _NC
    f = lambda a: np.ascontiguousarray(np.asarray(a, dtype=np.float32))
    shared = dict(norm_g=f(norm_g), w_in=f(w_in), qng=f(q_norm_g), kng=f(k_norm_g), lq1=f(lam_q1), lk1=f(lam_k1),
                  lq2=f(lam_q2), lk2=f(lam_k2), subg=f(subln_g), w_ap=f(w_attn_proj), lamr=f(ssm_lambda_re),
                  lami=f(ssm_lambda_im), ldt=f(ssm_log_dt), bre=f(ssm_b_re), bim=f(ssm_b_im), cre=f(ssm_c_re),
                  cim=f(ssm_c_im), dsk=f(ssm_d), w_glu=f(w_glu), b_glu=f(b_glu), w_sp=f(w_ssm_proj), w_o=f(w_out))
    shared.update(_consts())
    x_prompt = np.asarray(x_prompt); x_sample = np.asarray(x_sample)
    cache_k = np.asarray(cache_k); cache_v = np.asarray(cache_v)
    state_ssm_re = np.asarray(state_ssm_re); state_ssm_im = np.asarray(state_ssm_im)
    in_maps = []
    for c in range(NCORES):
        m = dict(shared)
        m["x_p"] = f(x_prompt[c]); m["x_s"] = f(x_sample[2 * c:2 * c + 2])
        m["ck"] = f(cache_k[:, 2 * c:2 * c + 2]); m["cv"] = f(cache_v[:, 2 * c:2 * c + 2])
        m["sre"] = f(state_ssm_re[:, 2 * c:2 * c + 2]); m["sim"] = f(state_ssm_im[:, 2 * c:2 * c + 2])
        in_maps.append(m)
    res = run_bass_kernel_spmd(nc, in_maps, core_ids=list(range(NCORES)))
    R = res.results
    cat0 = lambda k: np.stack([R[c][k] for c in range(NCORES)], axis=0)
    y_p = cat0("y_p")
    y_s = np.concatenate([R[c]["y_s"] for c in range(NCORES)], axis=0)
    nk_p = np.stack([R[c]["nk_p"] for c in range(NCORES)], axis=1)
    nv_p = np.stack([R[c]["nv_p"] for c in range(NCORES)], axis=1)
    hr_p = np.stack([R[c]["hr_p"] for c in range(NCORES)], axis=1)
    hi_p = np.stack([R[c]["hi_p"] for c in range(NCORES)], axis=1)
    nk_s = np.concatenate([R[c]["nk_s"] for c in range(NCORES)], axis=1)
    nv_s = np.concatenate([R[c]["nv_s"] for c in range(NCORES)], axis=1)
    hr_s = np.concatenate([R[c]["hr_s"] for c in range(NCORES)], axis=1)
    hi_s = np.concatenate([R[c]["hi_s"] for c in range(NCORES)], axis=1)
    return tuple(np.ascontiguousarray(a.astype(np.float32)) for a in (y_p, y_s, nk_p, nv_p, hr_p, hi_p, nk_s, nv_s, hr_s, hi_s))
```

```python
import math
import numpy as np
import concourse.bass as bass
import concourse.mybir as mybir
from concourse.bass_utils import run_bass_kernel_spmd

F32 = mybir.dt.float32
BF16 = mybir.dt.bfloat16
I32 = mybir.dt.int32
ALU = mybir.AluOpType
AF = mybir.ActivationFunctionType
AX = mybir.AxisListType

D = 1024
T = 2048
NS = 16
PAST = 1024
NT = T + 2 * NS
H = 8
DEPTH = 2
EPS = 1e-6
INC = 8192
NCORES = 8
SLOPES = [2.0 ** (-(h + 1)) for h in range(H)]


class Tr:
    def __init__(self, nc):
        self.nc = nc
        self.eng = {}
        for name, h in (("pe", nc.tensor), ("act", nc.scalar), ("dve", nc.vector), ("pool", nc.gpsimd), ("sp", nc.sync)):
            self.eng[name] = dict(h=h, sem=nc.alloc_semaphore(name="s_" + name), cnt=0, seen={})
        self.lastw = {}
        self.readers = {}
        self.dsem = {}
        self.out_tokens = []

    def _deps(self, reads, writes):
        deps = []
        for k in reads:
            if k in self.lastw:
                deps.append(self.lastw[k])
        for k in writes:
            if k in self.lastw:
                deps.append(self.lastw[k])
            deps.extend(self.readers.get(k, ()))
        return deps

    def _wait(self, ename, deps):
        e = self.eng[ename]
        best = {}
        for (sname, sem, val) in deps:
            if ename == "pe" and sname == "s_pe":
                continue
            if val > best.get(sname, (None, 0))[1]:
                best[sname] = (sem, val)
        for sname, (sem, val) in best.items():
            if e["seen"].get(sname, 0) < val:
                e["h"].wait_ge(sem, val)
                e["seen"][sname] = val

    def _commit(self, tok, reads, writes):
        for k in writes:
            self.lastw[k] = tok
            self.readers[k] = []
        for k in reads:
            self.readers.setdefault(k, []).append(tok)

    def op(self, ename, fn, reads=(), writes=()):
        e = self.eng[ename]
        self._wait(ename, self._deps(reads, writes))
        ins = fn(e["h"])
        e["cnt"] += 1
        ins.then_inc(e["sem"], 1)
        self._commit(("s_" + ename, e["sem"], e["cnt"]), reads, writes)

    def mms(self, lst, reads=(), writes=()):
        e = self.eng["pe"]
        self._wait("pe", self._deps(reads, writes))
        ins = None
        for (out, lhsT, rhs, kw) in lst:
            ins = self.nc.tensor.matmul(out, lhsT=lhsT, rhs=rhs, **kw)
        e["cnt"] += 1
        ins.then_inc(e["sem"], 1)
        self._commit(("s_pe", e["sem"], e["cnt"]), reads, writes)

    def dma(self, q, slot, out, in_, reads=(), writes=(), is_out=False):
        if slot.startswith("o_") or slot == "dbg":
            slot = "st_" + "_".join(str(x) for x in (reads[0] if isinstance(reads[0], tuple) else (reads[0],))) if reads else slot
        else:
            slot = "ld_" + "_".join(str(x) for x in (writes[0] if isinstance(writes[0], tuple) else (writes[0],)))
        if slot not in self.dsem:
            self.dsem[slot] = [self.nc.alloc_semaphore(name="d_" + slot), 0]
        ds = self.dsem[slot]
        self._wait(q, self._deps(reads, writes))
        self.eng[q]["h"].dma_start(out=out, in_=in_, allow_slow_non_contiguous=True).then_inc(ds[0], 16)
        ds[1] += 16
        tok = ("d_" + slot, ds[0], ds[1])
        self._commit(tok, reads, writes)
        if is_out:
            self.out_tokens.append(tok)

    def barrier(self):
        toks = [("s_" + n, e["sem"], e["cnt"]) for n, e in self.eng.items() if e["cnt"] > 0]
        toks += [("d_" + k, v[0], v[1]) for k, v in self.dsem.items() if v[1] > 0]
        for n in self.eng:
            e = self.eng[n]
            for (sname, sem, val) in toks:
                if sname == "s_" + n:
                    continue
                if e["seen"].get(sname, 0) < val:
                    e["h"].wait_ge(sem, val)
                    e["seen"][sname] = val
        self.lastw = {}
        self.readers = {}

    def finish(self):
        best = {}
        for (sname, sem, val) in self.out_tokens:
            if val > best.get(sname, (None, 0))[1]:
                best[sname] = (sem, val)
        for sname, (sem, val) in best.items():
            self.nc.sync.wait_ge(sem, val)


def dap(t, off, dims):
    return bass.AP(t, off, [list(d) for d in dims])


class StopBuild(Exception):
    pass


def build(stop=None):
    nc = bass.Bass("TRN2", target_bir_lowering=False)
    dt_in = lambda n, s: nc.dram_tensor(n, s, F32, kind="ExternalInput")
    dt_out = lambda n, s: nc.dram_tensor(n, s, F32, kind="ExternalOutput")
    x_p = dt_in("x_p", [T, D]); x_s = dt_in("x_s", [2, NS, D])
    ck = dt_in("ck", [DEPTH, 2, PAST, 2, H, 64]); cv = dt_in("cv", [DEPTH, 2, PAST, H, 128])
    sre = dt_in("sre", [DEPTH, 2, 64, 64]); sim = dt_in("sim", [DEPTH, 2, 64, 64])
    norm_g = dt_in("norm_g", [DEPTH, D]); w_in = dt_in("w_in", [DEPTH, D, INC])
    qng = dt_in("qng", [DEPTH, 64]); kng = dt_in("kng", [DEPTH, 64])
    lq1 = dt_in("lq1", [DEPTH, 64]); lk1 = dt_in("lk1", [DEPTH, 64]); lq2 = dt_in("lq2", [DEPTH, 64]); lk2 = dt_in("lk2", [DEPTH, 64])
    subg = dt_in("subg", [DEPTH, 128]); w_ap = dt_in("w_ap", [DEPTH, D, D])
    lamr = dt_in("lamr", [DEPTH, 64, 64]); lami = dt_in("lami", [DEPTH, 64, 64]); ldt = dt_in("ldt", [DEPTH, 64])
    bre = dt_in("bre", [DEPTH, 64, 64, 16]); bim = dt_in("bim", [DEPTH, 64, 64, 16])
    cre = dt_in("cre", [DEPTH, 64, 16, 64]); cim = dt_in("cim", [DEPTH, 64, 16, 64])
    dsk = dt_in("dsk", [DEPTH, D]); w_glu = dt_in("w_glu", [DEPTH, D, D]); b_glu = dt_in("b_glu", [DEPTH, D])
    w_sp = dt_in("w_sp", [DEPTH, D, D]); w_o = dt_in("w_o", [DEPTH, D, D])
    c_id = dt_in("c_id", [128, 128]); c_dm = dt_in("c_dm", [128, H, 128])
    c_qa = dt_in("c_qa", [4, H, T]); c_ka = dt_in("c_ka", [4, H, T]); c_par = dt_in("c_par", [128, 2]); c_io = dt_in("c_io", [128, 128])
    y_p = dt_out("y_p", [T, D]); y_s = dt_out("y_s", [2, NS, D])
    nk_p = dt_out("nk_p", [DEPTH, T, 2, H, 64]); nv_p = dt_out("nv_p", [DEPTH, T, H, 128])
    hr_p = dt_out("hr_p", [DEPTH, 64, 64]); hi_p = dt_out("hi_p", [DEPTH, 64, 64])
    nk_s = dt_out("nk_s", [DEPTH, 2, NS, 2, H, 64]); nv_s = dt_out("nv_s", [DEPTH, 2, NS, H, 128])
    hr_s = dt_out("hr_s", [DEPTH, 2, 64, 64]); hi_s = dt_out("hi_s", [DEPTH, 2, 64, 64])
    xmid = nc.dram_tensor("xmid", [NT, D], F32, kind="Internal")

    tr = Tr(nc)
    import contextlib

    ucnt = [0]

    def SBT(es, n, s, d=F32):
        ucnt[0] += 1
        return es.enter_context(nc.sbuf_tensor("%s_%d" % (n, ucnt[0]), s, d))

    def stop_here(name, dumps):
        if stop != name:
            return
        tr.barrier()
        for dn, ap in dumps.items():
            shp = list(ap.shape)
            dtn = nc.dram_tensor("dbg_" + dn, shp, ap.dtype, kind="ExternalOutput")
            tr.dma("sp", "dbg", dtn.ap(), ap, (), [("dbg", dn)], is_out=True)
        raise StopBuild()

    ges = contextlib.ExitStack()
    ps = [ges.enter_context(nc.psum_tensor("ps%d" % i, [128, 512], F32)) for i in range(7)]
    psT = ges.enter_context(nc.psum_tensor("psT", [128, 1024], BF16))
    xnT = SBT(ges, "xnT", [128, 8, NT], BF16)
    ident = SBT(ges, "ident", [128, 128], BF16)
    dmat = SBT(ges, "dmat", [128, H, 128], BF16)
    ones64 = SBT(ges, "ones64", [64, 64], BF16)
    par = SBT(ges, "par", [128, 2])
    epsc = SBT(ges, "epsc", [128, 1])
    sm = SBT(ges, "sm", [128, 16])
    pp = SBT(ges, "pp", [128, 32])
    lv = SBT(ges, "lv", [128, 4, 64])
    subB = SBT(ges, "subB", [128, 128])
    identF = SBT(ges, "identF", [128, 128])
    stg = SBT(ges, "stg", [64, 64]); stg2 = SBT(ges, "stg2", [64, 64]); stg3 = SBT(ges, "stg3", [64, 64])
    prow = SBT(ges, "prow", [32, 128]); dtb = SBT(ges, "dtb", [128, 64])

    def ACT(out, in_, func, reads, writes, **kw):
        tr.op("act", lambda e: e.activation(out=out, in_=in_, func=func, **kw), reads, writes)

    def VTT(out, a, b, op, reads, writes, eng="dve"):
        tr.op(eng, lambda e: e.tensor_tensor(out=out, in0=a, in1=b, op=op), reads, writes)

    def VTS(out, a, s1, s2, op0, op1, reads, writes, eng="dve"):
        if op1 is None:
            tr.op(eng, lambda e: e.tensor_scalar(out=out, in0=a, scalar1=s1, scalar2=None, op0=op0), reads, writes)
        else:
            tr.op(eng, lambda e: e.tensor_scalar(out=out, in0=a, scalar1=s1, scalar2=s2, op0=op0, op1=op1), reads, writes)

    def STT(out, a, s, b, op0, op1, reads, writes, eng="dve"):
        tr.op(eng, lambda e: e.scalar_tensor_tensor(out=out, in0=a, scalar=s, in1=b, op0=op0, op1=op1), reads, writes)

    def VCP(out, in_, reads, writes, eng="dve"):
        tr.op(eng, lambda e: e.tensor_copy(out=out, in_=in_), reads, writes)

    def ACP(out, in_, reads, writes):
        tr.op("act", lambda e: e.copy(out=out, in_=in_), reads, writes)

    def MEMSET(ap, val, writes, eng="dve"):
        tr.op(eng, lambda e: e.memset(ap, val), (), writes)

    def RSUM(out, in_, reads, writes):
        tr.op("dve", lambda e: e.reduce_sum(out=out, in_=in_, axis=AX.X), reads, writes)

    def RSQ(out, in_, scale, n, reads, writes):
        ACT(out, in_, AF.Ln, list(reads) + ["epsc"], writes, scale=scale, bias=epsc[0:n, :])
        ACT(out, out, AF.Exp, writes, writes, scale=-0.5)

    def TRANSP(out, in_, n, reads, writes):
        tr.op("pe", lambda e: e.transpose(out=out, in_=in_, identity=ident[0:n, 0:n]), list(reads) + ["ident"], writes)

    tr.dma("pool", "c0", ident[:], c_id.ap(), (), ["ident"])
    tr.dma("sp", "c1", identF[:], c_id.ap(), (), ["identF"])

    def TRANSPF(out, in_, n, reads, writes):
        tr.op("pe", lambda e: e.transpose(out=out, in_=in_, identity=identF[0:n, 0:n]), list(reads) + ["identF"], writes)

    def load_GP(dst, dkey, tensor, off):
        tr.dma("sp", "stg", stg[:], dap(tensor, off, [[64, 64], [1, 64]]), (), ["stg"])
        TRANSPF(ps[4][0:64, 0:64], stg[:], 64, ["stg"], ["ps4"])
        VCP(dst[0:64, :], ps[4][0:64, 0:64:2], ["ps4"], [dkey])
        VCP(dst[64:128, :], ps[4][0:64, 1:64:2], ["ps4"], [dkey])

    def store_GP(tensor, off, src, skey, okey):
        VCP(stg2[:, 0:64:2], src[0:64, :], [skey], ["stg2"])
        VCP(stg2[:, 1:64:2], src[64:128, :], [skey], ["stg2"])
        TRANSPF(ps[4][0:64, 0:64], stg2[:], 64, ["stg2"], ["ps4"])
        VCP(stg3[:], ps[4][0:64, 0:64], ["ps4"], ["stg3"])
        tr.dma("sp", "o_h", dap(tensor, off, [[64, 64], [1, 64]]), stg3[:], ["stg3"], [okey], is_out=True)
    tr.dma("pool", "c0", dmat[:], c_dm.ap(), (), ["dmat"])
    tr.dma("sp", "c1", par[:], c_par.ap(), (), ["par"])
    MEMSET(ones64[:], 1.0, ["ones64"])
    MEMSET(epsc[:], EPS, ["epsc"])

    tiles = [("p", i, 128, 128 * i) for i in range(16)] + [("s", b, NS, T + NS * b) for b in range(2)]
    blocks = [(512 * i, 512) for i in range(4)] + [(T, 2 * NS)]
    TWO_PI = 2.0 * math.pi

    try:
      for l in range(DEPTH):
        lam_init = 0.8 - 0.6 * math.exp(-0.3 * l)
        with contextlib.ExitStack() as es:
            ngb = SBT(es, "ngb", [128, D]); xt = [SBT(es, "xt%d" % i, [128, D]) for i in range(3)]
            scrs = [SBT(es, "scr%d" % i, [128, D]) for i in range(2)]; xnbs = [SBT(es, "xnb%d" % i, [128, D], BF16) for i in range(2)]
            tr.dma("sp", "c1", ngb[:], dap(norm_g, l * D, [[0, 128], [1, D]]), (), ["ngb"])
            MEMSET(prow[:], 0.0, ["prow"])
            tr.dma("sp", "c1", prow[0:1, :].rearrange("p (a b) -> p a b", b=64), dap(qng, l * 64, [[64, 1], [0, 2], [1, 64]]), (), ["prow"])
            tr.dma("sp", "c1", prow[1:2, :].rearrange("p (a b) -> p a b", b=64), dap(kng, l * 64, [[64, 1], [0, 2], [1, 64]]), (), ["prow"])
            tr.dma("sp", "c1", prow[3:11, :], dap(b_glu, l * D, [[128, 8], [1, 128]]), (), ["prow"])
            tr.dma("sp", "c1", prow[11:19, :], dap(dsk, l * D, [[128, 8], [1, 128]]), (), ["prow"])
            TRANSPF(ps[4][:, 0:32], prow[:], 32, ["prow"], ["ps4"])
            VCP(pp[:], ps[4][:, 0:32], ["ps4"], ["pp0", "pp1", "pp2", "pp3", "pp11"])
            VTS(pp[:, 0:1], pp[:, 0:1], 0.125, None, ALU.mult, None, ["pp0"], ["pp0"])
            for j, tnsr in enumerate((lq1, lk1, lq2, lk2)):
                tr.dma("sp", "c1", lv[:, j, :], dap(tnsr, l * 64, [[0, 128], [1, 64]]), (), ["lv%d" % j])
            VTT(lv[:, 0, :], lv[:, 0, :], lv[:, 1, :], ALU.mult, ["lv0", "lv1"], ["lv0"])
            VTT(lv[:, 2, :], lv[:, 2, :], lv[:, 3, :], ALU.mult, ["lv2", "lv3"], ["lv2"])
            RSUM(sm[:, 0:1], lv[:, 0, :], ["lv0"], ["sm0"])
            RSUM(sm[:, 1:2], lv[:, 2, :], ["lv2"], ["sm1"])
            ACT(sm[:, 0:1], sm[:, 0:1], AF.Exp, ["sm0"], ["sm0"])
            ACT(sm[:, 1:2], sm[:, 1:2], AF.Exp, ["sm1"], ["sm1"])
            VTT(sm[:, 0:1], sm[:, 1:2], sm[:, 0:1], ALU.subtract, ["sm0", "sm1"], ["sm0"])
            VTS(pp[:, 2:3], sm[:, 0:1], -lam_init, None, ALU.add, None, ["sm0"], ["pp2"])
            tr.dma("sp", "c1", subB[:], dap(subg, l * 128, [[0, 128], [1, 128]]), (), ["subB"])
            VTS(subB[:], subB[:], 1.0 - lam_init, None, ALU.mult, None, ["subB"], ["subB"])
            def rn_A(ti):
                kind, i, n, c0 = tiles[ti]
                xb = xt[ti % 3]; xk = "xt%d" % (ti % 3)
                sc = scrs[ti % 2]; sck = "scr%d" % (ti % 2); smk = "smr%d" % (ti % 3); col = 4 + ti % 3
                if l == 0:
                    src = x_p.ap()[128 * i:128 * i + 128, :] if kind == "p" else x_s.ap()[i]
                else:
                    src = xmid.ap()[c0:c0 + n, :]
                tr.dma("sp", xk, xb[0:n, :], src, [("xmid", ti)], [xk])
                VTT(sc[0:n, :], xb[0:n, :], xb[0:n, :], ALU.mult, [xk], [sck])
                RSUM(sm[0:n, col:col + 1], sc[0:n, :], [sck], [smk])
                RSQ(sm[0:n, col:col + 1], sm[0:n, col:col + 1], 1.0 / D, n, [smk], [smk])

            def rn_B(ti):
                kind, i, n, c0 = tiles[ti]
                xb = xt[ti % 3]; xk = "xt%d" % (ti % 3)
                xn_ = xnbs[ti % 2]; xnk = "xnb%d" % (ti % 2); smk = "smr%d" % (ti % 3); col = 4 + ti % 3
                STT(xn_[0:n, :], xb[0:n, :], sm[0:n, col:col + 1], ngb[0:n, :], ALU.mult, ALU.mult, [xk, smk, "ngb"], [xnk])
                for kc in range(8):
                    TRANSP(psT[:, 128 * kc:128 * kc + n], xn_[0:n, 128 * kc:128 * kc + 128], n, [xnk], ["psT"])

            def rn_C(ti):
                kind, i, n, c0 = tiles[ti]
                VCP(xnT[:, :, c0:c0 + n], psT[:].rearrange("p (k t) -> p k t", t=128)[:, :, 0:n], ["psT"], ["xnT"])

            ntl0 = len(tiles)
            rn_A(0); rn_A(1)
            for ti in range(ntl0):
                rn_B(ti)
                if ti + 2 < ntl0:
                    rn_A(ti + 2)
                rn_C(ti)
            tr.barrier()
            stop_here("p0_%d" % l, dict(xnT=xnT[:], pp=pp[:], subB=subB[:]))

        yes = contextlib.ExitStack()
        ygT = SBT(yes, "ygT", [128, 8, NT], BF16)

        with contextlib.ExitStack() as es:
            tabCb = SBT(es, "tabCb", [128, 32, 128], BF16); tabSb = SBT(es, "tabSb", [128, 32, 128], BF16)
            tabCl = SBT(es, "tabCl", [128, 2, 32]); tabSl = SBT(es, "tabSl", [128, 2, 32])
            TLa = SBT(es, "TLa", [128, 2, 2, 32]); TLb = SBT(es, "TLb", [128, 2, 2, 32])
            RHO2 = SBT(es, "RHO2", [128, 2, 32])
            Dd = SBT(es, "Dd", [128, 8, 128], BF16)
            for c_ in range(8):
                VTS(Dd[:, c_, :], identF[:], pp[:, 11 + c_:12 + c_], None, ALU.mult, None, ["identF", "pp11"], ["Dd"])
            SinT = SBT(es, "SinT", [128, 64, 128], BF16); Cst = SBT(es, "Cst", [128, 64, 128], BF16)
            P = SBT(es, "P", [128, 20, 32])
            PI = SBT(es, "PI", [128, 32], I32)
            Hc = SBT(es, "Hc", [128, 3, 2, 32])
            RH = SBT(es, "RH", [128, 2, 4])
            es2 = contextlib.ExitStack()
            tabC = SBT(es2, "tabC", [128, 32, 128]); tabS = SBT(es2, "tabS", [128, 32, 128])
            es3 = contextlib.ExitStack()
            tmpA = SBT(es3, "tmpA", [128, 32, 128]); tmpB = SBT(es3, "tmpB", [128, 32, 128], I32); io = SBT(es3, "io", [128, 128])
            LR, LI, DT, ER, ANG, RHO, AR, AI, S1, C1, WR, WI, DEN, TA, TB, TC = [P[:, i, :] for i in range(16)]
            pk = lambda i: "P%d" % i
            load_GP(LR, pk(0), lamr, l * 4096)
            load_GP(LI, pk(1), lami, l * 4096)
            tr.dma("sp", "c1", dtb[:], dap(ldt, l * 64, [[0, 128], [1, 64]]), (), ["dtb"])
            VCP(P[0:64, 2, :], dtb[0:64, 0:64:2], ["dtb"], [pk(2)])
            VCP(P[64:128, 2, :], dtb[64:128, 1:64:2], ["dtb"], [pk(2)])
            VTS(TC, DT, 1.0 / 8, None, ALU.mult, None, [pk(2)], [pk(15)])
            VTS(DT, TC, 1.0 / 14, 1.0, ALU.mult, ALU.add, [pk(15)], [pk(2)])
            for kk in range(13, 0, -1):
                VTT(DT, DT, TC, ALU.mult, [pk(2), pk(15)], [pk(2)])
                VTS(DT, DT, 1.0 / kk, 1.0, ALU.mult, ALU.add, [pk(2)], [pk(2)])
            for _ in range(3):
                VTT(DT, DT, DT, ALU.mult, [pk(2)], [pk(2)])
            VTT(ER, LR, DT, ALU.mult, [pk(0), pk(2)], [pk(3)])
            VTT(ANG, LI, DT, ALU.mult, [pk(1), pk(2)], [pk(4)])
            VTS(TA, ANG, 1.0 / TWO_PI, None, ALU.mult, None, [pk(4)], [pk(13)])
            VCP(PI[:], TA, [pk(13)], ["PI"])
            VCP(TA, PI[:], ["PI"], [pk(13)])
            STT(ANG, TA, -TWO_PI, ANG, ALU.mult, ALU.add, [pk(13), pk(4)], [pk(4)])
            VTS(ANG, ANG, -3.14159, 3.14159, ALU.max, ALU.min, [pk(4)], [pk(4)])
            VTS(RHO, ER, 1.0 / 8, 1.0, ALU.mult, ALU.add, [pk(3)], [pk(5)])
            for kk in (7, 6, 5, 4, 3, 2, 1):
                VTT(RHO, RHO, ER, ALU.mult, [pk(5), pk(3)], [pk(5)])
                VTS(RHO, RHO, 1.0 / kk, 1.0, ALU.mult, ALU.add, [pk(5)], [pk(5)])
            ACT(S1, ANG, AF.Sin, [pk(4)], [pk(8)])
            ACT(TB, ANG, AF.Sin, [pk(4)], [pk(14)], scale=0.5)
            VTT(TB, TB, TB, ALU.mult, [pk(14)], [pk(14)])
            VTS(C1, TB, -2.0, 1.0, ALU.mult, ALU.add, [pk(14)], [pk(9)])
            VTT(AR, RHO, C1, ALU.mult, [pk(5), pk(9)], [pk(6)])
            VTT(AI, RHO, S1, ALU.mult, [pk(5), pk(8)], [pk(7)])
            VTS(TA, AR, -1.0, None, ALU.add, None, [pk(6)], [pk(13)])
            VTT(DEN, LR, LR, ALU.mult, [pk(0)], [pk(12)])
            VTT(TB, LI, LI, ALU.mult, [pk(1)], [pk(14)])
            VTT(DEN, DEN, TB, ALU.add, [pk(12), pk(14)], [pk(12)])
            tr.op("dve", lambda e: e.reciprocal(out=DEN, in_=DEN), [pk(12)], [pk(12)])
            VTT(WR, TA, LR, ALU.mult, [pk(13), pk(0)], [pk(10)])
            VTT(TB, AI, LI, ALU.mult, [pk(7), pk(1)], [pk(14)])
            VTT(WR, WR, TB, ALU.add, [pk(10), pk(14)], [pk(10)])
            VTT(WR, WR, DEN, ALU.mult, [pk(10), pk(12)], [pk(10)])
            VTT(WI, AI, LR, ALU.mult, [pk(7), pk(0)], [pk(11)])
            VTT(TB, TA, LI, ALU.mult, [pk(13), pk(1)], [pk(14)])
            VTT(WI, WI, TB, ALU.subtract, [pk(11), pk(14)], [pk(11)])
            VTT(WI, WI, DEN, ALU.mult, [pk(11), pk(12)], [pk(11)])
            tr.dma("sp", "c1", io[:], c_io.ap(), (), ["io"])
            angb = ANG.unsqueeze(2).broadcast_to([128, 32, 128]); iob = io[:].unsqueeze(1).broadcast_to([128, 32, 128])
            VTT(tabS[:], angb, iob, ALU.mult, [pk(4), "io"], ["tabS"])

            def rred(tb, key):
                VTS(tmpA[:], tb[:], 1.0 / TWO_PI, None, ALU.mult, None, [key], ["tmpA"])
                VCP(tmpB[:], tmpA[:], ["tmpA"], ["tmpB"])
                VCP(tmpA[:], tmpB[:], ["tmpB"], ["tmpA"])
                STT(tb[:], tmpA[:], -TWO_PI, tb[:], ALU.mult, ALU.add, ["tmpA", key], [key])
                VTS(tb[:], tb[:], -3.14159, 3.14159, ALU.max, ALU.min, [key], [key])

            rred(tabS, "tabS")
            VTS(tabC[:], tabS[:], math.pi / 2, None, ALU.add, None, ["tabS"], ["tabC"])
            rred(tabC, "tabC")
            ACT(tabS[:], tabS[:], AF.Sin, ["tabS"], ["tabS"])
            ACT(tabC[:], tabC[:], AF.Sin, ["tabC"], ["tabC"])
            ACP(tabCb[:], tabC[:], ["tabC"], ["tabCb"])
            ACP(tabSb[:], tabS[:], ["tabS"], ["tabSb"])
            for j_, col_ in enumerate((127, NS - 1)):
                VCP(tabCl[:, j_, :], tabC[:, :, col_], ["tabC"], ["tabCl"])
                VCP(tabSl[:, j_, :], tabS[:, :, col_], ["tabS"], ["tabSl"])
            for j_ in range(2):
                VCP(TLa[:, j_, 0, :], tabCl[:, j_, :], ["tabCl"], ["TLa"])
                VCP(TLa[:, j_, 1, :], tabSl[:, j_, :], ["tabSl"], ["TLa"])
                VCP(TLb[:, j_, 0, :], tabSl[:, j_, :], ["tabSl"], ["TLb"])
                VCP(TLb[:, j_, 1, :], tabCl[:, j_, :], ["tabCl"], ["TLb"])
            for j_ in range(2):
                VCP(RHO2[:, j_, :], RHO, [pk(5)], ["RHO2"])
            tr.barrier()
            es3.close()
            BBw = SBT(es2, "BBw", [128, 64, 128], BF16)
            Bn = SBT(es2, "Bn", [128, 2, 32, 16]); bb = SBT(es2, "bb", [128, 2, 32, 16]); btmp = SBT(es2, "btmp", [128, 32, 16])
            Cn = SBT(es2, "Cn", [128, 2, 8, 64]); Cw = SBT(es2, "Cw", [128, 2, 8, 128], BF16)
            tr.dma("sp", "c1", Bn[:, 0], dap(bre, l * 65536, [[16, 128], [2048, 32], [1, 16]]), (), ["Bn"])
            tr.dma("sp", "c1", Bn[:, 1], dap(bim, l * 65536, [[16, 128], [2048, 32], [1, 16]]), (), ["Bn"])
            wrb = WR.unsqueeze(2).broadcast_to([128, 32, 16]); wib = WI.unsqueeze(2).broadcast_to([128, 32, 16])
            VTT(bb[:, 0], Bn[:, 0], wrb, ALU.mult, ["Bn", pk(10)], ["bb"])
            VTT(btmp[:], Bn[:, 1], wib, ALU.mult, ["Bn", pk(11)], ["btmp"])
            VTT(bb[:, 0], bb[:, 0], btmp[:], ALU.subtract, ["bb", "btmp"], ["bb"])
            VTT(bb[:, 1], Bn[:, 1], wrb, ALU.mult, ["Bn", pk(10)], ["bb"])
            VTT(btmp[:], Bn[:, 0], wib, ALU.mult, ["Bn", pk(11)], ["btmp"])
            VTT(bb[:, 1], bb[:, 1], btmp[:], ALU.add, ["bb", "btmp"], ["bb"])
            MEMSET(BBw[:], 0.0, ["BBw"])
            BBw4 = BBw[:].rearrange("p (q r) c -> p q r c", r=2)
            for gs in range(2):
                for r in range(4):
                    for ri in range(2):
                        VCP(BBw4[64 * gs:64 * gs + 64, r::4, ri, 32 * r + 16 * gs:32 * r + 16 * gs + 16],
                            bb[64 * gs:64 * gs + 64, ri, r::4, :], ["bb", "BBw"], ["BBw"])
            for g4 in range(16):
                for j in range(4):
                    TRANSP(psT[:, 128 * j:128 * j + 128], BBw[:, 4 * g4 + j, :], 128, ["BBw"], ["psT"])
                VCP(SinT[:, 4 * g4:4 * g4 + 4, :], psT[:, 0:512].rearrange("p (a b) -> p a b", b=128), ["psT"], ["SinT"])
            tr.dma("sp", "c1", Cn[:, 0], dap(cre, l * 65536, [[64, 128], [8192, 8], [1, 64]]), (), ["Cn"])
            tr.dma("sp", "c1", Cn[:, 1], dap(cim, l * 65536, [[64, 128], [8192, 8], [1, 64]]), (), ["Cn"])
            VTS(Cn[:, 1], Cn[:, 1], -1.0, None, ALU.mult, None, ["Cn"], ["Cn"])
            for ri in range(2):
                for gs in range(2):
                    VTS(Cw[:, ri, :, 64 * gs:64 * gs + 64], Cn[:, ri], par[:, gs:gs + 1], None, ALU.mult, None, ["Cn", "par"], ["Cw"])
            MEMSET(Cst[:], 0.0, ["Cst"])
            Cst4 = Cst[:].rearrange("p (c r i) m -> p c r i m", r=4, i=2)
            for ri in range(2):
                for c4 in range(2):
                    for j in range(4):
                        TRANSP(psT[:, 128 * j:128 * j + 128], Cw[:, ri, 4 * c4 + j, :], 128, ["Cw"], ["psT"])
                    pv = psT[:, 0:512].rearrange("p (a b) -> p a b", b=128)
                    for r in range(4):
                        VCP(Cst4[:, 4 * c4:4 * c4 + 4, r, ri, 32 * r:32 * r + 32], pv[:, :, 32 * r:32 * r + 32], ["psT", "Cst"], ["Cst"])
            MEMSET(Hc[:, 0], 0.0, ["Hc0"])
            for b in range(2):
                load_GP(Hc[:, 1 + b, 0, :], "Hc%d" % (1 + b), sre, (l * 2 + b) * 4096)
                load_GP(Hc[:, 1 + b, 1, :], "Hc%d" % (1 + b), sim, (l * 2 + b) * 4096)
            tr.barrier()
            stop_here("ssmsetup_%d" % l, dict(P=P[:], tabC=tabC[:], tabS=tabS[:], SinT=SinT[:], Cst=Cst[:], Hc=Hc[:], bb=bb[:]))
            es2.close()
            uTcs = [SBT(es, "uTc%d" % i, [128, NT], BF16) for i in range(2)]
            wus = [SBT(es, "wu%d" % i, [128, 8, 128], BF16) for i in range(2)]
            d0s = [SBT(es, "d0%d" % i, [128, 4, 128]) for i in range(2)]
            W2 = SBT(es, "W2", [128, 2, 4, 128])
            W = [W2[:, i] for i in range(2)]
            Z2 = [SBT(es, "Z2%d" % j, [128, 2, 4, 128]) for j in range(2)]
            Z = [[Z2[j][:, i] for j in range(2)] for i in range(2)]
            RH2 = SBT(es, "RH2", [128, 2, 4]); cs2 = SBT(es, "cs2", [128, 2, 2, 4])
            T1 = SBT(es, "T1", [128, 4, 128], BF16); T2 = SBT(es, "T2", [128, 4, 128], BF16)
            Q1 = SBT(es, "Q1", [128, 4, 128], BF16); Q2 = SBT(es, "Q2", [128, 4, 128], BF16)
            BUb = [SBT(es, "BUb%d" % i, [128, 4, 128], BF16) for i in range(2)]
            Zb = [SBT(es, "Zb%d" % i, [128, 4, 128], BF16) for i in range(2)]
            Hf = [[SBT(es, "Hf%d%d" % (i, j), [128, 4, 128]) for j in range(2)] for i in range(2)]
            Hb = [[SBT(es, "Hb%d%d" % (i, j), [128, 4, 128], BF16) for j in range(2)] for i in range(2)]
            yts = [SBT(es, "yt%d" % i, [128, 128]) for i in range(2)]
            cs = SBT(es, "cs", [128, 4, 4])

            def load_wu(cc):
                tr.dma("pool", "wu", wus[cc % 2][:], dap(w_in, l * D * INC + 4096 + 128 * cc, [[INC, 128], [INC * 128, 8], [1, 128]]), (), ["wu%d" % (cc % 2)])

            load_wu(0)
            ucnt2 = [0]
            for c in range(8):
                wu = wus[c % 2]; wuk = "wu%d" % (c % 2)
                uTc = uTcs[c % 2]; uk = "uTc%d" % (c % 2)
                d0 = d0s[c % 2]; dk = "d0%d" % (c % 2)
                if c + 1 < 8:
                    load_wu(c + 1)
                for bi, (b0, bn) in enumerate(blocks):
                    u_ = ucnt2[0] % 2; ucnt2[0] += 1
                    pb = ps[5 + u_]; pkk = "ps%d" % (5 + u_)
                    tr.mms([(pb[:, 0:bn], wu[:, kc, :], xnT[:, kc, b0:b0 + bn], dict(start=(kc == 0), stop=(kc == 7))) for kc in range(8)],
                           [wuk, "xnT"], [pkk])
                    ACP(uTc[:, b0:b0 + bn], pb[:, 0:bn], [pkk], [uk])
                VCP(d0[:], RHO[:, 4 * c:4 * c + 4].unsqueeze(2).broadcast_to([128, 4, 128]), [pk(5)], [dk], eng="pool")
                MEMSET(d0[:, :, 0:1], 0.0, [dk], eng="pool")

                def emit_bu(ti):
                    kind, i, nt, c0 = tiles[ti]
                    for ri in range(2):
                        tr.mms([(ps[ri][:, 128 * r:128 * r + nt], SinT[:, (4 * c + r) * 2 + ri, :], uTc[:, c0:c0 + nt], dict(start=True, stop=True)) for r in range(4)],
                               ["SinT", uk], ["ps%d" % ri])

                def emit_cast(ti):
                    kind, i, nt, c0 = tiles[ti]
                    for ri in range(2):
                        ACP(BUb[ri][:, :, 0:nt], ps[ri][:].rearrange("p (r i) -> p r i", i=128)[:, :, 0:nt], ["ps%d" % ri], ["BUb%d" % ri])

                def emit_in(ti):
                    kind, i, nt, c0 = tiles[ti]
                    par_ = ti % 2
                    slot = 0 if kind == "p" else 1 + i
                    hk = "Hc%d" % slot
                    tcs = tabCb[:, 4 * c:4 * c + 4, 0:nt]; tss = tabSb[:, 4 * c:4 * c + 4, 0:nt]
                    bur = BUb[0][:, :, 0:nt]; bui = BUb[1][:, :, 0:nt]
                    VTT(T1[:, :, 0:nt], bur, tcs, ALU.mult, ["BUb0", "tabCb"], ["T1"])
                    VTT(T2[:, :, 0:nt], bui, tss, ALU.mult, ["BUb1", "tabSb"], ["T2"])
                    VTT(W[0][:, :, 0:nt], T1[:, :, 0:nt], T2[:, :, 0:nt], ALU.add, ["T1", "T2"], ["W0"])
                    VTT(T1[:, :, 0:nt], bui, tcs, ALU.mult, ["BUb1", "tabCb"], ["T1"])
                    VTT(T2[:, :, 0:nt], bur, tss, ALU.mult, ["BUb0", "tabSb"], ["T2"])
                    VTT(W[1][:, :, 0:nt], T1[:, :, 0:nt], T2[:, :, 0:nt], ALU.subtract, ["T1", "T2"], ["W1"])
                    VTT(RH2[:], Hc[:, slot, :, 4 * c:4 * c + 4], RHO2[:, :, 4 * c:4 * c + 4], ALU.mult, [hk, "RHO2"], ["RH2"])
                    VTT(W2[:, :, :, 0:1], W2[:, :, :, 0:1], RH2[:].unsqueeze(3), ALU.add, ["W0", "W1", "RH2"], ["W0", "W1"])
                    for ri in range(2):
                        Zt = Z[ri][par_]; zk = "Z%d%d" % (ri, par_)
                        if nt == 128:
                            zin = W[ri].rearrange("p r i -> p (r i)"); zout = Zt.rearrange("p r i -> p (r i)")
                            dd = d0[:].rearrange("p r i -> p (r i)")
                            tr.op("dve", lambda e: e.tensor_tensor_scan(out=zout, data0=dd, data1=zin, initial=0.0, op0=ALU.mult, op1=ALU.add),
                                  ["W%d" % ri, dk], [zk])
                        else:
                            for r in range(4):
                                tr.op("dve", lambda e: e.tensor_tensor_scan(out=Zt[:, r, 0:nt], data0=d0[:, r, 0:nt], data1=W[ri][:, r, 0:nt],
                                                                            initial=0.0, op0=ALU.mult, op1=ALU.add),
                                      ["W%d" % ri, dk], [zk])
                    jl_ = 0 if nt == 128 else 1
                    zl = Z2[par_][:, :, :, nt - 1]
                    zk0 = "Z0%d" % par_; zk1 = "Z1%d" % par_
                    VTT(cs2[:, 0], zl, TLa[:, jl_, :, 4 * c:4 * c + 4], ALU.mult, [zk0, zk1, "TLa"], ["cs2a"])
                    VTT(cs2[:, 1], zl, TLb[:, jl_, :, 4 * c:4 * c + 4], ALU.mult, [zk0, zk1, "TLb"], ["cs2b"])
                    VTT(Hc[:, slot, 0, 4 * c:4 * c + 4], cs2[:, 0, 0, :], cs2[:, 0, 1, :], ALU.subtract, ["cs2a", hk], [hk])
                    VTT(Hc[:, slot, 1, 4 * c:4 * c + 4], cs2[:, 1, 0, :], cs2[:, 1, 1, :], ALU.add, ["cs2b", hk], [hk])

                def emit_out(ti):
                    kind, i, nt, c0 = tiles[ti]
                    par_ = ti % 2
                    tcs = tabCb[:, 4 * c:4 * c + 4, 0:nt]; tss = tabSb[:, 4 * c:4 * c + 4, 0:nt]
                    for ri in range(2):
                        ACP(Zb[ri][:, :, 0:nt], Z[ri][par_][:, :, 0:nt], ["Z%d%d" % (ri, par_)], ["Zb%d" % ri])
                    zr = Zb[0][:, :, 0:nt]; zi = Zb[1][:, :, 0:nt]
                    VTT(Q1[:, :, 0:nt], zr, tcs, ALU.mult, ["Zb0", "tabCb"], ["Q1"])
                    VTT(Q2[:, :, 0:nt], zi, tss, ALU.mult, ["Zb1", "tabSb"], ["Q2"])
                    VTT(Hb[0][par_][:, :, 0:nt], Q1[:, :, 0:nt], Q2[:, :, 0:nt], ALU.subtract, ["Q1", "Q2"], ["Hb0%d" % par_])
                    VTT(Q1[:, :, 0:nt], zr, tss, ALU.mult, ["Zb0", "tabSb"], ["Q1"])
                    VTT(Q2[:, :, 0:nt], zi, tcs, ALU.mult, ["Zb1", "tabCb"], ["Q2"])
                    VTT(Hb[1][par_][:, :, 0:nt], Q1[:, :, 0:nt], Q2[:, :, 0:nt], ALU.add, ["Q1", "Q2"], ["Hb1%d" % par_])

                def emit_y(ti):
                    kind, i, nt, c0 = tiles[ti]
                    pb = ps[2 + ti % 2]; pkk = "ps%d" % (2 + ti % 2)
                    yt = yts[ti % 2]; yk = "yt%d" % (ti % 2)
                    par_ = ti % 2
                    lst = []
                    for r in range(4):
                        for ri in range(2):
                            lst.append((pb[:, 0:nt], Cst[:, (4 * c + r) * 2 + ri, :], Hb[ri][par_][:, r, 0:nt], dict(start=(len(lst) == 0), stop=(len(lst) == 7))))
                    lst[-1][3]["stop"] = False
                    lst.append((pb[:, 0:nt], Dd[:, c, :], uTc[:, c0:c0 + nt], dict(start=False, stop=True)))
                    tr.mms(lst, ["Cst", "Hb0%d" % par_, "Hb1%d" % par_, "Dd", uk], [pkk])
                    ACT(ygT[:, c, c0:c0 + nt], pb[:, 0:nt], AF.Gelu, [pkk], ["ygT"])

                ntl = len(tiles)
                emit_bu(0)
                emit_cast(0)
                for ti in range(ntl):
                    emit_in(ti)
                    if ti + 1 < ntl:
                        emit_bu(ti + 1)
                        emit_cast(ti + 1)
                    emit_out(ti)
                    if ti >= 1:
                        emit_y(ti - 1)
                emit_y(ntl - 1)
            for slot, (tr_, ti_) in enumerate(((hr_p, hi_p), (hr_s, hi_s), (hr_s, hi_s))):
                for ri, tt in enumerate((tr_, ti_)):
                    off = l * 4096 if slot == 0 else (l * 2 + slot - 1) * 4096
                    store_GP(tt, off, Hc[:, slot, ri, :], "Hc%d" % slot, ("oh", l, slot, ri))
            tr.barrier()
            stop_here("ssm_%d" % l, dict(ygT=ygT[:], Hc=Hc[:]))

        oes = contextlib.ExitStack()
        oT = SBT(oes, "oT", [128, 8, NT], BF16)
        with contextlib.ExitStack() as es:
            wbs = [SBT(es, "wb%d" % i, [128, 8, 512], BF16) for i in range(2)]
            Qt = [SBT(es, "Qt%d" % m, [128, NT], BF16) for m in range(2)]
            Kt = [SBT(es, "Kt%d" % m, [128, NT], BF16) for m in range(2)]
            Kp = [[SBT(es, "Kp%d%d" % (b, m), [128, PAST + NS], BF16) for m in range(2)] for b in range(2)]
            Vx = SBT(es, "Vx", [128, 18, 160], BF16)
            Vp = [SBT(es, "Vp%d" % b, [128, 8, 160], BF16) for b in range(2)]
            Gt = SBT(es, "Gt", [128, 18, 128])
            Pb = [SBT(es, "Pb%d" % i, [128, 512], BF16) for i in range(4)]
            sqs = [SBT(es, "sq%d" % i, [128, 512], BF16) for i in range(2)]
            rss = [SBT(es, "rs%d" % i, [128, 512]) for i in range(2)]
            ksts = [SBT(es, "kst%d" % i, [128, 2, 64]) for i in range(4)]
            vsts = [SBT(es, "vst%d" % i, [128, 128]) for i in range(4)]
            sgs = [SBT(es, "sg%d" % i, [128, 128]) for i in range(2)]
            ckbs = [SBT(es, "ckb%d" % i, [128, 8, 128], BF16) for i in range(2)]
            osbs = [SBT(es, "osb%d" % i, [128, 128]) for i in range(4)]
            ots = [SBT(es, "ot%d" % i, [128, 128]) for i in range(4)]
            onbs = [SBT(es, "onb%d" % i, [128, 128], BF16) for i in range(4)]
            fss = [SBT(es, "fs%d" % i, [128, 8]) for i in range(4)]
            onesbd = SBT(es, "onesbd", [128, 128], BF16)
            MEMSET(onesbd[:], 0.0, ["onesbd"])
            MEMSET(onesbd[0:64, 0:64], 1.0, ["onesbd"])
            MEMSET(onesbd[64:128, 64:128], 1.0, ["onesbd"])
            MEMSET(Vx[:, :, 128:129], 1.0, ["Vx"])
            for b in range(2):
                MEMSET(Vp[b][:, :, 128:129], 1.0, ["Vp%d" % b])
            pcount = [0]
            fcount = [0]
            ocnt = [0]
            Ocp = [[SBT(es, "Ocp%d%d" % (i, j), [128, 387]) for j in range(3)] for i in range(2)]
            wbase = l * D * INC

            def load_wb(hh):
                wbt = wbs[hh % 2]; wk = "wb%d" % (hh % 2)
                for j, colo in enumerate((hh * 64, 512 + hh * 64, 1024 + hh * 64, 1536 + hh * 64)):
                    tr.dma("pool", "wb", wbt[:, :, 64 * j:64 * j + 64], dap(w_in, wbase + colo, [[INC, 128], [INC * 128, 8], [1, 64]]), (), [wk])
                tr.dma("pool", "wb", wbt[:, :, 256:384], dap(w_in, wbase + 2048 + hh * 128, [[INC, 128], [INC * 128, 8], [1, 128]]), (), [wk])
                tr.dma("pool", "wb", wbt[:, :, 384:512], dap(w_in, wbase + 3072 + hh * 128, [[INC, 128], [INC * 128, 8], [1, 128]]), (), [wk])

            load_wb(0)
            for h in range(H):
                wb = wbs[h % 2]; wk = "wb%d" % (h % 2)
                for m in range(2):
                    tr.dma("pool", "aug", Qt[m][64:68, 0:T], dap(c_qa, h * T, [[H * T, 4], [1, T]]), (), ["QtA%d" % m])
                    tr.dma("pool", "aug", Kt[m][64:68, 0:T], dap(c_ka, h * T, [[H * T, 4], [1, T]]), (), ["KtA%d" % m])
                    for b in range(2):
                        tr.dma("pool", "aug", Qt[m][64:68, T + NS * b:T + NS * b + NS], dap(c_qa, h * T + PAST, [[H * T, 4], [1, NS]]), (), ["QtA%d" % m])
                        tr.dma("pool", "aug", Kp[b][m][64:68, :], dap(c_ka, h * T, [[H * T, 4], [1, PAST + NS]]), (), ["KpA%d%d" % (b, m)])
                for b in range(2):
                    for m in range(2):
                        tr.dma("pool", "ck", ckbs[b][:, :, 64 * m:64 * m + 64],
                               dap(ck, ((l * 2 + b) * PAST) * 1024 + m * 512 + h * 64, [[1024, 128], [1024 * 128, 8], [1, 64]]), (), ["ckb%d" % b])
                    tr.dma("pool", "cv", Vp[b][:, :, 0:128], dap(cv, ((l * 2 + b) * PAST) * 1024 + h * 128, [[1024, 128], [1024 * 128, 8], [1, 128]]), (), ["Vp%d" % b])
                if h + 1 < H:
                    load_wb(h + 1)
                def emit_vga(ti):
                    kind, i, n, c0 = tiles[ti]
                    u_ = ti % 2
                    pbv = ps[2 + u_]; pkv = "ps%d" % (2 + u_)
                    pbg = ps[0 + u_]; pkg = "ps%d" % (0 + u_)
                    r4 = ti % 4
                    vst = vsts[r4]; vk_ = "vst%d" % r4; sg = sgs[ti % 2]; gk_ = "sg%d" % (ti % 2)
                    tr.mms([(pbv[0:n, 0:128], xnT[:, kc, c0:c0 + n], wb[:, kc, 256:384], dict(start=(kc == 0), stop=(kc == 7))) for kc in range(8)],
                           [wk, "xnT"], [pkv])
                    tr.mms([(pbg[0:n, 0:128], xnT[:, kc, c0:c0 + n], wb[:, kc, 384:512], dict(start=(kc == 0), stop=(kc == 7))) for kc in range(8)],
                           [wk, "xnT"], [pkg])
                    VCP(vst[0:n, :], pbv[0:n, 0:128], [pkv], [vk_])
                    VCP(Vx[0:n, ti, 0:128], pbv[0:n, 0:128], [pkv], ["Vx"])
                    if kind == "p":
                        dst = dap(nv_p, (l * T + c0) * 1024 + h * 128, [[1024, n], [1, 128]])
                    else:
                        dst = dap(nv_s, ((l * 2 + i) * NS) * 1024 + h * 128, [[1024, n], [1, 128]])
                    tr.dma("sp", "o_v", dst, vst[0:n, :], [vk_], [("ov", l, h, ti)], is_out=True)
                    ACT(sg[0:n, :], pbg[0:n, 0:128], AF.Silu, [pkg], [gk_])
                    VTT(Gt[0:n, ti, :], sg[0:n, :], subB[0:n, :], ALU.mult, [gk_, "subB"], ["Gt"])

                vga_next = [0]
                for pr in range(2):
                    isq = pr == 0
                    dsts = Qt if isq else Kt
                    dks = ["Qt0", "Qt1"] if isq else ["Kt0", "Kt1"]
                    gci = 0 if isq else 1
                    for (b0, bn) in blocks:
                        u_ = pcount[0] % 2; pcount[0] += 1
                        pb = ps[5 + u_]; pkk = "ps%d" % (5 + u_)
                        sq = sqs[u_]; rs = rss[u_]; sk_ = "sq%d" % u_; rk_ = "rs%d" % u_
                        tr.mms([(pb[:, 0:bn], wb[:, kc, 128 * pr:128 * pr + 128], xnT[:, kc, b0:b0 + bn], dict(start=(kc == 0), stop=(kc == 7))) for kc in range(8)],
                               [wk, "xnT"], [pkk])
                        ACT(sq[:, 0:bn], pb[:, 0:bn], AF.Square, [pkk], [sk_])
                        tr.mms([(ps[4][:, 0:bn], onesbd[:], sq[:, 0:bn], dict(start=True, stop=True))], [sk_, "onesbd"], ["ps4"])
                        ACT(rs[:, 0:bn], ps[4][:, 0:bn], AF.Ln, ["ps4", "epsc"], [rk_], scale=1.0 / 64, bias=epsc[:, :])
                        ACT(rs[:, 0:bn], rs[:, 0:bn], AF.Exp, [rk_], [rk_], scale=-0.5)
                        for m in range(2):
                            STT(dsts[m][0:64, b0:b0 + bn], pb[64 * m:64 * m + 64, 0:bn], pp[64 * m:64 * m + 64, gci:gci + 1], rs[64 * m:64 * m + 64, 0:bn],
                                ALU.mult, ALU.mult, [pkk, rk_, "pp0", "pp1"], [dks[m]])
                while vga_next[0] < len(tiles):
                    emit_vga(vga_next[0]); vga_next[0] += 1
                stop_here("attnB", dict(Qt0=Qt[0][:], Kt0=Kt[0][:], Qt1=Qt[1][:], Kt1=Kt[1][:]))
                bg = []

                def mk_newk(ti):
                    def f():
                        kind, i, n, c0 = tiles[ti]
                        r4 = ti % 4; r2 = ti % 2
                        kst = ksts[r4]; kk_ = "kst%d" % r4; tk_ = "psT"
                        for m in range(2):
                            TRANSP(psT[0:n, 256 + 128 * r2 + 64 * m:256 + 128 * r2 + 64 * m + 64], Kt[m][0:64, c0:c0 + n], 64, ["Kt%d" % m], [tk_])
                        VCP(kst[0:n].rearrange("p m d -> p (m d)"), psT[0:n, 256 + 128 * r2:256 + 128 * r2 + 128], [tk_], [kk_])
                        if kind == "p":
                            dst = dap(nk_p, (l * T + c0) * 1024 + h * 64, [[1024, n], [512, 2], [1, 64]])
                        else:
                            dst = dap(nk_s, ((l * 2 + i) * NS) * 1024 + h * 64, [[1024, n], [512, 2], [1, 64]])
                        tr.dma("sp", "o_k", dst, kst[0:n], [kk_], [("ok", l, h, ti)], is_out=True)
                    return f

                def mk_cache(b, m):
                    def f():
                        ckb = ckbs[b]
                        for tt in range(8):
                            TRANSP(psT[0:64, 128 * tt:128 * tt + 128], ckb[:, tt, 64 * m:64 * m + 64], 128, ["ckb%d" % b], ["psT"])
                        VCP(Kp[b][m][0:64, 0:PAST], psT[0:64, :], ["psT"], ["Kp%d%d" % (b, m)])
                        VCP(Kp[b][m][0:64, PAST:PAST + NS], Kt[m][0:64, T + NS * b:T + NS * b + NS], ["Kt%d" % m], ["Kp%d%d" % (b, m)])
                    return f

                for b in range(2):
                    for m in range(2):
                        bg.append(mk_cache(b, m))
                for ti in range(len(tiles)):
                    bg.append(mk_newk(ti))
                stop_here("attnD", dict(Gt=Gt[:]))
                stop_here("attnE", dict(Kp00=Kp[0][0][:]))
                pending = []

                def attend(qcols, qpos0, nq, ktiles, gt_tile):
                    steps = []
                    blocks_ = []
                    for qb0 in range(0, nq, 512):
                        nqb = min(512, nq - qb0)
                        qts = [(qb0 + 128 * i, min(128, nq - qb0 - 128 * i)) for i in range((nqb + 127) // 128)]
                        lastk = {}
                        vis = []
                        for kj, (Kap, Vap, nk, kp0) in enumerate(ktiles):
                            first = None
                            diag = False
                            for qi, (q0, nqi) in enumerate(qts):
                                tpos = qpos0 + q0
                                if kp0 // 64 > (tpos + nqi - 1) // 64:
                                    continue
                                if first is None:
                                    first = qi
                                    diag = (kp0 + nk - 1) > tpos
                                lastk[qi] = kj
                            if first is not None:
                                vis.append((kj, first, diag))
                        blk = dict(qb0=qb0, nqb=nqb, qts=qts, lastk=lastk, started=set(), nsteps=2 * len(vis))
                        blocks_.append(blk)
                        for (kj, first, diag) in vis:
                            for m in range(2):
                                steps.append(dict(blk=blk, kj=kj, first=first, diag=diag, m=m))
                        steps[-1]["last"] = True

                    def emit_S(st):
                        blk = st["blk"]; qts = blk["qts"]; m = st["m"]
                        Kap, Vap, nk, kp0 = ktiles[st["kj"]]
                        c0 = qts[st["first"]][0]; ncol = blk["qb0"] + blk["nqb"] - c0
                        si = pcount[0] % 2; pbi = pcount[0] % 4; pcount[0] += 1
                        S = ps[si]; sk = "ps%d" % si
                        lst = [(S[0:nk, 0:ncol], Kap[m], Qt[m][0:68, qcols + c0:qcols + c0 + ncol], dict(start=True, stop=not st["diag"]))]
                        if st["diag"]:
                            nqd = qts[st["first"]][1]
                            lst.append((S[0:nk, 0:nqd], ident[0:nk, 0:nk], dmat[0:nk, h, 0:nqd], dict(start=False, stop=True)))
                        tr.mms(lst, ["Qt%d" % m, "Kt%d" % m, "Kp0%d" % m, "Kp1%d" % m, "QtA%d" % m, "KtA%d" % m, "KpA0%d" % m, "KpA1%d" % m, "ident", "dmat"], [sk])
                        Pt = Pb[pbi]; pk_ = "Pb%d" % pbi
                        ACT(Pt[0:nk, 0:ncol], S[0:nk, 0:ncol], AF.Exp, [sk], [pk_])
                        st["P"] = (Pt, pk_, c0)

                    def emit_PV(st):
                        blk = st["blk"]; qts = blk["qts"]; m = st["m"]
                        Kap, Vap, nk, kp0 = ktiles[st["kj"]]
                        Pt, pk_, c0 = st["P"]
                        for qi in range(st["first"], len(qts)):
                            q0, nqi = qts[qi]
                            a_ = m * 4 + qi
                            bank = 2 + a_ // 3
                            O = ps[bank][0:nqi, 129 * (a_ % 3):129 * (a_ % 3) + 129]
                            stf = bank not in blk["started"]
                            blk["started"].add(bank)
                            tr.mms([(O, Pt[0:nk, q0 - c0:q0 - c0 + nqi], Vap, dict(start=stf, stop=(blk["lastk"][qi] == st["kj"]), skip_group_check=True))],
                                   [pk_, "Vx", "Vp0", "Vp1"], ["ps%d" % bank])

                    def finalize(blk):
                        qts = blk["qts"]
                        nqt = len(qts)
                        oset = ocnt[0] % 2; ocnt[0] += 1
                        nmax = max(n for (_, n) in qts)
                        used = sorted(set(2 + (m * 4 + qi) // 3 for qi in range(nqt) for m in range(2)))
                        for bank in used:
                            ACP(Ocp[oset][bank - 2][0:nmax, :], ps[bank][0:nmax, 0:387], ["ps%d" % bank], ["ocp%d_%d" % (oset, bank)])
                        Aps = []
                        for qi, (q0, nqi) in enumerate(qts):
                            A = []
                            for m in range(2):
                                a_ = m * 4 + qi
                                bank = 2 + a_ // 3
                                A.append((Ocp[oset][bank - 2][0:nqi, 129 * (a_ % 3):129 * (a_ % 3) + 129], "ocp%d_%d" % (oset, bank)))
                            Aps.append(A)
                        for qi, (q0, n) in enumerate(qts):
                            A = Aps[qi]; fs = fss[qi]; ot = ots[qi]; osb = osbs[qi]
                            fk = "fs%d_" % qi; otk = "ot%d" % qi; osk = "osb%d" % qi
                            tr.op("dve", lambda e: e.reciprocal(out=fs[0:n, 0:1], in_=A[0][0][:, 128:129]), [A[0][1]], [fk + "0"])
                            tr.op("dve", lambda e: e.reciprocal(out=fs[0:n, 1:2], in_=A[1][0][:, 128:129]), [A[1][1]], [fk + "1"])
                            VTT(fs[0:n, 1:2], fs[0:n, 1:2], pp[0:n, 2:3], ALU.mult, [fk + "1", "pp2"], [fk + "1"])
                            VTS(ot[0:n, :], A[1][0][:, 0:128], fs[0:n, 1:2], None, ALU.mult, None, [A[1][1], fk + "1"], [otk])
                            STT(osb[0:n, :], A[0][0][:, 0:128], fs[0:n, 0:1], ot[0:n, :], ALU.mult, ALU.add, [A[0][1], fk + "0", otk], [osk])
                            VTT(ot[0:n, :], osb[0:n, :], osb[0:n, :], ALU.mult, [osk], [otk])
                            RSUM(fs[0:n, 2:3], ot[0:n, :], [otk], [fk + "2"])
                        def stage2():
                            for qi, (q0, n) in enumerate(qts):
                                fs = fss[qi]; fk = "fs%d_" % qi
                                RSQ(fs[0:n, 2:3], fs[0:n, 2:3], 1.0 / 128, n, [fk + "2"], [fk + "2"])
                            for qi, (q0, n) in enumerate(qts):
                                fs = fss[qi]; osb = osbs[qi]; onb = onbs[qi]
                                STT(onb[0:n, :], osb[0:n, :], fs[0:n, 2:3], Gt[0:n, gt_tile(q0), :], ALU.mult, ALU.mult,
                                    ["osb%d" % qi, "fs%d_2" % qi, "Gt"], ["onb%d" % qi])

                        def deferred():
                            for qi, (q0, n) in enumerate(qts):
                                TRANSP(psT[:, 128 * qi:128 * qi + n], onbs[qi][0:n, :], n, ["onb%d" % qi], ["psT"])
                            qa = qts[0][0]; ntot = qts[-1][0] + qts[-1][1] - qa
                            if all(n == 128 for (_, n) in qts):
                                VCP(oT[:, h, qcols + qa:qcols + qa + ntot], psT[:, 0:ntot], ["psT"], ["oT"])
                            else:
                                for qi, (q0, n) in enumerate(qts):
                                    VCP(oT[:, h, qcols + q0:qcols + q0 + n], psT[:, 128 * qi:128 * qi + n], ["psT"], ["oT"])
                        return [[8, stage2], [11, deferred]]

                    emit_S(steps[0])
                    for i, st in enumerate(steps):
                        if i + 1 < len(steps):
                            emit_S(steps[i + 1])
                        emit_PV(st)
                        if bg and i >= 2:
                            bg.pop(0)()
                        for p_ in pending:
                            p_[0] -= 1
                        while pending and pending[0][0] <= 0:
                            pending.pop(0)[1]()
                        if st.get("last"):
                            while pending:
                                pending.pop(0)[1]()
                            pending.extend(finalize(st["blk"]))
                    while bg:
                        bg.pop(0)()

                kts = [([Kt[0][0:68, 128 * j:128 * j + 128], Kt[1][0:68, 128 * j:128 * j + 128]], Vx[:, j, 0:129], 128, 128 * j) for j in range(16)]
                attend(0, 0, T, kts, lambda q0: q0 // 128)
                stop_here("attnF", dict(oT=oT[:]))
                for b in range(2):
                    kts = [([Kp[b][0][0:68, 128 * j:128 * j + 128], Kp[b][1][0:68, 128 * j:128 * j + 128]], Vp[b][:, j, 0:129], 128, 128 * j) for j in range(8)]
                    kts.append(([Kp[b][0][0:68, PAST:PAST + NS], Kp[b][1][0:68, PAST:PAST + NS]], Vx[0:NS, 16 + b, 0:129], NS, PAST))
                    attend(T + NS * b, PAST, NS, kts, lambda q0, b=b: 16 + b)
                while pending:
                    pending.pop(0)[1]()
                stop_here("attnh%d_%d" % (h, l), dict(oT=oT[:], Qt0=Qt[0][:], Kt0=Kt[0][:], Qt1=Qt[1][:], Kt1=Kt[1][:], Vx=Vx[:], Gt=Gt[:], Kp00=Kp[0][0][:], Vp0=Vp[0][:]))
            tr.barrier()
            stop_here("attn_%d" % l, dict(oT=oT[:]))

        with contextlib.ExitStack() as es:
            ysgT = SBT(es, "ysgT", [128, 8, NT], BF16)
            wA = SBT(es, "wA", [128, 8, D], BF16); wB = SBT(es, "wB", [128, 8, D], BF16)
            tmp = [SBT(es, "tmp%d" % i, [128, 512]) for i in range(2)]
            tb16 = SBT(es, "tb16", [128, 512], BF16)
            xr = [SBT(es, "xr%d" % i, [128, D]) for i in range(2)]

            def loadw(dst, key, tnsr, base, rowlen):
                tr.dma("pool", key, dst[:], dap(tnsr, base, [[rowlen, 128], [rowlen * 128, 8], [1, D]]), (), [key])

            def proj(wt, wkey, src, skey, oc, b0, bn, pi):
                pb = ps[pi]
                tr.mms([(pb[:, 0:bn], wt[:, kc, 128 * oc:128 * oc + 128], src[:, kc, b0:b0 + bn], dict(start=(kc == 0), stop=(kc == 7))) for kc in range(8)],
                       [wkey, skey], ["ps%d" % pi])
                return pb[:, 0:bn], "ps%d" % pi

            cnt = [0]
            loadw(wA, "wA", w_glu, l * D * D, D)
            loadw(wB, "wB", w_in, l * D * INC + 5120, INC)
            for oc in range(8):
                for (b0, bn) in blocks:
                    pi = cnt[0] % 2; cnt[0] += 1
                    pa, ka = proj(wA, "wA", ygT, "ygT", oc, b0, bn, pi)
                    ACT(tmp[0][:, 0:bn], pa, AF.Sigmoid, [ka, "pp3"], ["tmp0"], bias=pp[:, 3 + oc:4 + oc])
                    VTT(ysgT[:, oc, b0:b0 + bn], ygT[:, oc, b0:b0 + bn], tmp[0][:, 0:bn], ALU.mult, ["ygT", "tmp0"], [("ysg", oc)])
            for oc in range(8):
                for (b0, bn) in blocks:
                    pi = cnt[0] % 2; cnt[0] += 1
                    pa, ka = proj(wB, "wB", xnT, "xnT", oc, b0, bn, pi)
                    ACT(tb16[:, 0:bn], pa, AF.Silu, [ka], ["tb16"])
                    VTT(ysgT[:, oc, b0:b0 + bn], ysgT[:, oc, b0:b0 + bn], tb16[:, 0:bn], ALU.mult, [("ysg", oc), "tb16"], [("ysg", oc)])
            allysg = [("ysg", oc) for oc in range(8)]
            loadw(wA, "wA", w_sp, l * D * D, D)
            loadw(wB, "wB", w_in, l * D * INC + 7168, INC)
            for oc in range(8):
                for (b0, bn) in blocks:
                    pi = cnt[0] % 2; cnt[0] += 1
                    pg, kg = proj(wB, "wB", xnT, "xnT", oc, b0, bn, 2 + pi)
                    ACT(tmp[0][:, 0:bn], pg, AF.Sigmoid, [kg], ["tmp0"])
                    pb = ps[pi]
                    tr.mms([(pb[:, 0:bn], wA[:, kc, 128 * oc:128 * oc + 128], ysgT[:, kc, b0:b0 + bn], dict(start=(kc == 0), stop=(kc == 7))) for kc in range(8)],
                           ["wA"] + allysg, ["ps%d" % pi])
                    VTT(ygT[:, oc, b0:b0 + bn], pb[:, 0:bn], tmp[0][:, 0:bn], ALU.mult, ["ps%d" % pi, "tmp0"], ["ygT"])
            loadw(wA, "wA", w_ap, l * D * D, D)
            loadw(wB, "wB", w_in, l * D * INC + 6144, INC)
            for oc in range(8):
                for (b0, bn) in blocks:
                    pi = cnt[0] % 2; cnt[0] += 1
                    pg, kg = proj(wB, "wB", xnT, "xnT", oc, b0, bn, 2 + pi)
                    ACT(tmp[0][:, 0:bn], pg, AF.Sigmoid, [kg], ["tmp0"])
                    pa, ka = proj(wA, "wA", oT, "oT", oc, b0, bn, pi)
                    VTT(tmp[1][:, 0:bn], pa, tmp[0][:, 0:bn], ALU.mult, [ka, "tmp0"], ["tmp1"])
                    VTT(ygT[:, oc, b0:b0 + bn], ygT[:, oc, b0:b0 + bn], tmp[1][:, 0:bn], ALU.add, ["ygT", "tmp1"], ["ygT"])
            loadw(wA, "wA", w_o, l * D * D, D)
            for ti, (kind, i, n, c0) in enumerate(tiles):
                xb = xr[ti % 2]; xk = "xr%d" % (ti % 2)
                if l == 0:
                    src = x_p.ap()[128 * i:128 * i + 128, :] if kind == "p" else x_s.ap()[i]
                else:
                    src = xmid.ap()[c0:c0 + n, :]
                tr.dma("sp", xk, xb[0:n, :], src, [("xmid", ti)], [xk])
                for hb in range(2):
                    pi = cnt[0] % 2; cnt[0] += 1
                    pb = ps[pi]
                    tr.mms([(pb[0:n, :], ygT[:, kc, c0:c0 + n], wA[:, kc, 512 * hb:512 * hb + 512], dict(start=(kc == 0), stop=(kc == 7))) for kc in range(8)],
                           ["wA", "ygT"], ["ps%d" % pi])
                    VTT(xb[0:n, 512 * hb:512 * hb + 512], xb[0:n, 512 * hb:512 * hb + 512], pb[0:n, :], ALU.add, [xk, "ps%d" % pi], [xk])
                if l == DEPTH - 1:
                    dst = y_p.ap()[128 * i:128 * i + 128, :] if kind == "p" else y_s.ap()[i]
                    tr.dma("sp", "o_y", dst, xb[0:n, :], [xk], [("oy", ti)], is_out=True)
                else:
                    tr.dma("sp", "o_x", xmid.ap()[c0:c0 + n, :], xb[0:n, :], [xk], [("xmid2", ti)])
            tr.barrier()
            stop_here("merge_%d" % l, dict(mT=ygT[:], ysgT=ysgT[:]))
        oes.close()
        yes.close()

    except StopBuild:
        pass
    tr.finish()
    return nc


_NC = None


def _consts():
    ident = np.eye(128, dtype=np.float32)
    dm = np.zeros((128, H, 128), np.float32)
    s = np.arange(128)[:, None]; t = np.arange(128)[None, :]
    for h in range(H):
        d = np.where(s > t, -2.0 * SLOPES[h] * (s - t), 0.0)
        d = np.where((s // 64) > (t // 64), -30000.0, d)
        dm[:, h, :] = d
    pos = np.arange(T)
    hi = (pos // 128) * 128.0; lo = (pos % 128) * 1.0
    qa = np.zeros((4, H, T), np.float32); ka = np.zeros((4, H, T), np.float32)
    for h in range(H):
        qa[0, h] = -SLOPES[h] * hi; qa[1, h] = -SLOPES[h] * lo; qa[2, h] = 1.0; qa[3, h] = 1.0
        ka[0, h] = 1.0; ka[1, h] = 1.0; ka[2, h] = SLOPES[h] * hi; ka[3, h] = SLOPES[h] * lo
    par = np.zeros((128, 2), np.float32)
    g8 = np.arange(128) // 16
    par[:, 0] = (g8 % 2 == 0); par[:, 1] = (g8 % 2 == 1)
    io = np.tile(np.arange(1, 129, dtype=np.float32)[None, :], (128, 1))
    return dict(c_id=ident, c_dm=dm, c_qa=qa, c_ka=ka, c_par=par, c_io=io)


def kernel(x_prompt, x_sample, cache_k, cache_v, state_ssm_re, state_ssm_im,
           norm_g, w_in, q_norm_g, k_norm_g, lam_q1, lam_k1, lam_q2, lam_k2, subln_g,
           w_attn_proj, ssm_lambda_re, ssm_lambda_im, ssm_log_dt, ssm_b_re, ssm_b_im,
           ssm_c_re, ssm_c_im, ssm_d, w_glu, b_glu, w_ssm_proj, w_out):
    global _NC
    if _NC is None:
        _NC = build()
    nc = _NC
    f = lambda a: np.ascontiguousarray(np.asarray(a, dtype=np.float32))
    shared = dict(norm_g=f(norm_g), w_in=f(w_in), qng=f(q_norm_g), kng=f(k_norm_g), lq1=f(lam_q1), lk1=f(lam_k1),
                  lq2=f(lam_q2), lk2=f(lam_k2), subg=f(subln_g), w_ap=f(w_attn_proj), lamr=f(ssm_lambda_re),
                  lami=f(ssm_lambda_im), ldt=f(ssm_log_dt), bre=f(ssm_b_re), bim=f(ssm_b_im), cre=f(ssm_c_re),
                  cim=f(ssm_c_im), dsk=f(ssm_d), w_glu=f(w_glu), b_glu=f(b_glu), w_sp=f(w_ssm_proj), w_o=f(w_out))
    shared.update(_consts())
    x_prompt = np.asarray(x_prompt); x_sample = np.asarray(x_sample)
    cache_k = np.asarray(cache_k); cache_v = np.asarray(cache_v)
    state_ssm_re = np.asarray(state_ssm_re); state_ssm_im = np.asarray(state_ssm_im)
    in_maps = []
    for c in range(NCORES):
        m = dict(shared)
        m["x_p"] = f(x_prompt[c]); m["x_s"] = f(x_sample[2 * c:2 * c + 2])
        m["ck"] = f(cache_k[:, 2 * c:2 * c + 2]); m["cv"] = f(cache_v[:, 2 * c:2 * c + 2])
        m["sre"] = f(state_ssm_re[:, 2 * c:2 * c + 2]); m["sim"] = f(state_ssm_im[:, 2 * c:2 * c + 2])
        in_maps.append(m)
    res = run_bass_kernel_spmd(nc, in_maps, core_ids=list(range(NCORES)))
    R = res.results
    cat0 = lambda k: np.stack([R[c][k] for c in range(NCORES)], axis=0)
    y_p = cat0("y_p")
    y_s = np.concatenate([R[c]["y_s"] for c in range(NCORES)], axis=0)
    nk_p = np.stack([R[c]["nk_p"] for c in range(NCORES)], axis=1)
    nv_p = np.stack([R[c]["nv_p"] for c in range(NCORES)], axis=1)
    hr_p = np.stack([R[c]["hr_p"] for c in range(NCORES)], axis=1)
    hi_p = np.stack([R[c]["hi_p"] for c in range(NCORES)], axis=1)
    nk_s = np.concatenate([R[c]["nk_s"] for c in range(NCORES)], axis=1)
    nv_s = np.concatenate([R[c]["nv_s"] for c in range(NCORES)], axis=1)
    hr_s = np.concatenate([R[c]["hr_s"] for c in range(NCORES)], axis=1)
    hi_s = np.concatenate([R[c]["hi_s"] for c in range(NCORES)], axis=1)
    return tuple(np.ascontiguousarray(a.astype(np.float32)) for a in (y_p, y_s, nk_p, nv_p, hr_p, hi_p, nk_s, nv_s, hr_s, hi_s))
```
